# Optimizing a Trainium2 kernel written in Bass

```python
import jax
import jax.numpy as jnp
from jax import lax
import numpy as np

D_MODEL = 1024
BATCH = 4
SEQ = 4096
DEPTH = 2
DEC_BATCH = 16
DEC_SEQ = 64
PAST_LEN = 2048

CHUNK = 64
N_META = 16
RMS_EPS = 1e-6
N_HEADS = 8
HEAD_DIM = 64
ATTN_WIDTH = N_HEADS * HEAD_DIM
ROT_DIM = HEAD_DIM // 4
ROPE_THETA = 500000.0
IDX_HEADS = 4
IDX_DIM = 64
TOPK_MAX = 256
QBLOCK = 128
PAD_CHUNK = 2 ** 30
LRU_WIDTH = 512
LRU_BLOCKS = 8
LRU_BLOCK_DIM = LRU_WIDTH // LRU_BLOCKS
CONV_WIDTH = 4
LRU_C = 8.0
POOL_WIDTH = 512
POOL_WINDOWS = (2, 4, 8, 16)
POOL_GROUPS = 4
POOL_GROUP_DIM = POOL_WIDTH // POOL_GROUPS
POOL_HIST = 15
N_BRANCH = 3
BRANCH_WIDTH = 512
COL_SIZES = (ATTN_WIDTH, ATTN_WIDTH, ATTN_WIDTH, ATTN_WIDTH, IDX_HEADS * IDX_DIM, IDX_DIM, IDX_HEADS,
             LRU_WIDTH, LRU_WIDTH, POOL_WIDTH, POOL_WIDTH, N_BRANCH * D_MODEL)
N_IN = 4 * ATTN_WIDTH + IDX_HEADS * IDX_DIM + IDX_DIM + IDX_HEADS + 2 * LRU_WIDTH + 2 * POOL_WIDTH + N_BRANCH * D_MODEL

kernel_name = 'hybrid_dsa_rglru_pool_stream_step'


def rmsnorm(x, g):
    xf = x.astype(jnp.float32)
    y = xf * lax.rsqrt(jnp.mean(xf * xf, axis=-1, keepdims=True) + RMS_EPS) * g.astype(jnp.float32)
    return y.astype(x.dtype)


def rope(x, pos):
    half = ROT_DIM // 2
    inv = ROPE_THETA ** (-jnp.arange(0, ROT_DIM, 2, dtype=jnp.float32) / ROT_DIM)
    ang = pos.astype(jnp.float32)[:, None] * inv[None, :]
    cos = jnp.cos(ang)[None, :, None, :]
    sin = jnp.sin(ang)[None, :, None, :]
    xf = x.astype(jnp.float32)
    x1 = xf[..., :half]
    x2 = xf[..., half:ROT_DIM]
    out = jnp.concatenate([x1 * cos - x2 * sin, x2 * cos + x1 * sin, xf[..., ROT_DIM:]], axis=-1)
    return out.astype(x.dtype)


def sparse_attention(q, k, v, qi, ki, wi, q_chunk, k_chunk, topk):
    B, T = q.shape[0], q.shape[1]
    blk = min(QBLOCK, T)
    nb = -(-T // blk)
    pad = nb * blk - T

    def to_blocks(a):
        a = jnp.pad(a, [(0, 0), (0, pad)] + [(0, 0)] * (a.ndim - 2))
        return jnp.moveaxis(a.reshape((B, nb, blk) + a.shape[2:]), 1, 0)

    qc = jnp.pad(q_chunk, (0, pad), constant_values=PAD_CHUNK).reshape(nb, blk)
    scale = HEAD_DIM ** -0.5

    def block(args):
        qb, qib, wib, qcb = args
        s_idx = jax.nn.relu(jnp.einsum('bthd,bsd->bths', qib, ki))
        score = jnp.einsum('bth,bths->bts', wib, s_idx).astype(jnp.float32)
        ok = k_chunk[None, :] <= qcb[:, None]
        score = jnp.where(ok[None], score, -jnp.inf)
        top_val, top_idx = lax.top_k(score, topk)
        valid = jnp.isfinite(top_val)
        kg = jax.vmap(lambda kk, ii: kk[ii])(k, top_idx)
        vg = jax.vmap(lambda vv, ii: vv[ii])(v, top_idx)
        logits = jnp.einsum('bthd,btjhd->bthj', qb, kg).astype(jnp.float32) * scale
        logits = jnp.where(valid[:, :, None, :], logits, -jnp.inf)
        p = jax.nn.softmax(logits, axis=-1).astype(v.dtype)
        return jnp.einsum('bthj,btjhd->bthd', p, vg)

    out = lax.map(block, (to_blocks(q), to_blocks(qi), to_blocks(wi), qc))
    out = jnp.moveaxis(out, 0, 1).reshape((B, nb * blk) + out.shape[3:])
    return out[:, :T]


def rglru(xb, conv_st, h0, conv_w, conv_b, wa, ba, wx, bx, lam):
    B, T, W = xb.shape
    xp = jnp.concatenate([conv_st.astype(xb.dtype), xb], axis=1)
    xc = conv_b + xp[:, 0:T] * conv_w[0]
    for j in range(1, CONV_WIDTH):
        xc = xc + xp[:, j:j + T] * conv_w[j]
    xblk = xc.reshape(B, T, LRU_BLOCKS, LRU_BLOCK_DIM)
    r = jax.nn.sigmoid(jnp.einsum('btnc,ncd->btnd', xblk, wa).reshape(B, T, W) + ba)
    i = jax.nn.sigmoid(jnp.einsum('btnc,ncd->btnd', xblk, wx).reshape(B, T, W) + bx)
    log_a = -LRU_C * r.astype(jnp.float32) * jax.nn.softplus(-lam.astype(jnp.float32))
    a = jnp.exp(log_a)
    b = jnp.sqrt(-jnp.expm1(2.0 * log_a)) * (i * xc).astype(jnp.float32)
    b = b.at[:, 0].add(a[:, 0] * h0.astype(jnp.float32))

    def combine(e1, e2):
        a1, b1 = e1
        a2, b2 = e2
        return a1 * a2, a2 * b1 + b2

    _, h = lax.associative_scan(combine, (a, b), axis=1)
    return h.astype(xb.dtype), xp[:, -(CONV_WIDTH - 1):], h[:, -1].astype(xb.dtype)


def pool_mix(xc, pool_st, n_hist, pool_w, pool_scale):
    B, T, W = xc.shape
    xp = jnp.concatenate([pool_st.astype(xc.dtype), xc], axis=1).astype(jnp.float32)
    cs = jnp.concatenate([jnp.zeros((B, 1, W), jnp.float32), jnp.cumsum(xp, axis=1)], axis=1)
    t = jnp.arange(T)
    outs = []
    for g, w in enumerate(POOL_WINDOWS):
        sl = slice(g * POOL_GROUP_DIM, (g + 1) * POOL_GROUP_DIM)
        upper = cs[:, POOL_HIST + 1:POOL_HIST + 1 + T, sl]
        lower = cs[:, POOL_HIST + 1 - w:POOL_HIST + 1 - w + T, sl]
        cnt = jnp.minimum(w, t + 1 + n_hist).astype(jnp.float32)[None, :, None]
        outs.append((upper - lower) / cnt)
    pooled = jnp.concatenate(outs, axis=-1) - xp[:, POOL_HIST:]
    mixed = jnp.einsum('btgc,gcd->btgd', pooled.reshape(B, T, POOL_GROUPS, POOL_GROUP_DIM),
                       pool_w.astype(jnp.float32)).reshape(B, T, W) * pool_scale.astype(jnp.float32)
    return mixed.astype(xc.dtype), xp[:, -POOL_HIST:].astype(xc.dtype)


def mixer_layer(x, pos, q_chunk, k_chunk, topk, n_hist, k_past, v_past, ki_past, conv_st, lru_st, pool_st,
                norm_g, w_in, conv_w, conv_b, lru_wa, lru_ba, lru_wx, lru_bx, lru_lambda, pool_w, pool_scale,
                w_branch_out, w_out):
    B, T, _ = x.shape
    hn = rmsnorm(x, norm_g)
    proj = jnp.einsum('btd,dn->btn', hn, w_in)
    splits = np.cumsum(COL_SIZES)[:-1].tolist()
    q, k, v, ga, qi, ki, wi, xb, gb, xc, gc, gm = jnp.split(proj, splits, axis=-1)
    q = rope(q.reshape(B, T, N_HEADS, HEAD_DIM), pos)
    k = rope(k.reshape(B, T, N_HEADS, HEAD_DIM), pos)
    v = v.reshape(B, T, N_HEADS, HEAD_DIM)
    qi = rope(qi.reshape(B, T, IDX_HEADS, IDX_DIM), pos)
    ki = rope(ki[:, :, None, :], pos)[:, :, 0]
    k_all = jnp.concatenate([k_past.astype(k.dtype), k], axis=1)
    v_all = jnp.concatenate([v_past.astype(v.dtype), v], axis=1)
    ki_all = jnp.concatenate([ki_past.astype(ki.dtype), ki], axis=1)
    attn = sparse_attention(q, k_all, v_all, qi, ki_all, wi, q_chunk, k_chunk, topk).reshape(B, T, ATTN_WIDTH)
    y_a = attn * jax.nn.silu(ga)
    hb, conv_new, lru_new = rglru(xb, conv_st, lru_st, conv_w, conv_b, lru_wa, lru_ba, lru_wx, lru_bx, lru_lambda)
    y_b = hb * jax.nn.silu(gb)
    hc, pool_new = pool_mix(xc, pool_st, n_hist, pool_w, pool_scale)
    y_c = hc * jax.nn.silu(gc)
    branches = jnp.stack([y_a, y_b, y_c], axis=2)
    proj_b = jnp.einsum('btnc,ncd->btnd', branches, w_branch_out)
    gates = jax.nn.sigmoid(gm.reshape(B, T, N_BRANCH, D_MODEL))
    merged = jnp.sum(gates * proj_b, axis=2)
    out = x + jnp.einsum('btd,de->bte', merged, w_out)
    return out, (k, v, ki, conv_new, lru_new, pool_new)


def run_trunk(x, pos, q_chunk, k_chunk, topk, n_hist, k_past, v_past, ki_past, conv_st, lru_st, pool_st,
              norm_g, w_in, conv_w, conv_b, lru_wa, lru_ba, lru_wx, lru_bx, lru_lambda, pool_w, pool_scale,
              w_branch_out, w_out, final_norm_g):
    news = [[], [], [], [], [], []]
    for l in range(DEPTH):
        x, new = mixer_layer(x, pos, q_chunk, k_chunk, topk, n_hist, k_past[l], v_past[l], ki_past[l],
                             conv_st[l], lru_st[l], pool_st[l], norm_g[l], w_in[l], conv_w[l], conv_b[l],
                             lru_wa[l], lru_ba[l], lru_wx[l], lru_bx[l], lru_lambda[l], pool_w[l],
                             pool_scale[l], w_branch_out[l], w_out[l])
        for lst, arr in zip(news, new):
            lst.append(arr)
    y = rmsnorm(x, final_norm_g)
    k_n, v_n, ki_n, conv_n, lru_n, pool_n = [jnp.stack(lst) for lst in news]
    return y, k_n, v_n, ki_n, conv_n, lru_n, pool_n


def setup_inputs(seed: int = 0) -> dict:
    key = jax.random.key(seed)
    ks = jax.random.split(key, 24)
    f32 = jnp.float32
    nrm = lambda k, shape, s: jax.random.normal(k, shape, f32) * s
    u = jax.random.uniform(ks[16], (DEPTH, LRU_WIDTH), f32, minval=0.9, maxval=0.999)
    a0 = u ** (1.0 / LRU_C)
    lru_lambda = jnp.log(a0) - jnp.log1p(-a0)
    return {
        'x_prompt': nrm(ks[0], (BATCH, SEQ, D_MODEL), 1.0),
        'x_sample': nrm(ks[1], (DEC_BATCH, DEC_SEQ, D_MODEL), 1.0),
        'cache_k': nrm(ks[2], (DEPTH, DEC_BATCH, PAST_LEN, N_HEADS, HEAD_DIM), 1.0),
        'cache_v': nrm(ks[3], (DEPTH, DEC_BATCH, PAST_LEN, N_HEADS, HEAD_DIM), 1.0),
        'cache_kidx': nrm(ks[4], (DEPTH, DEC_BATCH, PAST_LEN, IDX_DIM), 1.0),
        'state_conv': nrm(ks[5], (DEPTH, DEC_BATCH, CONV_WIDTH - 1, LRU_WIDTH), 1.0),
        'state_lru': nrm(ks[6], (DEPTH, DEC_BATCH, LRU_WIDTH), 0.5),
        'state_pool': nrm(ks[7], (DEPTH, DEC_BATCH, POOL_HIST, POOL_WIDTH), 1.0),
        'meta_tokens': nrm(ks[8], (N_META, D_MODEL), 1.0),
        'norm_g': 1.0 + nrm(ks[9], (DEPTH, D_MODEL), 0.02),
        'w_in': nrm(ks[10], (DEPTH, D_MODEL, N_IN), D_MODEL ** -0.5),
        'conv_w': nrm(ks[11], (DEPTH, CONV_WIDTH, LRU_WIDTH), CONV_WIDTH ** -0.5),
        'conv_b': nrm(ks[12], (DEPTH, LRU_WIDTH), 0.01),
        'lru_wa': nrm(ks[13], (DEPTH, LRU_BLOCKS, LRU_BLOCK_DIM, LRU_BLOCK_DIM), LRU_BLOCK_DIM ** -0.5),
        'lru_ba': nrm(ks[14], (DEPTH, LRU_WIDTH), 0.01),
        'lru_wx': nrm(ks[15], (DEPTH, LRU_BLOCKS, LRU_BLOCK_DIM, LRU_BLOCK_DIM), LRU_BLOCK_DIM ** -0.5),
        'lru_bx': nrm(ks[17], (DEPTH, LRU_WIDTH), 0.01),
        'lru_lambda': lru_lambda,
        'pool_w': nrm(ks[18], (DEPTH, POOL_GROUPS, POOL_GROUP_DIM, POOL_GROUP_DIM), POOL_GROUP_DIM ** -0.5),
        'pool_scale': 1.0 + nrm(ks[19], (DEPTH, POOL_WIDTH), 0.02),
        'w_branch_out': nrm(ks[20], (DEPTH, N_BRANCH, BRANCH_WIDTH, D_MODEL), BRANCH_WIDTH ** -0.5),
        'w_out': nrm(ks[21], (DEPTH, D_MODEL, D_MODEL), D_MODEL ** -0.5),
        'final_norm_g': 1.0 + nrm(ks[22], (D_MODEL,), 0.02),
    }


def reference(x_prompt, x_sample, cache_k, cache_v, cache_kidx, state_conv, state_lru, state_pool,
              meta_tokens, norm_g, w_in, conv_w, conv_b, lru_wa, lru_ba, lru_wx, lru_bx, lru_lambda,
              pool_w, pool_scale, w_branch_out, w_out, final_norm_g):
    weights = (norm_g, w_in, conv_w, conv_b, lru_wa, lru_ba, lru_wx, lru_bx, lru_lambda, pool_w, pool_scale,
               w_branch_out, w_out, final_norm_g)
    dt = x_prompt.dtype
    B, S, _ = x_prompt.shape
    meta = jnp.broadcast_to(meta_tokens.astype(dt)[None], (B, N_META, D_MODEL))
    x0 = jnp.concatenate([meta, x_prompt], axis=1)
    pos_p = jnp.arange(N_META + S, dtype=jnp.int32)
    chunk_p = jnp.concatenate([jnp.zeros((N_META,), jnp.int32), jnp.arange(S, dtype=jnp.int32) // CHUNK + 1])
    y_full, k_p, v_p, ki_p, conv_p, lru_p, pool_p = run_trunk(
        x0, pos_p, chunk_p, chunk_p, min(TOPK_MAX, S // 4), 0,
        jnp.zeros((DEPTH, B, 0, N_HEADS, HEAD_DIM), dt), jnp.zeros((DEPTH, B, 0, N_HEADS, HEAD_DIM), dt),
        jnp.zeros((DEPTH, B, 0, IDX_DIM), dt), jnp.zeros((DEPTH, B, CONV_WIDTH - 1, LRU_WIDTH), dt),
        jnp.zeros((DEPTH, B, LRU_WIDTH), dt), jnp.zeros((DEPTH, B, POOL_HIST, POOL_WIDTH), dt),
        *weights)
    y_prompt = y_full[:, N_META:]
    T1 = x_sample.shape[1]
    P = cache_k.shape[2]
    pos_s = P + jnp.arange(T1, dtype=jnp.int32)
    y_sample, k_s, v_s, ki_s, conv_s, lru_s, pool_s = run_trunk(
        x_sample, pos_s, jnp.zeros((T1,), jnp.int32), jnp.zeros((P + T1,), jnp.int32),
        min(TOPK_MAX, (P + T1) // 4), P,
        cache_k, cache_v, cache_kidx, state_conv, state_lru, state_pool, *weights)
    return (y_prompt, y_sample, k_p, v_p, ki_p, conv_p, lru_p, pool_p, k_s, v_s, ki_s, conv_s, lru_s, pool_s)
```

```python
import math
import contextlib
import numpy as np
import concourse.bass as bass
import concourse.mybir as mybir
from concourse.bass_utils import run_bass_kernel_spmd

F32 = mybir.dt.float32
BF16 = mybir.dt.bfloat16
I32 = mybir.dt.int32
ALU = mybir.AluOpType
AF = mybir.ActivationFunctionType
AX = mybir.AxisListType

D = 1024
SEQ = 4096
NMETA = 16
TP = NMETA + SEQ
NIN = 7492
PAST = 2048
TS = 64
NKEY = TP
NKB = 33
TB = 256
NIT = 14
TOPK = 256.0
C_Q, C_K, C_V, C_GA, C_QI, C_KI, C_WI = 0, 512, 1024, 1536, 2048, 2304, 2368
C_XB, C_GB, C_XC, C_GC, C_GM = 2372, 2884, 3396, 3908, 4420
EPS = 1e-6
NEG = -1.0e30
MASK_BIAS = False


class Prog:
    EPOCH = 24000

    def __init__(self, nc, es):
        self.nc = nc
        self.es = es
        self.ops = []
        self.count = {}
        self.lastw = {}
        self.readers = {}
        self.conf = {}
        self.eng = {'pe': nc.tensor, 'act': nc.scalar, 'dve': nc.vector, 'pool': nc.gpsimd, 'sp': nc.sync}

    def alias(self, names_a, names_b):
        for a in names_a:
            for b in names_b:
                self.conf.setdefault(a, set()).add(b)
                self.conf.setdefault(b, set()).add(a)

    def _names(self, r):
        c = self.conf.get(r)
        if c:
            return [r] + list(c)
        return [r]

    stopped = False

    def _rec(self, agent, queue, fn, reads, writes, is_dma):
        if self.stopped:
            return
        seq = self.count.get(agent, 0) + 1
        self.count[agent] = seq
        deps = {}

        def need(a, s):
            if a.startswith('dma:'):
                s = self.count[a] - (1 if a == agent else 0)
                if s <= 0:
                    return
            if deps.get(a, 0) < s:
                deps[a] = s
        for r0 in list(reads) + list(writes):
            for r in self._names(r0):
                lw = self.lastw.get(r)
                if lw is not None:
                    a, s = lw
                    if a == agent and agent == 'pe':
                        continue
                    need(a, s)
        for w0 in writes:
            for w in self._names(w0):
                for (a, s) in self.readers.get(w, ()):
                    if a == agent and agent == 'pe':
                        continue
                    need(a, s)
        for r in reads:
            if r.startswith('ps'):
                for (a, s) in self.readers.get(r, ()):
                    if a != agent:
                        need(a, s)
        for r in reads:
            self.readers.setdefault(r, []).append((agent, seq))
        for w in writes:
            self.lastw[w] = (agent, seq)
            self.readers[w] = []
        self.ops.append((agent, queue, fn, deps, seq, is_dma))

    def op(self, eng, fn, reads=(), writes=()):
        self._rec(eng, eng, fn, reads, writes, False)

    def dma(self, queue, key, fn, reads=(), writes=()):
        self._rec('dma:' + key, queue, fn, reads, writes, True)

    def opk(self, eng, name, reads, writes, kw):
        m = getattr(self.eng[eng], name)
        self._rec(eng, eng, (lambda: m(**kw)), reads, writes, False)

    def dmak(self, queue, key, reads, writes, kw):
        m = self.eng[queue].dma_start
        self._rec('dma:' + key, queue, (lambda: m(**kw)), reads, writes, True)

    def emit(self):
        nc = self.nc
        waited = {}
        plan = []
        sig = set()
        for (agent, queue, fn, deps, seq, is_dma) in self.ops:
            w = waited.setdefault(queue, {})
            waits = []
            for a, s in deps.items():
                if w.get(a, 0) >= s:
                    continue
                w[a] = s
                waits.append((a, s))
                sig.add((a, s))
            if not is_dma:
                pass
            plan.append(waits)
        semmap = {}
        sems = {}
        sigcount = {}

        def get_sem(agent, ep):
            k = (agent, ep)
            if k not in sems:
                sems[k] = self.es.enter_context(nc.semaphore("s_%s_%d" % (agent.replace(':', '_'), ep)))
            return sems[k]
        incinfo = []
        per_dma = self.EPOCH // 16
        for (agent, queue, fn, deps, seq, is_dma) in self.ops:
            if is_dma:
                n = sigcount.get(agent, 0) + 1
                sigcount[agent] = n
                ep = (n - 1) // per_dma
                semmap[(agent, seq)] = (agent, ep, (n - ep * per_dma) * 16)
                incinfo.append((agent, ep, 16))
            elif (agent, seq) in sig:
                n = sigcount.get(agent, 0) + 1
                sigcount[agent] = n
                ep = (n - 1) // self.EPOCH
                semmap[(agent, seq)] = (agent, ep, n - ep * self.EPOCH)
                incinfo.append((agent, ep, 1))
            else:
                incinfo.append(None)
        nw = 0
        for i, (agent, queue, fn, deps, seq, is_dma) in enumerate(self.ops):
            e = self.eng[queue]
            for (a, s) in plan[i]:
                ag, ep, val = semmap[(a, s)]
                e.wait_ge(get_sem(ag, ep), val)
                nw += 1
            ins = fn()
            if incinfo[i] is not None:
                ag, ep, inc = incinfo[i]
                ins.then_inc(get_sem(ag, ep), inc)
        for agent, n in sigcount.items():
            if agent.startswith('dma:'):
                ep = (n - 1) // per_dma
                nc.sync.wait_ge(get_sem(agent, ep), (n - ep * per_dma) * 16)
        self.stats = (len(self.ops), nw, dict(sigcount))


def build_program(debug=None):
    debug = debug or {}
    nc = bass.Bass("TRN2", target_bir_lowering=False)

    def din(name, shape, dt=F32):
        return nc.dram_tensor(name, list(shape), dt, kind="ExternalInput").ap()

    def dout(name, shape, dt=F32):
        return nc.dram_tensor(name, list(shape), dt, kind="ExternalOutput").ap()

    def dint(name, shape, dt=F32):
        return nc.dram_tensor(name, list(shape), dt, kind="Internal").ap()

    xp_in = din("xp", [TP, D])
    xs_in = din("xs", [2, TS, D])
    ck_in = din("ck", [2, 2, PAST, 512])
    cv_in = din("cv", [2, 2, PAST, 512])
    cki_in = din("cki", [2, 2, PAST, 64])
    sconv_in = din("sconv", [2, 2, 3, 512])
    slru_in = din("slru", [2, 2, 512])
    spool_in = din("spool", [2, 2, 15, 512])
    posp_in = din("posp", [128, NKB])
    poss_in = din("poss", [128, 1])
    norm_g = din("norm_g", [2, D])
    w_in = din("w_in", [2, D, NIN])
    conv_w = din("conv_w", [2, 4, 512])
    conv_b = din("conv_b", [2, 512])
    lru_wa = din("lru_wa", [2, 8, 64, 64])
    lru_ba = din("lru_ba", [2, 512])
    lru_wx = din("lru_wx", [2, 8, 64, 64])
    lru_bx = din("lru_bx", [2, 512])
    lru_lam = din("lru_lambda", [2, 512])
    pool_w = din("pool_w", [2, 4, 128, 128])
    pool_scale = din("pool_scale", [2, 512])
    w_bo = din("w_branch_out", [2, 3, 512, D])
    w_out = din("w_out", [2, D, D])
    fin_g = din("final_norm_g", [D])
    y_p = dout("y_p", [SEQ, D])
    k_p = dout("k_p", [2, TP, 512])
    v_p = dout("v_p", [2, TP, 512])
    ki_p = dout("ki_p", [2, TP, 64])
    conv_p = dout("conv_p", [2, 3, 512])
    lru_p = dout("lru_p", [2, 512])
    pool_p = dout("pool_p", [2, 15, 512])
    y_s = dout("y_s", [2, TS, D])
    k_s = dout("k_s", [2, 2, TS, 512])
    v_s = dout("v_s", [2, 2, TS, 512])
    ki_s = dout("ki_s", [2, 2, TS, 64])
    conv_s = dout("conv_s", [2, 2, 3, 512])
    lru_s = dout("lru_s", [2, 2, 512])
    pool_s = dout("pool_s", [2, 2, 15, 512])
    win_bf = dint("win_bf", [2, 15, 128, 8, 512], BF16)
    wbo_bf = dint("wbo_bf", [2, 3, 128, 4, D], BF16)
    wout_bf = dint("wout_bf", [2, 2, 128, 8, 512], BF16)
    xscr_p = dint("xscr_p", [TP, D])
    xscr_s = dint("xscr_s", [2, TS, D])

    WGROUPS = [(C_Q, C_K), (C_K, C_V), (C_V, C_GA), (C_GA, C_QI), (C_QI, C_XB), (C_XB, C_GB), (C_GB, C_XC), (C_XC, C_GC), (C_GC, C_GM)]
    for n_ in (1, 2, 0):
        for e4_ in range(2):
            WGROUPS.append((C_GM + n_ * 1024 + e4_ * 512, C_GM + n_ * 1024 + e4_ * 512 + 512))
    es = contextlib.ExitStack()
    with es:
        P = Prog(nc, es)

        def sb(name, shape, dt=F32):
            return es.enter_context(nc.sbuf_tensor(name, list(shape), dt))

        def ps(name, shape, dt=F32):
            return es.enter_context(nc.psum_tensor(name, list(shape), dt))

        def V(name, r, w, **kw):
            P.opk('dve', name, r, w, kw)

        def A(name, r, w, **kw):
            P.opk('act', name, r, w, kw)

        def G(name, r, w, **kw):
            P.opk('pool', name, r, w, kw)

        def T(name, r, w, **kw):
            P.opk('pe', name, r, w, kw)

        def DMA(q, key, r, w, **kw):
            P.dmak(q, key, r, w, kw)

        def CK(label):
            if debug.get('stop') == label:
                P.stopped = True

        kT = sb("kT", [128, 4, NKEY], BF16)
        Vt = sb("Vt", [128, NKB, 8, 65], BF16)
        kiT = sb("kiT", [128, NKEY], BF16)
        arA = sb("arA", [128, 4128], F32)
        mk = sb("mk", [128, NKEY], BF16)
        mkT = sb("mkT", [128, NKB, 128], BF16)
        NWB = 3
        wbuf = [sb("wbuf%d" % i, [128, 4096], BF16) for i in range(NWB)]
        xt = [sb("xt%d" % i, [128, D], F32) for i in range(2)]
        hn = sb("hn", [128, D], BF16)
        hnT = sb("hnT", [128, 8, TB], BF16)
        gfbc = sb("gfbc", [128, D], F32)
        qT = sb("qT", [128, 4, 2, TB], BF16)
        qiT = sb("qiT", [128, 2, TB], BF16)
        sga = sb("sga", [128, 2, 512], F32)
        yT = [sb("y%sT" % n, [128, 4, TB], BF16) for n in "abc"]
        kst = sb("kst", [128, 2, 512], F32)
        vst = sb("vst", [128, 2, 512], F32)
        kist = sb("kist", [128, 2, 64], F32)
        rt = [sb("rt%d" % i, [128, 8, 8], F32) for i in range(4)]
        qbf = sb("qbf", [128, 512], BF16)
        kbf = sb("kbf", [128, 512], BF16)
        qibf = sb("qibf", [128, 256], BF16)
        kibf = sb("kibf", [128, 2, 64], BF16)
        kif = sb("kif", [128, 64], F32)
        NRL = 4
        rl = [sb("rl%d" % i, [128, 512], F32) for i in range(NRL)]
        pex = [sb("pex%d" % i, [128, 8, 128], BF16) for i in range(2)]
        pm = [sb("pm%d" % i, [128, 8, 128], BF16) for i in range(2)]
        att = sb("att", [128, 8, 64], F32)
        yab = sb("yab", [128, 512], BF16)
        ctmp = sb("ctmp", [128, 4, 16 + TB], F32)
        cpl = sb("cpl", [128, TB], BF16)
        ctg = sb("ctg", [128, TB], F32)
        xcb = sb("xcb", [128, TB], BF16)
        ident = sb("ident", [128, 128], BF16)
        cosP = sb("cosP", [128, NKB, 8], F32)
        sinP = sb("sinP", [128, NKB, 8], F32)
        cosS = sb("cosS", [128, 1, 8], F32)
        sinS = sb("sinS", [128, 1, 8], F32)
        invf = sb("invf", [128, 8], F32)
        posP = sb("posP", [128, NKB], F32)
        posS = sb("posS", [128, 1], F32)
        hpow = sb("hpow", [128, NIT + 1], F32)
        rcnt = sb("rcnt", [128, 4, 16], F32)
        gT = sb("gT", [128, 8], F32)
        cw = sb("cw", [128, 4, 4], F32)
        cb = sb("cb", [128, 4], F32)
        hba = sb("hba", [128, 4], F32)
        hbx = sb("hbx", [128, 4], F32)
        lam = sb("lam", [128, 4], F32)
        clf = sb("clf", [128, 4], F32)
        psc = sb("psc", [128, 4], F32)
        waBD = sb("waBD", [128, 4, 128], BF16)
        wxBD = sb("wxBD", [128, 4, 128], BF16)
        pwB = sb("pwB", [128, 4, 128], BF16)
        convst = sb("convst", [128, 4, 3], F32)
        hst = sb("hst", [128, 4], F32)
        poolst = sb("poolst", [128, 4, 15], F32)
        st_ss = sb("st_ss", [128, 2], F32)
        st_rs = sb("st_rs", [128, 2], F32)
        wabs = sb("wabs", [128, 2, 4], F32)
        wsgn = sb("wsgn", [128, 2, 4], F32)
        bs = sb("bs", [128, 16], F32)
        steps = sb("steps", [128, NIT + 1], F32)
        rec = sb("rec", [128, 8], F32)
        psA = [ps("psA%d" % i, [128, 512], F32) for i in range(4)]
        psT = [ps("psT%d" % i, [128, 1024], BF16) for i in range(2)]
        psB = [ps("psB%d" % i, [128, 512], F32) for i in range(2)]

        identf = arA[:, 0:128]
        onesf = arA[:, 128:256]
        rtmp = arA[:, 256:256 + NKB * 8].rearrange("p (a b) -> p a b", b=8)
        rtmp2 = arA[:, 768:768 + NKB * 8].rearrange("p (a b) -> p a b", b=8)
        rtmpi = arA[:, 1280:1280 + NKB * 8].bitcast(I32).rearrange("p (a b) -> p a b", b=8)
        sc = arA[:, 0:NKEY]
        om = arA[:, 0:4 * TB]
        ixc = arA[:, 4 * TB:8 * TB]
        o = 8 * TB
        Bxp = []
        for i in range(2):
            Bxp.append(arA[:, o:o + 3 + TB]); o += 4 + TB
        Bxc = []
        for i in range(2):
            Bxc.append(arA[:, o:o + TB]); o += TB
        Bta = arA[:, o:o + TB]; o += TB
        Btx = arA[:, o:o + TB]; o += TB
        Ba2 = arA[:, o:o + TB]; o += TB
        assert o <= 4128, o
        o = 8 * TB
        Baj = []
        for i in range(4):
            Baj.append(arA[:, o:o + TB]); o += TB
        Bbb = arA[:, o:o + TB]; o += TB
        Bh = arA[:, o:o + TB]; o += TB
        Btg = arA[:, o:o + TB]; o += TB
        assert o <= 4128, o
        mg = arA[:, 0:8 * TB]
        mtg = [arA[:, 8 * TB + i * TB: 8 * TB + (i + 1) * TB] for i in range(2)]
        mgb = mk[:, 0:8 * TB]
        p1 = ['Bxp0a', 'Bxp0b', 'Bxp1a', 'Bxp1b', 'Bxc0', 'Bxc1', 'Bta', 'Btx', 'Ba2']
        p2 = ['Baj0', 'Baj1', 'Baj2', 'Baj3', 'Bbb', 'Bh', 'Btg']
        P.alias(['sc'], ['om', 'ixc', 'mg', 'mtg0', 'mtg1'] + p1 + p2)
        P.alias(['mg'], ['om', 'ixc'])
        P.alias(p1, p2 + ['mtg0', 'mtg1'])
        P.alias(['mtg0', 'mtg1'], p2)
        P.alias(['onesf', 'identf', 'rtmp', 'rtmp2', 'rtmpi'], ['sc', 'om', 'ixc', 'mg', 'mtg0', 'mtg1'] + p1 + p2)
        P.alias(['mk'], ['mgb', 'mkV', 'mkA'])
        P.alias(['mgb'], ['mkV', 'mkA'])

        build_program.sbuf_left = nc.sbuf_bytes_remaining
        cnt = {'wb': 0, 'pa': 0, 'pt': 0}

        def next_w(pin=False):
            while True:
                i = cnt['wb'] % NWB
                cnt['wb'] += 1
                if i != cnt.get('pin'):
                    break
            if pin:
                cnt['pin'] = i
            return wbuf[i], 'wbuf%d' % i

        def unpin_w():
            cnt['pin'] = None

        pa_ring = [(psA[0], 'psA0'), (psA[1], 'psA1'), (psA[2], 'psA2'), (psA[3], 'psA3'), (psB[0], 'psB0'), (psB[1], 'psB1')]

        def next_pa():
            i = cnt['pa'] % 6
            cnt['pa'] += 1
            return pa_ring[i]

        def next_pt():
            i = cnt['pt'] % 2
            cnt['pt'] += 1
            return psT[i], 'psT%d' % i

        def small_dma(out, in_, r=(), w=(), q='sp', key=None):
            key = key or ('m_' + (list(w) + list(r))[0])
            DMA(q, key, r, w, out=out, in_=in_, allow_slow_non_contiguous=True)

        CK('casts')
        G('memset', (), ['onesf'], ap=onesf, constant=1.0)
        G('affine_select', ['onesf'], ['identf'], out=identf, in_=onesf, pattern=[[-1, 128]],
          compare_op=ALU.is_equal, fill=0.0, base=0, channel_multiplier=1)
        V('tensor_copy', ['identf'], ['ident'], out=ident[:], in_=identf)
        G('memset', (), ['Vt'], ap=Vt[:, :, :, 64:65], constant=1.0)
        G('memset', (), ['qT'], ap=qT[:], constant=0.0)
        for j in range(8):
            G('memset', (), ['invf'], ap=invf[:, j:j + 1], constant=float(500000.0 ** (-(2.0 * j) / 16.0)))
        for k in range(NIT + 1):
            G('memset', (), ['hpow'], ap=hpow[:, k:k + 1], constant=float(0.5 ** k))
        for g in range(4):
            wdw = 2 ** (g + 1)
            for t in range(16):
                G('memset', (), ['rcnt'], ap=rcnt[:, g, t:t + 1], constant=1.0 / float(min(wdw, t + 1)))
        small_dma(posP[:], posp_in, (), ['posP'])
        small_dma(posS[:], poss_in, (), ['posS'])
        small_dma(gfbc[:], fin_g.partition_broadcast(128), (), ['gfbc'])

        def rope_tables(pos, nt, cosT, sinT, tag):
            a3 = rtmp[:, 0:nt, :]
            b3 = rtmp2[:, 0:nt, :]
            i3 = rtmpi[:, 0:nt, :]
            for shift, dst in ((0.0, sinT), (math.pi / 2.0, cosT)):
                V('tensor_tensor', ['pos' + tag, 'invf'], ['rtmp'], out=a3, in0=pos.unsqueeze(2).to_broadcast([128, nt, 8]),
                  in1=invf[:].unsqueeze(1).to_broadcast([128, nt, 8]), op=ALU.mult)
                if shift:
                    V('tensor_scalar', ['rtmp'], ['rtmp'], out=a3, in0=a3, scalar1=shift, scalar2=None, op0=ALU.add)
                V('tensor_scalar', ['rtmp'], ['rtmp2'], out=b3, in0=a3, scalar1=1.0 / (2.0 * math.pi), scalar2=None, op0=ALU.mult)
                V('tensor_copy', ['rtmp2'], ['rtmpi'], out=i3, in_=b3)
                V('tensor_copy', ['rtmpi'], ['rtmp2'], out=b3, in_=i3)
                V('scalar_tensor_tensor', ['rtmp2', 'rtmp'], ['rtmp'], out=a3, in0=b3, scalar=-2.0 * math.pi, in1=a3, op0=ALU.mult, op1=ALU.add)
                V('tensor_scalar', ['rtmp'], ['rtmp'], out=a3, in0=a3, scalar1=3.14159, scalar2=-3.14159, op0=ALU.min, op1=ALU.max)
                A('activation', ['rtmp'], ['rope' + tag], out=dst, in_=a3, func=AF.Sin)

        def cast_weights(l):
            for gi, (c0, c1) in enumerate(WGROUPS):
                DMA('pool', 'c_win%d_%d' % (l, gi), (), ['win_bf%d_%d' % (l, gi)], out=win_bf[l, gi].rearrange("k kt c -> kt k c")[:, :, 0:c1 - c0],
                    in_=w_in[l, :, c0:c1].rearrange("(kt k) c -> kt k c", k=128))
            for n in range(3):
                DMA('pool', 'c_wbo%d' % l, (), ['wbo_bf%d' % l], out=wbo_bf[l, n].rearrange("c ct e -> ct c e"),
                    in_=w_bo[l, n].rearrange("(ct c) e -> ct c e", c=128))
            for r in range(2):
                DMA('pool', 'c_wout%d' % l, (), ['wout_bf%d' % l], out=wout_bf[l, r].rearrange("e et c -> et e c"),
                    in_=w_out[l, :, r * 512:(r + 1) * 512].rearrange("(et e) c -> et e c", e=128))

        cast_weights(0)
        if debug.get('skip_prompt'):
            cast_weights(1)
        CK('consts')
        rope_tables(posP[:], NKB, cosP[:], sinP[:], 'P')
        rope_tables(posS[:], 1, cosS[:], sinS[:], 'S')
        CK('prologue')

        def fm(v):
            return v.rearrange("(j p) -> p j", p=128)

        def layer_setup(l):
            small_dma(gT[:], norm_g[l].rearrange("(j p) -> p j", p=128), (), ['gT'])
            small_dma(cb[:], fm(conv_b[l]), (), ['cb'])
            for tap in range(4):
                small_dma(cw[:, :, tap], fm(conv_w[l, tap]), (), ['cw'])
            small_dma(hba[:], fm(lru_ba[l]), (), ['hba'])
            small_dma(hbx[:], fm(lru_bx[l]), (), ['hbx'])
            small_dma(lam[:], fm(lru_lam[l]), (), ['lam'])
            small_dma(psc[:], fm(pool_scale[l]), (), ['psc'])
            V('tensor_scalar', ['hba'], ['hba'], out=hba[:], in0=hba[:], scalar1=0.5, scalar2=None, op0=ALU.mult)
            V('tensor_scalar', ['hbx'], ['hbx'], out=hbx[:], in0=hbx[:], scalar1=0.5, scalar2=None, op0=ALU.mult)
            A('activation', ['lam'], ['lam'], out=lam[:], in_=lam[:], func=AF.Exp, scale=-1.0)
            A('activation', ['lam'], ['lam'], out=lam[:], in_=lam[:], func=AF.Ln, bias=1.0, scale=1.0)
            V('tensor_scalar', ['lam'], ['clf'], out=clf[:], in0=lam[:], scalar1=-8.0, scalar2=None, op0=ALU.mult)
            G('memset', (), ['waBD'], ap=waBD[:], constant=0.0)
            G('memset', (), ['wxBD'], ap=wxBD[:], constant=0.0)
            for j in range(4):
                for hh in range(2):
                    sl = slice(hh * 64, hh * 64 + 64)
                    DMA('pool', 'c_waBD', (), ['waBD'], out=waBD[sl, j, sl], in_=lru_wa[l, 2 * j + hh])
                    DMA('pool', 'c_wxBD', (), ['wxBD'], out=wxBD[sl, j, sl], in_=lru_wx[l, 2 * j + hh])
                DMA('pool', 'c_pwB', (), ['pwB'], out=pwB[:, j, :], in_=pool_w[l, j])

        def load_w_in(l, c0, ncol, pin=False):
            wb, wn = next_w(pin)
            v = wb[:, 0:8 * ncol].rearrange("p (a b) -> p a b", a=8)
            gi = [i for i, (a0, a1) in enumerate(WGROUPS) if a0 == c0 and c0 + ncol == a1]
            assert len(gi) == 1, (c0, ncol)
            src = win_bf[l, gi[0], :, :, 0:ncol]
            DMA('sp', wn, ['win_bf%d_%d' % (l, gi[0])], [wn], out=v, in_=src)
            return v, wn

        def load_w_bo(l, n):
            wb, wn = next_w()
            v = wb[:].rearrange("p (a b) -> p a b", a=4)
            src = wbo_bf[l, n]
            DMA('sp', wn, ['wbo_bf%d' % l], [wn], out=v, in_=src)
            return v, wn

        def load_w_out(l, half):
            wb, wn = next_w()
            v = wb[:].rearrange("p (a b) -> p a b", a=8)
            src = wout_bf[l, half]
            DMA('sp', wn, ['wout_bf%d' % l], [wn], out=v, in_=src)
            return v, wn

        def rope(src, sname, dst, dname, H, tsz, ct, st, tname):
            s3 = src.rearrange("p (h d) -> p h d", h=H)
            d3 = dst.rearrange("p (h d) -> p h d", h=H)
            cb_ = ct.unsqueeze(1).to_broadcast([tsz, H, 8])
            sb_ = st.unsqueeze(1).to_broadcast([tsz, H, 8])
            t = [rt[i][0:tsz, 0:H, :] for i in range(4)]
            CK('r0')
            A('copy', [sname], [dname + 'n'], out=d3[:, :, 16:64], in_=s3[:, :, 16:64])
            CK('r1')
            V('tensor_tensor', [sname, tname], ['rt0'], out=t[0], in0=s3[:, :, 0:8], in1=cb_, op=ALU.mult)
            CK('r2')
            V('tensor_tensor', [sname, tname], ['rt1'], out=t[1], in0=s3[:, :, 8:16], in1=sb_, op=ALU.mult)
            V('tensor_tensor', [sname, tname], ['rt2'], out=t[2], in0=s3[:, :, 8:16], in1=cb_, op=ALU.mult)
            V('tensor_tensor', [sname, tname], ['rt3'], out=t[3], in0=s3[:, :, 0:8], in1=sb_, op=ALU.mult)
            V('tensor_tensor', ['rt0', 'rt1'], [dname + 'a'], out=d3[:, :, 0:8], in0=t[0], in1=t[1], op=ALU.subtract)
            V('tensor_tensor', ['rt2', 'rt3'], [dname + 'b'], out=d3[:, :, 8:16], in0=t[2], in1=t[3], op=ALU.add)
            return [dname + 'n', dname + 'a', dname + 'b']

        def transpose_blocks(src, sres, nblk, tsz):
            pt, ptn = next_pt()
            for c in range(nblk):
                T('transpose', list(sres) + ['ident'], [ptn], out=pt[:, c * 128:c * 128 + tsz], in_=src[0:tsz, c * 128:(c + 1) * 128],
                  identity=ident[0:tsz, 0:tsz])
            return pt, ptn

        def blkview(pt, nblk, tsz):
            return pt[:, 0:nblk * 128].rearrange("p (c t) -> p c t", c=nblk)[:, :, 0:tsz]

        def process_block(S, l, blk, last_layer):
            Tn = blk['T']
            tiles = blk['tiles']
            xsrc = blk['xsrc'][l]
            xdst = blk['xdst'][l]
            cosT, sinT = S['tabs']
            tabres = S['tabres']
            past = S['past']
            stores = []

            def run_zip(ga, na, gb, nb):
                ia = ib = 0
                da = db = False
                while not (da and db):
                    if not da and (db or ia * nb <= ib * na):
                        try:
                            next(ga)
                            ia += 1
                        except StopIteration:
                            da = True
                    elif not db:
                        try:
                            next(gb)
                            ib += 1
                        except StopIteration:
                            db = True

            def drain(g):
                for _ in g:
                    pass


            def rms_rstd(i, tsz, xr):
                A('activation', [xr], ['hn', 'ss%d' % i], out=hn[0:tsz, :], in_=xt[i][0:tsz, :], func=AF.Square,
                  accum_out=st_ss[0:tsz, i:i + 1])
                V('tensor_scalar', ['ss%d' % i], ['rs%d' % i], out=st_rs[0:tsz, i:i + 1], in0=st_ss[0:tsz, i:i + 1],
                  scalar1=1.0 / D, scalar2=EPS, op0=ALU.mult, op1=ALU.add)
                A('activation', ['rs%d' % i], ['rs%d' % i], out=st_rs[0:tsz, i:i + 1], in_=st_rs[0:tsz, i:i + 1], func=AF.Sqrt)
                V('reciprocal', ['rs%d' % i], ['rs%d' % i], out=st_rs[0:tsz, i:i + 1], in_=st_rs[0:tsz, i:i + 1])

            for i, (q0, tsz, a, kb, hm, tix) in enumerate(tiles):
                xr = 'xt%d' % i
                DMA('sp', xr, [blk['xres'][l]], [xr], out=xt[i][0:tsz, :], in_=xsrc[q0:q0 + tsz, :])
                rms_rstd(i, tsz, xr)
                V('tensor_scalar', [xr, 'rs%d' % i], ['hn'], out=hn[0:tsz, :], in0=xt[i][0:tsz, :], scalar1=st_rs[0:tsz, i:i + 1],
                  scalar2=None, op0=ALU.mult)
                pt, ptn = transpose_blocks(hn, ['hn'], 8, tsz)
                for kt in range(8):
                    A('activation', [ptn, 'gT'], ['hnT'], out=hnT[:, kt, q0:q0 + tsz], in_=pt[:, kt * 128:kt * 128 + tsz],
                      func=AF.Identity, scale=gT[:, kt:kt + 1])

            def tm_group(c0, ncol, evac):
                wv, wn = load_w_in(l, c0, ncol)
                for i, (q0, tsz, a, kb, hm, tix) in enumerate(tiles):
                    pa, pan = next_pa()
                    for kt in range(8):
                        T('matmul', ['hnT', wn], [pan], out=pa[0:tsz, 0:ncol], lhsT=hnT[:, kt, q0:q0 + tsz], rhs=wv[:, kt, :],
                          start=(kt == 0), stop=(kt == 7))
                    ct_ = cosT[0:tsz, tix, :]
                    st_ = sinT[0:tsz, tix, :]
                    evac(i, q0, tsz, a, kb, pa, pan, ct_, st_)
                    yield

            def ev_q(i, q0, tsz, a, kb, pa, pan, ct_, st_):
                res = rope(pa[0:tsz, :], pan, qbf[0:tsz, :], 'qbf', 8, tsz, ct_, st_, tabres)
                CK('r3')
                pt, ptn = transpose_blocks(qbf, res, 4, tsz)
                CK('r4')
                A('copy', [ptn], ['qT'], out=qT[0:64, :, 0, q0:q0 + tsz], in_=blkview(pt, 4, tsz)[0:64])
                A('copy', [ptn], ['qT'], out=qT[64:128, :, 1, q0:q0 + tsz], in_=blkview(pt, 4, tsz)[64:128])

            def ev_k(i, q0, tsz, a, kb, pa, pan, ct_, st_):
                res = rope(pa[0:tsz, :], pan, kst[0:tsz, i, :], 'kst%d' % i, 8, tsz, ct_, st_, tabres)
                G('tensor_copy', res, ['kbf'], out=kbf[0:tsz, :], in_=kst[0:tsz, i, :])
                pt, ptn = transpose_blocks(kbf, ['kbf'], 4, tsz)
                A('copy', [ptn], ['kT'], out=kT[:, :, a:a + tsz], in_=blkview(pt, 4, tsz))
                stores.append((res, dict(out=S['k_out'][l][a - past:a - past + tsz, :], in_=kst[0:tsz, i, :])))

            def ev_v(i, q0, tsz, a, kb, pa, pan, ct_, st_):
                A('copy', [pan], ['vst%d' % i], out=vst[0:tsz, i, :], in_=pa[0:tsz, :])
                V('tensor_copy', [pan], ['Vt'], out=Vt[0:tsz, kb, :, 0:64], in_=pa[0:tsz, :].rearrange("p (h d) -> p h d", h=8))
                stores.append((['vst%d' % i], dict(out=S['v_out'][l][a - past:a - past + tsz, :], in_=vst[0:tsz, i, :])))

            def ev_ga(i, q0, tsz, a, kb, pa, pan, ct_, st_):
                A('activation', [pan], ['sga%d' % i], out=sga[0:tsz, i, :], in_=pa[0:tsz, :], func=AF.Tanh, scale=0.5)
                V('scalar_tensor_tensor', [pan, 'sga%d' % i], ['sga%d' % i], out=sga[0:tsz, i, :], in0=sga[0:tsz, i, :], scalar=1.0,
                  in1=pa[0:tsz, :], op0=ALU.add, op1=ALU.mult)

            def ev_idx(i, q0, tsz, a, kb, pa, pan, ct_, st_):
                res = rope(pa[0:tsz, 0:256], pan, qibf[0:tsz, :], 'qibf', 4, tsz, ct_, st_, tabres)
                pt, ptn = transpose_blocks(qibf, res, 2, tsz)
                A('copy', [ptn], ['qiT'], out=qiT[:, :, q0:q0 + tsz], in_=blkview(pt, 2, tsz))
                res2 = rope(pa[0:tsz, 256:320], pan, kist[0:tsz, i, :], 'kist%d' % i, 1, tsz, ct_, st_, tabres)
                G('tensor_copy', res2, ['kibf0'], out=kibf[0:tsz, 0, :], in_=kist[0:tsz, i, :])
                G('tensor_copy', res2, ['kibf1'], out=kibf[0:tsz, 1, :], in_=kist[0:tsz, i, :])
                pt2, ptn2 = next_pt()
                T('transpose', ['kibf0', 'kibf1', 'ident'], [ptn2], out=pt2[:, 0:tsz], in_=kibf[0:tsz, :, :].rearrange("p a b -> p (a b)"),
                  identity=ident[0:tsz, 0:tsz])
                A('copy', [ptn2], ['kiT'], out=kiT[:, a:a + tsz], in_=pt2[:, 0:tsz])
                A('activation', [pan], ['wabs%d' % i], out=wabs[0:tsz, i, :], in_=pa[0:tsz, 320:324], func=AF.Abs)
                A('activation', [pan], ['wsgn%d' % i], out=wsgn[0:tsz, i, :], in_=pa[0:tsz, 320:324], func=AF.Sign)
                stores.append((res2, dict(out=S['ki_out'][l][a - past:a - past + tsz, :], in_=kist[0:tsz, i, :])))

            CK('step1')

            def gen_step2():
                yield from tm_group(C_Q, 512, ev_q)
                yield from tm_group(C_K, 512, ev_k)
                yield from tm_group(C_V, 512, ev_v)
                yield from tm_group(C_GA, 512, ev_ga)
                yield from tm_group(C_QI, 324, ev_idx)

            def fm_proj(wv, wn, j):
                pa, pan = next_pa()
                for kt in range(8):
                    T('matmul', ['hnT', wn], [pan], out=pa[:, 0:Tn], lhsT=wv[:, kt, j * 128:(j + 1) * 128], rhs=hnT[:, kt, 0:Tn],
                      start=(kt == 0), stop=(kt == 7))
                return pa, pan

            def gen_B1():
              yield
              wxb, wxbn = load_w_in(l, C_XB, 512, pin=True)
              for j in range(4):
                  pa, pan = fm_proj(wxb, wxbn, j)
                  xp_, xpn = Bxp[j % 2], 'Bxp%d' % (j % 2)
                  xc_, xcn = Bxc[j % 2], 'Bxc%d' % (j % 2)
                  A('copy', [pan], [xpn + 'b'], out=xp_[:, 3:3 + Tn], in_=pa[:, 0:Tn])
                  G('tensor_copy', ['convst%d' % j], [xpn + 'a'], out=xp_[:, 0:3], in_=convst[:, j, :])
                  G('tensor_copy', [xpn + 'a', xpn + 'b'], ['convst%d' % j], out=convst[:, j, :], in_=xp_[:, Tn:Tn + 3])
                  V('tensor_scalar', [xpn + 'a', xpn + 'b', 'cw', 'cb'], [xcn], out=xc_[:, 0:Tn], in0=xp_[:, 0:Tn], scalar1=cw[:, j, 0:1],
                    scalar2=cb[:, j:j + 1], op0=ALU.mult, op1=ALU.add)
                  for tap in range(1, 4):
                      V('scalar_tensor_tensor', [xpn + 'a', xpn + 'b', 'cw', xcn], [xcn], out=xc_[:, 0:Tn], in0=xp_[:, tap:tap + Tn],
                        scalar=cw[:, j, tap:tap + 1], in1=xc_[:, 0:Tn], op0=ALU.mult, op1=ALU.add)
                  G('tensor_copy', [xcn], ['xcb'], out=xcb[:, 0:Tn], in_=xc_[:, 0:Tn])
                  yield
                  pa1, pan1 = next_pa()
                  T('matmul', ['waBD', 'xcb'], [pan1], out=pa1[:, 0:Tn], lhsT=waBD[:, j, :], rhs=xcb[:, 0:Tn], start=True, stop=True)
                  pa2, pan2 = next_pa()
                  T('matmul', ['wxBD', 'xcb'], [pan2], out=pa2[:, 0:Tn], lhsT=wxBD[:, j, :], rhs=xcb[:, 0:Tn], start=True, stop=True)
                  A('activation', [pan1, 'hba'], ['Bta'], out=Bta[:, 0:Tn], in_=pa1[:, 0:Tn], func=AF.Tanh, bias=hba[:, j:j + 1], scale=0.5)
                  A('activation', [pan2, 'hbx'], ['Btx'], out=Btx[:, 0:Tn], in_=pa2[:, 0:Tn], func=AF.Tanh, bias=hbx[:, j:j + 1], scale=0.5)
                  A('activation', ['Bta', 'clf'], ['Ba2'], out=Ba2[:, 0:Tn], in_=Bta[:, 0:Tn], func=AF.Exp, bias=clf[:, j:j + 1], scale=clf[:, j:j + 1])
                  V('tensor_scalar', ['Ba2'], ['om'], out=om[:, j * TB:j * TB + Tn], in0=Ba2[:, 0:Tn], scalar1=-1.0, scalar2=1.0,
                    op0=ALU.mult, op1=ALU.add)
                  V('scalar_tensor_tensor', ['Btx', xcn], ['ixc'], out=ixc[:, j * TB:j * TB + Tn], in0=Btx[:, 0:Tn], scalar=1.0,
                    in1=xc_[:, 0:Tn], op0=ALU.add, op1=ALU.mult)
                  yield

            run_zip(gen_step2(), 2 * len(tiles) * 5 // 2, gen_B1(), 9)
            unpin_w()
            CK('step2')

            wgb, wgbn = load_w_in(l, C_GB, 512)
            for j in range(4):
                A('activation', ['om'], ['Baj%d' % j], out=Baj[j][:, 0:Tn], in_=om[:, j * TB:j * TB + Tn], func=AF.Sqrt, bias=1.0, scale=-1.0)
            for j in range(4):
                A('activation', ['om'], ['om'], out=om[:, j * TB:j * TB + Tn], in_=om[:, j * TB:j * TB + Tn], func=AF.Sqrt)
            for j in range(4):
                V('scalar_tensor_tensor', ['ixc', 'om'], ['Bbb'], out=Bbb[:, 0:Tn], in0=ixc[:, j * TB:j * TB + Tn], scalar=0.5,
                  in1=om[:, j * TB:j * TB + Tn], op0=ALU.mult, op1=ALU.mult)
                V('tensor_tensor_scan', ['Baj%d' % j, 'Bbb', 'hst%d' % j], ['Bh'], out=Bh[:, 0:Tn], data0=Baj[j][:, 0:Tn], data1=Bbb[:, 0:Tn],
                  initial=hst[:, j:j + 1], op0=ALU.mult, op1=ALU.add)
                G('tensor_copy', ['Bh'], ['hst%d' % j], out=hst[:, j:j + 1], in_=Bh[:, Tn - 1:Tn])
                pa, pan = fm_proj(wgb, wgbn, j)
                A('activation', [pan], ['Btg'], out=Btg[:, 0:Tn], in_=pa[:, 0:Tn], func=AF.Tanh, scale=0.5)
                V('scalar_tensor_tensor', [pan, 'Btg'], ['Btg'], out=Btg[:, 0:Tn], in0=Btg[:, 0:Tn], scalar=1.0, in1=pa[:, 0:Tn],
                  op0=ALU.add, op1=ALU.mult)
                V('scalar_tensor_tensor', ['Btg', 'Bh'], ['ybT'], out=yT[1][:, j, 0:Tn], in0=Btg[:, 0:Tn], scalar=0.5, in1=Bh[:, 0:Tn],
                  op0=ALU.mult, op1=ALU.mult)

            for r, kw in stores:
                DMA('sp', 's_' + r[0], r, (), **kw)

            def gen_branchC():
                wxc, wxcn = load_w_in(l, C_XC, 512)
                wgc, wgcn = load_w_in(l, C_GC, 512)
                L = 15 + Tn
                for g in range(4):
                    wdw = 2 ** (g + 1)
                    pa, pan = fm_proj(wxc, wxcn, g)
                    G('tensor_copy', ['poolst%d' % g], ['ct0a'], out=ctmp[:, 0, 0:15], in_=poolst[:, g, :])
                    A('copy', [pan], ['ct0b'], out=ctmp[:, 0, 15:L], in_=pa[:, 0:Tn])
                    G('tensor_copy', ['ct0a', 'ct0b'], ['poolst%d' % g], out=poolst[:, g, :], in_=ctmp[:, 0, Tn:L])
                    prev, prevn = 0, ['ct0a', 'ct0b']
                    m = 1
                    slot = 1
                    while m < wdw:
                        G('tensor_tensor', prevn, ['ct%d' % slot], out=ctmp[:, slot, 2 * m - 1:L], in0=ctmp[:, prev, 2 * m - 1:L],
                          in1=ctmp[:, prev, m - 1:L - m], op=ALU.add)
                        prev, prevn = slot, ['ct%d' % slot]
                        slot = 1 + (slot % 3)
                        m *= 2
                    yield
                    if blk['first']:
                        V('tensor_tensor', prevn + ['rcnt'], ['ctg'], out=ctg[:, 0:Tn], in0=ctmp[:, prev, 15:L], in1=rcnt[:, g, 0:Tn], op=ALU.mult)
                        V('tensor_tensor', ['ctg', 'ct0b'], ['cpl'], out=cpl[:, 0:Tn], in0=ctg[:, 0:Tn], in1=ctmp[:, 0, 15:L], op=ALU.subtract)
                    else:
                        V('scalar_tensor_tensor', prevn + ['ct0b'], ['cpl'], out=cpl[:, 0:Tn], in0=ctmp[:, prev, 15:L], scalar=1.0 / wdw,
                          in1=ctmp[:, 0, 15:L], op0=ALU.mult, op1=ALU.subtract)
                    pa1, pan1 = next_pa()
                    T('matmul', ['pwB', 'cpl'], [pan1], out=pa1[:, 0:Tn], lhsT=pwB[:, g, :], rhs=cpl[:, 0:Tn], start=True, stop=True)
                    pa2, pan2 = fm_proj(wgc, wgcn, g)
                    A('activation', [pan2], ['ctg'], out=ctg[:, 0:Tn], in_=pa2[:, 0:Tn], func=AF.Tanh, scale=0.5)
                    V('scalar_tensor_tensor', [pan2, 'ctg'], ['ctg'], out=ctg[:, 0:Tn], in0=ctg[:, 0:Tn], scalar=1.0, in1=pa2[:, 0:Tn],
                      op0=ALU.add, op1=ALU.mult)
                    V('tensor_scalar', ['ctg', 'psc'], ['ctg'], out=ctg[:, 0:Tn], in0=ctg[:, 0:Tn], scalar1=psc[:, g:g + 1], scalar2=0.5,
                      op0=ALU.mult, op1=ALU.mult)
                    V('tensor_tensor', ['ctg', pan1], ['ycT'], out=yT[2][:, g, 0:Tn], in0=ctg[:, 0:Tn], in1=pa1[:, 0:Tn], op=ALU.mult)
                    yield

            def rec_scores(i):
                q0, tsz, a, kb, hm, tix = tiles[i]
                Sv = a + tsz
                for c0 in range(0, Sv, 512):
                    n = min(512, Sv - c0)
                    for h in range(4):
                        hs = slice((h % 2) * 64, (h % 2) * 64 + 64)
                        pa, pan = next_pa()
                        T('matmul', ['qiT', 'kiT'], [pan], out=pa[0:tsz, 0:n], lhsT=qiT[hs, h // 2, q0:q0 + tsz], rhs=kiT[hs, c0:c0 + n],
                          start=True, stop=True)
                        r_, rn = rl[h % NRL], 'rl%d' % (h % NRL)
                        A('activation', [pan, 'wabs%d' % i], [rn], out=r_[0:tsz, 0:n], in_=pa[0:tsz, 0:n], func=AF.Relu, scale=wabs[0:tsz, i, h:h + 1])
                        if h == 0:
                            V('tensor_scalar', [rn, 'wsgn%d' % i], ['sc'], out=sc[0:tsz, c0:c0 + n], in0=r_[0:tsz, 0:n], scalar1=wsgn[0:tsz, i, 0:1],
                              scalar2=None, op0=ALU.mult)
                        else:
                            V('scalar_tensor_tensor', [rn, 'wsgn%d' % i, 'sc'], ['sc'], out=sc[0:tsz, c0:c0 + n], in0=r_[0:tsz, 0:n],
                              scalar=wsgn[0:tsz, i, h:h + 1], in1=sc[0:tsz, c0:c0 + n], op0=ALU.mult, op1=ALU.add)

            def gen_bisect(i):
                q0, tsz, a, kb, hm, tix = tiles[i]
                Sv = a + tsz
                V('tensor_reduce', ['sc'], ['bs0'], out=bs[0:tsz, 0:1], in_=sc[0:tsz, 0:Sv], axis=AX.X, op=ALU.min)
                V('tensor_reduce', ['sc'], ['bs1'], out=bs[0:tsz, 1:2], in_=sc[0:tsz, 0:Sv], axis=AX.X, op=ALU.max)
                if hm:
                    G('memset', ['bs0', 'bs1'], ['sc'], ap=sc[0:64, Sv - 64:Sv], constant=NEG)
                V('tensor_scalar', ['bs0', 'bs1'], ['bs2'], out=bs[0:tsz, 2:3], in0=bs[0:tsz, 1:2], scalar1=bs[0:tsz, 0:1], scalar2=1.003,
                  op0=ALU.subtract, op1=ALU.mult)
                V('scalar_tensor_tensor', ['bs0', 'bs2'], ['bs6'], out=bs[0:tsz, 6:7], in0=bs[0:tsz, 2:3], scalar=-0.001, in1=bs[0:tsz, 0:1],
                  op0=ALU.mult, op1=ALU.add)
                V('tensor_scalar', ['bs2', 'hpow'], ['steps'], out=steps[0:tsz, :], in0=hpow[0:tsz, :], scalar1=bs[0:tsz, 2:3], scalar2=None, op0=ALU.mult)
                V('tensor_tensor', ['bs6', 'steps'], ['mid'], out=bs[0:tsz, 3:4], in0=bs[0:tsz, 6:7], in1=steps[0:tsz, 1:2], op=ALU.add)
                yield
                cA = int(Sv * (0.46 if i == 0 else 0.52)) // 2 * 2 if Sv >= 512 else Sv
                nA = Sv - cA
                thr = TOPK - nA / 2.0
                cc = thr - 0.25 - cA
                for k in range(1, NIT + 1):
                    kk = k + 1 if k < NIT else k
                    V('tensor_scalar', ['sc', 'mid'], ['mkV', 'cnt'], out=mk[0:tsz, 0:cA], in0=sc[0:tsz, 0:cA], scalar1=bs[0:tsz, 3:4], scalar2=cc,
                      op0=ALU.is_lt, op1=ALU.add, accum_out=bs[0:tsz, 4:5])
                    if nA:
                        A('activation', ['sc', 'mid'], ['mkA', 'cntA'], out=mk[0:tsz, cA:Sv], in_=sc[0:tsz, cA:Sv], func=AF.Sign, bias=bs[0:tsz, 3:4],
                          scale=-1.0, accum_out=bs[0:tsz, 7:8])
                    V('tensor_scalar', ['mid', 'steps'], ['bu'], out=bs[0:tsz, 9:10], in0=bs[0:tsz, 3:4], scalar1=steps[0:tsz, kk:kk + 1], scalar2=None,
                      op0=ALU.subtract)
                    if nA:
                        V('scalar_tensor_tensor', ['cnt', 'cntA'], ['bd'], out=bs[0:tsz, 5:6], in0=bs[0:tsz, 7:8], scalar=-0.5, in1=bs[0:tsz, 4:5],
                          op0=ALU.mult, op1=ALU.is_ge)
                    else:
                        V('tensor_scalar', ['cnt'], ['bd'], out=bs[0:tsz, 5:6], in0=bs[0:tsz, 4:5], scalar1=0.0, scalar2=None, op0=ALU.is_le)
                    V('scalar_tensor_tensor', ['bd', 'steps', 'bu'], ['mid'], out=bs[0:tsz, 3:4], in0=bs[0:tsz, 5:6], scalar=steps[0:tsz, k:k + 1],
                      in1=bs[0:tsz, 9:10], op0=ALU.mult, op1=ALU.add)
                    yield
                V('tensor_scalar', ['sc', 'mid'], ['mk'], out=mk[0:tsz, 0:Sv], in0=sc[0:tsz, 0:Sv], scalar1=bs[0:tsz, 3:4], scalar2=None, op0=ALU.is_ge)
                yield

            def mask_bias(i):
                return MASK_BIAS and len(tiles) == 2 and i == 0

            def rec_masktrans(i):
                q0, tsz, a, kb, hm, tix = tiles[i]
                Sv = a + tsz
                kbs = [b for b in S['kblocks'] if b[0] + b[1] <= Sv]
                for g0 in range(0, len(kbs), 8):
                    grp = kbs[g0:g0 + 8]
                    pt, ptn = next_pt()
                    for s_, (c0, kr, kbi) in enumerate(grp):
                        T('transpose', ['mk', 'ident'], [ptn], out=pt[0:kr, s_ * 128:s_ * 128 + tsz], in_=mk[0:tsz, c0:c0 + kr], identity=ident[0:tsz, 0:tsz])
                    if mask_bias(i):
                        kwm = dict(func=AF.Identity, scale=30000.0, bias=-30000.0)
                    else:
                        kwm = dict(func=AF.Identity)
                    if all(x[1] == 128 for x in grp):
                        A('activation', [ptn], ['mkT'], out=mkT[:, g0:g0 + len(grp), 0:tsz], in_=blkview(pt, len(grp), tsz), **kwm)
                    else:
                        for s_, (c0, kr, kbi) in enumerate(grp):
                            A('activation', [ptn], ['mkT'], out=mkT[0:kr, g0 + s_, 0:tsz], in_=pt[0:kr, s_ * 128:s_ * 128 + tsz], **kwm)

            def gen_attention(i, npool=4):
                q0, tsz, a, kb, hm, tix = tiles[i]
                Sv = a + tsz
                kbs = [b for b in S['kblocks'] if b[0] + b[1] <= Sv]
                nkb = len(kbs)
                MB = mask_bias(i)

                def logits(bi):
                    c0, kr, kbi = kbs[bi]
                    par = bi % 2
                    pl = [psA[2 * par], psA[2 * par + 1]]
                    pln = ['psA%d' % (2 * par), 'psA%d' % (2 * par + 1)]
                    for hp in range(4):
                        hb = hp // 2
                        o = (hp % 2) * 256
                        if tsz == 128:
                            T('matmul', ['kT', 'qT'], [pln[hb]], out=pl[hb][0:kr, o:o + 256],
                              lhsT=kT[:, hp, c0:c0 + kr], rhs=qT[:, hp, :, q0:q0 + tsz], start=(hp % 2 == 0), stop=(not MB and hp % 2 == 1))
                        else:
                            for w in range(2):
                                T('matmul', ['kT', 'qT'], [pln[hb]], out=pl[hb][0:kr, o + w * 128:o + w * 128 + tsz],
                                  lhsT=kT[:, hp, c0:c0 + kr], rhs=qT[:, hp, w, q0:q0 + tsz], start=(hp % 2 == 0 and w == 0),
                                  stop=(not MB and hp % 2 == 1 and w == 1))
                    for hb in (range(2) if MB else ()):
                        if tsz == 128:
                            T('matmul', ['mkT', 'ident'], [pln[hb]], out=pl[hb][0:kr, :], lhsT=ident[0:kr, 0:kr],
                              rhs=mkT[0:kr, bi, 0:tsz].unsqueeze(1).to_broadcast([kr, 4, tsz]), start=False, stop=True)
                        else:
                            for s4 in range(4):
                                T('matmul', ['mkT', 'ident'], [pln[hb]], out=pl[hb][0:kr, s4 * 128:s4 * 128 + tsz], lhsT=ident[0:kr, 0:kr],
                                  rhs=mkT[0:kr, bi, 0:tsz], start=False, stop=(s4 == 3))

                def pv(bi):
                    c0, kr, kbi = kbs[bi]
                    px, pxn = pex[bi % 2], 'pex%d' % (bi % 2)
                    pdeps = [pxn + '0', pxn + '1']
                    if not MB:
                        px = pm[bi % 2]
                        pdeps = ['pm%da' % (bi % 2), 'pm%db' % (bi % 2)]
                    for h in range(8):
                        T('matmul', pdeps + ['Vt'], ['psB%d' % (h // 4)], out=psB[h // 4][0:tsz, (h % 4) * 65:(h % 4) * 65 + 65], lhsT=px[0:kr, h, 0:tsz],
                          rhs=Vt[0:kr, kbi, h, :], start=(bi == 0 and h % 4 == 0), stop=(bi == nkb - 1 and h % 4 == 3))
                logits(0)
                if nkb > 1:
                    logits(1)
                for bi, (c0, kr, kbi) in enumerate(kbs):
                    par = bi % 2
                    pl = [psA[2 * par], psA[2 * par + 1]]
                    pln = ['psA%d' % (2 * par), 'psA%d' % (2 * par + 1)]
                    px, pxn = pex[par], 'pex%d' % par
                    for hh in range(2):
                        A('activation', [pln[hh]], [pxn + str(hh)], out=px[0:kr, hh * 4:hh * 4 + 4, 0:tsz],
                          in_=pl[hh][0:kr, :].rearrange("p (h t) -> p h t", h=4)[:, :, 0:tsz], func=AF.Exp, scale=0.125)
                    if not MB:
                        pm_ = pm[par]
                        mbp = mkT[0:kr, bi, 0:tsz].unsqueeze(1).to_broadcast([kr, npool, tsz])
                        mbv = mkT[0:kr, bi, 0:tsz].unsqueeze(1).to_broadcast([kr, 8 - npool, tsz])
                        G('tensor_tensor', [pxn + '0', pxn + '1', 'mkT'], ['pm%da' % par], out=pm_[0:kr, 0:npool, 0:tsz], in0=px[0:kr, 0:npool, 0:tsz], in1=mbp, op=ALU.mult)
                        V('tensor_tensor', [pxn + '0', pxn + '1', 'mkT'], ['pm%db' % par], out=pm_[0:kr, npool:8, 0:tsz], in0=px[0:kr, npool:8, 0:tsz], in1=mbv, op=ALU.mult)
                    if bi + 2 < nkb:
                        logits(bi + 2)
                    if bi >= 1:
                        pv(bi - 1)
                    yield
                pv(nkb - 1)

            def rec_attn_final(i):
                q0, tsz, a, kb, hm, tix = tiles[i]
                for hh in range(2):
                    pv = psB[hh][0:tsz, 0:260].rearrange("p (h e) -> p h e", h=4)
                    V('tensor_scalar', ['psB%d' % hh], ['rec%d' % hh], out=rec[0:tsz, hh * 4:hh * 4 + 4], in0=pv[:, :, 64], scalar1=2.0, scalar2=None,
                      op0=ALU.mult)
                    V('reciprocal', ['rec%d' % hh], ['rec%d' % hh], out=rec[0:tsz, hh * 4:hh * 4 + 4], in_=rec[0:tsz, hh * 4:hh * 4 + 4])
                    V('tensor_tensor', ['psB%d' % hh, 'rec%d' % hh], ['att%d' % hh], out=att[0:tsz, hh * 4:hh * 4 + 4, :], in0=pv[:, :, 0:64],
                      in1=rec[0:tsz, hh * 4:hh * 4 + 4].unsqueeze(2).to_broadcast([tsz, 4, 64]), op=ALU.mult)
                G('tensor_tensor', ['att0', 'att1', 'sga%d' % i], ['yab'], out=yab[0:tsz, :], in0=att[0:tsz, :, :].rearrange("p h d -> p (h d)"),
                  in1=sga[0:tsz, i, :], op=ALU.mult)
                pt, ptn = transpose_blocks(yab, ['yab'], 4, tsz)
                A('copy', [ptn], ['yaT'], out=yT[0][:, :, q0:q0 + tsz], in_=blkview(pt, 4, tsz))

            ynames = ['yaT', 'ybT', 'ycT']

            def gen_merge(ns, first_n, last_n, banks=None):
                bc = [0]

                def bank():
                    if banks is None:
                        return next_pa()
                    b = banks[bc[0] % len(banks)]
                    bc[0] += 1
                    return b
                for n in ns:
                    wb_, wbn = load_w_bo(l, n)
                    for e4 in range(2):
                        wg_, wgn = load_w_in(l, C_GM + n * 1024 + e4 * 512, 512)
                        for ee in range(4):
                            e = e4 * 4 + ee
                            pa, pan = bank()
                            for kt in range(8):
                                T('matmul', ['hnT', wgn], [pan], out=pa[:, 0:Tn], lhsT=wg_[:, kt, ee * 128:(ee + 1) * 128], rhs=hnT[:, kt, 0:Tn],
                                  start=(kt == 0), stop=(kt == 7))
                            pb, pbn = bank()
                            for ct in range(4):
                                T('matmul', [wbn, ynames[n]], [pbn], out=pb[:, 0:Tn], lhsT=wb_[:, ct, e * 128:(e + 1) * 128], rhs=yT[n][:, ct, 0:Tn],
                                  start=(ct == 0), stop=(ct == 3))
                            tg_, tgn = mtg[e % 2], 'mtg%d' % (e % 2)
                            A('activation', [pan], [tgn], out=tg_[:, 0:Tn], in_=pa[:, 0:Tn], func=AF.Tanh, scale=0.5)
                            mge = mg[:, e * TB:e * TB + Tn]
                            if n == first_n:
                                V('scalar_tensor_tensor', [tgn, pbn], ['mg'], out=mge, in0=tg_[:, 0:Tn], scalar=1.0, in1=pb[:, 0:Tn], op0=ALU.add, op1=ALU.mult)
                            else:
                                V('scalar_tensor_tensor', [tgn, pbn], [tgn], out=tg_[:, 0:Tn], in0=tg_[:, 0:Tn], scalar=1.0, in1=pb[:, 0:Tn],
                                  op0=ALU.add, op1=ALU.mult)
                                if n != last_n:
                                    G('tensor_tensor', [tgn, 'mg'], ['mg'], out=mge, in0=mge, in1=tg_[:, 0:Tn], op=ALU.add)
                                else:
                                    G('tensor_tensor', [tgn, 'mg'], ['mgb'], out=mgb[:, e * TB:e * TB + Tn], in0=mge, in1=tg_[:, 0:Tn], op=ALU.add)
                            yield

            psM = [(psT[0][:, :].bitcast(F32), 'psT0'), (psT[1][:, :].bitcast(F32), 'psT1')]

            def nkb_of(i):
                q0, tsz, a, kb, hm, tix = tiles[i]
                return len([b for b in S['kblocks'] if b[0] + b[1] <= a + tsz])

            rec_scores(0)
            run_zip(gen_branchC(), 8, gen_bisect(0), NIT + 2)
            rec_masktrans(0)
            CK('step3')
            if len(tiles) == 2:
                rec_scores(1)
                run_zip(gen_attention(0, 6), nkb_of(0), gen_bisect(1), NIT + 2)
                rec_attn_final(0)
                rec_masktrans(1)
                run_zip(gen_attention(1, 3), nkb_of(1), gen_merge([1, 2], 1, 0, psM), 16)
                rec_attn_final(1)
            else:
                run_zip(gen_attention(0, 3), nkb_of(0), gen_merge([1, 2], 1, 0, psM), 16)
                rec_attn_final(0)
            CK('step4')
            drain(gen_merge([0], 1, 0))
            for half in range(2):
                wo_, won = load_w_out(l, half)
                for i, (q0, tsz, a, kb, hm, tix) in enumerate(tiles):
                    pa, pan = next_pa()
                    for et in range(8):
                        T('matmul', ['mgb', won], [pan], out=pa[0:tsz, :], lhsT=mgb[:, et * TB + q0:et * TB + q0 + tsz], rhs=wo_[:, et, :],
                          start=(et == 0), stop=(et == 7))
                    xs_ = xt[i][0:tsz, half * 512:(half + 1) * 512]
                    V('scalar_tensor_tensor', [pan, 'xt%d' % i], ['xt%d' % i], out=xs_, in0=pa[0:tsz, :], scalar=0.5, in1=xs_, op0=ALU.mult, op1=ALU.add)
            for i, (q0, tsz, a, kb, hm, tix) in enumerate(tiles):
                xr = 'xt%d' % i
                if not last_layer:
                    DMA('act', 's_' + xr, [xr], [blk['xres'][l + 1]], out=xdst[q0:q0 + tsz, :], in_=xt[i][0:tsz, :])
                elif xdst is not None:
                    rms_rstd(i, tsz, xr)
                    V('scalar_tensor_tensor', [xr, 'rs%d' % i, 'gfbc'], [xr], out=xt[i][0:tsz, :], in0=xt[i][0:tsz, :], scalar=st_rs[0:tsz, i:i + 1],
                      in1=gfbc[0:tsz, :], op0=ALU.mult, op1=ALU.mult)
                    DMA('act', 's_' + xr, [xr], (), out=xdst[q0:q0 + tsz, :], in_=xt[i][0:tsz, :])

        def make_prompt():
            S = {'past': 0, 'tabs': (cosP, sinP), 'tabres': 'ropeP',
                 'k_out': [k_p[0], k_p[1]], 'v_out': [v_p[0], v_p[1]], 'ki_out': [ki_p[0], ki_p[1]]}
            S['kblocks'] = [(0, 16, 0)] + [(16 + 128 * j, 128, 1 + j) for j in range(32)]
            blocks = [{'T': 16, 'tiles': [(0, 16, 0, 0, False, 0)], 'first': True,
                       'xsrc': [xp_in[0:16, :], xscr_p[0:16, :]], 'xdst': [xscr_p[0:16, :], None], 'xres': ['xin', 'xscrp_m', 'none']}]
            nb = debug.get('nblk', SEQ // TB)
            for b in range(nb):
                a0 = 16 + b * TB
                tiles = []
                for i in range(TB // 128):
                    a = a0 + i * 128
                    kb = 1 + (a - 16) // 128
                    tiles.append((i * 128, 128, a, kb, True, kb))
                blocks.append({'T': TB, 'tiles': tiles, 'first': False,
                               'xsrc': [xp_in[a0:a0 + TB, :], xscr_p[a0:a0 + TB, :]],
                               'xdst': [xscr_p[a0:a0 + TB, :], y_p[a0 - 16:a0 - 16 + TB, :]], 'xres': ['xin', 'xscrp_%d' % b, 'none']})
            S['blocks'] = blocks
            return S

        def make_sample(si):
            S = {'past': PAST, 'tabs': (cosS, sinS), 'tabres': 'ropeS',
                 'k_out': [k_s[0, si], k_s[1, si]], 'v_out': [v_s[0, si], v_s[1, si]], 'ki_out': [ki_s[0, si], ki_s[1, si]]}
            S['kblocks'] = [(128 * j, 128, j) for j in range(16)] + [(PAST, 64, 16)]
            S['blocks'] = [{'T': TS, 'tiles': [(0, TS, PAST, 16, False, 0)], 'first': False,
                            'xsrc': [xs_in[si], xscr_s[si]], 'xdst': [xscr_s[si], y_s[si]], 'xres': ['xin', 'xscrs_%d' % si, 'none']}]
            return S

        st_names = ['convst%d' % j for j in range(4)] + ['hst%d' % j for j in range(4)] + ['poolst%d' % j for j in range(4)]

        def zero_states():
            G('memset', (), ['convst%d' % j for j in range(4)], ap=convst[:], constant=0.0)
            G('memset', (), ['hst%d' % j for j in range(4)], ap=hst[:], constant=0.0)
            G('memset', (), ['poolst%d' % j for j in range(4)], ap=poolst[:], constant=0.0)

        def load_states(l, si):
            for j in range(4):
                small_dma(convst[:, j, :], sconv_in[l, si][:, j * 128:(j + 1) * 128].rearrange("t p -> p t"), (), ['convst%d' % j])
                small_dma(poolst[:, j, :], spool_in[l, si][:, j * 128:(j + 1) * 128].rearrange("t p -> p t"), (), ['poolst%d' % j])
            small_dma(hst[:], fm(slru_in[l, si]), (), ['hst%d' % j for j in range(4)])

        def store_states(conv_o, lru_o, pool_o):
            for j in range(4):
                small_dma(conv_o[:, j * 128:(j + 1) * 128].rearrange("t p -> p t"), convst[:, j, :], ['convst%d' % j], ())
                small_dma(pool_o[:, j * 128:(j + 1) * 128].rearrange("t p -> p t"), poolst[:, j, :], ['poolst%d' % j], ())
            small_dma(fm(lru_o), hst[:], ['hst%d' % j for j in range(4)], ())

        def load_cache(l, si):
            for j in range(16):
                sl = slice(j * 128, (j + 1) * 128)
                kb_, kbn = (kbf, 'kbf') if j % 2 == 0 else (qbf, 'qbf')
                DMA('pool', 'c_' + kbn, (), [kbn], out=kb_[:], in_=ck_in[l, si, sl, :])
                pt, ptn = transpose_blocks(kb_, [kbn], 4, 128)
                A('copy', [ptn], ['kT'], out=kT[:, :, sl], in_=blkview(pt, 4, 128))
                DMA('pool', 'c_Vt', (), ['Vt'], out=Vt[:, j, :, 0:64], in_=cv_in[l, si, sl, :].rearrange("p (h d) -> p h d", h=8))
                if j % 2 == 0:
                    ki_ = kibf[:, :, :]
                    kin = ['kibf0', 'kibf1']
                else:
                    ki_ = qibf[:, 0:128].rearrange("p (a b) -> p a b", a=2)
                    kin = ['qibf', 'qibf']
                DMA('pool', 'c_' + kin[0], (), [kin[0]], out=ki_[:, 0, :], in_=cki_in[l, si, sl, :])
                DMA('pool', 'c_' + kin[1] + 'x', (), [kin[1]], out=ki_[:, 1, :], in_=cki_in[l, si, sl, :])
                pt2, ptn2 = next_pt()
                T('transpose', list(set(kin)) + ['ident'], [ptn2], out=pt2[:, 0:128], in_=ki_.rearrange("p a b -> p (a b)"), identity=ident[:, :])
                A('copy', [ptn2], ['kiT'], out=kiT[:, sl], in_=pt2[:, 0:128])

        nlayers = debug.get('nlayers', 2)
        if not debug.get('skip_prompt'):
            Sp = make_prompt()
            for l in range(nlayers):
                layer_setup(l)
                zero_states()
                for bix, blk in enumerate(Sp['blocks']):
                    process_block(Sp, l, blk, l == 1)
                    if l == 0 and bix == min(2, len(Sp['blocks']) - 1):
                        cast_weights(1)
                store_states(conv_p[l], lru_p[l], pool_p[l])
        if not debug.get('skip_sample'):
            for si in range(debug.get('nsample', 2)):
                Ss = make_sample(si)
                for l in range(nlayers):
                    layer_setup(l)
                    CK('setup')
                    load_states(l, si)
                    CK('states')
                    load_cache(l, si)
                    CK('cache')
                    for blk in Ss['blocks']:
                        process_block(Ss, l, blk, l == 1)
                    store_states(conv_s[l, si], lru_s[l, si], pool_s[l, si])
        P.emit()
        build_program.stats = P.stats
    return nc


_CACHE = {}


def kernel(x_prompt, x_sample, cache_k, cache_v, cache_kidx, state_conv, state_lru, state_pool,
           meta_tokens, norm_g, w_in, conv_w, conv_b, lru_wa, lru_ba, lru_wx, lru_bx, lru_lambda,
           pool_w, pool_scale, w_branch_out, w_out, final_norm_g):
    if 'nc' not in _CACHE:
        _CACHE['nc'] = build_program()
    nc = _CACHE['nc']
    in_maps = _make_in_maps(x_prompt, x_sample, cache_k, cache_v, cache_kidx, state_conv, state_lru, state_pool,
                            meta_tokens, norm_g, w_in, conv_w, conv_b, lru_wa, lru_ba, lru_wx, lru_bx, lru_lambda,
                            pool_w, pool_scale, w_branch_out, w_out, final_norm_g)
    res = run_bass_kernel_spmd(nc, in_maps, core_ids=list(range(8)))
    return _assemble(res.results)


def _make_in_maps(x_prompt, x_sample, cache_k, cache_v, cache_kidx, state_conv, state_lru, state_pool,
                  meta_tokens, norm_g, w_in, conv_w, conv_b, lru_wa, lru_ba, lru_wx, lru_bx, lru_lambda,
                  pool_w, pool_scale, w_branch_out, w_out, final_norm_g):
    f = lambda a: np.ascontiguousarray(np.asarray(a, dtype=np.float32))
    x_prompt = f(x_prompt); x_sample = f(x_sample); meta = f(meta_tokens)
    ck = f(cache_k).reshape(2, 16, PAST, 512)
    cv = f(cache_v).reshape(2, 16, PAST, 512)
    cki = f(cache_kidx)
    sconv = f(state_conv); slru = f(state_lru); spool = f(state_pool)
    posp = np.zeros((128, NKB), np.float32)
    posp[:, 0] = np.arange(128)
    for j in range(1, NKB):
        posp[:, j] = 16 + 128 * (j - 1) + np.arange(128)
    poss = (PAST + np.arange(128, dtype=np.float32)).reshape(128, 1).astype(np.float32)
    shared = {
        "posp": posp, "poss": poss, "norm_g": f(norm_g), "w_in": f(w_in), "conv_w": f(conv_w), "conv_b": f(conv_b),
        "lru_wa": f(lru_wa), "lru_ba": f(lru_ba), "lru_wx": f(lru_wx), "lru_bx": f(lru_bx), "lru_lambda": f(lru_lambda),
        "pool_w": f(pool_w), "pool_scale": f(pool_scale), "w_branch_out": f(w_branch_out), "w_out": f(w_out),
        "final_norm_g": f(final_norm_g),
    }
    in_maps = []
    for c in range(8):
        b = c % 4
        ss = [2 * c, 2 * c + 1]
        m = dict(shared)
        m["xp"] = np.ascontiguousarray(np.concatenate([meta, x_prompt[b]], axis=0))
        m["xs"] = np.ascontiguousarray(x_sample[ss])
        m["ck"] = np.ascontiguousarray(ck[:, ss])
        m["cv"] = np.ascontiguousarray(cv[:, ss])
        m["cki"] = np.ascontiguousarray(cki[:, ss])
        m["sconv"] = np.ascontiguousarray(sconv[:, ss])
        m["slru"] = np.ascontiguousarray(slru[:, ss])
        m["spool"] = np.ascontiguousarray(spool[:, ss])
        in_maps.append(m)
    return in_maps


def _assemble(R):
    y_prompt = np.stack([R[b]["y_p"] for b in range(4)], axis=0)
    y_sample = np.concatenate([R[c]["y_s"] for c in range(8)], axis=0)
    k_prompt = np.stack([R[b]["k_p"] for b in range(4)], axis=1).reshape(2, 4, TP, 8, 64)
    v_prompt = np.stack([R[b]["v_p"] for b in range(4)], axis=1).reshape(2, 4, TP, 8, 64)
    ki_prompt = np.stack([R[b]["ki_p"] for b in range(4)], axis=1)
    conv_prompt = np.stack([R[b]["conv_p"] for b in range(4)], axis=1)
    lru_prompt = np.stack([R[b]["lru_p"] for b in range(4)], axis=1)
    pool_prompt = np.stack([R[b]["pool_p"] for b in range(4)], axis=1)
    k_sample = np.concatenate([R[c]["k_s"] for c in range(8)], axis=1).reshape(2, 16, TS, 8, 64)
    v_sample = np.concatenate([R[c]["v_s"] for c in range(8)], axis=1).reshape(2, 16, TS, 8, 64)
    ki_sample = np.concatenate([R[c]["ki_s"] for c in range(8)], axis=1)
    conv_sample = np.concatenate([R[c]["conv_s"] for c in range(8)], axis=1)
    lru_sample = np.concatenate([R[c]["lru_s"] for c in range(8)], axis=1)
    pool_sample = np.concatenate([R[c]["pool_s"] for c in range(8)], axis=1)
    outs = (y_prompt, y_sample, k_prompt, v_prompt, ki_prompt, conv_prompt, lru_prompt, pool_prompt,
            k_sample, v_sample, ki_sample, conv_sample, lru_sample, pool_sample)
    return tuple(np.ascontiguousarray(o, dtype=np.float32) for o in outs)
```

```python
import math
import contextlib
import numpy as np
import concourse.bass as bass
import concourse.mybir as mybir
from concourse.bass_utils import run_bass_kernel_spmd

F32 = mybir.dt.float32
BF16 = mybir.dt.bfloat16
I32 = mybir.dt.int32
ALU = mybir.AluOpType
AF = mybir.ActivationFunctionType
AX = mybir.AxisListType

D = 1024
SEQ = 4096
NMETA = 16
TP = NMETA + SEQ
NIN = 7492
PAST = 2048
TS = 64
NKEY = TP
NKB = 33
TB = 256
NIT = 14
TOPK = 256.0
C_Q, C_K, C_V, C_GA, C_QI, C_KI, C_WI = 0, 512, 1024, 1536, 2048, 2304, 2368
C_XB, C_GB, C_XC, C_GC, C_GM = 2372, 2884, 3396, 3908, 4420
EPS = 1e-6
NEG = -1.0e30
MASK_BIAS = False


class Prog:
    EPOCH = 24000

    def __init__(self, nc, es):
        self.nc = nc
        self.es = es
        self.ops = []
        self.count = {}
        self.lastw = {}
        self.readers = {}
        self.conf = {}
        self.eng = {'pe': nc.tensor, 'act': nc.scalar, 'dve': nc.vector, 'pool': nc.gpsimd, 'sp': nc.sync}

    def alias(self, names_a, names_b):
        for a in names_a:
            for b in names_b:
                self.conf.setdefault(a, set()).add(b)
                self.conf.setdefault(b, set()).add(a)

    def _names(self, r):
        c = self.conf.get(r)
        if c:
            return [r] + list(c)
        return [r]

    stopped = False

    def _rec(self, agent, queue, fn, reads, writes, is_dma):
        if self.stopped:
            return
        seq = self.count.get(agent, 0) + 1
        self.count[agent] = seq
        deps = {}

        def need(a, s):
            if a.startswith('dma:'):
                s = self.count[a] - (1 if a == agent else 0)
                if s <= 0:
                    return
            if deps.get(a, 0) < s:
                deps[a] = s
        for r0 in list(reads) + list(writes):
            for r in self._names(r0):
                lw = self.lastw.get(r)
                if lw is not None:
                    a, s = lw
                    if a == agent and agent == 'pe':
                        continue
                    need(a, s)
        for w0 in writes:
            for w in self._names(w0):
                for (a, s) in self.readers.get(w, ()):
                    if a == agent and agent == 'pe':
                        continue
                    need(a, s)
        for r in reads:
            if r.startswith('ps'):
                for (a, s) in self.readers.get(r, ()):
                    if a != agent:
                        need(a, s)
        for r in reads:
            self.readers.setdefault(r, []).append((agent, seq))
        for w in writes:
            self.lastw[w] = (agent, seq)
            self.readers[w] = []
        self.ops.append((agent, queue, fn, deps, seq, is_dma))

    def op(self, eng, fn, reads=(), writes=()):
        self._rec(eng, eng, fn, reads, writes, False)

    def dma(self, queue, key, fn, reads=(), writes=()):
        self._rec('dma:' + key, queue, fn, reads, writes, True)

    def opk(self, eng, name, reads, writes, kw):
        m = getattr(self.eng[eng], name)
        self._rec(eng, eng, (lambda: m(**kw)), reads, writes, False)

    def dmak(self, queue, key, reads, writes, kw):
        m = self.eng[queue].dma_start
        self._rec('dma:' + key, queue, (lambda: m(**kw)), reads, writes, True)

    def emit(self):
        nc = self.nc
        waited = {}
        plan = []
        sig = set()
        for (agent, queue, fn, deps, seq, is_dma) in self.ops:
            w = waited.setdefault(queue, {})
            waits = []
            for a, s in deps.items():
                if w.get(a, 0) >= s:
                    continue
                w[a] = s
                waits.append((a, s))
                sig.add((a, s))
            if not is_dma:
                pass
            plan.append(waits)
        semmap = {}
        sems = {}
        sigcount = {}

        def get_sem(agent, ep):
            k = (agent, ep)
            if k not in sems:
                sems[k] = self.es.enter_context(nc.semaphore("s_%s_%d" % (agent.replace(':', '_'), ep)))
            return sems[k]
        incinfo = []
        per_dma = self.EPOCH // 16
        for (agent, queue, fn, deps, seq, is_dma) in self.ops:
            if is_dma:
                n = sigcount.get(agent, 0) + 1
                sigcount[agent] = n
                ep = (n - 1) // per_dma
                semmap[(agent, seq)] = (agent, ep, (n - ep * per_dma) * 16)
                incinfo.append((agent, ep, 16))
            elif (agent, seq) in sig:
                n = sigcount.get(agent, 0) + 1
                sigcount[agent] = n
                ep = (n - 1) // self.EPOCH
                semmap[(agent, seq)] = (agent, ep, n - ep * self.EPOCH)
                incinfo.append((agent, ep, 1))
            else:
                incinfo.append(None)
        nw = 0
        for i, (agent, queue, fn, deps, seq, is_dma) in enumerate(self.ops):
            e = self.eng[queue]
            for (a, s) in plan[i]:
                ag, ep, val = semmap[(a, s)]
                e.wait_ge(get_sem(ag, ep), val)
                nw += 1
            ins = fn()
            if incinfo[i] is not None:
                ag, ep, inc = incinfo[i]
                ins.then_inc(get_sem(ag, ep), inc)
        for agent, n in sigcount.items():
            if agent.startswith('dma:'):
                ep = (n - 1) // per_dma
                nc.sync.wait_ge(get_sem(agent, ep), (n - ep * per_dma) * 16)
        self.stats = (len(self.ops), nw, dict(sigcount))


def build_program(debug=None):
    debug = debug or {}
    nc = bass.Bass("TRN2", target_bir_lowering=False)

    def din(name, shape, dt=F32):
        return nc.dram_tensor(name, list(shape), dt, kind="ExternalInput").ap()

    def dout(name, shape, dt=F32):
        return nc.dram_tensor(name, list(shape), dt, kind="ExternalOutput").ap()

    def dint(name, shape, dt=F32):
        return nc.dram_tensor(name, list(shape), dt, kind="Internal").ap()

    xp_in = din("xp", [TP, D])
    xs_in = din("xs", [2, TS, D])
    ck_in = din("ck", [2, 2, PAST, 512])
    cv_in = din("cv", [2, 2, PAST, 512])
    cki_in = din("cki", [2, 2, PAST, 64])
    sconv_in = din("sconv", [2, 2, 3, 512])
    slru_in = din("slru", [2, 2, 512])
    spool_in = din("spool", [2, 2, 15, 512])
    posp_in = din("posp", [128, NKB])
    poss_in = din("poss", [128, 1])
    norm_g = din("norm_g", [2, D])
    w_in = din("w_in", [2, D, NIN])
    conv_w = din("conv_w", [2, 4, 512])
    conv_b = din("conv_b", [2, 512])
    lru_wa = din("lru_wa", [2, 8, 64, 64])
    lru_ba = din("lru_ba", [2, 512])
    lru_wx = din("lru_wx", [2, 8, 64, 64])
    lru_bx = din("lru_bx", [2, 512])
    lru_lam = din("lru_lambda", [2, 512])
    pool_w = din("pool_w", [2, 4, 128, 128])
    pool_scale = din("pool_scale", [2, 512])
    w_bo = din("w_branch_out", [2, 3, 512, D])
    w_out = din("w_out", [2, D, D])
    fin_g = din("final_norm_g", [D])
    y_p = dout("y_p", [SEQ, D])
    k_p = dout("k_p", [2, TP, 512])
    v_p = dout("v_p", [2, TP, 512])
    ki_p = dout("ki_p", [2, TP, 64])
    conv_p = dout("conv_p", [2, 3, 512])
    lru_p = dout("lru_p", [2, 512])
    pool_p = dout("pool_p", [2, 15, 512])
    y_s = dout("y_s", [2, TS, D])
    k_s = dout("k_s", [2, 2, TS, 512])
    v_s = dout("v_s", [2, 2, TS, 512])
    ki_s = dout("ki_s", [2, 2, TS, 64])
    conv_s = dout("conv_s", [2, 2, 3, 512])
    lru_s = dout("lru_s", [2, 2, 512])
    pool_s = dout("pool_s", [2, 2, 15, 512])
    win_bf = dint("win_bf", [2, 15, 128, 8, 512], BF16)
    wbo_bf = dint("wbo_bf", [2, 3, 128, 4, D], BF16)
    wout_bf = dint("wout_bf", [2, 2, 128, 8, 512], BF16)
    xscr_p = dint("xscr_p", [TP, D])
    xscr_s = dint("xscr_s", [2, TS, D])

    WGROUPS = [(C_Q, C_K), (C_K, C_V), (C_V, C_GA), (C_GA, C_QI), (C_QI, C_XB), (C_XB, C_GB), (C_GB, C_XC), (C_XC, C_GC), (C_GC, C_GM)]
    for n_ in (1, 2, 0):
        for e4_ in range(2):
            WGROUPS.append((C_GM + n_ * 1024 + e4_ * 512, C_GM + n_ * 1024 + e4_ * 512 + 512))
    es = contextlib.ExitStack()
    with es:
        P = Prog(nc, es)

        def sb(name, shape, dt=F32):
            return es.enter_context(nc.sbuf_tensor(name, list(shape), dt))

        def ps(name, shape, dt=F32):
            return es.enter_context(nc.psum_tensor(name, list(shape), dt))

        def V(name, r, w, **kw):
            P.opk('dve', name, r, w, kw)

        def A(name, r, w, **kw):
            P.opk('act', name, r, w, kw)

        def G(name, r, w, **kw):
            P.opk('pool', name, r, w, kw)

        def T(name, r, w, **kw):
            P.opk('pe', name, r, w, kw)

        def DMA(q, key, r, w, **kw):
            P.dmak(q, key, r, w, kw)

        def CK(label):
            if debug.get('stop') == label:
                P.stopped = True

        kT = sb("kT", [128, 4, NKEY], BF16)
        Vt = sb("Vt", [128, NKB, 8, 65], BF16)
        kiT = sb("kiT", [128, NKEY], BF16)
        arA = sb("arA", [128, 4128], F32)
        mk = sb("mk", [128, NKEY], BF16)
        mkT = sb("mkT", [128, NKB, 128], BF16)
        NWB = 3
        wbuf = [sb("wbuf%d" % i, [128, 4096], BF16) for i in range(NWB)]
        xt = [sb("xt%d" % i, [128, D], F32) for i in range(2)]
        hn = sb("hn", [128, D], BF16)
        hnT = sb("hnT", [128, 8, TB], BF16)
        gfbc = sb("gfbc", [128, D], F32)
        qT = sb("qT", [128, 4, 2, TB], BF16)
        qiT = sb("qiT", [128, 2, TB], BF16)
        sga = sb("sga", [128, 2, 512], F32)
        yT = [sb("y%sT" % n, [128, 4, TB], BF16) for n in "abc"]
        kst = sb("kst", [128, 2, 512], F32)
        vst = sb("vst", [128, 2, 512], F32)
        kist = sb("kist", [128, 2, 64], F32)
        rt = [sb("rt%d" % i, [128, 8, 8], F32) for i in range(4)]
        qbf = sb("qbf", [128, 512], BF16)
        kbf = sb("kbf", [128, 512], BF16)
        qibf = sb("qibf", [128, 256], BF16)
        kibf = sb("kibf", [128, 2, 64], BF16)
        kif = sb("kif", [128, 64], F32)
        NRL = 4
        rl = [sb("rl%d" % i, [128, 512], F32) for i in range(NRL)]
        pex = [sb("pex%d" % i, [128, 8, 128], BF16) for i in range(2)]
        pm = [sb("pm%d" % i, [128, 8, 128], BF16) for i in range(2)]
        att = sb("att", [128, 8, 64], F32)
        yab = sb("yab", [128, 512], BF16)
        ctmp = sb("ctmp", [128, 4, 16 + TB], F32)
        cpl = sb("cpl", [128, TB], BF16)
        ctg = sb("ctg", [128, TB], F32)
        xcb = sb("xcb", [128, TB], BF16)
        ident = sb("ident", [128, 128], BF16)
        cosP = sb("cosP", [128, NKB, 8], F32)
        sinP = sb("sinP", [128, NKB, 8], F32)
        cosS = sb("cosS", [128, 1, 8], F32)
        sinS = sb("sinS", [128, 1, 8], F32)
        invf = sb("invf", [128, 8], F32)
        posP = sb("posP", [128, NKB], F32)
        posS = sb("posS", [128, 1], F32)
        hpow = sb("hpow", [128, NIT + 1], F32)
        rcnt = sb("rcnt", [128, 4, 16], F32)
        gT = sb("gT", [128, 8], F32)
        cw = sb("cw", [128, 4, 4], F32)
        cb = sb("cb", [128, 4], F32)
        hba = sb("hba", [128, 4], F32)
        hbx = sb("hbx", [128, 4], F32)
        lam = sb("lam", [128, 4], F32)
        clf = sb("clf", [128, 4], F32)
        psc = sb("psc", [128, 4], F32)
        waBD = sb("waBD", [128, 4, 128], BF16)
        wxBD = sb("wxBD", [128, 4, 128], BF16)
        pwB = sb("pwB", [128, 4, 128], BF16)
        convst = sb("convst", [128, 4, 3], F32)
        hst = sb("hst", [128, 4], F32)
        poolst = sb("poolst", [128, 4, 15], F32)
        st_ss = sb("st_ss", [128, 2], F32)
        st_rs = sb("st_rs", [128, 2], F32)
        wabs = sb("wabs", [128, 2, 4], F32)
        wsgn = sb("wsgn", [128, 2, 4], F32)
        bs = sb("bs", [128, 16], F32)
        steps = sb("steps", [128, NIT + 1], F32)
        rec = sb("rec", [128, 8], F32)
        psA = [ps("psA%d" % i, [128, 512], F32) for i in range(4)]
        psT = [ps("psT%d" % i, [128, 1024], BF16) for i in range(2)]
        psB = [ps("psB%d" % i, [128, 512], F32) for i in range(2)]

        identf = arA[:, 0:128]
        onesf = arA[:, 128:256]
        rtmp = arA[:, 256:256 + NKB * 8].rearrange("p (a b) -> p a b", b=8)
        rtmp2 = arA[:, 768:768 + NKB * 8].rearrange("p (a b) -> p a b", b=8)
        rtmpi = arA[:, 1280:1280 + NKB * 8].bitcast(I32).rearrange("p (a b) -> p a b", b=8)
        sc = arA[:, 0:NKEY]
        om = arA[:, 0:4 * TB]
        ixc = arA[:, 4 * TB:8 * TB]
        o = 8 * TB
        Bxp = []
        for i in range(2):
            Bxp.append(arA[:, o:o + 3 + TB]); o += 4 + TB
        Bxc = []
        for i in range(2):
            Bxc.append(arA[:, o:o + TB]); o += TB
        Bta = arA[:, o:o + TB]; o += TB
        Btx = arA[:, o:o + TB]; o += TB
        Ba2 = arA[:, o:o + TB]; o += TB
        assert o <= 4128, o
        o = 8 * TB
        Baj = []
        for i in range(4):
            Baj.append(arA[:, o:o + TB]); o += TB
        Bbb = arA[:, o:o + TB]; o += TB
        Bh = arA[:, o:o + TB]; o += TB
        Btg = arA[:, o:o + TB]; o += TB
        assert o <= 4128, o
        mg = arA[:, 0:8 * TB]
        mtg = [arA[:, 8 * TB + i * TB: 8 * TB + (i + 1) * TB] for i in range(2)]
        mgb = mk[:, 0:8 * TB]
        p1 = ['Bxp0a', 'Bxp0b', 'Bxp1a', 'Bxp1b', 'Bxc0', 'Bxc1', 'Bta', 'Btx', 'Ba2']
        p2 = ['Baj0', 'Baj1', 'Baj2', 'Baj3', 'Bbb', 'Bh', 'Btg']
        P.alias(['sc'], ['om', 'ixc', 'mg', 'mtg0', 'mtg1'] + p1 + p2)
        P.alias(['mg'], ['om', 'ixc'])
        P.alias(p1, p2 + ['mtg0', 'mtg1'])
        P.alias(['mtg0', 'mtg1'], p2)
        P.alias(['onesf', 'identf', 'rtmp', 'rtmp2', 'rtmpi'], ['sc', 'om', 'ixc', 'mg', 'mtg0', 'mtg1'] + p1 + p2)
        P.alias(['mk'], ['mgb', 'mkV', 'mkA'])
        P.alias(['mgb'], ['mkV', 'mkA'])

        build_program.sbuf_left = nc.sbuf_bytes_remaining
        cnt = {'wb': 0, 'pa': 0, 'pt': 0}

        def next_w(pin=False):
            while True:
                i = cnt['wb'] % NWB
                cnt['wb'] += 1
                if i != cnt.get('pin'):
                    break
            if pin:
                cnt['pin'] = i
            return wbuf[i], 'wbuf%d' % i

        def unpin_w():
            cnt['pin'] = None

        pa_ring = [(psA[0], 'psA0'), (psA[1], 'psA1'), (psA[2], 'psA2'), (psA[3], 'psA3'), (psB[0], 'psB0'), (psB[1], 'psB1')]

        def next_pa():
            i = cnt['pa'] % 6
            cnt['pa'] += 1
            return pa_ring[i]

        def next_pt():
            i = cnt['pt'] % 2
            cnt['pt'] += 1
            return psT[i], 'psT%d' % i

        def small_dma(out, in_, r=(), w=(), q='sp', key=None):
            key = key or ('m_' + (list(w) + list(r))[0])
            DMA(q, key, r, w, out=out, in_=in_, allow_slow_non_contiguous=True)

        CK('casts')
        G('memset', (), ['onesf'], ap=onesf, constant=1.0)
        G('affine_select', ['onesf'], ['identf'], out=identf, in_=onesf, pattern=[[-1, 128]],
          compare_op=ALU.is_equal, fill=0.0, base=0, channel_multiplier=1)
        V('tensor_copy', ['identf'], ['ident'], out=ident[:], in_=identf)
        G('memset', (), ['Vt'], ap=Vt[:, :, :, 64:65], constant=1.0)
        G('memset', (), ['qT'], ap=qT[:], constant=0.0)
        for j in range(8):
            G('memset', (), ['invf'], ap=invf[:, j:j + 1], constant=float(500000.0 ** (-(2.0 * j) / 16.0)))
        for k in range(NIT + 1):
            G('memset', (), ['hpow'], ap=hpow[:, k:k + 1], constant=float(0.5 ** k))
        for g in range(4):
            wdw = 2 ** (g + 1)
            for t in range(16):
                G('memset', (), ['rcnt'], ap=rcnt[:, g, t:t + 1], constant=1.0 / float(min(wdw, t + 1)))
        small_dma(posP[:], posp_in, (), ['posP'])
        small_dma(posS[:], poss_in, (), ['posS'])
        small_dma(gfbc[:], fin_g.partition_broadcast(128), (), ['gfbc'])

        def rope_tables(pos, nt, cosT, sinT, tag):
            a3 = rtmp[:, 0:nt, :]
            b3 = rtmp2[:, 0:nt, :]
            i3 = rtmpi[:, 0:nt, :]
            for shift, dst in ((0.0, sinT), (math.pi / 2.0, cosT)):
                V('tensor_tensor', ['pos' + tag, 'invf'], ['rtmp'], out=a3, in0=pos.unsqueeze(2).to_broadcast([128, nt, 8]),
                  in1=invf[:].unsqueeze(1).to_broadcast([128, nt, 8]), op=ALU.mult)
                if shift:
                    V('tensor_scalar', ['rtmp'], ['rtmp'], out=a3, in0=a3, scalar1=shift, scalar2=None, op0=ALU.add)
                V('tensor_scalar', ['rtmp'], ['rtmp2'], out=b3, in0=a3, scalar1=1.0 / (2.0 * math.pi), scalar2=None, op0=ALU.mult)
                V('tensor_copy', ['rtmp2'], ['rtmpi'], out=i3, in_=b3)
                V('tensor_copy', ['rtmpi'], ['rtmp2'], out=b3, in_=i3)
                V('scalar_tensor_tensor', ['rtmp2', 'rtmp'], ['rtmp'], out=a3, in0=b3, scalar=-2.0 * math.pi, in1=a3, op0=ALU.mult, op1=ALU.add)
                V('tensor_scalar', ['rtmp'], ['rtmp'], out=a3, in0=a3, scalar1=3.14159, scalar2=-3.14159, op0=ALU.min, op1=ALU.max)
                A('activation', ['rtmp'], ['rope' + tag], out=dst, in_=a3, func=AF.Sin)

        def cast_weights(l):
            for gi, (c0, c1) in enumerate(WGROUPS):
                DMA('pool', 'c_win%d_%d' % (l, gi), (), ['win_bf%d_%d' % (l, gi)], out=win_bf[l, gi].rearrange("k kt c -> kt k c")[:, :, 0:c1 - c0],
                    in_=w_in[l, :, c0:c1].rearrange("(kt k) c -> kt k c", k=128))
            for n in range(3):
                DMA('pool', 'c_wbo%d' % l, (), ['wbo_bf%d' % l], out=wbo_bf[l, n].rearrange("c ct e -> ct c e"),
                    in_=w_bo[l, n].rearrange("(ct c) e -> ct c e", c=128))
            for r in range(2):
                DMA('pool', 'c_wout%d' % l, (), ['wout_bf%d' % l], out=wout_bf[l, r].rearrange("e et c -> et e c"),
                    in_=w_out[l, :, r * 512:(r + 1) * 512].rearrange("(et e) c -> et e c", e=128))

        cast_weights(0)
        if debug.get('skip_prompt'):
            cast_weights(1)
        CK('consts')
        rope_tables(posP[:], NKB, cosP[:], sinP[:], 'P')
        rope_tables(posS[:], 1, cosS[:], sinS[:], 'S')
        CK('prologue')

        def fm(v):
            return v.rearrange("(j p) -> p j", p=128)

        def layer_setup(l):
            small_dma(gT[:], norm_g[l].rearrange("(j p) -> p j", p=128), (), ['gT'])
            small_dma(cb[:], fm(conv_b[l]), (), ['cb'])
            for tap in range(4):
                small_dma(cw[:, :, tap], fm(conv_w[l, tap]), (), ['cw'])
            small_dma(hba[:], fm(lru_ba[l]), (), ['hba'])
            small_dma(hbx[:], fm(lru_bx[l]), (), ['hbx'])
            small_dma(lam[:], fm(lru_lam[l]), (), ['lam'])
            small_dma(psc[:], fm(pool_scale[l]), (), ['psc'])
            V('tensor_scalar', ['hba'], ['hba'], out=hba[:], in0=hba[:], scalar1=0.5, scalar2=None, op0=ALU.mult)
            V('tensor_scalar', ['hbx'], ['hbx'], out=hbx[:], in0=hbx[:], scalar1=0.5, scalar2=None, op0=ALU.mult)
            A('activation', ['lam'], ['lam'], out=lam[:], in_=lam[:], func=AF.Exp, scale=-1.0)
            A('activation', ['lam'], ['lam'], out=lam[:], in_=lam[:], func=AF.Ln, bias=1.0, scale=1.0)
            V('tensor_scalar', ['lam'], ['clf'], out=clf[:], in0=lam[:], scalar1=-8.0, scalar2=None, op0=ALU.mult)
            G('memset', (), ['waBD'], ap=waBD[:], constant=0.0)
            G('memset', (), ['wxBD'], ap=wxBD[:], constant=0.0)
            for j in range(4):
                for hh in range(2):
                    sl = slice(hh * 64, hh * 64 + 64)
                    DMA('pool', 'c_waBD', (), ['waBD'], out=waBD[sl, j, sl], in_=lru_wa[l, 2 * j + hh])
                    DMA('pool', 'c_wxBD', (), ['wxBD'], out=wxBD[sl, j, sl], in_=lru_wx[l, 2 * j + hh])
                DMA('pool', 'c_pwB', (), ['pwB'], out=pwB[:, j, :], in_=pool_w[l, j])

        def load_w_in(l, c0, ncol, pin=False):
            wb, wn = next_w(pin)
            v = wb[:, 0:8 * ncol].rearrange("p (a b) -> p a b", a=8)
            gi = [i for i, (a0, a1) in enumerate(WGROUPS) if a0 == c0 and c0 + ncol == a1]
            assert len(gi) == 1, (c0, ncol)
            src = win_bf[l, gi[0], :, :, 0:ncol]
            DMA('sp', wn, ['win_bf%d_%d' % (l, gi[0])], [wn], out=v, in_=src)
            return v, wn

        def load_w_bo(l, n):
            wb, wn = next_w()
            v = wb[:].rearrange("p (a b) -> p a b", a=4)
            src = wbo_bf[l, n]
            DMA('sp', wn, ['wbo_bf%d' % l], [wn], out=v, in_=src)
            return v, wn

        def load_w_out(l, half):
            wb, wn = next_w()
            v = wb[:].rearrange("p (a b) -> p a b", a=8)
            src = wout_bf[l, half]
            DMA('sp', wn, ['wout_bf%d' % l], [wn], out=v, in_=src)
            return v, wn

        def rope(src, sname, dst, dname, H, tsz, ct, st, tname):
            s3 = src.rearrange("p (h d) -> p h d", h=H)
            d3 = dst.rearrange("p (h d) -> p h d", h=H)
            cb_ = ct.unsqueeze(1).to_broadcast([tsz, H, 8])
            sb_ = st.unsqueeze(1).to_broadcast([tsz, H, 8])
            t = [rt[i][0:tsz, 0:H, :] for i in range(4)]
            CK('r0')
            A('copy', [sname], [dname + 'n'], out=d3[:, :, 16:64], in_=s3[:, :, 16:64])
            CK('r1')
            V('tensor_tensor', [sname, tname], ['rt0'], out=t[0], in0=s3[:, :, 0:8], in1=cb_, op=ALU.mult)
            CK('r2')
            V('tensor_tensor', [sname, tname], ['rt1'], out=t[1], in0=s3[:, :, 8:16], in1=sb_, op=ALU.mult)
            V('tensor_tensor', [sname, tname], ['rt2'], out=t[2], in0=s3[:, :, 8:16], in1=cb_, op=ALU.mult)
            V('tensor_tensor', [sname, tname], ['rt3'], out=t[3], in0=s3[:, :, 0:8], in1=sb_, op=ALU.mult)
            V('tensor_tensor', ['rt0', 'rt1'], [dname + 'a'], out=d3[:, :, 0:8], in0=t[0], in1=t[1], op=ALU.subtract)
            V('tensor_tensor', ['rt2', 'rt3'], [dname + 'b'], out=d3[:, :, 8:16], in0=t[2], in1=t[3], op=ALU.add)
            return [dname + 'n', dname + 'a', dname + 'b']

        def transpose_blocks(src, sres, nblk, tsz):
            pt, ptn = next_pt()
            for c in range(nblk):
                T('transpose', list(sres) + ['ident'], [ptn], out=pt[:, c * 128:c * 128 + tsz], in_=src[0:tsz, c * 128:(c + 1) * 128],
                  identity=ident[0:tsz, 0:tsz])
            return pt, ptn

        def blkview(pt, nblk, tsz):
            return pt[:, 0:nblk * 128].rearrange("p (c t) -> p c t", c=nblk)[:, :, 0:tsz]

        def process_block(S, l, blk, last_layer):
            Tn = blk['T']
            tiles = blk['tiles']
            xsrc = blk['xsrc'][l]
            xdst = blk['xdst'][l]
            cosT, sinT = S['tabs']
            tabres = S['tabres']
            past = S['past']
            stores = []

            def run_zip(ga, na, gb, nb):
                ia = ib = 0
                da = db = False
                while not (da and db):
                    if not da and (db or ia * nb <= ib * na):
                        try:
                            next(ga)
                            ia += 1
                        except StopIteration:
                            da = True
                    elif not db:
                        try:
                            next(gb)
                            ib += 1
                        except StopIteration:
                            db = True

            def drain(g):
                for _ in g:
                    pass


            def rms_rstd(i, tsz, xr):
                A('activation', [xr], ['hn', 'ss%d' % i], out=hn[0:tsz, :], in_=xt[i][0:tsz, :], func=AF.Square,
                  accum_out=st_ss[0:tsz, i:i + 1])
                V('tensor_scalar', ['ss%d' % i], ['rs%d' % i], out=st_rs[0:tsz, i:i + 1], in0=st_ss[0:tsz, i:i + 1],
                  scalar1=1.0 / D, scalar2=EPS, op0=ALU.mult, op1=ALU.add)
                A('activation', ['rs%d' % i], ['rs%d' % i], out=st_rs[0:tsz, i:i + 1], in_=st_rs[0:tsz, i:i + 1], func=AF.Sqrt)
                V('reciprocal', ['rs%d' % i], ['rs%d' % i], out=st_rs[0:tsz, i:i + 1], in_=st_rs[0:tsz, i:i + 1])

            for i, (q0, tsz, a, kb, hm, tix) in enumerate(tiles):
                xr = 'xt%d' % i
                DMA('sp', xr, [blk['xres'][l]], [xr], out=xt[i][0:tsz, :], in_=xsrc[q0:q0 + tsz, :])
                rms_rstd(i, tsz, xr)
                V('tensor_scalar', [xr, 'rs%d' % i], ['hn'], out=hn[0:tsz, :], in0=xt[i][0:tsz, :], scalar1=st_rs[0:tsz, i:i + 1],
                  scalar2=None, op0=ALU.mult)
                pt, ptn = transpose_blocks(hn, ['hn'], 8, tsz)
                for kt in range(8):
                    A('activation', [ptn, 'gT'], ['hnT'], out=hnT[:, kt, q0:q0 + tsz], in_=pt[:, kt * 128:kt * 128 + tsz],
                      func=AF.Identity, scale=gT[:, kt:kt + 1])

            def tm_group(c0, ncol, evac):
                wv, wn = load_w_in(l, c0, ncol)
                for i, (q0, tsz, a, kb, hm, tix) in enumerate(tiles):
                    pa, pan = next_pa()
                    for kt in range(8):
                        T('matmul', ['hnT', wn], [pan], out=pa[0:tsz, 0:ncol], lhsT=hnT[:, kt, q0:q0 + tsz], rhs=wv[:, kt, :],
                          start=(kt == 0), stop=(kt == 7))
                    ct_ = cosT[0:tsz, tix, :]
                    st_ = sinT[0:tsz, tix, :]
                    evac(i, q0, tsz, a, kb, pa, pan, ct_, st_)
                    yield

            def ev_q(i, q0, tsz, a, kb, pa, pan, ct_, st_):
                res = rope(pa[0:tsz, :], pan, qbf[0:tsz, :], 'qbf', 8, tsz, ct_, st_, tabres)
                CK('r3')
                pt, ptn = transpose_blocks(qbf, res, 4, tsz)
                CK('r4')
                A('copy', [ptn], ['qT'], out=qT[0:64, :, 0, q0:q0 + tsz], in_=blkview(pt, 4, tsz)[0:64])
                A('copy', [ptn], ['qT'], out=qT[64:128, :, 1, q0:q0 + tsz], in_=blkview(pt, 4, tsz)[64:128])

            def ev_k(i, q0, tsz, a, kb, pa, pan, ct_, st_):
                res = rope(pa[0:tsz, :], pan, kst[0:tsz, i, :], 'kst%d' % i, 8, tsz, ct_, st_, tabres)
                G('tensor_copy', res, ['kbf'], out=kbf[0:tsz, :], in_=kst[0:tsz, i, :])
                pt, ptn = transpose_blocks(kbf, ['kbf'], 4, tsz)
                A('copy', [ptn], ['kT'], out=kT[:, :, a:a + tsz], in_=blkview(pt, 4, tsz))
                stores.append((res, dict(out=S['k_out'][l][a - past:a - past + tsz, :], in_=kst[0:tsz, i, :])))

            def ev_v(i, q0, tsz, a, kb, pa, pan, ct_, st_):
                A('copy', [pan], ['vst%d' % i], out=vst[0:tsz, i, :], in_=pa[0:tsz, :])
                V('tensor_copy', [pan], ['Vt'], out=Vt[0:tsz, kb, :, 0:64], in_=pa[0:tsz, :].rearrange("p (h d) -> p h d", h=8))
                stores.append((['vst%d' % i], dict(out=S['v_out'][l][a - past:a - past + tsz, :], in_=vst[0:tsz, i, :])))

            def ev_ga(i, q0, tsz, a, kb, pa, pan, ct_, st_):
                A('activation', [pan], ['sga%d' % i], out=sga[0:tsz, i, :], in_=pa[0:tsz, :], func=AF.Tanh, scale=0.5)
                V('scalar_tensor_tensor', [pan, 'sga%d' % i], ['sga%d' % i], out=sga[0:tsz, i, :], in0=sga[0:tsz, i, :], scalar=1.0,
                  in1=pa[0:tsz, :], op0=ALU.add, op1=ALU.mult)

            def ev_idx(i, q0, tsz, a, kb, pa, pan, ct_, st_):
                res = rope(pa[0:tsz, 0:256], pan, qibf[0:tsz, :], 'qibf', 4, tsz, ct_, st_, tabres)
                pt, ptn = transpose_blocks(qibf, res, 2, tsz)
                A('copy', [ptn], ['qiT'], out=qiT[:, :, q0:q0 + tsz], in_=blkview(pt, 2, tsz))
                res2 = rope(pa[0:tsz, 256:320], pan, kist[0:tsz, i, :], 'kist%d' % i, 1, tsz, ct_, st_, tabres)
                G('tensor_copy', res2, ['kibf0'], out=kibf[0:tsz, 0, :], in_=kist[0:tsz, i, :])
                G('tensor_copy', res2, ['kibf1'], out=kibf[0:tsz, 1, :], in_=kist[0:tsz, i, :])
                pt2, ptn2 = next_pt()
                T('transpose', ['kibf0', 'kibf1', 'ident'], [ptn2], out=pt2[:, 0:tsz], in_=kibf[0:tsz, :, :].rearrange("p a b -> p (a b)"),
                  identity=ident[0:tsz, 0:tsz])
                A('copy', [ptn2], ['kiT'], out=kiT[:, a:a + tsz], in_=pt2[:, 0:tsz])
                A('activation', [pan], ['wabs%d' % i], out=wabs[0:tsz, i, :], in_=pa[0:tsz, 320:324], func=AF.Abs)
                A('activation', [pan], ['wsgn%d' % i], out=wsgn[0:tsz, i, :], in_=pa[0:tsz, 320:324], func=AF.Sign)
                stores.append((res2, dict(out=S['ki_out'][l][a - past:a - past + tsz, :], in_=kist[0:tsz, i, :])))

            CK('step1')

            def gen_step2():
                yield from tm_group(C_Q, 512, ev_q)
                yield from tm_group(C_K, 512, ev_k)
                yield from tm_group(C_V, 512, ev_v)
                yield from tm_group(C_GA, 512, ev_ga)
                yield from tm_group(C_QI, 324, ev_idx)

            def fm_proj(wv, wn, j):
                pa, pan = next_pa()
                for kt in range(8):
                    T('matmul', ['hnT', wn], [pan], out=pa[:, 0:Tn], lhsT=wv[:, kt, j * 128:(j + 1) * 128], rhs=hnT[:, kt, 0:Tn],
                      start=(kt == 0), stop=(kt == 7))
                return pa, pan

            def gen_B1():
              yield
              wxb, wxbn = load_w_in(l, C_XB, 512, pin=True)
              for j in range(4):
                  pa, pan = fm_proj(wxb, wxbn, j)
                  xp_, xpn = Bxp[j % 2], 'Bxp%d' % (j % 2)
                  xc_, xcn = Bxc[j % 2], 'Bxc%d' % (j % 2)
                  A('copy', [pan], [xpn + 'b'], out=xp_[:, 3:3 + Tn], in_=pa[:, 0:Tn])
                  G('tensor_copy', ['convst%d' % j], [xpn + 'a'], out=xp_[:, 0:3], in_=convst[:, j, :])
                  G('tensor_copy', [xpn + 'a', xpn + 'b'], ['convst%d' % j], out=convst[:, j, :], in_=xp_[:, Tn:Tn + 3])
                  V('tensor_scalar', [xpn + 'a', xpn + 'b', 'cw', 'cb'], [xcn], out=xc_[:, 0:Tn], in0=xp_[:, 0:Tn], scalar1=cw[:, j, 0:1],
                    scalar2=cb[:, j:j + 1], op0=ALU.mult, op1=ALU.add)
                  for tap in range(1, 4):
                      V('scalar_tensor_tensor', [xpn + 'a', xpn + 'b', 'cw', xcn], [xcn], out=xc_[:, 0:Tn], in0=xp_[:, tap:tap + Tn],
                        scalar=cw[:, j, tap:tap + 1], in1=xc_[:, 0:Tn], op0=ALU.mult, op1=ALU.add)
                  G('tensor_copy', [xcn], ['xcb'], out=xcb[:, 0:Tn], in_=xc_[:, 0:Tn])
                  yield
                  pa1, pan1 = next_pa()
                  T('matmul', ['waBD', 'xcb'], [pan1], out=pa1[:, 0:Tn], lhsT=waBD[:, j, :], rhs=xcb[:, 0:Tn], start=True, stop=True)
                  pa2, pan2 = next_pa()
                  T('matmul', ['wxBD', 'xcb'], [pan2], out=pa2[:, 0:Tn], lhsT=wxBD[:, j, :], rhs=xcb[:, 0:Tn], start=True, stop=True)
                  A('activation', [pan1, 'hba'], ['Bta'], out=Bta[:, 0:Tn], in_=pa1[:, 0:Tn], func=AF.Tanh, bias=hba[:, j:j + 1], scale=0.5)
                  A('activation', [pan2, 'hbx'], ['Btx'], out=Btx[:, 0:Tn], in_=pa2[:, 0:Tn], func=AF.Tanh, bias=hbx[:, j:j + 1], scale=0.5)
                  A('activation', ['Bta', 'clf'], ['Ba2'], out=Ba2[:, 0:Tn], in_=Bta[:, 0:Tn], func=AF.Exp, bias=clf[:, j:j + 1], scale=clf[:, j:j + 1])
                  V('tensor_scalar', ['Ba2'], ['om'], out=om[:, j * TB:j * TB + Tn], in0=Ba2[:, 0:Tn], scalar1=-1.0, scalar2=1.0,
                    op0=ALU.mult, op1=ALU.add)
                  V('scalar_tensor_tensor', ['Btx', xcn], ['ixc'], out=ixc[:, j * TB:j * TB + Tn], in0=Btx[:, 0:Tn], scalar=1.0,
                    in1=xc_[:, 0:Tn], op0=ALU.add, op1=ALU.mult)
                  yield

            run_zip(gen_step2(), 2 * len(tiles) * 5 // 2, gen_B1(), 9)
            unpin_w()
            CK('step2')

            wgb, wgbn = load_w_in(l, C_GB, 512)
            for j in range(4):
                A('activation', ['om'], ['Baj%d' % j], out=Baj[j][:, 0:Tn], in_=om[:, j * TB:j * TB + Tn], func=AF.Sqrt, bias=1.0, scale=-1.0)
            for j in range(4):
                A('activation', ['om'], ['om'], out=om[:, j * TB:j * TB + Tn], in_=om[:, j * TB:j * TB + Tn], func=AF.Sqrt)
            for j in range(4):
                V('scalar_tensor_tensor', ['ixc', 'om'], ['Bbb'], out=Bbb[:, 0:Tn], in0=ixc[:, j * TB:j * TB + Tn], scalar=0.5,
                  in1=om[:, j * TB:j * TB + Tn], op0=ALU.mult, op1=ALU.mult)
                V('tensor_tensor_scan', ['Baj%d' % j, 'Bbb', 'hst%d' % j], ['Bh'], out=Bh[:, 0:Tn], data0=Baj[j][:, 0:Tn], data1=Bbb[:, 0:Tn],
                  initial=hst[:, j:j + 1], op0=ALU.mult, op1=ALU.add)
                G('tensor_copy', ['Bh'], ['hst%d' % j], out=hst[:, j:j + 1], in_=Bh[:, Tn - 1:Tn])
                pa, pan = fm_proj(wgb, wgbn, j)
                A('activation', [pan], ['Btg'], out=Btg[:, 0:Tn], in_=pa[:, 0:Tn], func=AF.Tanh, scale=0.5)
                V('scalar_tensor_tensor', [pan, 'Btg'], ['Btg'], out=Btg[:, 0:Tn], in0=Btg[:, 0:Tn], scalar=1.0, in1=pa[:, 0:Tn],
                  op0=ALU.add, op1=ALU.mult)
                V('scalar_tensor_tensor', ['Btg', 'Bh'], ['ybT'], out=yT[1][:, j, 0:Tn], in0=Btg[:, 0:Tn], scalar=0.5, in1=Bh[:, 0:Tn],
                  op0=ALU.mult, op1=ALU.mult)

            for r, kw in stores:
                DMA('sp', 's_' + r[0], r, (), **kw)

            def gen_branchC():
                wxc, wxcn = load_w_in(l, C_XC, 512)
                wgc, wgcn = load_w_in(l, C_GC, 512)
                L = 15 + Tn
                for g in range(4):
                    wdw = 2 ** (g + 1)
                    pa, pan = fm_proj(wxc, wxcn, g)
                    G('tensor_copy', ['poolst%d' % g], ['ct0a'], out=ctmp[:, 0, 0:15], in_=poolst[:, g, :])
                    A('copy', [pan], ['ct0b'], out=ctmp[:, 0, 15:L], in_=pa[:, 0:Tn])
                    G('tensor_copy', ['ct0a', 'ct0b'], ['poolst%d' % g], out=poolst[:, g, :], in_=ctmp[:, 0, Tn:L])
                    prev, prevn = 0, ['ct0a', 'ct0b']
                    m = 1
                    slot = 1
                    while m < wdw:
                        G('tensor_tensor', prevn, ['ct%d' % slot], out=ctmp[:, slot, 2 * m - 1:L], in0=ctmp[:, prev, 2 * m - 1:L],
                          in1=ctmp[:, prev, m - 1:L - m], op=ALU.add)
                        prev, prevn = slot, ['ct%d' % slot]
                        slot = 1 + (slot % 3)
                        m *= 2
                    yield
                    if blk['first']:
                        V('tensor_tensor', prevn + ['rcnt'], ['ctg'], out=ctg[:, 0:Tn], in0=ctmp[:, prev, 15:L], in1=rcnt[:, g, 0:Tn], op=ALU.mult)
                        V('tensor_tensor', ['ctg', 'ct0b'], ['cpl'], out=cpl[:, 0:Tn], in0=ctg[:, 0:Tn], in1=ctmp[:, 0, 15:L], op=ALU.subtract)
                    else:
                        V('scalar_tensor_tensor', prevn + ['ct0b'], ['cpl'], out=cpl[:, 0:Tn], in0=ctmp[:, prev, 15:L], scalar=1.0 / wdw,
                          in1=ctmp[:, 0, 15:L], op0=ALU.mult, op1=ALU.subtract)
                    pa1, pan1 = next_pa()
                    T('matmul', ['pwB', 'cpl'], [pan1], out=pa1[:, 0:Tn], lhsT=pwB[:, g, :], rhs=cpl[:, 0:Tn], start=True, stop=True)
                    pa2, pan2 = fm_proj(wgc, wgcn, g)
                    A('activation', [pan2], ['ctg'], out=ctg[:, 0:Tn], in_=pa2[:, 0:Tn], func=AF.Tanh, scale=0.5)
                    V('scalar_tensor_tensor', [pan2, 'ctg'], ['ctg'], out=ctg[:, 0:Tn], in0=ctg[:, 0:Tn], scalar=1.0, in1=pa2[:, 0:Tn],
                      op0=ALU.add, op1=ALU.mult)
                    V('tensor_scalar', ['ctg', 'psc'], ['ctg'], out=ctg[:, 0:Tn], in0=ctg[:, 0:Tn], scalar1=psc[:, g:g + 1], scalar2=0.5,
                      op0=ALU.mult, op1=ALU.mult)
                    V('tensor_tensor', ['ctg', pan1], ['ycT'], out=yT[2][:, g, 0:Tn], in0=ctg[:, 0:Tn], in1=pa1[:, 0:Tn], op=ALU.mult)
                    yield

            def rec_scores(i):
                q0, tsz, a, kb, hm, tix = tiles[i]
                Sv = a + tsz
                for c0 in range(0, Sv, 512):
                    n = min(512, Sv - c0)
                    for h in range(4):
                        hs = slice((h % 2) * 64, (h % 2) * 64 + 64)
                        pa, pan = next_pa()
                        T('matmul', ['qiT', 'kiT'], [pan], out=pa[0:tsz, 0:n], lhsT=qiT[hs, h // 2, q0:q0 + tsz], rhs=kiT[hs, c0:c0 + n],
                          start=True, stop=True)
                        r_, rn = rl[h % NRL], 'rl%d' % (h % NRL)
                        A('activation', [pan, 'wabs%d' % i], [rn], out=r_[0:tsz, 0:n], in_=pa[0:tsz, 0:n], func=AF.Relu, scale=wabs[0:tsz, i, h:h + 1])
                        if h == 0:
                            V('tensor_scalar', [rn, 'wsgn%d' % i], ['sc'], out=sc[0:tsz, c0:c0 + n], in0=r_[0:tsz, 0:n], scalar1=wsgn[0:tsz, i, 0:1],
                              scalar2=None, op0=ALU.mult)
                        else:
                            V('scalar_tensor_tensor', [rn, 'wsgn%d' % i, 'sc'], ['sc'], out=sc[0:tsz, c0:c0 + n], in0=r_[0:tsz, 0:n],
                              scalar=wsgn[0:tsz, i, h:h + 1], in1=sc[0:tsz, c0:c0 + n], op0=ALU.mult, op1=ALU.add)

            def gen_bisect(i):
                q0, tsz, a, kb, hm, tix = tiles[i]
                Sv = a + tsz
                V('tensor_reduce', ['sc'], ['bs0'], out=bs[0:tsz, 0:1], in_=sc[0:tsz, 0:Sv], axis=AX.X, op=ALU.min)
                V('tensor_reduce', ['sc'], ['bs1'], out=bs[0:tsz, 1:2], in_=sc[0:tsz, 0:Sv], axis=AX.X, op=ALU.max)
                if hm:
                    G('memset', ['bs0', 'bs1'], ['sc'], ap=sc[0:64, Sv - 64:Sv], constant=NEG)
                V('tensor_scalar', ['bs0', 'bs1'], ['bs2'], out=bs[0:tsz, 2:3], in0=bs[0:tsz, 1:2], scalar1=bs[0:tsz, 0:1], scalar2=1.003,
                  op0=ALU.subtract, op1=ALU.mult)
                V('scalar_tensor_tensor', ['bs0', 'bs2'], ['bs6'], out=bs[0:tsz, 6:7], in0=bs[0:tsz, 2:3], scalar=-0.001, in1=bs[0:tsz, 0:1],
                  op0=ALU.mult, op1=ALU.add)
                V('tensor_scalar', ['bs2', 'hpow'], ['steps'], out=steps[0:tsz, :], in0=hpow[0:tsz, :], scalar1=bs[0:tsz, 2:3], scalar2=None, op0=ALU.mult)
                V('tensor_tensor', ['bs6', 'steps'], ['mid'], out=bs[0:tsz, 3:4], in0=bs[0:tsz, 6:7], in1=steps[0:tsz, 1:2], op=ALU.add)
                yield
                cA = int(Sv * (0.46 if i == 0 else 0.52)) // 2 * 2 if Sv >= 512 else Sv
                nA = Sv - cA
                thr = TOPK - nA / 2.0
                cc = thr - 0.25 - cA
                for k in range(1, NIT + 1):
                    kk = k + 1 if k < NIT else k
                    V('tensor_scalar', ['sc', 'mid'], ['mkV', 'cnt'], out=mk[0:tsz, 0:cA], in0=sc[0:tsz, 0:cA], scalar1=bs[0:tsz, 3:4], scalar2=cc,
                      op0=ALU.is_lt, op1=ALU.add, accum_out=bs[0:tsz, 4:5])
                    if nA:
                        A('activation', ['sc', 'mid'], ['mkA', 'cntA'], out=mk[0:tsz, cA:Sv], in_=sc[0:tsz, cA:Sv], func=AF.Sign, bias=bs[0:tsz, 3:4],
                          scale=-1.0, accum_out=bs[0:tsz, 7:8])
                    V('tensor_scalar', ['mid', 'steps'], ['bu'], out=bs[0:tsz, 9:10], in0=bs[0:tsz, 3:4], scalar1=steps[0:tsz, kk:kk + 1], scalar2=None,
                      op0=ALU.subtract)
                    if nA:
                        V('scalar_tensor_tensor', ['cnt', 'cntA'], ['bd'], out=bs[0:tsz, 5:6], in0=bs[0:tsz, 7:8], scalar=-0.5, in1=bs[0:tsz, 4:5],
                          op0=ALU.mult, op1=ALU.is_ge)
                    else:
                        V('tensor_scalar', ['cnt'], ['bd'], out=bs[0:tsz, 5:6], in0=bs[0:tsz, 4:5], scalar1=0.0, scalar2=None, op0=ALU.is_le)
                    V('scalar_tensor_tensor', ['bd', 'steps', 'bu'], ['mid'], out=bs[0:tsz, 3:4], in0=bs[0:tsz, 5:6], scalar=steps[0:tsz, k:k + 1],
                      in1=bs[0:tsz, 9:10], op0=ALU.mult, op1=ALU.add)
                    yield
                V('tensor_scalar', ['sc', 'mid'], ['mk'], out=mk[0:tsz, 0:Sv], in0=sc[0:tsz, 0:Sv], scalar1=bs[0:tsz, 3:4], scalar2=None, op0=ALU.is_ge)
                yield

            def mask_bias(i):
                return len(tiles) == 2 and i == 0

            def rec_masktrans(i):
                q0, tsz, a, kb, hm, tix = tiles[i]
                Sv = a + tsz
                kbs = [b for b in S['kblocks'] if b[0] + b[1] <= Sv]
                for g0 in range(0, len(kbs), 8):
                    grp = kbs[g0:g0 + 8]
                    pt, ptn = next_pt()
                    for s_, (c0, kr, kbi) in enumerate(grp):
                        T('transpose', ['mk', 'ident'], [ptn], out=pt[0:kr, s_ * 128:s_ * 128 + tsz], in_=mk[0:tsz, c0:c0 + kr], identity=ident[0:tsz, 0:tsz])
                    if mask_bias(i):
                        kwm = dict(func=AF.Identity, scale=30000.0, bias=-30000.0)
                    else:
                        kwm = dict(func=AF.Identity)
                    if all(x[1] == 128 for x in grp):
                        A('activation', [ptn], ['mkT'], out=mkT[:, g0:g0 + len(grp), 0:tsz], in_=blkview(pt, len(grp), tsz), **kwm)
                    else:
                        for s_, (c0, kr, kbi) in enumerate(grp):
                            A('activation', [ptn], ['mkT'], out=mkT[0:kr, g0 + s_, 0:tsz], in_=pt[0:kr, s_ * 128:s_ * 128 + tsz], **kwm)

            def gen_attention(i, npool=4):
                q0, tsz, a, kb, hm, tix = tiles[i]
                Sv = a + tsz
                kbs = [b for b in S['kblocks'] if b[0] + b[1] <= Sv]
                nkb = len(kbs)
                MB = mask_bias(i)

                lbufs = [([psA[0], psA[1]], ['psA0', 'psA1']), ([psA[2], psA[3]], ['psA2', 'psA3'])]
                if MB:
                    lbufs.append(([psT[0][:, :].bitcast(F32), psT[1][:, :].bitcast(F32)], ['psT0', 'psT1']))
                NLB = len(lbufs)

                def logits(bi):
                    c0, kr, kbi = kbs[bi]
                    pl, pln = lbufs[bi % NLB]
                    for hp in range(4):
                        hb = hp // 2
                        o = (hp % 2) * 256
                        if tsz == 128:
                            T('matmul', ['kT', 'qT'], [pln[hb]], out=pl[hb][0:kr, o:o + 256],
                              lhsT=kT[:, hp, c0:c0 + kr], rhs=qT[:, hp, :, q0:q0 + tsz], start=(hp % 2 == 0), stop=(not MB and hp % 2 == 1))
                        else:
                            for w in range(2):
                                T('matmul', ['kT', 'qT'], [pln[hb]], out=pl[hb][0:kr, o + w * 128:o + w * 128 + tsz],
                                  lhsT=kT[:, hp, c0:c0 + kr], rhs=qT[:, hp, w, q0:q0 + tsz], start=(hp % 2 == 0 and w == 0),
                                  stop=(not MB and hp % 2 == 1 and w == 1))
                    for hb in (range(2) if MB else ()):
                        if tsz == 128:
                            T('matmul', ['mkT', 'ident'], [pln[hb]], out=pl[hb][0:kr, :], lhsT=ident[0:kr, 0:kr],
                              rhs=mkT[0:kr, bi, 0:tsz].unsqueeze(1).to_broadcast([kr, 4, tsz]), start=False, stop=True)
                        else:
                            for s4 in range(4):
                                T('matmul', ['mkT', 'ident'], [pln[hb]], out=pl[hb][0:kr, s4 * 128:s4 * 128 + tsz], lhsT=ident[0:kr, 0:kr],
                                  rhs=mkT[0:kr, bi, 0:tsz], start=False, stop=(s4 == 3))

                def pv(bi):
                    c0, kr, kbi = kbs[bi]
                    px, pxn = pex[bi % 2], 'pex%d' % (bi % 2)
                    pdeps = [pxn + '0', pxn + '1']
                    if not MB:
                        px = pm[bi % 2]
                        pdeps = ['pm%da' % (bi % 2), 'pm%db' % (bi % 2)]
                    for h in range(8):
                        T('matmul', pdeps + ['Vt'], ['psB%d' % (h // 4)], out=psB[h // 4][0:tsz, (h % 4) * 65:(h % 4) * 65 + 65], lhsT=px[0:kr, h, 0:tsz],
                          rhs=Vt[0:kr, kbi, h, :], start=(bi == 0 and h % 4 == 0), stop=(bi == nkb - 1 and h % 4 == 3))
                for b0 in range(min(NLB, nkb)):
                    logits(b0)
                for bi, (c0, kr, kbi) in enumerate(kbs):
                    par = bi % 2
                    pl, pln = lbufs[bi % NLB]
                    px, pxn = pex[par], 'pex%d' % par
                    for hh in range(2):
                        A('activation', [pln[hh]], [pxn + str(hh)], out=px[0:kr, hh * 4:hh * 4 + 4, 0:tsz],
                          in_=pl[hh][0:kr, :].rearrange("p (h t) -> p h t", h=4)[:, :, 0:tsz], func=AF.Exp, scale=0.125)
                    if not MB:
                        pm_ = pm[par]
                        mbp = mkT[0:kr, bi, 0:tsz].unsqueeze(1).to_broadcast([kr, npool, tsz])
                        mbv = mkT[0:kr, bi, 0:tsz].unsqueeze(1).to_broadcast([kr, 8 - npool, tsz])
                        G('tensor_tensor', [pxn + '0', pxn + '1', 'mkT'], ['pm%da' % par], out=pm_[0:kr, 0:npool, 0:tsz], in0=px[0:kr, 0:npool, 0:tsz], in1=mbp, op=ALU.mult)
                        V('tensor_tensor', [pxn + '0', pxn + '1', 'mkT'], ['pm%db' % par], out=pm_[0:kr, npool:8, 0:tsz], in0=px[0:kr, npool:8, 0:tsz], in1=mbv, op=ALU.mult)
                    if bi + NLB < nkb:
                        logits(bi + NLB)
                    if bi >= 1:
                        pv(bi - 1)
                    yield
                pv(nkb - 1)

            def rec_attn_final(i):
                q0, tsz, a, kb, hm, tix = tiles[i]
                for hh in range(2):
                    pv = psB[hh][0:tsz, 0:260].rearrange("p (h e) -> p h e", h=4)
                    V('tensor_scalar', ['psB%d' % hh], ['rec%d' % hh], out=rec[0:tsz, hh * 4:hh * 4 + 4], in0=pv[:, :, 64], scalar1=2.0, scalar2=None,
                      op0=ALU.mult)
                    V('reciprocal', ['rec%d' % hh], ['rec%d' % hh], out=rec[0:tsz, hh * 4:hh * 4 + 4], in_=rec[0:tsz, hh * 4:hh * 4 + 4])
                    V('tensor_tensor', ['psB%d' % hh, 'rec%d' % hh], ['att%d' % hh], out=att[0:tsz, hh * 4:hh * 4 + 4, :], in0=pv[:, :, 0:64],
                      in1=rec[0:tsz, hh * 4:hh * 4 + 4].unsqueeze(2).to_broadcast([tsz, 4, 64]), op=ALU.mult)
                G('tensor_tensor', ['att0', 'att1', 'sga%d' % i], ['yab'], out=yab[0:tsz, :], in0=att[0:tsz, :, :].rearrange("p h d -> p (h d)"),
                  in1=sga[0:tsz, i, :], op=ALU.mult)
                pt, ptn = transpose_blocks(yab, ['yab'], 4, tsz)
                A('copy', [ptn], ['yaT'], out=yT[0][:, :, q0:q0 + tsz], in_=blkview(pt, 4, tsz))

            ynames = ['yaT', 'ybT', 'ycT']

            def gen_merge(ns, first_n, last_n, banks=None):
                bc = [0]

                def bank():
                    if banks is None:
                        return next_pa()
                    b = banks[bc[0] % len(banks)]
                    bc[0] += 1
                    return b
                for n in ns:
                    wb_, wbn = load_w_bo(l, n)
                    for e4 in range(2):
                        wg_, wgn = load_w_in(l, C_GM + n * 1024 + e4 * 512, 512)
                        for ee in range(4):
                            e = e4 * 4 + ee
                            pa, pan = bank()
                            for kt in range(8):
                                T('matmul', ['hnT', wgn], [pan], out=pa[:, 0:Tn], lhsT=wg_[:, kt, ee * 128:(ee + 1) * 128], rhs=hnT[:, kt, 0:Tn],
                                  start=(kt == 0), stop=(kt == 7))
                            pb, pbn = bank()
                            for ct in range(4):
                                T('matmul', [wbn, ynames[n]], [pbn], out=pb[:, 0:Tn], lhsT=wb_[:, ct, e * 128:(e + 1) * 128], rhs=yT[n][:, ct, 0:Tn],
                                  start=(ct == 0), stop=(ct == 3))
                            tg_, tgn = mtg[e % 2], 'mtg%d' % (e % 2)
                            A('activation', [pan], [tgn], out=tg_[:, 0:Tn], in_=pa[:, 0:Tn], func=AF.Tanh, scale=0.5)
                            mge = mg[:, e * TB:e * TB + Tn]
                            if n == first_n:
                                V('scalar_tensor_tensor', [tgn, pbn], ['mg'], out=mge, in0=tg_[:, 0:Tn], scalar=1.0, in1=pb[:, 0:Tn], op0=ALU.add, op1=ALU.mult)
                            else:
                                V('scalar_tensor_tensor', [tgn, pbn], [tgn], out=tg_[:, 0:Tn], in0=tg_[:, 0:Tn], scalar=1.0, in1=pb[:, 0:Tn],
                                  op0=ALU.add, op1=ALU.mult)
                                if n != last_n:
                                    G('tensor_tensor', [tgn, 'mg'], ['mg'], out=mge, in0=mge, in1=tg_[:, 0:Tn], op=ALU.add)
                                else:
                                    G('tensor_tensor', [tgn, 'mg'], ['mgb'], out=mgb[:, e * TB:e * TB + Tn], in0=mge, in1=tg_[:, 0:Tn], op=ALU.add)
                            yield

            psM = [(psT[0][:, :].bitcast(F32), 'psT0'), (psT[1][:, :].bitcast(F32), 'psT1')]

            def nkb_of(i):
                q0, tsz, a, kb, hm, tix = tiles[i]
                return len([b for b in S['kblocks'] if b[0] + b[1] <= a + tsz])

            rec_scores(0)
            run_zip(gen_branchC(), 8, gen_bisect(0), NIT + 2)
            rec_masktrans(0)
            CK('step3')
            if len(tiles) == 2:
                rec_scores(1)
                run_zip(gen_attention(0, 5), nkb_of(0), gen_bisect(1), NIT + 2)
                rec_attn_final(0)
                rec_masktrans(1)
                run_zip(gen_attention(1, 3), nkb_of(1), gen_merge([1, 2], 1, 0, psM), 16)
                rec_attn_final(1)
            else:
                run_zip(gen_attention(0, 3), nkb_of(0), gen_merge([1, 2], 1, 0, psM), 16)
                rec_attn_final(0)
            CK('step4')
            drain(gen_merge([0], 1, 0))
            for half in range(2):
                wo_, won = load_w_out(l, half)
                for i, (q0, tsz, a, kb, hm, tix) in enumerate(tiles):
                    pa, pan = next_pa()
                    for et in range(8):
                        T('matmul', ['mgb', won], [pan], out=pa[0:tsz, :], lhsT=mgb[:, et * TB + q0:et * TB + q0 + tsz], rhs=wo_[:, et, :],
                          start=(et == 0), stop=(et == 7))
                    xs_ = xt[i][0:tsz, half * 512:(half + 1) * 512]
                    V('scalar_tensor_tensor', [pan, 'xt%d' % i], ['xt%d' % i], out=xs_, in0=pa[0:tsz, :], scalar=0.5, in1=xs_, op0=ALU.mult, op1=ALU.add)
            for i, (q0, tsz, a, kb, hm, tix) in enumerate(tiles):
                xr = 'xt%d' % i
                if not last_layer:
                    DMA('act', 's_' + xr, [xr], [blk['xres'][l + 1]], out=xdst[q0:q0 + tsz, :], in_=xt[i][0:tsz, :])
                elif xdst is not None:
                    rms_rstd(i, tsz, xr)
                    V('scalar_tensor_tensor', [xr, 'rs%d' % i, 'gfbc'], [xr], out=xt[i][0:tsz, :], in0=xt[i][0:tsz, :], scalar=st_rs[0:tsz, i:i + 1],
                      in1=gfbc[0:tsz, :], op0=ALU.mult, op1=ALU.mult)
                    DMA('act', 's_' + xr, [xr], (), out=xdst[q0:q0 + tsz, :], in_=xt[i][0:tsz, :])

        def make_prompt():
            S = {'past': 0, 'tabs': (cosP, sinP), 'tabres': 'ropeP',
                 'k_out': [k_p[0], k_p[1]], 'v_out': [v_p[0], v_p[1]], 'ki_out': [ki_p[0], ki_p[1]]}
            S['kblocks'] = [(0, 16, 0)] + [(16 + 128 * j, 128, 1 + j) for j in range(32)]
            blocks = [{'T': 16, 'tiles': [(0, 16, 0, 0, False, 0)], 'first': True,
                       'xsrc': [xp_in[0:16, :], xscr_p[0:16, :]], 'xdst': [xscr_p[0:16, :], None], 'xres': ['xin', 'xscrp_m', 'none']}]
            nb = debug.get('nblk', SEQ // TB)
            for b in range(nb):
                a0 = 16 + b * TB
                tiles = []
                for i in range(TB // 128):
                    a = a0 + i * 128
                    kb = 1 + (a - 16) // 128
                    tiles.append((i * 128, 128, a, kb, True, kb))
                blocks.append({'T': TB, 'tiles': tiles, 'first': False,
                               'xsrc': [xp_in[a0:a0 + TB, :], xscr_p[a0:a0 + TB, :]],
                               'xdst': [xscr_p[a0:a0 + TB, :], y_p[a0 - 16:a0 - 16 + TB, :]], 'xres': ['xin', 'xscrp_%d' % b, 'none']})
            S['blocks'] = blocks
            return S

        def make_sample(si):
            S = {'past': PAST, 'tabs': (cosS, sinS), 'tabres': 'ropeS',
                 'k_out': [k_s[0, si], k_s[1, si]], 'v_out': [v_s[0, si], v_s[1, si]], 'ki_out': [ki_s[0, si], ki_s[1, si]]}
            S['kblocks'] = [(128 * j, 128, j) for j in range(16)] + [(PAST, 64, 16)]
            S['blocks'] = [{'T': TS, 'tiles': [(0, TS, PAST, 16, False, 0)], 'first': False,
                            'xsrc': [xs_in[si], xscr_s[si]], 'xdst': [xscr_s[si], y_s[si]], 'xres': ['xin', 'xscrs_%d' % si, 'none']}]
            return S

        st_names = ['convst%d' % j for j in range(4)] + ['hst%d' % j for j in range(4)] + ['poolst%d' % j for j in range(4)]

        def zero_states():
            G('memset', (), ['convst%d' % j for j in range(4)], ap=convst[:], constant=0.0)
            G('memset', (), ['hst%d' % j for j in range(4)], ap=hst[:], constant=0.0)
            G('memset', (), ['poolst%d' % j for j in range(4)], ap=poolst[:], constant=0.0)

        def load_states(l, si):
            for j in range(4):
                small_dma(convst[:, j, :], sconv_in[l, si][:, j * 128:(j + 1) * 128].rearrange("t p -> p t"), (), ['convst%d' % j])
                small_dma(poolst[:, j, :], spool_in[l, si][:, j * 128:(j + 1) * 128].rearrange("t p -> p t"), (), ['poolst%d' % j])
            small_dma(hst[:], fm(slru_in[l, si]), (), ['hst%d' % j for j in range(4)])

        def store_states(conv_o, lru_o, pool_o):
            for j in range(4):
                small_dma(conv_o[:, j * 128:(j + 1) * 128].rearrange("t p -> p t"), convst[:, j, :], ['convst%d' % j], ())
                small_dma(pool_o[:, j * 128:(j + 1) * 128].rearrange("t p -> p t"), poolst[:, j, :], ['poolst%d' % j], ())
            small_dma(fm(lru_o), hst[:], ['hst%d' % j for j in range(4)], ())

        def load_cache(l, si):
            for j in range(16):
                sl = slice(j * 128, (j + 1) * 128)
                kb_, kbn = (kbf, 'kbf') if j % 2 == 0 else (qbf, 'qbf')
                DMA('pool', 'c_' + kbn, (), [kbn], out=kb_[:], in_=ck_in[l, si, sl, :])
                pt, ptn = transpose_blocks(kb_, [kbn], 4, 128)
                A('copy', [ptn], ['kT'], out=kT[:, :, sl], in_=blkview(pt, 4, 128))
                DMA('pool', 'c_Vt', (), ['Vt'], out=Vt[:, j, :, 0:64], in_=cv_in[l, si, sl, :].rearrange("p (h d) -> p h d", h=8))
                if j % 2 == 0:
                    ki_ = kibf[:, :, :]
                    kin = ['kibf0', 'kibf1']
                else:
                    ki_ = qibf[:, 0:128].rearrange("p (a b) -> p a b", a=2)
                    kin = ['qibf', 'qibf']
                DMA('pool', 'c_' + kin[0], (), [kin[0]], out=ki_[:, 0, :], in_=cki_in[l, si, sl, :])
                DMA('pool', 'c_' + kin[1] + 'x', (), [kin[1]], out=ki_[:, 1, :], in_=cki_in[l, si, sl, :])
                pt2, ptn2 = next_pt()
                T('transpose', list(set(kin)) + ['ident'], [ptn2], out=pt2[:, 0:128], in_=ki_.rearrange("p a b -> p (a b)"), identity=ident[:, :])
                A('copy', [ptn2], ['kiT'], out=kiT[:, sl], in_=pt2[:, 0:128])

        nlayers = debug.get('nlayers', 2)
        if not debug.get('skip_prompt'):
            Sp = make_prompt()
            for l in range(nlayers):
                layer_setup(l)
                zero_states()
                for bix, blk in enumerate(Sp['blocks']):
                    process_block(Sp, l, blk, l == 1)
                    if l == 0 and bix == min(2, len(Sp['blocks']) - 1):
                        cast_weights(1)
                store_states(conv_p[l], lru_p[l], pool_p[l])
        if not debug.get('skip_sample'):
            for si in range(debug.get('nsample', 2)):
                Ss = make_sample(si)
                for l in range(nlayers):
                    layer_setup(l)
                    CK('setup')
                    load_states(l, si)
                    CK('states')
                    load_cache(l, si)
                    CK('cache')
                    for blk in Ss['blocks']:
                        process_block(Ss, l, blk, l == 1)
                    store_states(conv_s[l, si], lru_s[l, si], pool_s[l, si])
        P.emit()
        build_program.stats = P.stats
    return nc


_CACHE = {}


def kernel(x_prompt, x_sample, cache_k, cache_v, cache_kidx, state_conv, state_lru, state_pool,
           meta_tokens, norm_g, w_in, conv_w, conv_b, lru_wa, lru_ba, lru_wx, lru_bx, lru_lambda,
           pool_w, pool_scale, w_branch_out, w_out, final_norm_g):
    if 'nc' not in _CACHE:
        _CACHE['nc'] = build_program()
    nc = _CACHE['nc']
    in_maps = _make_in_maps(x_prompt, x_sample, cache_k, cache_v, cache_kidx, state_conv, state_lru, state_pool,
                            meta_tokens, norm_g, w_in, conv_w, conv_b, lru_wa, lru_ba, lru_wx, lru_bx, lru_lambda,
                            pool_w, pool_scale, w_branch_out, w_out, final_norm_g)
    res = run_bass_kernel_spmd(nc, in_maps, core_ids=list(range(8)))
    return _assemble(res.results)


def _make_in_maps(x_prompt, x_sample, cache_k, cache_v, cache_kidx, state_conv, state_lru, state_pool,
                  meta_tokens, norm_g, w_in, conv_w, conv_b, lru_wa, lru_ba, lru_wx, lru_bx, lru_lambda,
                  pool_w, pool_scale, w_branch_out, w_out, final_norm_g):
    f = lambda a: np.ascontiguousarray(np.asarray(a, dtype=np.float32))
    x_prompt = f(x_prompt); x_sample = f(x_sample); meta = f(meta_tokens)
    ck = f(cache_k).reshape(2, 16, PAST, 512)
    cv = f(cache_v).reshape(2, 16, PAST, 512)
    cki = f(cache_kidx)
    sconv = f(state_conv); slru = f(state_lru); spool = f(state_pool)
    posp = np.zeros((128, NKB), np.float32)
    posp[:, 0] = np.arange(128)
    for j in range(1, NKB):
        posp[:, j] = 16 + 128 * (j - 1) + np.arange(128)
    poss = (PAST + np.arange(128, dtype=np.float32)).reshape(128, 1).astype(np.float32)
    shared = {
        "posp": posp, "poss": poss, "norm_g": f(norm_g), "w_in": f(w_in), "conv_w": f(conv_w), "conv_b": f(conv_b),
        "lru_wa": f(lru_wa), "lru_ba": f(lru_ba), "lru_wx": f(lru_wx), "lru_bx": f(lru_bx), "lru_lambda": f(lru_lambda),
        "pool_w": f(pool_w), "pool_scale": f(pool_scale), "w_branch_out": f(w_branch_out), "w_out": f(w_out),
        "final_norm_g": f(final_norm_g),
    }
    in_maps = []
    for c in range(8):
        b = c % 4
        ss = [2 * c, 2 * c + 1]
        m = dict(shared)
        m["xp"] = np.ascontiguousarray(np.concatenate([meta, x_prompt[b]], axis=0))
        m["xs"] = np.ascontiguousarray(x_sample[ss])
        m["ck"] = np.ascontiguousarray(ck[:, ss])
        m["cv"] = np.ascontiguousarray(cv[:, ss])
        m["cki"] = np.ascontiguousarray(cki[:, ss])
        m["sconv"] = np.ascontiguousarray(sconv[:, ss])
        m["slru"] = np.ascontiguousarray(slru[:, ss])
        m["spool"] = np.ascontiguousarray(spool[:, ss])
        in_maps.append(m)
    return in_maps


def _assemble(R):
    y_prompt = np.stack([R[b]["y_p"] for b in range(4)], axis=0)
    y_sample = np.concatenate([R[c]["y_s"] for c in range(8)], axis=0)
    k_prompt = np.stack([R[b]["k_p"] for b in range(4)], axis=1).reshape(2, 4, TP, 8, 64)
    v_prompt = np.stack([R[b]["v_p"] for b in range(4)], axis=1).reshape(2, 4, TP, 8, 64)
    ki_prompt = np.stack([R[b]["ki_p"] for b in range(4)], axis=1)
    conv_prompt = np.stack([R[b]["conv_p"] for b in range(4)], axis=1)
    lru_prompt = np.stack([R[b]["lru_p"] for b in range(4)], axis=1)
    pool_prompt = np.stack([R[b]["pool_p"] for b in range(4)], axis=1)
    k_sample = np.concatenate([R[c]["k_s"] for c in range(8)], axis=1).reshape(2, 16, TS, 8, 64)
    v_sample = np.concatenate([R[c]["v_s"] for c in range(8)], axis=1).reshape(2, 16, TS, 8, 64)
    ki_sample = np.concatenate([R[c]["ki_s"] for c in range(8)], axis=1)
    conv_sample = np.concatenate([R[c]["conv_s"] for c in range(8)], axis=1)
    lru_sample = np.concatenate([R[c]["lru_s"] for c in range(8)], axis=1)
    pool_sample = np.concatenate([R[c]["pool_s"] for c in range(8)], axis=1)
    outs = (y_prompt, y_sample, k_prompt, v_prompt, ki_prompt, conv_prompt, lru_prompt, pool_prompt,
            k_sample, v_sample, ki_sample, conv_sample, lru_sample, pool_sample)
    return tuple(np.ascontiguousarray(o, dtype=np.float32) for o in outs)
```

```python
import math
import contextlib
import numpy as np
import concourse.bass as bass
import concourse.mybir as mybir
from concourse.bass_utils import run_bass_kernel_spmd

F32 = mybir.dt.float32
BF16 = mybir.dt.bfloat16
I32 = mybir.dt.int32
ALU = mybir.AluOpType
AF = mybir.ActivationFunctionType
AX = mybir.AxisListType

D = 1024
SEQ = 4096
NMETA = 16
TP = NMETA + SEQ
NIN = 7492
PAST = 2048
TS = 64
NKEY = TP
NKB = 33
TB = 256
NIT = 14
TOPK = 256.0
C_Q, C_K, C_V, C_GA, C_QI, C_KI, C_WI = 0, 512, 1024, 1536, 2048, 2304, 2368
C_XB, C_GB, C_XC, C_GC, C_GM = 2372, 2884, 3396, 3908, 4420
EPS = 1e-6
NEG = -1.0e30
MASK_BIAS = False


class Prog:
    EPOCH = 24000

    def __init__(self, nc, es):
        self.nc = nc
        self.es = es
        self.ops = []
        self.count = {}
        self.lastw = {}
        self.readers = {}
        self.conf = {}
        self.eng = {'pe': nc.tensor, 'act': nc.scalar, 'dve': nc.vector, 'pool': nc.gpsimd, 'sp': nc.sync}

    def alias(self, names_a, names_b):
        for a in names_a:
            for b in names_b:
                self.conf.setdefault(a, set()).add(b)
                self.conf.setdefault(b, set()).add(a)

    def _names(self, r):
        c = self.conf.get(r)
        if c:
            return [r] + list(c)
        return [r]

    stopped = False

    def _rec(self, agent, queue, fn, reads, writes, is_dma):
        if self.stopped:
            return
        seq = self.count.get(agent, 0) + 1
        self.count[agent] = seq
        deps = {}

        def need(a, s):
            if a.startswith('dma:'):
                s = self.count[a] - (1 if a == agent else 0)
                if s <= 0:
                    return
            if deps.get(a, 0) < s:
                deps[a] = s
        for r0 in list(reads) + list(writes):
            for r in self._names(r0):
                lw = self.lastw.get(r)
                if lw is not None:
                    a, s = lw
                    if a == agent and agent == 'pe':
                        continue
                    need(a, s)
        for w0 in writes:
            for w in self._names(w0):
                for (a, s) in self.readers.get(w, ()):
                    if a == agent and agent == 'pe':
                        continue
                    need(a, s)
        for r in reads:
            if r.startswith('ps'):
                for (a, s) in self.readers.get(r, ()):
                    if a != agent:
                        need(a, s)
        for r in reads:
            self.readers.setdefault(r, []).append((agent, seq))
        for w in writes:
            self.lastw[w] = (agent, seq)
            self.readers[w] = []
        self.ops.append((agent, queue, fn, deps, seq, is_dma))

    def op(self, eng, fn, reads=(), writes=()):
        self._rec(eng, eng, fn, reads, writes, False)

    def dma(self, queue, key, fn, reads=(), writes=()):
        self._rec('dma:' + key, queue, fn, reads, writes, True)

    def opk(self, eng, name, reads, writes, kw):
        m = getattr(self.eng[eng], name)
        self._rec(eng, eng, (lambda: m(**kw)), reads, writes, False)

    def dmak(self, queue, key, reads, writes, kw):
        m = self.eng[queue].dma_start
        self._rec('dma:' + key, queue, (lambda: m(**kw)), reads, writes, True)

    def emit(self):
        nc = self.nc
        waited = {}
        plan = []
        sig = set()
        for (agent, queue, fn, deps, seq, is_dma) in self.ops:
            w = waited.setdefault(queue, {})
            waits = []
            for a, s in deps.items():
                if w.get(a, 0) >= s:
                    continue
                w[a] = s
                waits.append((a, s))
                sig.add((a, s))
            if not is_dma:
                pass
            plan.append(waits)
        semmap = {}
        sems = {}
        sigcount = {}

        def get_sem(agent, ep):
            k = (agent, ep)
            if k not in sems:
                sems[k] = self.es.enter_context(nc.semaphore("s_%s_%d" % (agent.replace(':', '_'), ep)))
            return sems[k]
        incinfo = []
        per_dma = self.EPOCH // 16
        for (agent, queue, fn, deps, seq, is_dma) in self.ops:
            if is_dma:
                n = sigcount.get(agent, 0) + 1
                sigcount[agent] = n
                ep = (n - 1) // per_dma
                semmap[(agent, seq)] = (agent, ep, (n - ep * per_dma) * 16)
                incinfo.append((agent, ep, 16))
            elif (agent, seq) in sig:
                n = sigcount.get(agent, 0) + 1
                sigcount[agent] = n
                ep = (n - 1) // self.EPOCH
                semmap[(agent, seq)] = (agent, ep, n - ep * self.EPOCH)
                incinfo.append((agent, ep, 1))
            else:
                incinfo.append(None)
        nw = 0
        for i, (agent, queue, fn, deps, seq, is_dma) in enumerate(self.ops):
            e = self.eng[queue]
            for (a, s) in plan[i]:
                ag, ep, val = semmap[(a, s)]
                e.wait_ge(get_sem(ag, ep), val)
                nw += 1
            ins = fn()
            if incinfo[i] is not None:
                ag, ep, inc = incinfo[i]
                ins.then_inc(get_sem(ag, ep), inc)
        for agent, n in sigcount.items():
            if agent.startswith('dma:'):
                ep = (n - 1) // per_dma
                nc.sync.wait_ge(get_sem(agent, ep), (n - ep * per_dma) * 16)
        self.stats = (len(self.ops), nw, dict(sigcount))


def build_program(debug=None):
    debug = debug or {}
    nc = bass.Bass("TRN2", target_bir_lowering=False)

    def din(name, shape, dt=F32):
        return nc.dram_tensor(name, list(shape), dt, kind="ExternalInput").ap()

    def dout(name, shape, dt=F32):
        return nc.dram_tensor(name, list(shape), dt, kind="ExternalOutput").ap()

    def dint(name, shape, dt=F32):
        return nc.dram_tensor(name, list(shape), dt, kind="Internal").ap()

    xp_in = din("xp", [TP, D])
    xs_in = din("xs", [2, TS, D])
    ck_in = din("ck", [2, 2, PAST, 512])
    cv_in = din("cv", [2, 2, PAST, 512])
    cki_in = din("cki", [2, 2, PAST, 64])
    sconv_in = din("sconv", [2, 2, 3, 512])
    slru_in = din("slru", [2, 2, 512])
    spool_in = din("spool", [2, 2, 15, 512])
    posp_in = din("posp", [128, NKB])
    poss_in = din("poss", [128, 1])
    norm_g = din("norm_g", [2, D])
    w_in = din("w_in", [2, D, NIN])
    conv_w = din("conv_w", [2, 4, 512])
    conv_b = din("conv_b", [2, 512])
    lru_wa = din("lru_wa", [2, 8, 64, 64])
    lru_ba = din("lru_ba", [2, 512])
    lru_wx = din("lru_wx", [2, 8, 64, 64])
    lru_bx = din("lru_bx", [2, 512])
    lru_lam = din("lru_lambda", [2, 512])
    pool_w = din("pool_w", [2, 4, 128, 128])
    pool_scale = din("pool_scale", [2, 512])
    w_bo = din("w_branch_out", [2, 3, 512, D])
    w_out = din("w_out", [2, D, D])
    fin_g = din("final_norm_g", [D])
    y_p = dout("y_p", [SEQ, D])
    k_p = dout("k_p", [2, TP, 512])
    v_p = dout("v_p", [2, TP, 512])
    ki_p = dout("ki_p", [2, TP, 64])
    conv_p = dout("conv_p", [2, 3, 512])
    lru_p = dout("lru_p", [2, 512])
    pool_p = dout("pool_p", [2, 15, 512])
    y_s = dout("y_s", [2, TS, D])
    k_s = dout("k_s", [2, 2, TS, 512])
    v_s = dout("v_s", [2, 2, TS, 512])
    ki_s = dout("ki_s", [2, 2, TS, 64])
    conv_s = dout("conv_s", [2, 2, 3, 512])
    lru_s = dout("lru_s", [2, 2, 512])
    pool_s = dout("pool_s", [2, 2, 15, 512])
    win_bf = dint("win_bf", [2, 15, 128, 8, 512], BF16)
    wbo_bf = dint("wbo_bf", [2, 3, 128, 4, D], BF16)
    wout_bf = dint("wout_bf", [2, 2, 128, 8, 512], BF16)
    xscr_p = dint("xscr_p", [TP, D])
    xscr_s = dint("xscr_s", [2, TS, D])

    WGROUPS = [(C_Q, C_K), (C_K, C_V), (C_V, C_GA), (C_GA, C_QI), (C_QI, C_XB), (C_XB, C_GB), (C_GB, C_XC), (C_XC, C_GC), (C_GC, C_GM)]
    for n_ in (1, 2, 0):
        for e4_ in range(2):
            WGROUPS.append((C_GM + n_ * 1024 + e4_ * 512, C_GM + n_ * 1024 + e4_ * 512 + 512))
    es = contextlib.ExitStack()
    with es:
        P = Prog(nc, es)

        def sb(name, shape, dt=F32):
            return es.enter_context(nc.sbuf_tensor(name, list(shape), dt))

        def ps(name, shape, dt=F32):
            return es.enter_context(nc.psum_tensor(name, list(shape), dt))

        def V(name, r, w, **kw):
            P.opk('dve', name, r, w, kw)

        def A(name, r, w, **kw):
            P.opk('act', name, r, w, kw)

        def G(name, r, w, **kw):
            P.opk('pool', name, r, w, kw)

        def T(name, r, w, **kw):
            P.opk('pe', name, r, w, kw)

        def DMA(q, key, r, w, **kw):
            P.dmak(q, key, r, w, kw)

        def CK(label):
            if debug.get('stop') == label:
                P.stopped = True

        kT = sb("kT", [128, 4, NKEY], BF16)
        Vt = sb("Vt", [128, NKB, 8, 65], BF16)
        kiT = sb("kiT", [128, NKEY], BF16)
        arA = sb("arA", [128, 4128], F32)
        mk = sb("mk", [128, NKEY], BF16)
        mkT = sb("mkT", [128, NKB, 128], BF16)
        NWB = 3
        wbuf = [sb("wbuf%d" % i, [128, 4096], BF16) for i in range(NWB)]
        xt = [sb("xt%d" % i, [128, D], F32) for i in range(2)]
        hn = sb("hn", [128, D], BF16)
        hnT = sb("hnT", [128, 8, TB], BF16)
        gfbc = sb("gfbc", [128, D], F32)
        qT = sb("qT", [128, 4, 2, TB], BF16)
        qiT = sb("qiT", [128, 2, TB], BF16)
        sga = sb("sga", [128, 2, 512], F32)
        yT = [sb("y%sT" % n, [128, 4, TB], BF16) for n in "abc"]
        kst = sb("kst", [128, 2, 512], F32)
        vst = sb("vst", [128, 2, 512], F32)
        kist = sb("kist", [128, 2, 64], F32)
        rt = [sb("rt%d" % i, [128, 8, 8], F32) for i in range(4)]
        qbf = sb("qbf", [128, 512], BF16)
        kbf = sb("kbf", [128, 512], BF16)
        qibf = sb("qibf", [128, 256], BF16)
        kibf = sb("kibf", [128, 2, 64], BF16)
        kif = sb("kif", [128, 64], F32)
        NRL = 4
        rl = [sb("rl%d" % i, [128, 512], F32) for i in range(NRL)]
        pex = [sb("pex%d" % i, [128, 8, 128], BF16) for i in range(2)]
        pm = [sb("pm%d" % i, [128, 8, 128], BF16) for i in range(2)]
        att = sb("att", [128, 8, 64], F32)
        yab = sb("yab", [128, 512], BF16)
        ctmp = sb("ctmp", [128, 4, 16 + TB], F32)
        cpl = sb("cpl", [128, TB], BF16)
        ctg = sb("ctg", [128, TB], F32)
        xcb = sb("xcb", [128, TB], BF16)
        ident = sb("ident", [128, 128], BF16)
        cosP = sb("cosP", [128, NKB, 8], F32)
        sinP = sb("sinP", [128, NKB, 8], F32)
        cosS = sb("cosS", [128, 1, 8], F32)
        sinS = sb("sinS", [128, 1, 8], F32)
        invf = sb("invf", [128, 8], F32)
        posP = sb("posP", [128, NKB], F32)
        posS = sb("posS", [128, 1], F32)
        hpow = sb("hpow", [128, NIT + 1], F32)
        rcnt = sb("rcnt", [128, 4, 16], F32)
        gT = sb("gT", [128, 8], F32)
        cw = sb("cw", [128, 4, 4], F32)
        cb = sb("cb", [128, 4], F32)
        hba = sb("hba", [128, 4], F32)
        hbx = sb("hbx", [128, 4], F32)
        lam = sb("lam", [128, 4], F32)
        clf = sb("clf", [128, 4], F32)
        psc = sb("psc", [128, 4], F32)
        waBD = sb("waBD", [128, 4, 128], BF16)
        wxBD = sb("wxBD", [128, 4, 128], BF16)
        pwB = sb("pwB", [128, 4, 128], BF16)
        convst = sb("convst", [128, 4, 3], F32)
        hst = sb("hst", [128, 4], F32)
        poolst = sb("poolst", [128, 4, 15], F32)
        st_ss = sb("st_ss", [128, 2], F32)
        st_rs = sb("st_rs", [128, 2], F32)
        wabs = sb("wabs", [128, 2, 4], F32)
        wsgn = sb("wsgn", [128, 2, 4], F32)
        bs = sb("bs", [128, 16], F32)
        steps = sb("steps", [128, NIT + 1], F32)
        rec = sb("rec", [128, 8], F32)
        psA = [ps("psA%d" % i, [128, 512], F32) for i in range(4)]
        psT = [ps("psT%d" % i, [128, 1024], BF16) for i in range(2)]
        psB = [ps("psB%d" % i, [128, 512], F32) for i in range(2)]

        identf = arA[:, 0:128]
        onesf = arA[:, 128:256]
        rtmp = arA[:, 256:256 + NKB * 8].rearrange("p (a b) -> p a b", b=8)
        rtmp2 = arA[:, 768:768 + NKB * 8].rearrange("p (a b) -> p a b", b=8)
        rtmpi = arA[:, 1280:1280 + NKB * 8].bitcast(I32).rearrange("p (a b) -> p a b", b=8)
        sc = arA[:, 0:NKEY]
        om = arA[:, 0:4 * TB]
        ixc = arA[:, 4 * TB:8 * TB]
        o = 8 * TB
        Bxp = []
        for i in range(2):
            Bxp.append(arA[:, o:o + 3 + TB]); o += 4 + TB
        Bxc = []
        for i in range(2):
            Bxc.append(arA[:, o:o + TB]); o += TB
        Bta = arA[:, o:o + TB]; o += TB
        Btx = arA[:, o:o + TB]; o += TB
        Ba2 = arA[:, o:o + TB]; o += TB
        assert o <= 4128, o
        o = 8 * TB
        Baj = []
        for i in range(4):
            Baj.append(arA[:, o:o + TB]); o += TB
        Bbb = arA[:, o:o + TB]; o += TB
        Bh = arA[:, o:o + TB]; o += TB
        Btg = arA[:, o:o + TB]; o += TB
        assert o <= 4128, o
        mg = arA[:, 0:8 * TB]
        mtg = [arA[:, 8 * TB + i * TB: 8 * TB + (i + 1) * TB] for i in range(2)]
        mgb = mk[:, 0:8 * TB]
        p1 = ['Bxp0a', 'Bxp0b', 'Bxp1a', 'Bxp1b', 'Bxc0', 'Bxc1', 'Bta', 'Btx', 'Ba2']
        p2 = ['Baj0', 'Baj1', 'Baj2', 'Baj3', 'Bbb', 'Bh', 'Btg']
        P.alias(['sc'], ['om', 'ixc', 'mg', 'mtg0', 'mtg1'] + p1 + p2)
        P.alias(['mg'], ['om', 'ixc'])
        P.alias(p1, p2 + ['mtg0', 'mtg1'])
        P.alias(['mtg0', 'mtg1'], p2)
        P.alias(['onesf', 'identf', 'rtmp', 'rtmp2', 'rtmpi'], ['sc', 'om', 'ixc', 'mg', 'mtg0', 'mtg1'] + p1 + p2)
        P.alias(['mk'], ['mgb', 'mkV', 'mkA'])
        P.alias(['mgb'], ['mkV', 'mkA'])

        build_program.sbuf_left = nc.sbuf_bytes_remaining
        cnt = {'wb': 0, 'pa': 0, 'pt': 0}

        def next_w(pin=False):
            while True:
                i = cnt['wb'] % NWB
                cnt['wb'] += 1
                if i != cnt.get('pin'):
                    break
            if pin:
                cnt['pin'] = i
            return wbuf[i], 'wbuf%d' % i

        def unpin_w():
            cnt['pin'] = None

        pa_ring = [(psA[0], 'psA0'), (psA[1], 'psA1'), (psA[2], 'psA2'), (psA[3], 'psA3'), (psB[0], 'psB0'), (psB[1], 'psB1')]

        def next_pa():
            i = cnt['pa'] % 6
            cnt['pa'] += 1
            return pa_ring[i]

        def next_pt():
            i = cnt['pt'] % 2
            cnt['pt'] += 1
            return psT[i], 'psT%d' % i

        def small_dma(out, in_, r=(), w=(), q='sp', key=None):
            key = key or ('m_' + (list(w) + list(r))[0])
            DMA(q, key, r, w, out=out, in_=in_, allow_slow_non_contiguous=True)

        CK('casts')
        G('memset', (), ['onesf'], ap=onesf, constant=1.0)
        G('affine_select', ['onesf'], ['identf'], out=identf, in_=onesf, pattern=[[-1, 128]],
          compare_op=ALU.is_equal, fill=0.0, base=0, channel_multiplier=1)
        V('tensor_copy', ['identf'], ['ident'], out=ident[:], in_=identf)
        G('memset', (), ['Vt'], ap=Vt[:, :, :, 64:65], constant=1.0)
        G('memset', (), ['qT'], ap=qT[:], constant=0.0)
        for j in range(8):
            G('memset', (), ['invf'], ap=invf[:, j:j + 1], constant=float(500000.0 ** (-(2.0 * j) / 16.0)))
        for k in range(NIT + 1):
            G('memset', (), ['hpow'], ap=hpow[:, k:k + 1], constant=float(0.5 ** k))
        for g in range(4):
            wdw = 2 ** (g + 1)
            for t in range(16):
                G('memset', (), ['rcnt'], ap=rcnt[:, g, t:t + 1], constant=1.0 / float(min(wdw, t + 1)))
        small_dma(posP[:], posp_in, (), ['posP'])
        small_dma(posS[:], poss_in, (), ['posS'])
        small_dma(gfbc[:], fin_g.partition_broadcast(128), (), ['gfbc'])

        def rope_tables(pos, nt, cosT, sinT, tag):
            a3 = rtmp[:, 0:nt, :]
            b3 = rtmp2[:, 0:nt, :]
            i3 = rtmpi[:, 0:nt, :]
            for shift, dst in ((0.0, sinT), (math.pi / 2.0, cosT)):
                V('tensor_tensor', ['pos' + tag, 'invf'], ['rtmp'], out=a3, in0=pos.unsqueeze(2).to_broadcast([128, nt, 8]),
                  in1=invf[:].unsqueeze(1).to_broadcast([128, nt, 8]), op=ALU.mult)
                if shift:
                    V('tensor_scalar', ['rtmp'], ['rtmp'], out=a3, in0=a3, scalar1=shift, scalar2=None, op0=ALU.add)
                V('tensor_scalar', ['rtmp'], ['rtmp2'], out=b3, in0=a3, scalar1=1.0 / (2.0 * math.pi), scalar2=None, op0=ALU.mult)
                V('tensor_copy', ['rtmp2'], ['rtmpi'], out=i3, in_=b3)
                V('tensor_copy', ['rtmpi'], ['rtmp2'], out=b3, in_=i3)
                V('scalar_tensor_tensor', ['rtmp2', 'rtmp'], ['rtmp'], out=a3, in0=b3, scalar=-2.0 * math.pi, in1=a3, op0=ALU.mult, op1=ALU.add)
                V('tensor_scalar', ['rtmp'], ['rtmp'], out=a3, in0=a3, scalar1=3.14159, scalar2=-3.14159, op0=ALU.min, op1=ALU.max)
                A('activation', ['rtmp'], ['rope' + tag], out=dst, in_=a3, func=AF.Sin)

        def cast_weights(l):
            for gi, (c0, c1) in enumerate(WGROUPS):
                DMA('pool', 'c_win%d_%d' % (l, gi), (), ['win_bf%d_%d' % (l, gi)], out=win_bf[l, gi].rearrange("k kt c -> kt k c")[:, :, 0:c1 - c0],
                    in_=w_in[l, :, c0:c1].rearrange("(kt k) c -> kt k c", k=128))
            for n in range(3):
                DMA('pool', 'c_wbo%d' % l, (), ['wbo_bf%d' % l], out=wbo_bf[l, n].rearrange("c ct e -> ct c e"),
                    in_=w_bo[l, n].rearrange("(ct c) e -> ct c e", c=128))
            for r in range(2):
                DMA('pool', 'c_wout%d' % l, (), ['wout_bf%d' % l], out=wout_bf[l, r].rearrange("e et c -> et e c"),
                    in_=w_out[l, :, r * 512:(r + 1) * 512].rearrange("(et e) c -> et e c", e=128))

        cast_weights(0)
        if debug.get('skip_prompt'):
            cast_weights(1)
        CK('consts')
        rope_tables(posP[:], NKB, cosP[:], sinP[:], 'P')
        rope_tables(posS[:], 1, cosS[:], sinS[:], 'S')
        CK('prologue')

        def fm(v):
            return v.rearrange("(j p) -> p j", p=128)

        def layer_setup(l):
            small_dma(gT[:], norm_g[l].rearrange("(j p) -> p j", p=128), (), ['gT'])
            small_dma(cb[:], fm(conv_b[l]), (), ['cb'])
            for tap in range(4):
                small_dma(cw[:, :, tap], fm(conv_w[l, tap]), (), ['cw'])
            small_dma(hba[:], fm(lru_ba[l]), (), ['hba'])
            small_dma(hbx[:], fm(lru_bx[l]), (), ['hbx'])
            small_dma(lam[:], fm(lru_lam[l]), (), ['lam'])
            small_dma(psc[:], fm(pool_scale[l]), (), ['psc'])
            V('tensor_scalar', ['hba'], ['hba'], out=hba[:], in0=hba[:], scalar1=0.5, scalar2=None, op0=ALU.mult)
            V('tensor_scalar', ['hbx'], ['hbx'], out=hbx[:], in0=hbx[:], scalar1=0.5, scalar2=None, op0=ALU.mult)
            A('activation', ['lam'], ['lam'], out=lam[:], in_=lam[:], func=AF.Exp, scale=-1.0)
            A('activation', ['lam'], ['lam'], out=lam[:], in_=lam[:], func=AF.Ln, bias=1.0, scale=1.0)
            V('tensor_scalar', ['lam'], ['clf'], out=clf[:], in0=lam[:], scalar1=-8.0, scalar2=None, op0=ALU.mult)
            G('memset', (), ['waBD'], ap=waBD[:], constant=0.0)
            G('memset', (), ['wxBD'], ap=wxBD[:], constant=0.0)
            for j in range(4):
                for hh in range(2):
                    sl = slice(hh * 64, hh * 64 + 64)
                    DMA('pool', 'c_waBD', (), ['waBD'], out=waBD[sl, j, sl], in_=lru_wa[l, 2 * j + hh])
                    DMA('pool', 'c_wxBD', (), ['wxBD'], out=wxBD[sl, j, sl], in_=lru_wx[l, 2 * j + hh])
                DMA('pool', 'c_pwB', (), ['pwB'], out=pwB[:, j, :], in_=pool_w[l, j])

        def load_w_in(l, c0, ncol, pin=False):
            wb, wn = next_w(pin)
            v = wb[:, 0:8 * ncol].rearrange("p (a b) -> p a b", a=8)
            gi = [i for i, (a0, a1) in enumerate(WGROUPS) if a0 == c0 and c0 + ncol == a1]
            assert len(gi) == 1, (c0, ncol)
            src = win_bf[l, gi[0], :, :, 0:ncol]
            DMA('sp', wn, ['win_bf%d_%d' % (l, gi[0])], [wn], out=v, in_=src)
            return v, wn

        def load_w_bo(l, n):
            wb, wn = next_w()
            v = wb[:].rearrange("p (a b) -> p a b", a=4)
            src = wbo_bf[l, n]
            DMA('sp', wn, ['wbo_bf%d' % l], [wn], out=v, in_=src)
            return v, wn

        def load_w_out(l, half):
            wb, wn = next_w()
            v = wb[:].rearrange("p (a b) -> p a b", a=8)
            src = wout_bf[l, half]
            DMA('sp', wn, ['wout_bf%d' % l], [wn], out=v, in_=src)
            return v, wn

        def rope(src, sname, dst, dname, H, tsz, ct, st, tname):
            s3 = src.rearrange("p (h d) -> p h d", h=H)
            d3 = dst.rearrange("p (h d) -> p h d", h=H)
            cb_ = ct.unsqueeze(1).to_broadcast([tsz, H, 8])
            sb_ = st.unsqueeze(1).to_broadcast([tsz, H, 8])
            t = [rt[i][0:tsz, 0:H, :] for i in range(4)]
            CK('r0')
            A('copy', [sname], [dname + 'n'], out=d3[:, :, 16:64], in_=s3[:, :, 16:64])
            CK('r1')
            V('tensor_tensor', [sname, tname], ['rt0'], out=t[0], in0=s3[:, :, 0:8], in1=cb_, op=ALU.mult)
            CK('r2')
            V('tensor_tensor', [sname, tname], ['rt1'], out=t[1], in0=s3[:, :, 8:16], in1=sb_, op=ALU.mult)
            V('tensor_tensor', [sname, tname], ['rt2'], out=t[2], in0=s3[:, :, 8:16], in1=cb_, op=ALU.mult)
            V('tensor_tensor', [sname, tname], ['rt3'], out=t[3], in0=s3[:, :, 0:8], in1=sb_, op=ALU.mult)
            V('tensor_tensor', ['rt0', 'rt1'], [dname + 'a'], out=d3[:, :, 0:8], in0=t[0], in1=t[1], op=ALU.subtract)
            V('tensor_tensor', ['rt2', 'rt3'], [dname + 'b'], out=d3[:, :, 8:16], in0=t[2], in1=t[3], op=ALU.add)
            return [dname + 'n', dname + 'a', dname + 'b']

        def transpose_blocks(src, sres, nblk, tsz):
            pt, ptn = next_pt()
            for c in range(nblk):
                T('transpose', list(sres) + ['ident'], [ptn], out=pt[:, c * 128:c * 128 + tsz], in_=src[0:tsz, c * 128:(c + 1) * 128],
                  identity=ident[0:tsz, 0:tsz])
            return pt, ptn

        def blkview(pt, nblk, tsz):
            return pt[:, 0:nblk * 128].rearrange("p (c t) -> p c t", c=nblk)[:, :, 0:tsz]

        def process_block(S, l, blk, last_layer):
            Tn = blk['T']
            tiles = blk['tiles']
            xsrc = blk['xsrc'][l]
            xdst = blk['xdst'][l]
            cosT, sinT = S['tabs']
            tabres = S['tabres']
            past = S['past']
            stores = []

            def run_zip(ga, na, gb, nb):
                ia = ib = 0
                da = db = False
                while not (da and db):
                    if not da and (db or ia * nb <= ib * na):
                        try:
                            next(ga)
                            ia += 1
                        except StopIteration:
                            da = True
                    elif not db:
                        try:
                            next(gb)
                            ib += 1
                        except StopIteration:
                            db = True

            def drain(g):
                for _ in g:
                    pass


            def rms_rstd(i, tsz, xr):
                A('activation', [xr], ['hn', 'ss%d' % i], out=hn[0:tsz, :], in_=xt[i][0:tsz, :], func=AF.Square,
                  accum_out=st_ss[0:tsz, i:i + 1])
                V('tensor_scalar', ['ss%d' % i], ['rs%d' % i], out=st_rs[0:tsz, i:i + 1], in0=st_ss[0:tsz, i:i + 1],
                  scalar1=1.0 / D, scalar2=EPS, op0=ALU.mult, op1=ALU.add)
                A('activation', ['rs%d' % i], ['rs%d' % i], out=st_rs[0:tsz, i:i + 1], in_=st_rs[0:tsz, i:i + 1], func=AF.Sqrt)
                V('reciprocal', ['rs%d' % i], ['rs%d' % i], out=st_rs[0:tsz, i:i + 1], in_=st_rs[0:tsz, i:i + 1])

            for i, (q0, tsz, a, kb, hm, tix) in enumerate(tiles):
                xr = 'xt%d' % i
                DMA('sp', xr, [blk['xres'][l]], [xr], out=xt[i][0:tsz, :], in_=xsrc[q0:q0 + tsz, :])
                rms_rstd(i, tsz, xr)
                V('tensor_scalar', [xr, 'rs%d' % i], ['hn'], out=hn[0:tsz, :], in0=xt[i][0:tsz, :], scalar1=st_rs[0:tsz, i:i + 1],
                  scalar2=None, op0=ALU.mult)
                pt, ptn = transpose_blocks(hn, ['hn'], 8, tsz)
                for kt in range(8):
                    A('activation', [ptn, 'gT'], ['hnT'], out=hnT[:, kt, q0:q0 + tsz], in_=pt[:, kt * 128:kt * 128 + tsz],
                      func=AF.Identity, scale=gT[:, kt:kt + 1])

            def tm_group(c0, ncol, evac):
                wv, wn = load_w_in(l, c0, ncol)
                for i, (q0, tsz, a, kb, hm, tix) in enumerate(tiles):
                    pa, pan = next_pa()
                    for kt in range(8):
                        T('matmul', ['hnT', wn], [pan], out=pa[0:tsz, 0:ncol], lhsT=hnT[:, kt, q0:q0 + tsz], rhs=wv[:, kt, :],
                          start=(kt == 0), stop=(kt == 7))
                    ct_ = cosT[0:tsz, tix, :]
                    st_ = sinT[0:tsz, tix, :]
                    evac(i, q0, tsz, a, kb, pa, pan, ct_, st_)
                    yield

            def ev_q(i, q0, tsz, a, kb, pa, pan, ct_, st_):
                res = rope(pa[0:tsz, :], pan, qbf[0:tsz, :], 'qbf', 8, tsz, ct_, st_, tabres)
                CK('r3')
                pt, ptn = transpose_blocks(qbf, res, 4, tsz)
                CK('r4')
                A('copy', [ptn], ['qT'], out=qT[0:64, :, 0, q0:q0 + tsz], in_=blkview(pt, 4, tsz)[0:64])
                A('copy', [ptn], ['qT'], out=qT[64:128, :, 1, q0:q0 + tsz], in_=blkview(pt, 4, tsz)[64:128])

            def ev_k(i, q0, tsz, a, kb, pa, pan, ct_, st_):
                res = rope(pa[0:tsz, :], pan, kst[0:tsz, i, :], 'kst%d' % i, 8, tsz, ct_, st_, tabres)
                G('tensor_copy', res, ['kbf'], out=kbf[0:tsz, :], in_=kst[0:tsz, i, :])
                pt, ptn = transpose_blocks(kbf, ['kbf'], 4, tsz)
                A('copy', [ptn], ['kT'], out=kT[:, :, a:a + tsz], in_=blkview(pt, 4, tsz))
                stores.append((res, dict(out=S['k_out'][l][a - past:a - past + tsz, :], in_=kst[0:tsz, i, :])))

            def ev_v(i, q0, tsz, a, kb, pa, pan, ct_, st_):
                A('copy', [pan], ['vst%d' % i], out=vst[0:tsz, i, :], in_=pa[0:tsz, :])
                V('tensor_copy', [pan], ['Vt'], out=Vt[0:tsz, kb, :, 0:64], in_=pa[0:tsz, :].rearrange("p (h d) -> p h d", h=8))
                stores.append((['vst%d' % i], dict(out=S['v_out'][l][a - past:a - past + tsz, :], in_=vst[0:tsz, i, :])))

            def ev_ga(i, q0, tsz, a, kb, pa, pan, ct_, st_):
                A('activation', [pan], ['sga%d' % i], out=sga[0:tsz, i, :], in_=pa[0:tsz, :], func=AF.Tanh, scale=0.5)
                V('scalar_tensor_tensor', [pan, 'sga%d' % i], ['sga%d' % i], out=sga[0:tsz, i, :], in0=sga[0:tsz, i, :], scalar=1.0,
                  in1=pa[0:tsz, :], op0=ALU.add, op1=ALU.mult)

            def ev_idx(i, q0, tsz, a, kb, pa, pan, ct_, st_):
                res = rope(pa[0:tsz, 0:256], pan, qibf[0:tsz, :], 'qibf', 4, tsz, ct_, st_, tabres)
                pt, ptn = transpose_blocks(qibf, res, 2, tsz)
                A('copy', [ptn], ['qiT'], out=qiT[:, :, q0:q0 + tsz], in_=blkview(pt, 2, tsz))
                res2 = rope(pa[0:tsz, 256:320], pan, kist[0:tsz, i, :], 'kist%d' % i, 1, tsz, ct_, st_, tabres)
                G('tensor_copy', res2, ['kibf0'], out=kibf[0:tsz, 0, :], in_=kist[0:tsz, i, :])
                G('tensor_copy', res2, ['kibf1'], out=kibf[0:tsz, 1, :], in_=kist[0:tsz, i, :])
                pt2, ptn2 = next_pt()
                T('transpose', ['kibf0', 'kibf1', 'ident'], [ptn2], out=pt2[:, 0:tsz], in_=kibf[0:tsz, :, :].rearrange("p a b -> p (a b)"),
                  identity=ident[0:tsz, 0:tsz])
                A('copy', [ptn2], ['kiT'], out=kiT[:, a:a + tsz], in_=pt2[:, 0:tsz])
                A('activation', [pan], ['wabs%d' % i], out=wabs[0:tsz, i, :], in_=pa[0:tsz, 320:324], func=AF.Abs)
                A('activation', [pan], ['wsgn%d' % i], out=wsgn[0:tsz, i, :], in_=pa[0:tsz, 320:324], func=AF.Sign)
                stores.append((res2, dict(out=S['ki_out'][l][a - past:a - past + tsz, :], in_=kist[0:tsz, i, :])))

            CK('step1')

            def gen_step2():
                yield from tm_group(C_Q, 512, ev_q)
                yield from tm_group(C_K, 512, ev_k)
                yield from tm_group(C_V, 512, ev_v)
                yield from tm_group(C_GA, 512, ev_ga)
                yield from tm_group(C_QI, 324, ev_idx)

            def fm_proj(wv, wn, j):
                pa, pan = next_pa()
                for kt in range(8):
                    T('matmul', ['hnT', wn], [pan], out=pa[:, 0:Tn], lhsT=wv[:, kt, j * 128:(j + 1) * 128], rhs=hnT[:, kt, 0:Tn],
                      start=(kt == 0), stop=(kt == 7))
                return pa, pan

            def gen_B1():
              yield
              wxb, wxbn = load_w_in(l, C_XB, 512, pin=True)
              for j in range(4):
                  pa, pan = fm_proj(wxb, wxbn, j)
                  xp_, xpn = Bxp[j % 2], 'Bxp%d' % (j % 2)
                  xc_, xcn = Bxc[j % 2], 'Bxc%d' % (j % 2)
                  A('copy', [pan], [xpn + 'b'], out=xp_[:, 3:3 + Tn], in_=pa[:, 0:Tn])
                  G('tensor_copy', ['convst%d' % j], [xpn + 'a'], out=xp_[:, 0:3], in_=convst[:, j, :])
                  G('tensor_copy', [xpn + 'a', xpn + 'b'], ['convst%d' % j], out=convst[:, j, :], in_=xp_[:, Tn:Tn + 3])
                  V('tensor_scalar', [xpn + 'a', xpn + 'b', 'cw', 'cb'], [xcn], out=xc_[:, 0:Tn], in0=xp_[:, 0:Tn], scalar1=cw[:, j, 0:1],
                    scalar2=cb[:, j:j + 1], op0=ALU.mult, op1=ALU.add)
                  for tap in range(1, 4):
                      V('scalar_tensor_tensor', [xpn + 'a', xpn + 'b', 'cw', xcn], [xcn], out=xc_[:, 0:Tn], in0=xp_[:, tap:tap + Tn],
                        scalar=cw[:, j, tap:tap + 1], in1=xc_[:, 0:Tn], op0=ALU.mult, op1=ALU.add)
                  G('tensor_copy', [xcn], ['xcb'], out=xcb[:, 0:Tn], in_=xc_[:, 0:Tn])
                  yield
                  pa1, pan1 = next_pa()
                  T('matmul', ['waBD', 'xcb'], [pan1], out=pa1[:, 0:Tn], lhsT=waBD[:, j, :], rhs=xcb[:, 0:Tn], start=True, stop=True)
                  pa2, pan2 = next_pa()
                  T('matmul', ['wxBD', 'xcb'], [pan2], out=pa2[:, 0:Tn], lhsT=wxBD[:, j, :], rhs=xcb[:, 0:Tn], start=True, stop=True)
                  A('activation', [pan1, 'hba'], ['Bta'], out=Bta[:, 0:Tn], in_=pa1[:, 0:Tn], func=AF.Tanh, bias=hba[:, j:j + 1], scale=0.5)
                  A('activation', [pan2, 'hbx'], ['Btx'], out=Btx[:, 0:Tn], in_=pa2[:, 0:Tn], func=AF.Tanh, bias=hbx[:, j:j + 1], scale=0.5)
                  A('activation', ['Bta', 'clf'], ['Ba2'], out=Ba2[:, 0:Tn], in_=Bta[:, 0:Tn], func=AF.Exp, bias=clf[:, j:j + 1], scale=clf[:, j:j + 1])
                  V('tensor_scalar', ['Ba2'], ['om'], out=om[:, j * TB:j * TB + Tn], in0=Ba2[:, 0:Tn], scalar1=-1.0, scalar2=1.0,
                    op0=ALU.mult, op1=ALU.add)
                  V('scalar_tensor_tensor', ['Btx', xcn], ['ixc'], out=ixc[:, j * TB:j * TB + Tn], in0=Btx[:, 0:Tn], scalar=1.0,
                    in1=xc_[:, 0:Tn], op0=ALU.add, op1=ALU.mult)
                  yield

            run_zip(gen_step2(), 2 * len(tiles) * 5 // 2, gen_B1(), 9)
            unpin_w()
            CK('step2')

            wgb, wgbn = load_w_in(l, C_GB, 512)
            for j in range(4):
                A('activation', ['om'], ['Baj%d' % j], out=Baj[j][:, 0:Tn], in_=om[:, j * TB:j * TB + Tn], func=AF.Sqrt, bias=1.0, scale=-1.0)
            for j in range(4):
                A('activation', ['om'], ['om'], out=om[:, j * TB:j * TB + Tn], in_=om[:, j * TB:j * TB + Tn], func=AF.Sqrt)
            for j in range(4):
                V('scalar_tensor_tensor', ['ixc', 'om'], ['Bbb'], out=Bbb[:, 0:Tn], in0=ixc[:, j * TB:j * TB + Tn], scalar=0.5,
                  in1=om[:, j * TB:j * TB + Tn], op0=ALU.mult, op1=ALU.mult)
                V('tensor_tensor_scan', ['Baj%d' % j, 'Bbb', 'hst%d' % j], ['Bh'], out=Bh[:, 0:Tn], data0=Baj[j][:, 0:Tn], data1=Bbb[:, 0:Tn],
                  initial=hst[:, j:j + 1], op0=ALU.mult, op1=ALU.add)
                G('tensor_copy', ['Bh'], ['hst%d' % j], out=hst[:, j:j + 1], in_=Bh[:, Tn - 1:Tn])
                pa, pan = fm_proj(wgb, wgbn, j)
                A('activation', [pan], ['Btg'], out=Btg[:, 0:Tn], in_=pa[:, 0:Tn], func=AF.Tanh, scale=0.5)
                V('scalar_tensor_tensor', [pan, 'Btg'], ['Btg'], out=Btg[:, 0:Tn], in0=Btg[:, 0:Tn], scalar=1.0, in1=pa[:, 0:Tn],
                  op0=ALU.add, op1=ALU.mult)
                V('scalar_tensor_tensor', ['Btg', 'Bh'], ['ybT'], out=yT[1][:, j, 0:Tn], in0=Btg[:, 0:Tn], scalar=0.5, in1=Bh[:, 0:Tn],
                  op0=ALU.mult, op1=ALU.mult)

            for r, kw in stores:
                DMA('sp', 's_' + r[0], r, (), **kw)

            def gen_branchC():
                wxc, wxcn = load_w_in(l, C_XC, 512)
                wgc, wgcn = load_w_in(l, C_GC, 512)
                L = 15 + Tn
                for g in range(4):
                    wdw = 2 ** (g + 1)
                    pa, pan = fm_proj(wxc, wxcn, g)
                    G('tensor_copy', ['poolst%d' % g], ['ct0a'], out=ctmp[:, 0, 0:15], in_=poolst[:, g, :])
                    A('copy', [pan], ['ct0b'], out=ctmp[:, 0, 15:L], in_=pa[:, 0:Tn])
                    G('tensor_copy', ['ct0a', 'ct0b'], ['poolst%d' % g], out=poolst[:, g, :], in_=ctmp[:, 0, Tn:L])
                    prev, prevn = 0, ['ct0a', 'ct0b']
                    m = 1
                    slot = 1
                    while m < wdw:
                        G('tensor_tensor', prevn, ['ct%d' % slot], out=ctmp[:, slot, 2 * m - 1:L], in0=ctmp[:, prev, 2 * m - 1:L],
                          in1=ctmp[:, prev, m - 1:L - m], op=ALU.add)
                        prev, prevn = slot, ['ct%d' % slot]
                        slot = 1 + (slot % 3)
                        m *= 2
                    yield
                    if blk['first']:
                        V('tensor_tensor', prevn + ['rcnt'], ['ctg'], out=ctg[:, 0:Tn], in0=ctmp[:, prev, 15:L], in1=rcnt[:, g, 0:Tn], op=ALU.mult)
                        V('tensor_tensor', ['ctg', 'ct0b'], ['cpl'], out=cpl[:, 0:Tn], in0=ctg[:, 0:Tn], in1=ctmp[:, 0, 15:L], op=ALU.subtract)
                    else:
                        V('scalar_tensor_tensor', prevn + ['ct0b'], ['cpl'], out=cpl[:, 0:Tn], in0=ctmp[:, prev, 15:L], scalar=1.0 / wdw,
                          in1=ctmp[:, 0, 15:L], op0=ALU.mult, op1=ALU.subtract)
                    pa1, pan1 = next_pa()
                    T('matmul', ['pwB', 'cpl'], [pan1], out=pa1[:, 0:Tn], lhsT=pwB[:, g, :], rhs=cpl[:, 0:Tn], start=True, stop=True)
                    pa2, pan2 = fm_proj(wgc, wgcn, g)
                    A('activation', [pan2], ['ctg'], out=ctg[:, 0:Tn], in_=pa2[:, 0:Tn], func=AF.Tanh, scale=0.5)
                    V('scalar_tensor_tensor', [pan2, 'ctg'], ['ctg'], out=ctg[:, 0:Tn], in0=ctg[:, 0:Tn], scalar=1.0, in1=pa2[:, 0:Tn],
                      op0=ALU.add, op1=ALU.mult)
                    V('tensor_scalar', ['ctg', 'psc'], ['ctg'], out=ctg[:, 0:Tn], in0=ctg[:, 0:Tn], scalar1=psc[:, g:g + 1], scalar2=0.5,
                      op0=ALU.mult, op1=ALU.mult)
                    V('tensor_tensor', ['ctg', pan1], ['ycT'], out=yT[2][:, g, 0:Tn], in0=ctg[:, 0:Tn], in1=pa1[:, 0:Tn], op=ALU.mult)
                    yield

            def rec_scores(i):
                q0, tsz, a, kb, hm, tix = tiles[i]
                Sv = a + tsz
                for c0 in range(0, Sv, 512):
                    n = min(512, Sv - c0)
                    for h in range(4):
                        hs = slice((h % 2) * 64, (h % 2) * 64 + 64)
                        pa, pan = next_pa()
                        T('matmul', ['qiT', 'kiT'], [pan], out=pa[0:tsz, 0:n], lhsT=qiT[hs, h // 2, q0:q0 + tsz], rhs=kiT[hs, c0:c0 + n],
                          start=True, stop=True)
                        r_, rn = rl[h % NRL], 'rl%d' % (h % NRL)
                        A('activation', [pan, 'wabs%d' % i], [rn], out=r_[0:tsz, 0:n], in_=pa[0:tsz, 0:n], func=AF.Relu, scale=wabs[0:tsz, i, h:h + 1])
                        if h == 0:
                            V('tensor_scalar', [rn, 'wsgn%d' % i], ['sc'], out=sc[0:tsz, c0:c0 + n], in0=r_[0:tsz, 0:n], scalar1=wsgn[0:tsz, i, 0:1],
                              scalar2=None, op0=ALU.mult)
                        else:
                            V('scalar_tensor_tensor', [rn, 'wsgn%d' % i, 'sc'], ['sc'], out=sc[0:tsz, c0:c0 + n], in0=r_[0:tsz, 0:n],
                              scalar=wsgn[0:tsz, i, h:h + 1], in1=sc[0:tsz, c0:c0 + n], op0=ALU.mult, op1=ALU.add)

            def gen_bisect(i):
                q0, tsz, a, kb, hm, tix = tiles[i]
                Sv = a + tsz
                V('tensor_reduce', ['sc'], ['bs0'], out=bs[0:tsz, 0:1], in_=sc[0:tsz, 0:Sv], axis=AX.X, op=ALU.min)
                V('tensor_reduce', ['sc'], ['bs1'], out=bs[0:tsz, 1:2], in_=sc[0:tsz, 0:Sv], axis=AX.X, op=ALU.max)
                if hm:
                    G('memset', ['bs0', 'bs1'], ['sc'], ap=sc[0:64, Sv - 64:Sv], constant=NEG)
                V('tensor_scalar', ['bs0', 'bs1'], ['bs2'], out=bs[0:tsz, 2:3], in0=bs[0:tsz, 1:2], scalar1=bs[0:tsz, 0:1], scalar2=1.003,
                  op0=ALU.subtract, op1=ALU.mult)
                V('scalar_tensor_tensor', ['bs0', 'bs2'], ['bs6'], out=bs[0:tsz, 6:7], in0=bs[0:tsz, 2:3], scalar=-0.001, in1=bs[0:tsz, 0:1],
                  op0=ALU.mult, op1=ALU.add)
                V('tensor_scalar', ['bs2', 'hpow'], ['steps'], out=steps[0:tsz, :], in0=hpow[0:tsz, :], scalar1=bs[0:tsz, 2:3], scalar2=None, op0=ALU.mult)
                V('tensor_tensor', ['bs6', 'steps'], ['mid'], out=bs[0:tsz, 3:4], in0=bs[0:tsz, 6:7], in1=steps[0:tsz, 1:2], op=ALU.add)
                yield
                cA = int(Sv * (0.46 if i == 0 else 0.52)) // 2 * 2 if Sv >= 512 else Sv
                nA = Sv - cA
                thr = TOPK - nA / 2.0
                cc = thr - 0.25 - cA
                for k in range(1, NIT + 1):
                    kk = k + 1 if k < NIT else k
                    V('tensor_scalar', ['sc', 'mid'], ['mkV', 'cnt'], out=mk[0:tsz, 0:cA], in0=sc[0:tsz, 0:cA], scalar1=bs[0:tsz, 3:4], scalar2=cc,
                      op0=ALU.is_lt, op1=ALU.add, accum_out=bs[0:tsz, 4:5])
                    if nA:
                        A('activation', ['sc', 'mid'], ['mkA', 'cntA'], out=mk[0:tsz, cA:Sv], in_=sc[0:tsz, cA:Sv], func=AF.Sign, bias=bs[0:tsz, 3:4],
                          scale=-1.0, accum_out=bs[0:tsz, 7:8])
                    V('tensor_scalar', ['mid', 'steps'], ['bu'], out=bs[0:tsz, 9:10], in0=bs[0:tsz, 3:4], scalar1=steps[0:tsz, kk:kk + 1], scalar2=None,
                      op0=ALU.subtract)
                    if nA:
                        V('scalar_tensor_tensor', ['cnt', 'cntA'], ['bd'], out=bs[0:tsz, 5:6], in0=bs[0:tsz, 7:8], scalar=-0.5, in1=bs[0:tsz, 4:5],
                          op0=ALU.mult, op1=ALU.is_ge)
                    else:
                        V('tensor_scalar', ['cnt'], ['bd'], out=bs[0:tsz, 5:6], in0=bs[0:tsz, 4:5], scalar1=0.0, scalar2=None, op0=ALU.is_le)
                    V('scalar_tensor_tensor', ['bd', 'steps', 'bu'], ['mid'], out=bs[0:tsz, 3:4], in0=bs[0:tsz, 5:6], scalar=steps[0:tsz, k:k + 1],
                      in1=bs[0:tsz, 9:10], op0=ALU.mult, op1=ALU.add)
                    yield
                V('tensor_scalar', ['sc', 'mid'], ['mk'], out=mk[0:tsz, 0:Sv], in0=sc[0:tsz, 0:Sv], scalar1=bs[0:tsz, 3:4], scalar2=None, op0=ALU.is_ge)
                yield

            def mask_bias(i):
                return len(tiles) == 2 and i == 0

            def rec_masktrans(i):
                q0, tsz, a, kb, hm, tix = tiles[i]
                Sv = a + tsz
                kbs = [b for b in S['kblocks'] if b[0] + b[1] <= Sv]
                for g0 in range(0, len(kbs), 8):
                    grp = kbs[g0:g0 + 8]
                    pt, ptn = next_pt()
                    for s_, (c0, kr, kbi) in enumerate(grp):
                        T('transpose', ['mk', 'ident'], [ptn], out=pt[0:kr, s_ * 128:s_ * 128 + tsz], in_=mk[0:tsz, c0:c0 + kr], identity=ident[0:tsz, 0:tsz])
                    if mask_bias(i):
                        kwm = dict(func=AF.Identity, scale=30000.0, bias=-30000.0)
                    else:
                        kwm = dict(func=AF.Identity)
                    if all(x[1] == 128 for x in grp):
                        A('activation', [ptn], ['mkT'], out=mkT[:, g0:g0 + len(grp), 0:tsz], in_=blkview(pt, len(grp), tsz), **kwm)
                    else:
                        for s_, (c0, kr, kbi) in enumerate(grp):
                            A('activation', [ptn], ['mkT'], out=mkT[0:kr, g0 + s_, 0:tsz], in_=pt[0:kr, s_ * 128:s_ * 128 + tsz], **kwm)

            def gen_attention(i, npool=4):
                q0, tsz, a, kb, hm, tix = tiles[i]
                Sv = a + tsz
                kbs = [b for b in S['kblocks'] if b[0] + b[1] <= Sv]
                nkb = len(kbs)
                MB = mask_bias(i)

                lbufs = [([psA[0], psA[1]], ['psA0', 'psA1']), ([psA[2], psA[3]], ['psA2', 'psA3'])]
                NLB = len(lbufs)

                def logits(bi):
                    c0, kr, kbi = kbs[bi]
                    pl, pln = lbufs[bi % NLB]
                    for hp in range(4):
                        hb = hp // 2
                        o = (hp % 2) * 256
                        if tsz == 128:
                            T('matmul', ['kT', 'qT'], [pln[hb]], out=pl[hb][0:kr, o:o + 256],
                              lhsT=kT[:, hp, c0:c0 + kr], rhs=qT[:, hp, :, q0:q0 + tsz], start=(hp % 2 == 0), stop=(not MB and hp % 2 == 1))
                        else:
                            for w in range(2):
                                T('matmul', ['kT', 'qT'], [pln[hb]], out=pl[hb][0:kr, o + w * 128:o + w * 128 + tsz],
                                  lhsT=kT[:, hp, c0:c0 + kr], rhs=qT[:, hp, w, q0:q0 + tsz], start=(hp % 2 == 0 and w == 0),
                                  stop=(not MB and hp % 2 == 1 and w == 1))
                    for hb in (range(2) if MB else ()):
                        if tsz == 128:
                            T('matmul', ['mkT', 'ident'], [pln[hb]], out=pl[hb][0:kr, :], lhsT=ident[0:kr, 0:kr],
                              rhs=mkT[0:kr, bi, 0:tsz].unsqueeze(1).to_broadcast([kr, 4, tsz]), start=False, stop=True)
                        else:
                            for s4 in range(4):
                                T('matmul', ['mkT', 'ident'], [pln[hb]], out=pl[hb][0:kr, s4 * 128:s4 * 128 + tsz], lhsT=ident[0:kr, 0:kr],
                                  rhs=mkT[0:kr, bi, 0:tsz], start=False, stop=(s4 == 3))

                def pv(bi):
                    c0, kr, kbi = kbs[bi]
                    px, pxn = pex[bi % 2], 'pex%d' % (bi % 2)
                    pdeps = [pxn + '0', pxn + '1']
                    if not MB:
                        px = pm[bi % 2]
                        pdeps = ['pm%da' % (bi % 2), 'pm%db' % (bi % 2)]
                    for h in range(8):
                        T('matmul', pdeps + ['Vt'], ['psB%d' % (h // 4)], out=psB[h // 4][0:tsz, (h % 4) * 65:(h % 4) * 65 + 65], lhsT=px[0:kr, h, 0:tsz],
                          rhs=Vt[0:kr, kbi, h, :], start=(bi == 0 and h % 4 == 0), stop=(bi == nkb - 1 and h % 4 == 3))
                for b0 in range(min(NLB, nkb)):
                    logits(b0)
                for bi, (c0, kr, kbi) in enumerate(kbs):
                    par = bi % 2
                    pl, pln = lbufs[bi % NLB]
                    px, pxn = pex[par], 'pex%d' % par
                    for hh in range(2):
                        A('activation', [pln[hh]], [pxn + str(hh)], out=px[0:kr, hh * 4:hh * 4 + 4, 0:tsz],
                          in_=pl[hh][0:kr, :].rearrange("p (h t) -> p h t", h=4)[:, :, 0:tsz], func=AF.Exp, scale=0.125)
                    if not MB:
                        pm_ = pm[par]
                        mbp = mkT[0:kr, bi, 0:tsz].unsqueeze(1).to_broadcast([kr, npool, tsz])
                        mbv = mkT[0:kr, bi, 0:tsz].unsqueeze(1).to_broadcast([kr, 8 - npool, tsz])
                        G('tensor_tensor', [pxn + '0', pxn + '1', 'mkT'], ['pm%da' % par], out=pm_[0:kr, 0:npool, 0:tsz], in0=px[0:kr, 0:npool, 0:tsz], in1=mbp, op=ALU.mult)
                        V('tensor_tensor', [pxn + '0', pxn + '1', 'mkT'], ['pm%db' % par], out=pm_[0:kr, npool:8, 0:tsz], in0=px[0:kr, npool:8, 0:tsz], in1=mbv, op=ALU.mult)
                    if bi + NLB < nkb:
                        logits(bi + NLB)
                    if bi >= 1:
                        pv(bi - 1)
                    yield
                pv(nkb - 1)

            def rec_attn_final(i):
                q0, tsz, a, kb, hm, tix = tiles[i]
                for hh in range(2):
                    pv = psB[hh][0:tsz, 0:260].rearrange("p (h e) -> p h e", h=4)
                    V('tensor_scalar', ['psB%d' % hh], ['rec%d' % hh], out=rec[0:tsz, hh * 4:hh * 4 + 4], in0=pv[:, :, 64], scalar1=2.0, scalar2=None,
                      op0=ALU.mult)
                    V('reciprocal', ['rec%d' % hh], ['rec%d' % hh], out=rec[0:tsz, hh * 4:hh * 4 + 4], in_=rec[0:tsz, hh * 4:hh * 4 + 4])
                    V('tensor_tensor', ['psB%d' % hh, 'rec%d' % hh], ['att%d' % hh], out=att[0:tsz, hh * 4:hh * 4 + 4, :], in0=pv[:, :, 0:64],
                      in1=rec[0:tsz, hh * 4:hh * 4 + 4].unsqueeze(2).to_broadcast([tsz, 4, 64]), op=ALU.mult)
                G('tensor_tensor', ['att0', 'att1', 'sga%d' % i], ['yab'], out=yab[0:tsz, :], in0=att[0:tsz, :, :].rearrange("p h d -> p (h d)"),
                  in1=sga[0:tsz, i, :], op=ALU.mult)
                pt, ptn = transpose_blocks(yab, ['yab'], 4, tsz)
                A('copy', [ptn], ['yaT'], out=yT[0][:, :, q0:q0 + tsz], in_=blkview(pt, 4, tsz))

            ynames = ['yaT', 'ybT', 'ycT']

            def gen_merge(ns, first_n, last_n, banks=None):
                bc = [0]

                def bank():
                    if banks is None:
                        return next_pa()
                    b = banks[bc[0] % len(banks)]
                    bc[0] += 1
                    return b
                for n in ns:
                    wb_, wbn = load_w_bo(l, n)
                    for e4 in range(2):
                        wg_, wgn = load_w_in(l, C_GM + n * 1024 + e4 * 512, 512)
                        for ee in range(4):
                            e = e4 * 4 + ee
                            pa, pan = bank()
                            for kt in range(8):
                                T('matmul', ['hnT', wgn], [pan], out=pa[:, 0:Tn], lhsT=wg_[:, kt, ee * 128:(ee + 1) * 128], rhs=hnT[:, kt, 0:Tn],
                                  start=(kt == 0), stop=(kt == 7))
                            pb, pbn = bank()
                            for ct in range(4):
                                T('matmul', [wbn, ynames[n]], [pbn], out=pb[:, 0:Tn], lhsT=wb_[:, ct, e * 128:(e + 1) * 128], rhs=yT[n][:, ct, 0:Tn],
                                  start=(ct == 0), stop=(ct == 3))
                            tg_, tgn = mtg[e % 2], 'mtg%d' % (e % 2)
                            A('activation', [pan], [tgn], out=tg_[:, 0:Tn], in_=pa[:, 0:Tn], func=AF.Tanh, scale=0.5)
                            mge = mg[:, e * TB:e * TB + Tn]
                            if n == first_n:
                                V('scalar_tensor_tensor', [tgn, pbn], ['mg'], out=mge, in0=tg_[:, 0:Tn], scalar=1.0, in1=pb[:, 0:Tn], op0=ALU.add, op1=ALU.mult)
                            else:
                                V('scalar_tensor_tensor', [tgn, pbn], [tgn], out=tg_[:, 0:Tn], in0=tg_[:, 0:Tn], scalar=1.0, in1=pb[:, 0:Tn],
                                  op0=ALU.add, op1=ALU.mult)
                                if n != last_n:
                                    G('tensor_tensor', [tgn, 'mg'], ['mg'], out=mge, in0=mge, in1=tg_[:, 0:Tn], op=ALU.add)
                                else:
                                    G('tensor_tensor', [tgn, 'mg'], ['mgb'], out=mgb[:, e * TB:e * TB + Tn], in0=mge, in1=tg_[:, 0:Tn], op=ALU.add)
                            yield

            psM = [(psT[0][:, :].bitcast(F32), 'psT0'), (psT[1][:, :].bitcast(F32), 'psT1')]

            def nkb_of(i):
                q0, tsz, a, kb, hm, tix = tiles[i]
                return len([b for b in S['kblocks'] if b[0] + b[1] <= a + tsz])

            rec_scores(0)
            run_zip(gen_branchC(), 8, gen_bisect(0), NIT + 2)
            rec_masktrans(0)
            CK('step3')
            if len(tiles) == 2:
                rec_scores(1)
                run_zip(gen_attention(0, 5), nkb_of(0), gen_bisect(1), NIT + 2)
                rec_attn_final(0)
                rec_masktrans(1)
                run_zip(gen_attention(1, 3), nkb_of(1), gen_merge([1, 2], 1, 0, psM), 16)
                rec_attn_final(1)
            else:
                run_zip(gen_attention(0, 3), nkb_of(0), gen_merge([1, 2], 1, 0, psM), 16)
                rec_attn_final(0)
            CK('step4')
            drain(gen_merge([0], 1, 0))
            for half in range(2):
                wo_, won = load_w_out(l, half)
                for i, (q0, tsz, a, kb, hm, tix) in enumerate(tiles):
                    pa, pan = next_pa()
                    for et in range(8):
                        T('matmul', ['mgb', won], [pan], out=pa[0:tsz, :], lhsT=mgb[:, et * TB + q0:et * TB + q0 + tsz], rhs=wo_[:, et, :],
                          start=(et == 0), stop=(et == 7))
                    xs_ = xt[i][0:tsz, half * 512:(half + 1) * 512]
                    V('scalar_tensor_tensor', [pan, 'xt%d' % i], ['xt%d' % i], out=xs_, in0=pa[0:tsz, :], scalar=0.5, in1=xs_, op0=ALU.mult, op1=ALU.add)
            for i, (q0, tsz, a, kb, hm, tix) in enumerate(tiles):
                xr = 'xt%d' % i
                if not last_layer:
                    DMA('act', 's_' + xr, [xr], [blk['xres'][l + 1]], out=xdst[q0:q0 + tsz, :], in_=xt[i][0:tsz, :])
                elif xdst is not None:
                    rms_rstd(i, tsz, xr)
                    V('scalar_tensor_tensor', [xr, 'rs%d' % i, 'gfbc'], [xr], out=xt[i][0:tsz, :], in0=xt[i][0:tsz, :], scalar=st_rs[0:tsz, i:i + 1],
                      in1=gfbc[0:tsz, :], op0=ALU.mult, op1=ALU.mult)
                    DMA('act', 's_' + xr, [xr], (), out=xdst[q0:q0 + tsz, :], in_=xt[i][0:tsz, :])

        def make_prompt():
            S = {'past': 0, 'tabs': (cosP, sinP), 'tabres': 'ropeP',
                 'k_out': [k_p[0], k_p[1]], 'v_out': [v_p[0], v_p[1]], 'ki_out': [ki_p[0], ki_p[1]]}
            S['kblocks'] = [(0, 16, 0)] + [(16 + 128 * j, 128, 1 + j) for j in range(32)]
            blocks = [{'T': 16, 'tiles': [(0, 16, 0, 0, False, 0)], 'first': True,
                       'xsrc': [xp_in[0:16, :], xscr_p[0:16, :]], 'xdst': [xscr_p[0:16, :], None], 'xres': ['xin', 'xscrp_m', 'none']}]
            nb = debug.get('nblk', SEQ // TB)
            for b in range(nb):
                a0 = 16 + b * TB
                tiles = []
                for i in range(TB // 128):
                    a = a0 + i * 128
                    kb = 1 + (a - 16) // 128
                    tiles.append((i * 128, 128, a, kb, True, kb))
                blocks.append({'T': TB, 'tiles': tiles, 'first': False,
                               'xsrc': [xp_in[a0:a0 + TB, :], xscr_p[a0:a0 + TB, :]],
                               'xdst': [xscr_p[a0:a0 + TB, :], y_p[a0 - 16:a0 - 16 + TB, :]], 'xres': ['xin', 'xscrp_%d' % b, 'none']})
            S['blocks'] = blocks
            return S

        def make_sample(si):
            S = {'past': PAST, 'tabs': (cosS, sinS), 'tabres': 'ropeS',
                 'k_out': [k_s[0, si], k_s[1, si]], 'v_out': [v_s[0, si], v_s[1, si]], 'ki_out': [ki_s[0, si], ki_s[1, si]]}
            S['kblocks'] = [(128 * j, 128, j) for j in range(16)] + [(PAST, 64, 16)]
            S['blocks'] = [{'T': TS, 'tiles': [(0, TS, PAST, 16, False, 0)], 'first': False,
                            'xsrc': [xs_in[si], xscr_s[si]], 'xdst': [xscr_s[si], y_s[si]], 'xres': ['xin', 'xscrs_%d' % si, 'none']}]
            return S

        st_names = ['convst%d' % j for j in range(4)] + ['hst%d' % j for j in range(4)] + ['poolst%d' % j for j in range(4)]

        def zero_states():
            G('memset', (), ['convst%d' % j for j in range(4)], ap=convst[:], constant=0.0)
            G('memset', (), ['hst%d' % j for j in range(4)], ap=hst[:], constant=0.0)
            G('memset', (), ['poolst%d' % j for j in range(4)], ap=poolst[:], constant=0.0)

        def load_states(l, si):
            for j in range(4):
                small_dma(convst[:, j, :], sconv_in[l, si][:, j * 128:(j + 1) * 128].rearrange("t p -> p t"), (), ['convst%d' % j])
                small_dma(poolst[:, j, :], spool_in[l, si][:, j * 128:(j + 1) * 128].rearrange("t p -> p t"), (), ['poolst%d' % j])
            small_dma(hst[:], fm(slru_in[l, si]), (), ['hst%d' % j for j in range(4)])

        def store_states(conv_o, lru_o, pool_o):
            for j in range(4):
                small_dma(conv_o[:, j * 128:(j + 1) * 128].rearrange("t p -> p t"), convst[:, j, :], ['convst%d' % j], ())
                small_dma(pool_o[:, j * 128:(j + 1) * 128].rearrange("t p -> p t"), poolst[:, j, :], ['poolst%d' % j], ())
            small_dma(fm(lru_o), hst[:], ['hst%d' % j for j in range(4)], ())

        def load_cache(l, si):
            for j in range(16):
                sl = slice(j * 128, (j + 1) * 128)
                rk, rkn = rl[j % 2], 'rl%d' % (j % 2)
                rv, rvn = rl[2 + j % 2], 'rl%d' % (2 + j % 2)
                kb_, kbn = (kbf, 'kbf') if j % 2 == 0 else (qbf, 'qbf')
                DMA('sp', rkn, (), [rkn], out=rk[:], in_=ck_in[l, si, sl, :])
                DMA('sp', rvn, (), [rvn], out=rv[:], in_=cv_in[l, si, sl, :])
                DMA('sp', 'kif', (), ['kif'], out=kif[:], in_=cki_in[l, si, sl, :])
                V('tensor_copy', [rkn], [kbn], out=kb_[:], in_=rk[:])
                pt, ptn = transpose_blocks(kb_, [kbn], 4, 128)
                A('copy', [ptn], ['kT'], out=kT[:, :, sl], in_=blkview(pt, 4, 128))
                V('tensor_copy', [rvn], ['Vt'], out=Vt[:, j, :, 0:64], in_=rv[:].rearrange("p (h d) -> p h d", h=8))
                A('copy', ['kif'], ['kibf0'], out=kibf[:, 0, :], in_=kif[:])
                V('tensor_copy', ['kif'], ['kibf1'], out=kibf[:, 1, :], in_=kif[:])
                pt2, ptn2 = next_pt()
                T('transpose', ['kibf0', 'kibf1', 'ident'], [ptn2], out=pt2[:, 0:128], in_=kibf[:, :, :].rearrange("p a b -> p (a b)"), identity=ident[:, :])
                A('copy', [ptn2], ['kiT'], out=kiT[:, sl], in_=pt2[:, 0:128])

        nlayers = debug.get('nlayers', 2)
        if not debug.get('skip_prompt'):
            Sp = make_prompt()
            for l in range(nlayers):
                layer_setup(l)
                zero_states()
                for bix, blk in enumerate(Sp['blocks']):
                    process_block(Sp, l, blk, l == 1)
                    if l == 0 and bix == min(2, len(Sp['blocks']) - 1):
                        cast_weights(1)
                store_states(conv_p[l], lru_p[l], pool_p[l])
        if not debug.get('skip_sample'):
            for si in range(debug.get('nsample', 2)):
                Ss = make_sample(si)
                for l in range(nlayers):
                    layer_setup(l)
                    CK('setup')
                    load_states(l, si)
                    CK('states')
                    load_cache(l, si)
                    CK('cache')
                    for blk in Ss['blocks']:
                        process_block(Ss, l, blk, l == 1)
                    store_states(conv_s[l, si], lru_s[l, si], pool_s[l, si])
        P.emit()
        build_program.stats = P.stats
    return nc


_CACHE = {}


def kernel(x_prompt, x_sample, cache_k, cache_v, cache_kidx, state_conv, state_lru, state_pool,
           meta_tokens, norm_g, w_in, conv_w, conv_b, lru_wa, lru_ba, lru_wx, lru_bx, lru_lambda,
           pool_w, pool_scale, w_branch_out, w_out, final_norm_g):
    if 'nc' not in _CACHE:
        _CACHE['nc'] = build_program()
    nc = _CACHE['nc']
    in_maps = _make_in_maps(x_prompt, x_sample, cache_k, cache_v, cache_kidx, state_conv, state_lru, state_pool,
                            meta_tokens, norm_g, w_in, conv_w, conv_b, lru_wa, lru_ba, lru_wx, lru_bx, lru_lambda,
                            pool_w, pool_scale, w_branch_out, w_out, final_norm_g)
    res = run_bass_kernel_spmd(nc, in_maps, core_ids=list(range(8)))
    return _assemble(res.results)


def _make_in_maps(x_prompt, x_sample, cache_k, cache_v, cache_kidx, state_conv, state_lru, state_pool,
                  meta_tokens, norm_g, w_in, conv_w, conv_b, lru_wa, lru_ba, lru_wx, lru_bx, lru_lambda,
                  pool_w, pool_scale, w_branch_out, w_out, final_norm_g):
    f = lambda a: np.ascontiguousarray(np.asarray(a, dtype=np.float32))
    x_prompt = f(x_prompt); x_sample = f(x_sample); meta = f(meta_tokens)
    ck = f(cache_k).reshape(2, 16, PAST, 512)
    cv = f(cache_v).reshape(2, 16, PAST, 512)
    cki = f(cache_kidx)
    sconv = f(state_conv); slru = f(state_lru); spool = f(state_pool)
    posp = np.zeros((128, NKB), np.float32)
    posp[:, 0] = np.arange(128)
    for j in range(1, NKB):
        posp[:, j] = 16 + 128 * (j - 1) + np.arange(128)
    poss = (PAST + np.arange(128, dtype=np.float32)).reshape(128, 1).astype(np.float32)
    shared = {
        "posp": posp, "poss": poss, "norm_g": f(norm_g), "w_in": f(w_in), "conv_w": f(conv_w), "conv_b": f(conv_b),
        "lru_wa": f(lru_wa), "lru_ba": f(lru_ba), "lru_wx": f(lru_wx), "lru_bx": f(lru_bx), "lru_lambda": f(lru_lambda),
        "pool_w": f(pool_w), "pool_scale": f(pool_scale), "w_branch_out": f(w_branch_out), "w_out": f(w_out),
        "final_norm_g": f(final_norm_g),
    }
    in_maps = []
    for c in range(8):
        b = c % 4
        ss = [2 * c, 2 * c + 1]
        m = dict(shared)
        m["xp"] = np.ascontiguousarray(np.concatenate([meta, x_prompt[b]], axis=0))
        m["xs"] = np.ascontiguousarray(x_sample[ss])
        m["ck"] = np.ascontiguousarray(ck[:, ss])
        m["cv"] = np.ascontiguousarray(cv[:, ss])
        m["cki"] = np.ascontiguousarray(cki[:, ss])
        m["sconv"] = np.ascontiguousarray(sconv[:, ss])
        m["slru"] = np.ascontiguousarray(slru[:, ss])
        m["spool"] = np.ascontiguousarray(spool[:, ss])
        in_maps.append(m)
    return in_maps


def _assemble(R):
    y_prompt = np.stack([R[b]["y_p"] for b in range(4)], axis=0)
    y_sample = np.concatenate([R[c]["y_s"] for c in range(8)], axis=0)
    k_prompt = np.stack([R[b]["k_p"] for b in range(4)], axis=1).reshape(2, 4, TP, 8, 64)
    v_prompt = np.stack([R[b]["v_p"] for b in range(4)], axis=1).reshape(2, 4, TP, 8, 64)
    ki_prompt = np.stack([R[b]["ki_p"] for b in range(4)], axis=1)
    conv_prompt = np.stack([R[b]["conv_p"] for b in range(4)], axis=1)
    lru_prompt = np.stack([R[b]["lru_p"] for b in range(4)], axis=1)
    pool_prompt = np.stack([R[b]["pool_p"] for b in range(4)], axis=1)
    k_sample = np.concatenate([R[c]["k_s"] for c in range(8)], axis=1).reshape(2, 16, TS, 8, 64)
    v_sample = np.concatenate([R[c]["v_s"] for c in range(8)], axis=1).reshape(2, 16, TS, 8, 64)
    ki_sample = np.concatenate([R[c]["ki_s"] for c in range(8)], axis=1)
    conv_sample = np.concatenate([R[c]["conv_s"] for c in range(8)], axis=1)
    lru_sample = np.concatenate([R[c]["lru_s"] for c in range(8)], axis=1)
    pool_sample = np.concatenate([R[c]["pool_s"] for c in range(8)], axis=1)
    outs = (y_prompt, y_sample, k_prompt, v_prompt, ki_prompt, conv_prompt, lru_prompt, pool_prompt,
            k_sample, v_sample, ki_sample, conv_sample, lru_sample, pool_sample)
    return tuple(np.ascontiguousarray(o, dtype=np.float32) for o in outs)
```

```python
import math
import contextlib
import numpy as np
import concourse.bass as bass
import concourse.mybir as mybir
from concourse.bass_utils import run_bass_kernel_spmd

F32 = mybir.dt.float32
BF16 = mybir.dt.bfloat16
I32 = mybir.dt.int32
ALU = mybir.AluOpType
AF = mybir.ActivationFunctionType
AX = mybir.AxisListType

D = 1024
SEQ = 4096
NMETA = 16
TP = NMETA + SEQ
NIN = 7492
PAST = 2048
TS = 64
NKEY = TP
NKB = 33
TB = 256
NIT = 14
TOPK = 256.0
C_Q, C_K, C_V, C_GA, C_QI, C_KI, C_WI = 0, 512, 1024, 1536, 2048, 2304, 2368
C_XB, C_GB, C_XC, C_GC, C_GM = 2372, 2884, 3396, 3908, 4420
EPS = 1e-6
NEG = -1.0e30
MASK_BIAS = False


class Prog:
    EPOCH = 24000

    def __init__(self, nc, es):
        self.nc = nc
        self.es = es
        self.ops = []
        self.count = {}
        self.lastw = {}
        self.readers = {}
        self.conf = {}
        self.eng = {'pe': nc.tensor, 'act': nc.scalar, 'dve': nc.vector, 'pool': nc.gpsimd, 'sp': nc.sync}

    def alias(self, names_a, names_b):
        for a in names_a:
            for b in names_b:
                self.conf.setdefault(a, set()).add(b)
                self.conf.setdefault(b, set()).add(a)

    def _names(self, r):
        c = self.conf.get(r)
        if c:
            return [r] + list(c)
        return [r]

    stopped = False

    def _rec(self, agent, queue, fn, reads, writes, is_dma):
        if self.stopped:
            return
        seq = self.count.get(agent, 0) + 1
        self.count[agent] = seq
        deps = {}

        def need(a, s):
            if a.startswith('dma:'):
                s = self.count[a] - (1 if a == agent else 0)
                if s <= 0:
                    return
            if deps.get(a, 0) < s:
                deps[a] = s
        for r0 in list(reads) + list(writes):
            for r in self._names(r0):
                lw = self.lastw.get(r)
                if lw is not None:
                    a, s = lw
                    if a == agent and agent == 'pe':
                        continue
                    need(a, s)
        for w0 in writes:
            for w in self._names(w0):
                for (a, s) in self.readers.get(w, ()):
                    if a == agent and agent == 'pe':
                        continue
                    need(a, s)
        for r in reads:
            if r.startswith('ps'):
                for (a, s) in self.readers.get(r, ()):
                    if a != agent:
                        need(a, s)
        for r in reads:
            self.readers.setdefault(r, []).append((agent, seq))
        for w in writes:
            self.lastw[w] = (agent, seq)
            self.readers[w] = []
        self.ops.append((agent, queue, fn, deps, seq, is_dma))

    def op(self, eng, fn, reads=(), writes=()):
        self._rec(eng, eng, fn, reads, writes, False)

    def dma(self, queue, key, fn, reads=(), writes=()):
        self._rec('dma:' + key, queue, fn, reads, writes, True)

    def opk(self, eng, name, reads, writes, kw):
        m = getattr(self.eng[eng], name)
        self._rec(eng, eng, (lambda: m(**kw)), reads, writes, False)

    def dmak(self, queue, key, reads, writes, kw):
        m = self.eng[queue].dma_start
        self._rec('dma:' + key, queue, (lambda: m(**kw)), reads, writes, True)

    def emit(self):
        nc = self.nc
        waited = {}
        plan = []
        sig = set()
        for (agent, queue, fn, deps, seq, is_dma) in self.ops:
            w = waited.setdefault(queue, {})
            waits = []
            for a, s in deps.items():
                if w.get(a, 0) >= s:
                    continue
                w[a] = s
                waits.append((a, s))
                sig.add((a, s))
            if not is_dma:
                pass
            plan.append(waits)
        semmap = {}
        sems = {}
        sigcount = {}

        def get_sem(agent, ep):
            k = (agent, ep)
            if k not in sems:
                sems[k] = self.es.enter_context(nc.semaphore("s_%s_%d" % (agent.replace(':', '_'), ep)))
            return sems[k]
        incinfo = []
        per_dma = self.EPOCH // 16
        for (agent, queue, fn, deps, seq, is_dma) in self.ops:
            if is_dma:
                n = sigcount.get(agent, 0) + 1
                sigcount[agent] = n
                ep = (n - 1) // per_dma
                semmap[(agent, seq)] = (agent, ep, (n - ep * per_dma) * 16)
                incinfo.append((agent, ep, 16))
            elif (agent, seq) in sig:
                n = sigcount.get(agent, 0) + 1
                sigcount[agent] = n
                ep = (n - 1) // self.EPOCH
                semmap[(agent, seq)] = (agent, ep, n - ep * self.EPOCH)
                incinfo.append((agent, ep, 1))
            else:
                incinfo.append(None)
        nw = 0
        for i, (agent, queue, fn, deps, seq, is_dma) in enumerate(self.ops):
            e = self.eng[queue]
            for (a, s) in plan[i]:
                ag, ep, val = semmap[(a, s)]
                e.wait_ge(get_sem(ag, ep), val)
                nw += 1
            ins = fn()
            if incinfo[i] is not None:
                ag, ep, inc = incinfo[i]
                ins.then_inc(get_sem(ag, ep), inc)
        for agent, n in sigcount.items():
            if agent.startswith('dma:'):
                ep = (n - 1) // per_dma
                nc.sync.wait_ge(get_sem(agent, ep), (n - ep * per_dma) * 16)
        self.stats = (len(self.ops), nw, dict(sigcount))


def build_program(debug=None):
    debug = debug or {}
    nc = bass.Bass("TRN2", target_bir_lowering=False)

    def din(name, shape, dt=F32):
        return nc.dram_tensor(name, list(shape), dt, kind="ExternalInput").ap()

    def dout(name, shape, dt=F32):
        return nc.dram_tensor(name, list(shape), dt, kind="ExternalOutput").ap()

    def dint(name, shape, dt=F32):
        return nc.dram_tensor(name, list(shape), dt, kind="Internal").ap()

    xp_in = din("xp", [TP, D])
    xs_in = din("xs", [2, TS, D])
    ck_in = din("ck", [2, 2, PAST, 512])
    cv_in = din("cv", [2, 2, PAST, 512])
    cki_in = din("cki", [2, 2, PAST, 64])
    sconv_in = din("sconv", [2, 2, 3, 512])
    slru_in = din("slru", [2, 2, 512])
    spool_in = din("spool", [2, 2, 15, 512])
    posp_in = din("posp", [128, NKB])
    poss_in = din("poss", [128, 1])
    norm_g = din("norm_g", [2, D])
    w_in = din("w_in", [2, D, NIN])
    conv_w = din("conv_w", [2, 4, 512])
    conv_b = din("conv_b", [2, 512])
    lru_wa = din("lru_wa", [2, 8, 64, 64])
    lru_ba = din("lru_ba", [2, 512])
    lru_wx = din("lru_wx", [2, 8, 64, 64])
    lru_bx = din("lru_bx", [2, 512])
    lru_lam = din("lru_lambda", [2, 512])
    pool_w = din("pool_w", [2, 4, 128, 128])
    pool_scale = din("pool_scale", [2, 512])
    w_bo = din("w_branch_out", [2, 3, 512, D])
    w_out = din("w_out", [2, D, D])
    fin_g = din("final_norm_g", [D])
    y_p = dout("y_p", [SEQ, D])
    k_p = dout("k_p", [2, TP, 512])
    v_p = dout("v_p", [2, TP, 512])
    ki_p = dout("ki_p", [2, TP, 64])
    conv_p = dout("conv_p", [2, 3, 512])
    lru_p = dout("lru_p", [2, 512])
    pool_p = dout("pool_p", [2, 15, 512])
    y_s = dout("y_s", [2, TS, D])
    k_s = dout("k_s", [2, 2, TS, 512])
    v_s = dout("v_s", [2, 2, TS, 512])
    ki_s = dout("ki_s", [2, 2, TS, 64])
    conv_s = dout("conv_s", [2, 2, 3, 512])
    lru_s = dout("lru_s", [2, 2, 512])
    pool_s = dout("pool_s", [2, 2, 15, 512])
    win_bf = dint("win_bf", [2, 15, 128, 8, 512], BF16)
    wbo_bf = dint("wbo_bf", [2, 3, 128, 4, D], BF16)
    wout_bf = dint("wout_bf", [2, 2, 128, 8, 512], BF16)
    xscr_p = dint("xscr_p", [TP, D])
    xscr_s = dint("xscr_s", [2, TS, D])

    WGROUPS = [(C_Q, C_K), (C_K, C_V), (C_V, C_GA), (C_GA, C_QI), (C_QI, C_XB), (C_XB, C_GB), (C_GB, C_XC), (C_XC, C_GC), (C_GC, C_GM)]
    for n_ in (1, 2, 0):
        for e4_ in range(2):
            WGROUPS.append((C_GM + n_ * 1024 + e4_ * 512, C_GM + n_ * 1024 + e4_ * 512 + 512))
    es = contextlib.ExitStack()
    with es:
        P = Prog(nc, es)

        def sb(name, shape, dt=F32):
            return es.enter_context(nc.sbuf_tensor(name, list(shape), dt))

        def ps(name, shape, dt=F32):
            return es.enter_context(nc.psum_tensor(name, list(shape), dt))

        def V(name, r, w, **kw):
            P.opk('dve', name, r, w, kw)

        def A(name, r, w, **kw):
            P.opk('act', name, r, w, kw)

        def G(name, r, w, **kw):
            P.opk('pool', name, r, w, kw)

        def T(name, r, w, **kw):
            P.opk('pe', name, r, w, kw)

        def DMA(q, key, r, w, **kw):
            P.dmak(q, key, r, w, kw)

        def CK(label):
            if debug.get('stop') == label:
                P.stopped = True

        kT = sb("kT", [128, 4, NKEY], BF16)
        Vt = sb("Vt", [128, NKB, 8, 65], BF16)
        kiT = sb("kiT", [128, NKEY], BF16)
        arA = sb("arA", [128, 4128], F32)
        mk = sb("mk", [128, NKEY], BF16)
        mkT = sb("mkT", [128, NKB, 128], BF16)
        NWB = 3
        wbuf = [sb("wbuf%d" % i, [128, 4096], BF16) for i in range(NWB)]
        xt = [sb("xt%d" % i, [128, D], F32) for i in range(2)]
        hn = sb("hn", [128, D], BF16)
        hnT = sb("hnT", [128, 8, TB], BF16)
        gfbc = sb("gfbc", [128, D], F32)
        qT = sb("qT", [128, 4, 2, TB], BF16)
        qiT = sb("qiT", [128, 2, TB], BF16)
        sga = sb("sga", [128, 2, 512], F32)
        yT = [sb("y%sT" % n, [128, 4, TB], BF16) for n in "abc"]
        kst = sb("kst", [128, 2, 512], F32)
        vst = sb("vst", [128, 2, 512], F32)
        kist = sb("kist", [128, 2, 64], F32)
        rt = [sb("rt%d" % i, [128, 8, 8], F32) for i in range(4)]
        qbf = sb("qbf", [128, 512], BF16)
        kbf = sb("kbf", [128, 512], BF16)
        qibf = sb("qibf", [128, 256], BF16)
        kibf = sb("kibf", [128, 2, 64], BF16)
        kif = sb("kif", [128, 64], F32)
        NRL = 4
        rl = [sb("rl%d" % i, [128, 512], F32) for i in range(NRL)]
        pex = [sb("pex%d" % i, [128, 8, 128], BF16) for i in range(2)]
        pm = [sb("pm%d" % i, [128, 8, 128], BF16) for i in range(2)]
        att = sb("att", [128, 8, 64], F32)
        yab = sb("yab", [128, 512], BF16)
        ctmp = sb("ctmp", [128, 4, 16 + TB], F32)
        cpl = sb("cpl", [128, TB], BF16)
        ctg = sb("ctg", [128, TB], F32)
        xcb = sb("xcb", [128, TB], BF16)
        ident = sb("ident", [128, 128], BF16)
        cosP = sb("cosP", [128, NKB, 8], F32)
        sinP = sb("sinP", [128, NKB, 8], F32)
        cosS = sb("cosS", [128, 1, 8], F32)
        sinS = sb("sinS", [128, 1, 8], F32)
        invf = sb("invf", [128, 8], F32)
        posP = sb("posP", [128, NKB], F32)
        posS = sb("posS", [128, 1], F32)
        hpow = sb("hpow", [128, NIT + 1], F32)
        rcnt = sb("rcnt", [128, 4, 16], F32)
        gT = sb("gT", [128, 8], F32)
        cw = sb("cw", [128, 4, 4], F32)
        cb = sb("cb", [128, 4], F32)
        hba = sb("hba", [128, 4], F32)
        hbx = sb("hbx", [128, 4], F32)
        lam = sb("lam", [128, 4], F32)
        clf = sb("clf", [128, 4], F32)
        psc = sb("psc", [128, 4], F32)
        waBD = sb("waBD", [128, 4, 128], BF16)
        wxBD = sb("wxBD", [128, 4, 128], BF16)
        pwB = sb("pwB", [128, 4, 128], BF16)
        convst = sb("convst", [128, 4, 3], F32)
        hst = sb("hst", [128, 4], F32)
        poolst = sb("poolst", [128, 4, 15], F32)
        st_ss = sb("st_ss", [128, 2], F32)
        st_rs = sb("st_rs", [128, 2], F32)
        wabs = sb("wabs", [128, 2, 4], F32)
        wsgn = sb("wsgn", [128, 2, 4], F32)
        bs = sb("bs", [128, 16], F32)
        steps = sb("steps", [128, NIT + 1], F32)
        rec = sb("rec", [128, 8], F32)
        psA = [ps("psA%d" % i, [128, 512], F32) for i in range(4)]
        psT = [ps("psT%d" % i, [128, 1024], BF16) for i in range(2)]
        psB = [ps("psB%d" % i, [128, 512], F32) for i in range(2)]

        identf = arA[:, 0:128]
        onesf = arA[:, 128:256]
        rtmp = arA[:, 256:256 + NKB * 8].rearrange("p (a b) -> p a b", b=8)
        rtmp2 = arA[:, 768:768 + NKB * 8].rearrange("p (a b) -> p a b", b=8)
        rtmpi = arA[:, 1280:1280 + NKB * 8].bitcast(I32).rearrange("p (a b) -> p a b", b=8)
        sc = arA[:, 0:NKEY]
        om = arA[:, 0:4 * TB]
        ixc = arA[:, 4 * TB:8 * TB]
        o = 8 * TB
        Bxp = []
        for i in range(2):
            Bxp.append(arA[:, o:o + 3 + TB]); o += 4 + TB
        Bxc = []
        for i in range(2):
            Bxc.append(arA[:, o:o + TB]); o += TB
        Bta = arA[:, o:o + TB]; o += TB
        Btx = arA[:, o:o + TB]; o += TB
        Ba2 = arA[:, o:o + TB]; o += TB
        assert o <= 4128, o
        o = 8 * TB
        Baj = []
        for i in range(4):
            Baj.append(arA[:, o:o + TB]); o += TB
        Bbb = arA[:, o:o + TB]; o += TB
        Bh = arA[:, o:o + TB]; o += TB
        Btg = arA[:, o:o + TB]; o += TB
        assert o <= 4128, o
        mg = arA[:, 0:8 * TB]
        mtg = [arA[:, 8 * TB + i * TB: 8 * TB + (i + 1) * TB] for i in range(2)]
        mgb = mk[:, 0:8 * TB]
        p1 = ['Bxp0a', 'Bxp0b', 'Bxp1a', 'Bxp1b', 'Bxc0', 'Bxc1', 'Bta', 'Btx', 'Ba2']
        p2 = ['Baj0', 'Baj1', 'Baj2', 'Baj3', 'Bbb', 'Bh', 'Btg']
        P.alias(['sc'], ['om', 'ixc', 'mg', 'mtg0', 'mtg1'] + p1 + p2)
        P.alias(['mg'], ['om', 'ixc'])
        P.alias(p1, p2 + ['mtg0', 'mtg1'])
        P.alias(['mtg0', 'mtg1'], p2)
        P.alias(['onesf', 'identf', 'rtmp', 'rtmp2', 'rtmpi'], ['sc', 'om', 'ixc', 'mg', 'mtg0', 'mtg1'] + p1 + p2)
        P.alias(['mk'], ['mgb', 'mkV', 'mkA'])
        P.alias(['mgb'], ['mkV', 'mkA'])

        build_program.sbuf_left = nc.sbuf_bytes_remaining
        cnt = {'wb': 0, 'pa': 0, 'pt': 0}

        def next_w(pin=False):
            while True:
                i = cnt['wb'] % NWB
                cnt['wb'] += 1
                if i != cnt.get('pin'):
                    break
            if pin:
                cnt['pin'] = i
            return wbuf[i], 'wbuf%d' % i

        def unpin_w():
            cnt['pin'] = None

        pa_ring = [(psA[0], 'psA0'), (psA[1], 'psA1'), (psA[2], 'psA2'), (psA[3], 'psA3'), (psB[0], 'psB0'), (psB[1], 'psB1')]

        def next_pa():
            i = cnt['pa'] % 6
            cnt['pa'] += 1
            return pa_ring[i]

        def next_pt():
            i = cnt['pt'] % 2
            cnt['pt'] += 1
            return psT[i], 'psT%d' % i

        def small_dma(out, in_, r=(), w=(), q='sp', key=None):
            key = key or ('m_' + (list(w) + list(r))[0])
            DMA(q, key, r, w, out=out, in_=in_, allow_slow_non_contiguous=True)

        CK('casts')
        G('memset', (), ['onesf'], ap=onesf, constant=1.0)
        G('affine_select', ['onesf'], ['identf'], out=identf, in_=onesf, pattern=[[-1, 128]],
          compare_op=ALU.is_equal, fill=0.0, base=0, channel_multiplier=1)
        V('tensor_copy', ['identf'], ['ident'], out=ident[:], in_=identf)
        G('memset', (), ['Vt'], ap=Vt[:, :, :, 64:65], constant=1.0)
        G('memset', (), ['qT'], ap=qT[:], constant=0.0)
        for j in range(8):
            G('memset', (), ['invf'], ap=invf[:, j:j + 1], constant=float(500000.0 ** (-(2.0 * j) / 16.0)))
        for k in range(NIT + 1):
            G('memset', (), ['hpow'], ap=hpow[:, k:k + 1], constant=float(0.5 ** k))
        for g in range(4):
            wdw = 2 ** (g + 1)
            for t in range(16):
                G('memset', (), ['rcnt'], ap=rcnt[:, g, t:t + 1], constant=1.0 / float(min(wdw, t + 1)))
        small_dma(posP[:], posp_in, (), ['posP'])
        small_dma(posS[:], poss_in, (), ['posS'])
        small_dma(gfbc[:], fin_g.partition_broadcast(128), (), ['gfbc'])

        def rope_tables(pos, nt, cosT, sinT, tag):
            a3 = rtmp[:, 0:nt, :]
            b3 = rtmp2[:, 0:nt, :]
            i3 = rtmpi[:, 0:nt, :]
            for shift, dst in ((0.0, sinT), (math.pi / 2.0, cosT)):
                V('tensor_tensor', ['pos' + tag, 'invf'], ['rtmp'], out=a3, in0=pos.unsqueeze(2).to_broadcast([128, nt, 8]),
                  in1=invf[:].unsqueeze(1).to_broadcast([128, nt, 8]), op=ALU.mult)
                if shift:
                    V('tensor_scalar', ['rtmp'], ['rtmp'], out=a3, in0=a3, scalar1=shift, scalar2=None, op0=ALU.add)
                V('tensor_scalar', ['rtmp'], ['rtmp2'], out=b3, in0=a3, scalar1=1.0 / (2.0 * math.pi), scalar2=None, op0=ALU.mult)
                V('tensor_copy', ['rtmp2'], ['rtmpi'], out=i3, in_=b3)
                V('tensor_copy', ['rtmpi'], ['rtmp2'], out=b3, in_=i3)
                V('scalar_tensor_tensor', ['rtmp2', 'rtmp'], ['rtmp'], out=a3, in0=b3, scalar=-2.0 * math.pi, in1=a3, op0=ALU.mult, op1=ALU.add)
                V('tensor_scalar', ['rtmp'], ['rtmp'], out=a3, in0=a3, scalar1=3.14159, scalar2=-3.14159, op0=ALU.min, op1=ALU.max)
                A('activation', ['rtmp'], ['rope' + tag], out=dst, in_=a3, func=AF.Sin)

        def cast_weights(l):
            for gi, (c0, c1) in enumerate(WGROUPS):
                DMA('pool', 'c_win%d_%d' % (l, gi), (), ['win_bf%d_%d' % (l, gi)], out=win_bf[l, gi].rearrange("k kt c -> kt k c")[:, :, 0:c1 - c0],
                    in_=w_in[l, :, c0:c1].rearrange("(kt k) c -> kt k c", k=128))
            for n in range(3):
                DMA('pool', 'c_wbo%d' % l, (), ['wbo_bf%d' % l], out=wbo_bf[l, n].rearrange("c ct e -> ct c e"),
                    in_=w_bo[l, n].rearrange("(ct c) e -> ct c e", c=128))
            for r in range(2):
                DMA('pool', 'c_wout%d' % l, (), ['wout_bf%d' % l], out=wout_bf[l, r].rearrange("e et c -> et e c"),
                    in_=w_out[l, :, r * 512:(r + 1) * 512].rearrange("(et e) c -> et e c", e=128))

        cast_weights(0)
        if debug.get('skip_prompt'):
            cast_weights(1)
        CK('consts')
        rope_tables(posP[:], NKB, cosP[:], sinP[:], 'P')
        rope_tables(posS[:], 1, cosS[:], sinS[:], 'S')
        CK('prologue')

        def fm(v):
            return v.rearrange("(j p) -> p j", p=128)

        def layer_setup(l):
            small_dma(gT[:], norm_g[l].rearrange("(j p) -> p j", p=128), (), ['gT'])
            small_dma(cb[:], fm(conv_b[l]), (), ['cb'])
            for tap in range(4):
                small_dma(cw[:, :, tap], fm(conv_w[l, tap]), (), ['cw'])
            small_dma(hba[:], fm(lru_ba[l]), (), ['hba'])
            small_dma(hbx[:], fm(lru_bx[l]), (), ['hbx'])
            small_dma(lam[:], fm(lru_lam[l]), (), ['lam'])
            small_dma(psc[:], fm(pool_scale[l]), (), ['psc'])
            V('tensor_scalar', ['hba'], ['hba'], out=hba[:], in0=hba[:], scalar1=0.5, scalar2=None, op0=ALU.mult)
            V('tensor_scalar', ['hbx'], ['hbx'], out=hbx[:], in0=hbx[:], scalar1=0.5, scalar2=None, op0=ALU.mult)
            A('activation', ['lam'], ['lam'], out=lam[:], in_=lam[:], func=AF.Exp, scale=-1.0)
            A('activation', ['lam'], ['lam'], out=lam[:], in_=lam[:], func=AF.Ln, bias=1.0, scale=1.0)
            V('tensor_scalar', ['lam'], ['clf'], out=clf[:], in0=lam[:], scalar1=-8.0, scalar2=None, op0=ALU.mult)
            G('memset', (), ['waBD'], ap=waBD[:], constant=0.0)
            G('memset', (), ['wxBD'], ap=wxBD[:], constant=0.0)
            for wsrc, wdst, wname, stg, stgn in ((lru_wa, waBD, 'waBD', rl[0], 'rl0'), (lru_wx, wxBD, 'wxBD', rl[1], 'rl1')):
                v = stg[:, 0:256].rearrange("p (j d) -> p j d", j=4)
                DMA('sp', stgn, (), [stgn], out=v, in_=wsrc[l].rearrange("(j hh) c d -> (hh c) j d", hh=2))
                V('tensor_copy', [stgn, wname], [wname], out=wdst[0:64, :, 0:64], in_=v[0:64])
                V('tensor_copy', [stgn, wname], [wname], out=wdst[64:128, :, 64:128], in_=v[64:128])
            v = rl[2][:, :].rearrange("p (g d) -> p g d", g=4)
            DMA('sp', 'rl2', (), ['rl2'], out=v, in_=pool_w[l].rearrange("g c d -> c g d"))
            V('tensor_copy', ['rl2'], ['pwB'], out=pwB[:], in_=v)

        def load_w_in(l, c0, ncol, pin=False):
            wb, wn = next_w(pin)
            v = wb[:, 0:8 * ncol].rearrange("p (a b) -> p a b", a=8)
            gi = [i for i, (a0, a1) in enumerate(WGROUPS) if a0 == c0 and c0 + ncol == a1]
            assert len(gi) == 1, (c0, ncol)
            src = win_bf[l, gi[0], :, :, 0:ncol]
            DMA('sp', wn, ['win_bf%d_%d' % (l, gi[0])], [wn], out=v, in_=src)
            return v, wn

        def load_w_bo(l, n):
            wb, wn = next_w()
            v = wb[:].rearrange("p (a b) -> p a b", a=4)
            src = wbo_bf[l, n]
            DMA('sp', wn, ['wbo_bf%d' % l], [wn], out=v, in_=src)
            return v, wn

        def load_w_out(l, half):
            wb, wn = next_w()
            v = wb[:].rearrange("p (a b) -> p a b", a=8)
            src = wout_bf[l, half]
            DMA('sp', wn, ['wout_bf%d' % l], [wn], out=v, in_=src)
            return v, wn

        def rope(src, sname, dst, dname, H, tsz, ct, st, tname):
            s3 = src.rearrange("p (h d) -> p h d", h=H)
            d3 = dst.rearrange("p (h d) -> p h d", h=H)
            cb_ = ct.unsqueeze(1).to_broadcast([tsz, H, 8])
            sb_ = st.unsqueeze(1).to_broadcast([tsz, H, 8])
            t = [rt[i][0:tsz, 0:H, :] for i in range(4)]
            CK('r0')
            A('copy', [sname], [dname + 'n'], out=d3[:, :, 16:64], in_=s3[:, :, 16:64])
            CK('r1')
            V('tensor_tensor', [sname, tname], ['rt0'], out=t[0], in0=s3[:, :, 0:8], in1=cb_, op=ALU.mult)
            CK('r2')
            V('tensor_tensor', [sname, tname], ['rt1'], out=t[1], in0=s3[:, :, 8:16], in1=sb_, op=ALU.mult)
            V('tensor_tensor', [sname, tname], ['rt2'], out=t[2], in0=s3[:, :, 8:16], in1=cb_, op=ALU.mult)
            V('tensor_tensor', [sname, tname], ['rt3'], out=t[3], in0=s3[:, :, 0:8], in1=sb_, op=ALU.mult)
            V('tensor_tensor', ['rt0', 'rt1'], [dname + 'a'], out=d3[:, :, 0:8], in0=t[0], in1=t[1], op=ALU.subtract)
            V('tensor_tensor', ['rt2', 'rt3'], [dname + 'b'], out=d3[:, :, 8:16], in0=t[2], in1=t[3], op=ALU.add)
            return [dname + 'n', dname + 'a', dname + 'b']

        def transpose_blocks(src, sres, nblk, tsz):
            pt, ptn = next_pt()
            for c in range(nblk):
                T('transpose', list(sres) + ['ident'], [ptn], out=pt[:, c * 128:c * 128 + tsz], in_=src[0:tsz, c * 128:(c + 1) * 128],
                  identity=ident[0:tsz, 0:tsz])
            return pt, ptn

        def blkview(pt, nblk, tsz):
            return pt[:, 0:nblk * 128].rearrange("p (c t) -> p c t", c=nblk)[:, :, 0:tsz]

        def process_block(S, l, blk, last_layer):
            Tn = blk['T']
            tiles = blk['tiles']
            xsrc = blk['xsrc'][l]
            xdst = blk['xdst'][l]
            cosT, sinT = S['tabs']
            tabres = S['tabres']
            past = S['past']
            stores = []

            def run_zip(ga, na, gb, nb):
                ia = ib = 0
                da = db = False
                while not (da and db):
                    if not da and (db or ia * nb <= ib * na):
                        try:
                            next(ga)
                            ia += 1
                        except StopIteration:
                            da = True
                    elif not db:
                        try:
                            next(gb)
                            ib += 1
                        except StopIteration:
                            db = True

            def drain(g):
                for _ in g:
                    pass


            def rms_rstd(i, tsz, xr):
                A('activation', [xr], ['hn', 'ss%d' % i], out=hn[0:tsz, :], in_=xt[i][0:tsz, :], func=AF.Square,
                  accum_out=st_ss[0:tsz, i:i + 1])
                V('tensor_scalar', ['ss%d' % i], ['rs%d' % i], out=st_rs[0:tsz, i:i + 1], in0=st_ss[0:tsz, i:i + 1],
                  scalar1=1.0 / D, scalar2=EPS, op0=ALU.mult, op1=ALU.add)
                A('activation', ['rs%d' % i], ['rs%d' % i], out=st_rs[0:tsz, i:i + 1], in_=st_rs[0:tsz, i:i + 1], func=AF.Sqrt)
                V('reciprocal', ['rs%d' % i], ['rs%d' % i], out=st_rs[0:tsz, i:i + 1], in_=st_rs[0:tsz, i:i + 1])

            for i, (q0, tsz, a, kb, hm, tix) in enumerate(tiles):
                xr = 'xt%d' % i
                DMA('sp', xr, [blk['xres'][l]], [xr], out=xt[i][0:tsz, :], in_=xsrc[q0:q0 + tsz, :])
                rms_rstd(i, tsz, xr)
                V('tensor_scalar', [xr, 'rs%d' % i], ['hn'], out=hn[0:tsz, :], in0=xt[i][0:tsz, :], scalar1=st_rs[0:tsz, i:i + 1],
                  scalar2=None, op0=ALU.mult)
                pt, ptn = transpose_blocks(hn, ['hn'], 8, tsz)
                for kt in range(8):
                    A('activation', [ptn, 'gT'], ['hnT'], out=hnT[:, kt, q0:q0 + tsz], in_=pt[:, kt * 128:kt * 128 + tsz],
                      func=AF.Identity, scale=gT[:, kt:kt + 1])

            def tm_group(c0, ncol, evac):
                wv, wn = load_w_in(l, c0, ncol)
                for i, (q0, tsz, a, kb, hm, tix) in enumerate(tiles):
                    pa, pan = next_pa()
                    for kt in range(8):
                        T('matmul', ['hnT', wn], [pan], out=pa[0:tsz, 0:ncol], lhsT=hnT[:, kt, q0:q0 + tsz], rhs=wv[:, kt, :],
                          start=(kt == 0), stop=(kt == 7))
                    ct_ = cosT[0:tsz, tix, :]
                    st_ = sinT[0:tsz, tix, :]
                    evac(i, q0, tsz, a, kb, pa, pan, ct_, st_)
                    yield

            def ev_q(i, q0, tsz, a, kb, pa, pan, ct_, st_):
                res = rope(pa[0:tsz, :], pan, qbf[0:tsz, :], 'qbf', 8, tsz, ct_, st_, tabres)
                CK('r3')
                pt, ptn = transpose_blocks(qbf, res, 4, tsz)
                CK('r4')
                A('copy', [ptn], ['qT'], out=qT[0:64, :, 0, q0:q0 + tsz], in_=blkview(pt, 4, tsz)[0:64])
                A('copy', [ptn], ['qT'], out=qT[64:128, :, 1, q0:q0 + tsz], in_=blkview(pt, 4, tsz)[64:128])

            def ev_k(i, q0, tsz, a, kb, pa, pan, ct_, st_):
                res = rope(pa[0:tsz, :], pan, kst[0:tsz, i, :], 'kst%d' % i, 8, tsz, ct_, st_, tabres)
                G('tensor_copy', res, ['kbf'], out=kbf[0:tsz, :], in_=kst[0:tsz, i, :])
                pt, ptn = transpose_blocks(kbf, ['kbf'], 4, tsz)
                A('copy', [ptn], ['kT'], out=kT[:, :, a:a + tsz], in_=blkview(pt, 4, tsz))
                stores.append((res, dict(out=S['k_out'][l][a - past:a - past + tsz, :], in_=kst[0:tsz, i, :])))

            def ev_v(i, q0, tsz, a, kb, pa, pan, ct_, st_):
                A('copy', [pan], ['vst%d' % i], out=vst[0:tsz, i, :], in_=pa[0:tsz, :])
                V('tensor_copy', [pan], ['Vt'], out=Vt[0:tsz, kb, :, 0:64], in_=pa[0:tsz, :].rearrange("p (h d) -> p h d", h=8))
                stores.append((['vst%d' % i], dict(out=S['v_out'][l][a - past:a - past + tsz, :], in_=vst[0:tsz, i, :])))

            def ev_ga(i, q0, tsz, a, kb, pa, pan, ct_, st_):
                A('activation', [pan], ['sga%d' % i], out=sga[0:tsz, i, :], in_=pa[0:tsz, :], func=AF.Tanh, scale=0.5)
                V('scalar_tensor_tensor', [pan, 'sga%d' % i], ['sga%d' % i], out=sga[0:tsz, i, :], in0=sga[0:tsz, i, :], scalar=1.0,
                  in1=pa[0:tsz, :], op0=ALU.add, op1=ALU.mult)

            def ev_idx(i, q0, tsz, a, kb, pa, pan, ct_, st_):
                res = rope(pa[0:tsz, 0:256], pan, qibf[0:tsz, :], 'qibf', 4, tsz, ct_, st_, tabres)
                pt, ptn = transpose_blocks(qibf, res, 2, tsz)
                A('copy', [ptn], ['qiT'], out=qiT[:, :, q0:q0 + tsz], in_=blkview(pt, 2, tsz))
                res2 = rope(pa[0:tsz, 256:320], pan, kist[0:tsz, i, :], 'kist%d' % i, 1, tsz, ct_, st_, tabres)
                G('tensor_copy', res2, ['kibf0'], out=kibf[0:tsz, 0, :], in_=kist[0:tsz, i, :])
                G('tensor_copy', res2, ['kibf1'], out=kibf[0:tsz, 1, :], in_=kist[0:tsz, i, :])
                pt2, ptn2 = next_pt()
                T('transpose', ['kibf0', 'kibf1', 'ident'], [ptn2], out=pt2[:, 0:tsz], in_=kibf[0:tsz, :, :].rearrange("p a b -> p (a b)"),
                  identity=ident[0:tsz, 0:tsz])
                A('copy', [ptn2], ['kiT'], out=kiT[:, a:a + tsz], in_=pt2[:, 0:tsz])
                A('activation', [pan], ['wabs%d' % i], out=wabs[0:tsz, i, :], in_=pa[0:tsz, 320:324], func=AF.Abs)
                A('activation', [pan], ['wsgn%d' % i], out=wsgn[0:tsz, i, :], in_=pa[0:tsz, 320:324], func=AF.Sign)
                stores.append((res2, dict(out=S['ki_out'][l][a - past:a - past + tsz, :], in_=kist[0:tsz, i, :])))

            CK('step1')

            def gen_step2():
                yield from tm_group(C_Q, 512, ev_q)
                yield from tm_group(C_K, 512, ev_k)
                yield from tm_group(C_V, 512, ev_v)
                yield from tm_group(C_GA, 512, ev_ga)
                yield from tm_group(C_QI, 324, ev_idx)

            def fm_proj(wv, wn, j):
                pa, pan = next_pa()
                for kt in range(8):
                    T('matmul', ['hnT', wn], [pan], out=pa[:, 0:Tn], lhsT=wv[:, kt, j * 128:(j + 1) * 128], rhs=hnT[:, kt, 0:Tn],
                      start=(kt == 0), stop=(kt == 7))
                return pa, pan

            def gen_B1():
              yield
              wxb, wxbn = load_w_in(l, C_XB, 512, pin=True)
              for j in range(4):
                  pa, pan = fm_proj(wxb, wxbn, j)
                  xp_, xpn = Bxp[j % 2], 'Bxp%d' % (j % 2)
                  xc_, xcn = Bxc[j % 2], 'Bxc%d' % (j % 2)
                  A('copy', [pan], [xpn + 'b'], out=xp_[:, 3:3 + Tn], in_=pa[:, 0:Tn])
                  G('tensor_copy', ['convst%d' % j], [xpn + 'a'], out=xp_[:, 0:3], in_=convst[:, j, :])
                  G('tensor_copy', [xpn + 'a', xpn + 'b'], ['convst%d' % j], out=convst[:, j, :], in_=xp_[:, Tn:Tn + 3])
                  V('tensor_scalar', [xpn + 'a', xpn + 'b', 'cw', 'cb'], [xcn], out=xc_[:, 0:Tn], in0=xp_[:, 0:Tn], scalar1=cw[:, j, 0:1],
                    scalar2=cb[:, j:j + 1], op0=ALU.mult, op1=ALU.add)
                  for tap in range(1, 4):
                      V('scalar_tensor_tensor', [xpn + 'a', xpn + 'b', 'cw', xcn], [xcn], out=xc_[:, 0:Tn], in0=xp_[:, tap:tap + Tn],
                        scalar=cw[:, j, tap:tap + 1], in1=xc_[:, 0:Tn], op0=ALU.mult, op1=ALU.add)
                  G('tensor_copy', [xcn], ['xcb'], out=xcb[:, 0:Tn], in_=xc_[:, 0:Tn])
                  yield
                  pa1, pan1 = next_pa()
                  T('matmul', ['waBD', 'xcb'], [pan1], out=pa1[:, 0:Tn], lhsT=waBD[:, j, :], rhs=xcb[:, 0:Tn], start=True, stop=True)
                  pa2, pan2 = next_pa()
                  T('matmul', ['wxBD', 'xcb'], [pan2], out=pa2[:, 0:Tn], lhsT=wxBD[:, j, :], rhs=xcb[:, 0:Tn], start=True, stop=True)
                  A('activation', [pan1, 'hba'], ['Bta'], out=Bta[:, 0:Tn], in_=pa1[:, 0:Tn], func=AF.Tanh, bias=hba[:, j:j + 1], scale=0.5)
                  A('activation', [pan2, 'hbx'], ['Btx'], out=Btx[:, 0:Tn], in_=pa2[:, 0:Tn], func=AF.Tanh, bias=hbx[:, j:j + 1], scale=0.5)
                  A('activation', ['Bta', 'clf'], ['Ba2'], out=Ba2[:, 0:Tn], in_=Bta[:, 0:Tn], func=AF.Exp, bias=clf[:, j:j + 1], scale=clf[:, j:j + 1])
                  V('tensor_scalar', ['Ba2'], ['om'], out=om[:, j * TB:j * TB + Tn], in0=Ba2[:, 0:Tn], scalar1=-1.0, scalar2=1.0,
                    op0=ALU.mult, op1=ALU.add)
                  V('scalar_tensor_tensor', ['Btx', xcn], ['ixc'], out=ixc[:, j * TB:j * TB + Tn], in0=Btx[:, 0:Tn], scalar=1.0,
                    in1=xc_[:, 0:Tn], op0=ALU.add, op1=ALU.mult)
                  yield

            run_zip(gen_step2(), 2 * len(tiles) * 5 // 2, gen_B1(), 9)
            unpin_w()
            CK('step2')

            wgb, wgbn = load_w_in(l, C_GB, 512)
            for j in range(4):
                A('activation', ['om'], ['Baj%d' % j], out=Baj[j][:, 0:Tn], in_=om[:, j * TB:j * TB + Tn], func=AF.Sqrt, bias=1.0, scale=-1.0)
            for j in range(4):
                A('activation', ['om'], ['om'], out=om[:, j * TB:j * TB + Tn], in_=om[:, j * TB:j * TB + Tn], func=AF.Sqrt)
            for j in range(4):
                V('scalar_tensor_tensor', ['ixc', 'om'], ['Bbb'], out=Bbb[:, 0:Tn], in0=ixc[:, j * TB:j * TB + Tn], scalar=0.5,
                  in1=om[:, j * TB:j * TB + Tn], op0=ALU.mult, op1=ALU.mult)
                V('tensor_tensor_scan', ['Baj%d' % j, 'Bbb', 'hst%d' % j], ['Bh'], out=Bh[:, 0:Tn], data0=Baj[j][:, 0:Tn], data1=Bbb[:, 0:Tn],
                  initial=hst[:, j:j + 1], op0=ALU.mult, op1=ALU.add)
                G('tensor_copy', ['Bh'], ['hst%d' % j], out=hst[:, j:j + 1], in_=Bh[:, Tn - 1:Tn])
                pa, pan = fm_proj(wgb, wgbn, j)
                A('activation', [pan], ['Btg'], out=Btg[:, 0:Tn], in_=pa[:, 0:Tn], func=AF.Tanh, scale=0.5)
                V('scalar_tensor_tensor', [pan, 'Btg'], ['Btg'], out=Btg[:, 0:Tn], in0=Btg[:, 0:Tn], scalar=1.0, in1=pa[:, 0:Tn],
                  op0=ALU.add, op1=ALU.mult)
                V('scalar_tensor_tensor', ['Btg', 'Bh'], ['ybT'], out=yT[1][:, j, 0:Tn], in0=Btg[:, 0:Tn], scalar=0.5, in1=Bh[:, 0:Tn],
                  op0=ALU.mult, op1=ALU.mult)

            for r, kw in stores:
                DMA('sp', 's_' + r[0], r, (), **kw)

            def gen_branchC():
                wxc, wxcn = load_w_in(l, C_XC, 512)
                wgc, wgcn = load_w_in(l, C_GC, 512)
                L = 15 + Tn
                for g in range(4):
                    wdw = 2 ** (g + 1)
                    pa, pan = fm_proj(wxc, wxcn, g)
                    G('tensor_copy', ['poolst%d' % g], ['ct0a'], out=ctmp[:, 0, 0:15], in_=poolst[:, g, :])
                    A('copy', [pan], ['ct0b'], out=ctmp[:, 0, 15:L], in_=pa[:, 0:Tn])
                    G('tensor_copy', ['ct0a', 'ct0b'], ['poolst%d' % g], out=poolst[:, g, :], in_=ctmp[:, 0, Tn:L])
                    prev, prevn = 0, ['ct0a', 'ct0b']
                    m = 1
                    slot = 1
                    while m < wdw:
                        G('tensor_tensor', prevn, ['ct%d' % slot], out=ctmp[:, slot, 2 * m - 1:L], in0=ctmp[:, prev, 2 * m - 1:L],
                          in1=ctmp[:, prev, m - 1:L - m], op=ALU.add)
                        prev, prevn = slot, ['ct%d' % slot]
                        slot = 1 + (slot % 3)
                        m *= 2
                    yield
                    if blk['first']:
                        V('tensor_tensor', prevn + ['rcnt'], ['ctg'], out=ctg[:, 0:Tn], in0=ctmp[:, prev, 15:L], in1=rcnt[:, g, 0:Tn], op=ALU.mult)
                        V('tensor_tensor', ['ctg', 'ct0b'], ['cpl'], out=cpl[:, 0:Tn], in0=ctg[:, 0:Tn], in1=ctmp[:, 0, 15:L], op=ALU.subtract)
                    else:
                        V('scalar_tensor_tensor', prevn + ['ct0b'], ['cpl'], out=cpl[:, 0:Tn], in0=ctmp[:, prev, 15:L], scalar=1.0 / wdw,
                          in1=ctmp[:, 0, 15:L], op0=ALU.mult, op1=ALU.subtract)
                    pa1, pan1 = next_pa()
                    T('matmul', ['pwB', 'cpl'], [pan1], out=pa1[:, 0:Tn], lhsT=pwB[:, g, :], rhs=cpl[:, 0:Tn], start=True, stop=True)
                    pa2, pan2 = fm_proj(wgc, wgcn, g)
                    A('activation', [pan2], ['ctg'], out=ctg[:, 0:Tn], in_=pa2[:, 0:Tn], func=AF.Tanh, scale=0.5)
                    V('scalar_tensor_tensor', [pan2, 'ctg'], ['ctg'], out=ctg[:, 0:Tn], in0=ctg[:, 0:Tn], scalar=1.0, in1=pa2[:, 0:Tn],
                      op0=ALU.add, op1=ALU.mult)
                    V('tensor_scalar', ['ctg', 'psc'], ['ctg'], out=ctg[:, 0:Tn], in0=ctg[:, 0:Tn], scalar1=psc[:, g:g + 1], scalar2=0.5,
                      op0=ALU.mult, op1=ALU.mult)
                    V('tensor_tensor', ['ctg', pan1], ['ycT'], out=yT[2][:, g, 0:Tn], in0=ctg[:, 0:Tn], in1=pa1[:, 0:Tn], op=ALU.mult)
                    yield

            def rec_scores(i):
                q0, tsz, a, kb, hm, tix = tiles[i]
                Sv = a + tsz
                for c0 in range(0, Sv, 512):
                    n = min(512, Sv - c0)
                    for h in range(4):
                        hs = slice((h % 2) * 64, (h % 2) * 64 + 64)
                        pa, pan = next_pa()
                        T('matmul', ['qiT', 'kiT'], [pan], out=pa[0:tsz, 0:n], lhsT=qiT[hs, h // 2, q0:q0 + tsz], rhs=kiT[hs, c0:c0 + n],
                          start=True, stop=True)
                        r_, rn = rl[h % NRL], 'rl%d' % (h % NRL)
                        A('activation', [pan, 'wabs%d' % i], [rn], out=r_[0:tsz, 0:n], in_=pa[0:tsz, 0:n], func=AF.Relu, scale=wabs[0:tsz, i, h:h + 1])
                        if h == 0:
                            V('tensor_scalar', [rn, 'wsgn%d' % i], ['sc'], out=sc[0:tsz, c0:c0 + n], in0=r_[0:tsz, 0:n], scalar1=wsgn[0:tsz, i, 0:1],
                              scalar2=None, op0=ALU.mult)
                        else:
                            V('scalar_tensor_tensor', [rn, 'wsgn%d' % i, 'sc'], ['sc'], out=sc[0:tsz, c0:c0 + n], in0=r_[0:tsz, 0:n],
                              scalar=wsgn[0:tsz, i, h:h + 1], in1=sc[0:tsz, c0:c0 + n], op0=ALU.mult, op1=ALU.add)

            def gen_bisect(i):
                q0, tsz, a, kb, hm, tix = tiles[i]
                Sv = a + tsz
                V('tensor_reduce', ['sc'], ['bs0'], out=bs[0:tsz, 0:1], in_=sc[0:tsz, 0:Sv], axis=AX.X, op=ALU.min)
                V('tensor_reduce', ['sc'], ['bs1'], out=bs[0:tsz, 1:2], in_=sc[0:tsz, 0:Sv], axis=AX.X, op=ALU.max)
                if hm:
                    G('memset', ['bs0', 'bs1'], ['sc'], ap=sc[0:64, Sv - 64:Sv], constant=NEG)
                V('tensor_scalar', ['bs0', 'bs1'], ['bs2'], out=bs[0:tsz, 2:3], in0=bs[0:tsz, 1:2], scalar1=bs[0:tsz, 0:1], scalar2=1.003,
                  op0=ALU.subtract, op1=ALU.mult)
                V('scalar_tensor_tensor', ['bs0', 'bs2'], ['bs6'], out=bs[0:tsz, 6:7], in0=bs[0:tsz, 2:3], scalar=-0.001, in1=bs[0:tsz, 0:1],
                  op0=ALU.mult, op1=ALU.add)
                V('tensor_scalar', ['bs2', 'hpow'], ['steps'], out=steps[0:tsz, :], in0=hpow[0:tsz, :], scalar1=bs[0:tsz, 2:3], scalar2=None, op0=ALU.mult)
                V('tensor_tensor', ['bs6', 'steps'], ['mid'], out=bs[0:tsz, 3:4], in0=bs[0:tsz, 6:7], in1=steps[0:tsz, 1:2], op=ALU.add)
                yield
                cA = int(Sv * (0.46 if i == 0 else 0.52)) // 2 * 2 if Sv >= 512 else Sv
                nA = Sv - cA
                thr = TOPK - nA / 2.0
                cc = thr - 0.25 - cA
                for k in range(1, NIT + 1):
                    kk = k + 1 if k < NIT else k
                    V('tensor_scalar', ['sc', 'mid'], ['mkV', 'cnt'], out=mk[0:tsz, 0:cA], in0=sc[0:tsz, 0:cA], scalar1=bs[0:tsz, 3:4], scalar2=cc,
                      op0=ALU.is_lt, op1=ALU.add, accum_out=bs[0:tsz, 4:5])
                    if nA:
                        A('activation', ['sc', 'mid'], ['mkA', 'cntA'], out=mk[0:tsz, cA:Sv], in_=sc[0:tsz, cA:Sv], func=AF.Sign, bias=bs[0:tsz, 3:4],
                          scale=-1.0, accum_out=bs[0:tsz, 7:8])
                    V('tensor_scalar', ['mid', 'steps'], ['bu'], out=bs[0:tsz, 9:10], in0=bs[0:tsz, 3:4], scalar1=steps[0:tsz, kk:kk + 1], scalar2=None,
                      op0=ALU.subtract)
                    if nA:
                        V('scalar_tensor_tensor', ['cnt', 'cntA'], ['bd'], out=bs[0:tsz, 5:6], in0=bs[0:tsz, 7:8], scalar=-0.5, in1=bs[0:tsz, 4:5],
                          op0=ALU.mult, op1=ALU.is_ge)
                    else:
                        V('tensor_scalar', ['cnt'], ['bd'], out=bs[0:tsz, 5:6], in0=bs[0:tsz, 4:5], scalar1=0.0, scalar2=None, op0=ALU.is_le)
                    V('scalar_tensor_tensor', ['bd', 'steps', 'bu'], ['mid'], out=bs[0:tsz, 3:4], in0=bs[0:tsz, 5:6], scalar=steps[0:tsz, k:k + 1],
                      in1=bs[0:tsz, 9:10], op0=ALU.mult, op1=ALU.add)
                    yield
                V('tensor_scalar', ['sc', 'mid'], ['mk'], out=mk[0:tsz, 0:Sv], in0=sc[0:tsz, 0:Sv], scalar1=bs[0:tsz, 3:4], scalar2=None, op0=ALU.is_ge)
                yield

            def mask_bias(i):
                return len(tiles) == 2 and i == 0

            def rec_masktrans(i):
                q0, tsz, a, kb, hm, tix = tiles[i]
                Sv = a + tsz
                kbs = [b for b in S['kblocks'] if b[0] + b[1] <= Sv]
                for g0 in range(0, len(kbs), 8):
                    grp = kbs[g0:g0 + 8]
                    pt, ptn = next_pt()
                    for s_, (c0, kr, kbi) in enumerate(grp):
                        T('transpose', ['mk', 'ident'], [ptn], out=pt[0:kr, s_ * 128:s_ * 128 + tsz], in_=mk[0:tsz, c0:c0 + kr], identity=ident[0:tsz, 0:tsz])
                    if mask_bias(i):
                        kwm = dict(func=AF.Identity, scale=30000.0, bias=-30000.0)
                    else:
                        kwm = dict(func=AF.Identity)
                    if all(x[1] == 128 for x in grp):
                        A('activation', [ptn], ['mkT'], out=mkT[:, g0:g0 + len(grp), 0:tsz], in_=blkview(pt, len(grp), tsz), **kwm)
                    else:
                        for s_, (c0, kr, kbi) in enumerate(grp):
                            A('activation', [ptn], ['mkT'], out=mkT[0:kr, g0 + s_, 0:tsz], in_=pt[0:kr, s_ * 128:s_ * 128 + tsz], **kwm)

            def gen_attention(i, npool=4):
                q0, tsz, a, kb, hm, tix = tiles[i]
                Sv = a + tsz
                kbs = [b for b in S['kblocks'] if b[0] + b[1] <= Sv]
                nkb = len(kbs)
                MB = mask_bias(i)

                lbufs = [([psA[0], psA[1]], ['psA0', 'psA1']), ([psA[2], psA[3]], ['psA2', 'psA3'])]
                NLB = len(lbufs)

                def logits(bi):
                    c0, kr, kbi = kbs[bi]
                    pl, pln = lbufs[bi % NLB]
                    for hp in range(4):
                        hb = hp // 2
                        o = (hp % 2) * 256
                        if tsz == 128:
                            T('matmul', ['kT', 'qT'], [pln[hb]], out=pl[hb][0:kr, o:o + 256],
                              lhsT=kT[:, hp, c0:c0 + kr], rhs=qT[:, hp, :, q0:q0 + tsz], start=(hp % 2 == 0), stop=(not MB and hp % 2 == 1))
                        else:
                            for w in range(2):
                                T('matmul', ['kT', 'qT'], [pln[hb]], out=pl[hb][0:kr, o + w * 128:o + w * 128 + tsz],
                                  lhsT=kT[:, hp, c0:c0 + kr], rhs=qT[:, hp, w, q0:q0 + tsz], start=(hp % 2 == 0 and w == 0),
                                  stop=(not MB and hp % 2 == 1 and w == 1))
                    for hb in (range(2) if MB else ()):
                        if tsz == 128:
                            T('matmul', ['mkT', 'ident'], [pln[hb]], out=pl[hb][0:kr, :], lhsT=ident[0:kr, 0:kr],
                              rhs=mkT[0:kr, bi, 0:tsz].unsqueeze(1).to_broadcast([kr, 4, tsz]), start=False, stop=True)
                        else:
                            for s4 in range(4):
                                T('matmul', ['mkT', 'ident'], [pln[hb]], out=pl[hb][0:kr, s4 * 128:s4 * 128 + tsz], lhsT=ident[0:kr, 0:kr],
                                  rhs=mkT[0:kr, bi, 0:tsz], start=False, stop=(s4 == 3))

                def pv(bi):
                    c0, kr, kbi = kbs[bi]
                    px, pxn = pex[bi % 2], 'pex%d' % (bi % 2)
                    pdeps = [pxn + '0', pxn + '1']
                    if not MB:
                        px = pm[bi % 2]
                        pdeps = ['pm%da' % (bi % 2), 'pm%db' % (bi % 2)]
                    for h in range(8):
                        T('matmul', pdeps + ['Vt'], ['psB%d' % (h // 4)], out=psB[h // 4][0:tsz, (h % 4) * 65:(h % 4) * 65 + 65], lhsT=px[0:kr, h, 0:tsz],
                          rhs=Vt[0:kr, kbi, h, :], start=(bi == 0 and h % 4 == 0), stop=(bi == nkb - 1 and h % 4 == 3))
                for b0 in range(min(NLB, nkb)):
                    logits(b0)
                for bi, (c0, kr, kbi) in enumerate(kbs):
                    par = bi % 2
                    pl, pln = lbufs[bi % NLB]
                    px, pxn = pex[par], 'pex%d' % par
                    for hh in range(2):
                        A('activation', [pln[hh]], [pxn + str(hh)], out=px[0:kr, hh * 4:hh * 4 + 4, 0:tsz],
                          in_=pl[hh][0:kr, :].rearrange("p (h t) -> p h t", h=4)[:, :, 0:tsz], func=AF.Exp, scale=0.125)
                    if not MB:
                        pm_ = pm[par]
                        mbp = mkT[0:kr, bi, 0:tsz].unsqueeze(1).to_broadcast([kr, npool, tsz])
                        mbv = mkT[0:kr, bi, 0:tsz].unsqueeze(1).to_broadcast([kr, 8 - npool, tsz])
                        G('tensor_tensor', [pxn + '0', pxn + '1', 'mkT'], ['pm%da' % par], out=pm_[0:kr, 0:npool, 0:tsz], in0=px[0:kr, 0:npool, 0:tsz], in1=mbp, op=ALU.mult)
                        V('tensor_tensor', [pxn + '0', pxn + '1', 'mkT'], ['pm%db' % par], out=pm_[0:kr, npool:8, 0:tsz], in0=px[0:kr, npool:8, 0:tsz], in1=mbv, op=ALU.mult)
                    if bi + NLB < nkb:
                        logits(bi + NLB)
                    if bi >= 1:
                        pv(bi - 1)
                    yield
                pv(nkb - 1)

            def rec_attn_final(i):
                q0, tsz, a, kb, hm, tix = tiles[i]
                for hh in range(2):
                    pv = psB[hh][0:tsz, 0:260].rearrange("p (h e) -> p h e", h=4)
                    V('tensor_scalar', ['psB%d' % hh], ['rec%d' % hh], out=rec[0:tsz, hh * 4:hh * 4 + 4], in0=pv[:, :, 64], scalar1=2.0, scalar2=None,
                      op0=ALU.mult)
                    V('reciprocal', ['rec%d' % hh], ['rec%d' % hh], out=rec[0:tsz, hh * 4:hh * 4 + 4], in_=rec[0:tsz, hh * 4:hh * 4 + 4])
                    V('tensor_tensor', ['psB%d' % hh, 'rec%d' % hh], ['att%d' % hh], out=att[0:tsz, hh * 4:hh * 4 + 4, :], in0=pv[:, :, 0:64],
                      in1=rec[0:tsz, hh * 4:hh * 4 + 4].unsqueeze(2).to_broadcast([tsz, 4, 64]), op=ALU.mult)
                G('tensor_tensor', ['att0', 'att1', 'sga%d' % i], ['yab'], out=yab[0:tsz, :], in0=att[0:tsz, :, :].rearrange("p h d -> p (h d)"),
                  in1=sga[0:tsz, i, :], op=ALU.mult)
                pt, ptn = transpose_blocks(yab, ['yab'], 4, tsz)
                A('copy', [ptn], ['yaT'], out=yT[0][:, :, q0:q0 + tsz], in_=blkview(pt, 4, tsz))

            ynames = ['yaT', 'ybT', 'ycT']

            def gen_merge(ns, first_n, last_n, banks=None):
                bc = [0]

                def bank():
                    if banks is None:
                        return next_pa()
                    b = banks[bc[0] % len(banks)]
                    bc[0] += 1
                    return b
                for n in ns:
                    wb_, wbn = load_w_bo(l, n)
                    for e4 in range(2):
                        wg_, wgn = load_w_in(l, C_GM + n * 1024 + e4 * 512, 512)
                        for ee in range(4):
                            e = e4 * 4 + ee
                            pa, pan = bank()
                            for kt in range(8):
                                T('matmul', ['hnT', wgn], [pan], out=pa[:, 0:Tn], lhsT=wg_[:, kt, ee * 128:(ee + 1) * 128], rhs=hnT[:, kt, 0:Tn],
                                  start=(kt == 0), stop=(kt == 7))
                            pb, pbn = bank()
                            for ct in range(4):
                                T('matmul', [wbn, ynames[n]], [pbn], out=pb[:, 0:Tn], lhsT=wb_[:, ct, e * 128:(e + 1) * 128], rhs=yT[n][:, ct, 0:Tn],
                                  start=(ct == 0), stop=(ct == 3))
                            tg_, tgn = mtg[e % 2], 'mtg%d' % (e % 2)
                            A('activation', [pan], [tgn], out=tg_[:, 0:Tn], in_=pa[:, 0:Tn], func=AF.Tanh, scale=0.5)
                            mge = mg[:, e * TB:e * TB + Tn]
                            if n == first_n:
                                V('scalar_tensor_tensor', [tgn, pbn], ['mg'], out=mge, in0=tg_[:, 0:Tn], scalar=1.0, in1=pb[:, 0:Tn], op0=ALU.add, op1=ALU.mult)
                            else:
                                V('scalar_tensor_tensor', [tgn, pbn], [tgn], out=tg_[:, 0:Tn], in0=tg_[:, 0:Tn], scalar=1.0, in1=pb[:, 0:Tn],
                                  op0=ALU.add, op1=ALU.mult)
                                if n != last_n:
                                    G('tensor_tensor', [tgn, 'mg'], ['mg'], out=mge, in0=mge, in1=tg_[:, 0:Tn], op=ALU.add)
                                else:
                                    G('tensor_tensor', [tgn, 'mg'], ['mgb'], out=mgb[:, e * TB:e * TB + Tn], in0=mge, in1=tg_[:, 0:Tn], op=ALU.add)
                            yield

            psM = [(psT[0][:, :].bitcast(F32), 'psT0'), (psT[1][:, :].bitcast(F32), 'psT1')]

            def nkb_of(i):
                q0, tsz, a, kb, hm, tix = tiles[i]
                return len([b for b in S['kblocks'] if b[0] + b[1] <= a + tsz])

            rec_scores(0)
            run_zip(gen_branchC(), 8, gen_bisect(0), NIT + 2)
            rec_masktrans(0)
            CK('step3')
            if len(tiles) == 2:
                rec_scores(1)
                run_zip(gen_attention(0, 5), nkb_of(0), gen_bisect(1), NIT + 2)
                rec_attn_final(0)
                rec_masktrans(1)
                run_zip(gen_attention(1, 3), nkb_of(1), gen_merge([1, 2], 1, 0, psM), 16)
                rec_attn_final(1)
            else:
                run_zip(gen_attention(0, 3), nkb_of(0), gen_merge([1, 2], 1, 0, psM), 16)
                rec_attn_final(0)
            CK('step4')
            drain(gen_merge([0], 1, 0))
            for half in range(2):
                wo_, won = load_w_out(l, half)
                for i, (q0, tsz, a, kb, hm, tix) in enumerate(tiles):
                    pa, pan = next_pa()
                    for et in range(8):
                        T('matmul', ['mgb', won], [pan], out=pa[0:tsz, :], lhsT=mgb[:, et * TB + q0:et * TB + q0 + tsz], rhs=wo_[:, et, :],
                          start=(et == 0), stop=(et == 7))
                    xs_ = xt[i][0:tsz, half * 512:(half + 1) * 512]
                    V('scalar_tensor_tensor', [pan, 'xt%d' % i], ['xt%d' % i], out=xs_, in0=pa[0:tsz, :], scalar=0.5, in1=xs_, op0=ALU.mult, op1=ALU.add)
            for i, (q0, tsz, a, kb, hm, tix) in enumerate(tiles):
                xr = 'xt%d' % i
                if not last_layer:
                    DMA('act', 's_' + xr, [xr], [blk['xres'][l + 1]], out=xdst[q0:q0 + tsz, :], in_=xt[i][0:tsz, :])
                elif xdst is not None:
                    rms_rstd(i, tsz, xr)
                    V('scalar_tensor_tensor', [xr, 'rs%d' % i, 'gfbc'], [xr], out=xt[i][0:tsz, :], in0=xt[i][0:tsz, :], scalar=st_rs[0:tsz, i:i + 1],
                      in1=gfbc[0:tsz, :], op0=ALU.mult, op1=ALU.mult)
                    DMA('act', 's_' + xr, [xr], (), out=xdst[q0:q0 + tsz, :], in_=xt[i][0:tsz, :])

        def make_prompt():
            S = {'past': 0, 'tabs': (cosP, sinP), 'tabres': 'ropeP',
                 'k_out': [k_p[0], k_p[1]], 'v_out': [v_p[0], v_p[1]], 'ki_out': [ki_p[0], ki_p[1]]}
            S['kblocks'] = [(0, 16, 0)] + [(16 + 128 * j, 128, 1 + j) for j in range(32)]
            blocks = [{'T': 16, 'tiles': [(0, 16, 0, 0, False, 0)], 'first': True,
                       'xsrc': [xp_in[0:16, :], xscr_p[0:16, :]], 'xdst': [xscr_p[0:16, :], None], 'xres': ['xin', 'xscrp_m', 'none']}]
            nb = debug.get('nblk', SEQ // TB)
            for b in range(nb):
                a0 = 16 + b * TB
                tiles = []
                for i in range(TB // 128):
                    a = a0 + i * 128
                    kb = 1 + (a - 16) // 128
                    tiles.append((i * 128, 128, a, kb, True, kb))
                blocks.append({'T': TB, 'tiles': tiles, 'first': False,
                               'xsrc': [xp_in[a0:a0 + TB, :], xscr_p[a0:a0 + TB, :]],
                               'xdst': [xscr_p[a0:a0 + TB, :], y_p[a0 - 16:a0 - 16 + TB, :]], 'xres': ['xin', 'xscrp_%d' % b, 'none']})
            S['blocks'] = blocks
            return S

        def make_sample(si):
            S = {'past': PAST, 'tabs': (cosS, sinS), 'tabres': 'ropeS',
                 'k_out': [k_s[0, si], k_s[1, si]], 'v_out': [v_s[0, si], v_s[1, si]], 'ki_out': [ki_s[0, si], ki_s[1, si]]}
            S['kblocks'] = [(128 * j, 128, j) for j in range(16)] + [(PAST, 64, 16)]
            S['blocks'] = [{'T': TS, 'tiles': [(0, TS, PAST, 16, False, 0)], 'first': False,
                            'xsrc': [xs_in[si], xscr_s[si]], 'xdst': [xscr_s[si], y_s[si]], 'xres': ['xin', 'xscrs_%d' % si, 'none']}]
            return S

        st_names = ['convst%d' % j for j in range(4)] + ['hst%d' % j for j in range(4)] + ['poolst%d' % j for j in range(4)]

        def zero_states():
            G('memset', (), ['convst%d' % j for j in range(4)], ap=convst[:], constant=0.0)
            G('memset', (), ['hst%d' % j for j in range(4)], ap=hst[:], constant=0.0)
            G('memset', (), ['poolst%d' % j for j in range(4)], ap=poolst[:], constant=0.0)

        def load_states(l, si):
            for j in range(4):
                small_dma(convst[:, j, :], sconv_in[l, si][:, j * 128:(j + 1) * 128].rearrange("t p -> p t"), (), ['convst%d' % j])
                small_dma(poolst[:, j, :], spool_in[l, si][:, j * 128:(j + 1) * 128].rearrange("t p -> p t"), (), ['poolst%d' % j])
            small_dma(hst[:], fm(slru_in[l, si]), (), ['hst%d' % j for j in range(4)])

        def store_states(conv_o, lru_o, pool_o):
            for j in range(4):
                small_dma(conv_o[:, j * 128:(j + 1) * 128].rearrange("t p -> p t"), convst[:, j, :], ['convst%d' % j], ())
                small_dma(pool_o[:, j * 128:(j + 1) * 128].rearrange("t p -> p t"), poolst[:, j, :], ['poolst%d' % j], ())
            small_dma(fm(lru_o), hst[:], ['hst%d' % j for j in range(4)], ())

        def load_cache(l, si):
            for j in range(16):
                sl = slice(j * 128, (j + 1) * 128)
                rk, rkn = rl[j % 2], 'rl%d' % (j % 2)
                rv, rvn = rl[2 + j % 2], 'rl%d' % (2 + j % 2)
                kb_, kbn = (kbf, 'kbf') if j % 2 == 0 else (qbf, 'qbf')
                DMA('sp', rkn, (), [rkn], out=rk[:], in_=ck_in[l, si, sl, :])
                DMA('sp', rvn, (), [rvn], out=rv[:], in_=cv_in[l, si, sl, :])
                DMA('sp', 'kif', (), ['kif'], out=kif[:], in_=cki_in[l, si, sl, :])
                V('tensor_copy', [rkn], [kbn], out=kb_[:], in_=rk[:])
                pt, ptn = transpose_blocks(kb_, [kbn], 4, 128)
                A('copy', [ptn], ['kT'], out=kT[:, :, sl], in_=blkview(pt, 4, 128))
                V('tensor_copy', [rvn], ['Vt'], out=Vt[:, j, :, 0:64], in_=rv[:].rearrange("p (h d) -> p h d", h=8))
                A('copy', ['kif'], ['kibf0'], out=kibf[:, 0, :], in_=kif[:])
                V('tensor_copy', ['kif'], ['kibf1'], out=kibf[:, 1, :], in_=kif[:])
                pt2, ptn2 = next_pt()
                T('transpose', ['kibf0', 'kibf1', 'ident'], [ptn2], out=pt2[:, 0:128], in_=kibf[:, :, :].rearrange("p a b -> p (a b)"), identity=ident[:, :])
                A('copy', [ptn2], ['kiT'], out=kiT[:, sl], in_=pt2[:, 0:128])

        nlayers = debug.get('nlayers', 2)
        if not debug.get('skip_prompt'):
            Sp = make_prompt()
            for l in range(nlayers):
                layer_setup(l)
                zero_states()
                for bix, blk in enumerate(Sp['blocks']):
                    process_block(Sp, l, blk, l == 1)
                    if l == 0 and bix == min(2, len(Sp['blocks']) - 1):
                        cast_weights(1)
                store_states(conv_p[l], lru_p[l], pool_p[l])
        if not debug.get('skip_sample'):
            for si in range(debug.get('nsample', 2)):
                Ss = make_sample(si)
                for l in range(nlayers):
                    layer_setup(l)
                    CK('setup')
                    load_states(l, si)
                    CK('states')
                    load_cache(l, si)
                    CK('cache')
                    for blk in Ss['blocks']:
                        process_block(Ss, l, blk, l == 1)
                    store_states(conv_s[l, si], lru_s[l, si], pool_s[l, si])
        P.emit()
        build_program.stats = P.stats
    return nc


_CACHE = {}


def kernel(x_prompt, x_sample, cache_k, cache_v, cache_kidx, state_conv, state_lru, state_pool,
           meta_tokens, norm_g, w_in, conv_w, conv_b, lru_wa, lru_ba, lru_wx, lru_bx, lru_lambda,
           pool_w, pool_scale, w_branch_out, w_out, final_norm_g):
    if 'nc' not in _CACHE:
        _CACHE['nc'] = build_program()
    nc = _CACHE['nc']
    in_maps = _make_in_maps(x_prompt, x_sample, cache_k, cache_v, cache_kidx, state_conv, state_lru, state_pool,
                            meta_tokens, norm_g, w_in, conv_w, conv_b, lru_wa, lru_ba, lru_wx, lru_bx, lru_lambda,
                            pool_w, pool_scale, w_branch_out, w_out, final_norm_g)
    res = run_bass_kernel_spmd(nc, in_maps, core_ids=list(range(8)))
    return _assemble(res.results)


def _make_in_maps(x_prompt, x_sample, cache_k, cache_v, cache_kidx, state_conv, state_lru, state_pool,
                  meta_tokens, norm_g, w_in, conv_w, conv_b, lru_wa, lru_ba, lru_wx, lru_bx, lru_lambda,
                  pool_w, pool_scale, w_branch_out, w_out, final_norm_g):
    f = lambda a: np.ascontiguousarray(np.asarray(a, dtype=np.float32))
    x_prompt = f(x_prompt); x_sample = f(x_sample); meta = f(meta_tokens)
    ck = f(cache_k).reshape(2, 16, PAST, 512)
    cv = f(cache_v).reshape(2, 16, PAST, 512)
    cki = f(cache_kidx)
    sconv = f(state_conv); slru = f(state_lru); spool = f(state_pool)
    posp = np.zeros((128, NKB), np.float32)
    posp[:, 0] = np.arange(128)
    for j in range(1, NKB):
        posp[:, j] = 16 + 128 * (j - 1) + np.arange(128)
    poss = (PAST + np.arange(128, dtype=np.float32)).reshape(128, 1).astype(np.float32)
    shared = {
        "posp": posp, "poss": poss, "norm_g": f(norm_g), "w_in": f(w_in), "conv_w": f(conv_w), "conv_b": f(conv_b),
        "lru_wa": f(lru_wa), "lru_ba": f(lru_ba), "lru_wx": f(lru_wx), "lru_bx": f(lru_bx), "lru_lambda": f(lru_lambda),
        "pool_w": f(pool_w), "pool_scale": f(pool_scale), "w_branch_out": f(w_branch_out), "w_out": f(w_out),
        "final_norm_g": f(final_norm_g),
    }
    in_maps = []
    for c in range(8):
        b = c % 4
        ss = [2 * c, 2 * c + 1]
        m = dict(shared)
        m["xp"] = np.ascontiguousarray(np.concatenate([meta, x_prompt[b]], axis=0))
        m["xs"] = np.ascontiguousarray(x_sample[ss])
        m["ck"] = np.ascontiguousarray(ck[:, ss])
        m["cv"] = np.ascontiguousarray(cv[:, ss])
        m["cki"] = np.ascontiguousarray(cki[:, ss])
        m["sconv"] = np.ascontiguousarray(sconv[:, ss])
        m["slru"] = np.ascontiguousarray(slru[:, ss])
        m["spool"] = np.ascontiguousarray(spool[:, ss])
        in_maps.append(m)
    return in_maps


def _assemble(R):
    y_prompt = np.stack([R[b]["y_p"] for b in range(4)], axis=0)
    y_sample = np.concatenate([R[c]["y_s"] for c in range(8)], axis=0)
    k_prompt = np.stack([R[b]["k_p"] for b in range(4)], axis=1).reshape(2, 4, TP, 8, 64)
    v_prompt = np.stack([R[b]["v_p"] for b in range(4)], axis=1).reshape(2, 4, TP, 8, 64)
    ki_prompt = np.stack([R[b]["ki_p"] for b in range(4)], axis=1)
    conv_prompt = np.stack([R[b]["conv_p"] for b in range(4)], axis=1)
    lru_prompt = np.stack([R[b]["lru_p"] for b in range(4)], axis=1)
    pool_prompt = np.stack([R[b]["pool_p"] for b in range(4)], axis=1)
    k_sample = np.concatenate([R[c]["k_s"] for c in range(8)], axis=1).reshape(2, 16, TS, 8, 64)
    v_sample = np.concatenate([R[c]["v_s"] for c in range(8)], axis=1).reshape(2, 16, TS, 8, 64)
    ki_sample = np.concatenate([R[c]["ki_s"] for c in range(8)], axis=1)
    conv_sample = np.concatenate([R[c]["conv_s"] for c in range(8)], axis=1)
    lru_sample = np.concatenate([R[c]["lru_s"] for c in range(8)], axis=1)
    pool_sample = np.concatenate([R[c]["pool_s"] for c in range(8)], axis=1)
    outs = (y_prompt, y_sample, k_prompt, v_prompt, ki_prompt, conv_prompt, lru_prompt, pool_prompt,
            k_sample, v_sample, ki_sample, conv_sample, lru_sample, pool_sample)
    return tuple(np.ascontiguousarray(o, dtype=np.float32) for o in outs)
```

```python
import math
import contextlib
import numpy as np
import concourse.bass as bass
import concourse.mybir as mybir
from concourse.bass_utils import run_bass_kernel_spmd

F32 = mybir.dt.float32
BF16 = mybir.dt.bfloat16
I32 = mybir.dt.int32
ALU = mybir.AluOpType
AF = mybir.ActivationFunctionType
AX = mybir.AxisListType

D = 1024
SEQ = 4096
NMETA = 16
TP = NMETA + SEQ
NIN = 7492
PAST = 2048
TS = 64
NKEY = TP
NKB = 33
TB = 256
NIT = 14
TOPK = 256.0
C_Q, C_K, C_V, C_GA, C_QI, C_KI, C_WI = 0, 512, 1024, 1536, 2048, 2304, 2368
C_XB, C_GB, C_XC, C_GC, C_GM = 2372, 2884, 3396, 3908, 4420
EPS = 1e-6
NEG = -1.0e30
MASK_BIAS = False


class Prog:
    EPOCH = 24000

    def __init__(self, nc, es):
        self.nc = nc
        self.es = es
        self.ops = []
        self.count = {}
        self.lastw = {}
        self.readers = {}
        self.conf = {}
        self.eng = {'pe': nc.tensor, 'act': nc.scalar, 'dve': nc.vector, 'pool': nc.gpsimd, 'sp': nc.sync}

    def alias(self, names_a, names_b):
        for a in names_a:
            for b in names_b:
                self.conf.setdefault(a, set()).add(b)
                self.conf.setdefault(b, set()).add(a)

    def _names(self, r):
        c = self.conf.get(r)
        if c:
            return [r] + list(c)
        return [r]

    stopped = False

    def _rec(self, agent, queue, fn, reads, writes, is_dma):
        if self.stopped:
            return
        seq = self.count.get(agent, 0) + 1
        self.count[agent] = seq
        deps = {}

        def need(a, s):
            if a.startswith('dma:'):
                s = self.count[a] - (1 if a == agent else 0)
                if s <= 0:
                    return
            if deps.get(a, 0) < s:
                deps[a] = s
        for r0 in list(reads) + list(writes):
            for r in self._names(r0):
                lw = self.lastw.get(r)
                if lw is not None:
                    a, s = lw
                    if a == agent and agent == 'pe':
                        continue
                    need(a, s)
        for w0 in writes:
            for w in self._names(w0):
                for (a, s) in self.readers.get(w, ()):
                    if a == agent and agent == 'pe':
                        continue
                    need(a, s)
        for r in reads:
            if r.startswith('ps'):
                for (a, s) in self.readers.get(r, ()):
                    if a != agent:
                        need(a, s)
        for r in reads:
            self.readers.setdefault(r, []).append((agent, seq))
        for w in writes:
            self.lastw[w] = (agent, seq)
            self.readers[w] = []
        self.ops.append((agent, queue, fn, deps, seq, is_dma))

    def op(self, eng, fn, reads=(), writes=()):
        self._rec(eng, eng, fn, reads, writes, False)

    def dma(self, queue, key, fn, reads=(), writes=()):
        self._rec('dma:' + key, queue, fn, reads, writes, True)

    def opk(self, eng, name, reads, writes, kw):
        m = getattr(self.eng[eng], name)
        self._rec(eng, eng, (lambda: m(**kw)), reads, writes, False)

    def dmak(self, queue, key, reads, writes, kw):
        m = self.eng[queue].dma_start
        self._rec('dma:' + key, queue, (lambda: m(**kw)), reads, writes, True)

    def emit(self):
        nc = self.nc
        waited = {}
        plan = []
        sig = set()
        for (agent, queue, fn, deps, seq, is_dma) in self.ops:
            w = waited.setdefault(queue, {})
            waits = []
            for a, s in deps.items():
                if w.get(a, 0) >= s:
                    continue
                w[a] = s
                waits.append((a, s))
                sig.add((a, s))
            if not is_dma:
                pass
            plan.append(waits)
        semmap = {}
        sems = {}
        sigcount = {}

        def get_sem(agent, ep):
            k = (agent, ep)
            if k not in sems:
                sems[k] = self.es.enter_context(nc.semaphore("s_%s_%d" % (agent.replace(':', '_'), ep)))
            return sems[k]
        incinfo = []
        per_dma = self.EPOCH // 16
        for (agent, queue, fn, deps, seq, is_dma) in self.ops:
            if is_dma:
                n = sigcount.get(agent, 0) + 1
                sigcount[agent] = n
                ep = (n - 1) // per_dma
                semmap[(agent, seq)] = (agent, ep, (n - ep * per_dma) * 16)
                incinfo.append((agent, ep, 16))
            elif (agent, seq) in sig:
                n = sigcount.get(agent, 0) + 1
                sigcount[agent] = n
                ep = (n - 1) // self.EPOCH
                semmap[(agent, seq)] = (agent, ep, n - ep * self.EPOCH)
                incinfo.append((agent, ep, 1))
            else:
                incinfo.append(None)
        nw = 0
        for i, (agent, queue, fn, deps, seq, is_dma) in enumerate(self.ops):
            e = self.eng[queue]
            for (a, s) in plan[i]:
                ag, ep, val = semmap[(a, s)]
                e.wait_ge(get_sem(ag, ep), val)
                nw += 1
            ins = fn()
            if incinfo[i] is not None:
                ag, ep, inc = incinfo[i]
                ins.then_inc(get_sem(ag, ep), inc)
        for agent, n in sigcount.items():
            if agent.startswith('dma:'):
                ep = (n - 1) // per_dma
                nc.sync.wait_ge(get_sem(agent, ep), (n - ep * per_dma) * 16)
        self.stats = (len(self.ops), nw, dict(sigcount))


def build_program(debug=None):
    debug = debug or {}
    nc = bass.Bass("TRN2", target_bir_lowering=False)

    def din(name, shape, dt=F32):
        return nc.dram_tensor(name, list(shape), dt, kind="ExternalInput").ap()

    def dout(name, shape, dt=F32):
        return nc.dram_tensor(name, list(shape), dt, kind="ExternalOutput").ap()

    def dint(name, shape, dt=F32):
        return nc.dram_tensor(name, list(shape), dt, kind="Internal").ap()

    xp_in = din("xp", [TP, D])
    xs_in = din("xs", [2, TS, D])
    ck_in = din("ck", [2, 2, PAST, 512])
    cv_in = din("cv", [2, 2, PAST, 512])
    cki_in = din("cki", [2, 2, PAST, 64])
    sconv_in = din("sconv", [2, 2, 3, 512])
    slru_in = din("slru", [2, 2, 512])
    spool_in = din("spool", [2, 2, 15, 512])
    posp_in = din("posp", [128, NKB])
    poss_in = din("poss", [128, 1])
    norm_g = din("norm_g", [2, D])
    w_in = din("w_in", [2, D, NIN])
    conv_w = din("conv_w", [2, 4, 512])
    conv_b = din("conv_b", [2, 512])
    lru_wa = din("lru_wa", [2, 8, 64, 64])
    lru_ba = din("lru_ba", [2, 512])
    lru_wx = din("lru_wx", [2, 8, 64, 64])
    lru_bx = din("lru_bx", [2, 512])
    lru_lam = din("lru_lambda", [2, 512])
    pool_w = din("pool_w", [2, 4, 128, 128])
    pool_scale = din("pool_scale", [2, 512])
    w_bo = din("w_branch_out", [2, 3, 512, D])
    w_out = din("w_out", [2, D, D])
    fin_g = din("final_norm_g", [D])
    y_p = dout("y_p", [SEQ, D])
    k_p = dout("k_p", [2, TP, 512])
    v_p = dout("v_p", [2, TP, 512])
    ki_p = dout("ki_p", [2, TP, 64])
    conv_p = dout("conv_p", [2, 3, 512])
    lru_p = dout("lru_p", [2, 512])
    pool_p = dout("pool_p", [2, 15, 512])
    y_s = dout("y_s", [2, TS, D])
    k_s = dout("k_s", [2, 2, TS, 512])
    v_s = dout("v_s", [2, 2, TS, 512])
    ki_s = dout("ki_s", [2, 2, TS, 64])
    conv_s = dout("conv_s", [2, 2, 3, 512])
    lru_s = dout("lru_s", [2, 2, 512])
    pool_s = dout("pool_s", [2, 2, 15, 512])
    win_bf = dint("win_bf", [2, 15, 128, 8, 512], BF16)
    wbo_bf = dint("wbo_bf", [2, 3, 128, 4, D], BF16)
    wout_bf = dint("wout_bf", [2, 2, 128, 8, 512], BF16)
    xscr_p = dint("xscr_p", [TP, D])
    xscr_s = dint("xscr_s", [2, TS, D])

    WGROUPS = [(C_Q, C_K), (C_K, C_V), (C_V, C_GA), (C_GA, C_QI), (C_QI, C_XB), (C_XB, C_GB), (C_GB, C_XC), (C_XC, C_GC), (C_GC, C_GM)]
    for n_ in (1, 2, 0):
        for e4_ in range(2):
            WGROUPS.append((C_GM + n_ * 1024 + e4_ * 512, C_GM + n_ * 1024 + e4_ * 512 + 512))
    es = contextlib.ExitStack()
    with es:
        P = Prog(nc, es)

        def sb(name, shape, dt=F32):
            return es.enter_context(nc.sbuf_tensor(name, list(shape), dt))

        def ps(name, shape, dt=F32):
            return es.enter_context(nc.psum_tensor(name, list(shape), dt))

        def V(name, r, w, **kw):
            P.opk('dve', name, r, w, kw)

        def A(name, r, w, **kw):
            P.opk('act', name, r, w, kw)

        def G(name, r, w, **kw):
            P.opk('pool', name, r, w, kw)

        def T(name, r, w, **kw):
            P.opk('pe', name, r, w, kw)

        def DMA(q, key, r, w, **kw):
            P.dmak(q, key, r, w, kw)

        def CK(label):
            if debug.get('stop') == label:
                P.stopped = True

        kT = sb("kT", [128, 4, NKEY], BF16)
        Vt = sb("Vt", [128, NKB, 8, 65], BF16)
        kiT = sb("kiT", [128, NKEY], BF16)
        arA = sb("arA", [128, 4128], F32)
        mk = sb("mk", [128, NKEY], BF16)
        mkT = sb("mkT", [128, NKB, 128], BF16)
        NWB = 3
        wbuf = [sb("wbuf%d" % i, [128, 4096], BF16) for i in range(NWB)]
        xt = [sb("xt%d" % i, [128, D], F32) for i in range(2)]
        hn = sb("hn", [128, D], BF16)
        hnT = sb("hnT", [128, 8, TB], BF16)
        gfbc = sb("gfbc", [128, D], F32)
        qT = sb("qT", [128, 4, 2, TB], BF16)
        qiT = sb("qiT", [128, 2, TB], BF16)
        sga = sb("sga", [128, 2, 512], F32)
        yT = [sb("y%sT" % n, [128, 4, TB], BF16) for n in "abc"]
        kst = sb("kst", [128, 2, 512], F32)
        vst = sb("vst", [128, 2, 512], F32)
        kist = sb("kist", [128, 2, 64], F32)
        rt = [sb("rt%d" % i, [128, 8, 8], F32) for i in range(4)]
        qbf = sb("qbf", [128, 512], BF16)
        kbf = sb("kbf", [128, 512], BF16)
        qibf = sb("qibf", [128, 256], BF16)
        kibf = sb("kibf", [128, 2, 64], BF16)
        kif = sb("kif", [128, 64], F32)
        NRL = 4
        rl = [sb("rl%d" % i, [128, 512], F32) for i in range(NRL)]
        pex = [sb("pex%d" % i, [128, 8, 128], BF16) for i in range(2)]
        pm = [sb("pm%d" % i, [128, 8, 128], BF16) for i in range(2)]
        att = sb("att", [128, 8, 64], F32)
        yab = sb("yab", [128, 512], BF16)
        ctmp = sb("ctmp", [128, 4, 16 + TB], F32)
        cpl = sb("cpl", [128, TB], BF16)
        ctg = sb("ctg", [128, TB], F32)
        xcb = sb("xcb", [128, TB], BF16)
        ident = sb("ident", [128, 128], BF16)
        cosP = sb("cosP", [128, NKB, 8], F32)
        sinP = sb("sinP", [128, NKB, 8], F32)
        cosS = sb("cosS", [128, 1, 8], F32)
        sinS = sb("sinS", [128, 1, 8], F32)
        invf = sb("invf", [128, 8], F32)
        posP = sb("posP", [128, NKB], F32)
        posS = sb("posS", [128, 1], F32)
        hpow = sb("hpow", [128, NIT + 1], F32)
        rcnt = sb("rcnt", [128, 4, 16], F32)
        gT = sb("gT", [128, 8], F32)
        cw = sb("cw", [128, 4, 4], F32)
        cb = sb("cb", [128, 4], F32)
        hba = sb("hba", [128, 4], F32)
        hbx = sb("hbx", [128, 4], F32)
        lam = sb("lam", [128, 4], F32)
        clf = sb("clf", [128, 4], F32)
        psc = sb("psc", [128, 4], F32)
        waBD = sb("waBD", [128, 4, 128], BF16)
        wxBD = sb("wxBD", [128, 4, 128], BF16)
        pwB = sb("pwB", [128, 4, 128], BF16)
        convst = sb("convst", [128, 4, 3], F32)
        hst = sb("hst", [128, 4], F32)
        poolst = sb("poolst", [128, 4, 15], F32)
        st_ss = sb("st_ss", [128, 2], F32)
        st_rs = sb("st_rs", [128, 2], F32)
        wabs = sb("wabs", [128, 2, 4], F32)
        wsgn = sb("wsgn", [128, 2, 4], F32)
        bs = sb("bs", [128, 16], F32)
        steps = sb("steps", [128, NIT + 1], F32)
        rec = sb("rec", [128, 8], F32)
        psA = [ps("psA%d" % i, [128, 512], F32) for i in range(4)]
        psT = [ps("psT%d" % i, [128, 1024], BF16) for i in range(2)]
        psB = [ps("psB%d" % i, [128, 512], F32) for i in range(2)]

        identf = arA[:, 0:128]
        onesf = arA[:, 128:256]
        rtmp = arA[:, 256:256 + NKB * 8].rearrange("p (a b) -> p a b", b=8)
        rtmp2 = arA[:, 768:768 + NKB * 8].rearrange("p (a b) -> p a b", b=8)
        rtmpi = arA[:, 1280:1280 + NKB * 8].bitcast(I32).rearrange("p (a b) -> p a b", b=8)
        sc = arA[:, 0:NKEY]
        om = arA[:, 0:4 * TB]
        ixc = arA[:, 4 * TB:8 * TB]
        o = 8 * TB
        Bxp = []
        for i in range(2):
            Bxp.append(arA[:, o:o + 3 + TB]); o += 4 + TB
        Bxc = []
        for i in range(2):
            Bxc.append(arA[:, o:o + TB]); o += TB
        Bta = arA[:, o:o + TB]; o += TB
        Btx = arA[:, o:o + TB]; o += TB
        Ba2 = arA[:, o:o + TB]; o += TB
        assert o <= 4128, o
        o = 8 * TB
        Baj = []
        for i in range(4):
            Baj.append(arA[:, o:o + TB]); o += TB
        Bbb = arA[:, o:o + TB]; o += TB
        Bh = arA[:, o:o + TB]; o += TB
        Btg = arA[:, o:o + TB]; o += TB
        assert o <= 4128, o
        mg = arA[:, 0:8 * TB]
        mtg = [arA[:, 8 * TB + i * TB: 8 * TB + (i + 1) * TB] for i in range(2)]
        mgb = mk[:, 0:8 * TB]
        p1 = ['Bxp0a', 'Bxp0b', 'Bxp1a', 'Bxp1b', 'Bxc0', 'Bxc1', 'Bta', 'Btx', 'Ba2']
        p2 = ['Baj0', 'Baj1', 'Baj2', 'Baj3', 'Bbb', 'Bh', 'Btg']
        P.alias(['sc'], ['om', 'ixc', 'mg', 'mtg0', 'mtg1'] + p1 + p2)
        P.alias(['mg'], ['om', 'ixc'])
        P.alias(p1, p2 + ['mtg0', 'mtg1'])
        P.alias(['mtg0', 'mtg1'], p2)
        P.alias(['onesf', 'identf', 'rtmp', 'rtmp2', 'rtmpi'], ['sc', 'om', 'ixc', 'mg', 'mtg0', 'mtg1'] + p1 + p2)
        P.alias(['mk'], ['mgb', 'mkV', 'mkA'])
        P.alias(['mgb'], ['mkV', 'mkA'])

        build_program.sbuf_left = nc.sbuf_bytes_remaining
        cnt = {'wb': 0, 'pa': 0, 'pt': 0}

        def next_w(pin=False):
            while True:
                i = cnt['wb'] % NWB
                cnt['wb'] += 1
                if i != cnt.get('pin'):
                    break
            if pin:
                cnt['pin'] = i
            return wbuf[i], 'wbuf%d' % i

        def unpin_w():
            cnt['pin'] = None

        pa_ring = [(psA[0], 'psA0'), (psA[1], 'psA1'), (psA[2], 'psA2'), (psA[3], 'psA3'), (psB[0], 'psB0'), (psB[1], 'psB1')]

        def next_pa():
            i = cnt['pa'] % 6
            cnt['pa'] += 1
            return pa_ring[i]

        def next_pt():
            i = cnt['pt'] % 2
            cnt['pt'] += 1
            return psT[i], 'psT%d' % i

        def small_dma(out, in_, r=(), w=(), q='sp', key=None):
            key = key or ('m_' + (list(w) + list(r))[0])
            DMA(q, key, r, w, out=out, in_=in_, allow_slow_non_contiguous=True)

        CK('casts')
        G('memset', (), ['onesf'], ap=onesf, constant=1.0)
        G('affine_select', ['onesf'], ['identf'], out=identf, in_=onesf, pattern=[[-1, 128]],
          compare_op=ALU.is_equal, fill=0.0, base=0, channel_multiplier=1)
        V('tensor_copy', ['identf'], ['ident'], out=ident[:], in_=identf)
        G('memset', (), ['Vt'], ap=Vt[:, :, :, 64:65], constant=1.0)
        G('memset', (), ['qT'], ap=qT[:], constant=0.0)
        for j in range(8):
            G('memset', (), ['invf'], ap=invf[:, j:j + 1], constant=float(500000.0 ** (-(2.0 * j) / 16.0)))
        for k in range(NIT + 1):
            G('memset', (), ['hpow'], ap=hpow[:, k:k + 1], constant=float(0.5 ** k))
        for g in range(4):
            wdw = 2 ** (g + 1)
            for t in range(16):
                G('memset', (), ['rcnt'], ap=rcnt[:, g, t:t + 1], constant=1.0 / float(min(wdw, t + 1)))
        small_dma(posP[:], posp_in, (), ['posP'])
        small_dma(posS[:], poss_in, (), ['posS'])
        small_dma(gfbc[:], fin_g.partition_broadcast(128), (), ['gfbc'])

        def rope_tables(pos, nt, cosT, sinT, tag):
            a3 = rtmp[:, 0:nt, :]
            b3 = rtmp2[:, 0:nt, :]
            i3 = rtmpi[:, 0:nt, :]
            for shift, dst in ((0.0, sinT), (math.pi / 2.0, cosT)):
                V('tensor_tensor', ['pos' + tag, 'invf'], ['rtmp'], out=a3, in0=pos.unsqueeze(2).to_broadcast([128, nt, 8]),
                  in1=invf[:].unsqueeze(1).to_broadcast([128, nt, 8]), op=ALU.mult)
                if shift:
                    V('tensor_scalar', ['rtmp'], ['rtmp'], out=a3, in0=a3, scalar1=shift, scalar2=None, op0=ALU.add)
                V('tensor_scalar', ['rtmp'], ['rtmp2'], out=b3, in0=a3, scalar1=1.0 / (2.0 * math.pi), scalar2=None, op0=ALU.mult)
                V('tensor_copy', ['rtmp2'], ['rtmpi'], out=i3, in_=b3)
                V('tensor_copy', ['rtmpi'], ['rtmp2'], out=b3, in_=i3)
                V('scalar_tensor_tensor', ['rtmp2', 'rtmp'], ['rtmp'], out=a3, in0=b3, scalar=-2.0 * math.pi, in1=a3, op0=ALU.mult, op1=ALU.add)
                V('tensor_scalar', ['rtmp'], ['rtmp'], out=a3, in0=a3, scalar1=3.14159, scalar2=-3.14159, op0=ALU.min, op1=ALU.max)
                A('activation', ['rtmp'], ['rope' + tag], out=dst, in_=a3, func=AF.Sin)

        def cast_weights(l):
            for gi, (c0, c1) in enumerate(WGROUPS):
                DMA('pool', 'c_win%d_%d' % (l, gi), (), ['win_bf%d_%d' % (l, gi)], out=win_bf[l, gi].rearrange("k kt c -> kt k c")[:, :, 0:c1 - c0],
                    in_=w_in[l, :, c0:c1].rearrange("(kt k) c -> kt k c", k=128))
            for n in range(3):
                DMA('pool', 'c_wbo%d' % l, (), ['wbo_bf%d' % l], out=wbo_bf[l, n].rearrange("c ct e -> ct c e"),
                    in_=w_bo[l, n].rearrange("(ct c) e -> ct c e", c=128))
            for r in range(2):
                DMA('pool', 'c_wout%d' % l, (), ['wout_bf%d' % l], out=wout_bf[l, r].rearrange("e et c -> et e c"),
                    in_=w_out[l, :, r * 512:(r + 1) * 512].rearrange("(et e) c -> et e c", e=128))

        cast_weights(0)
        if debug.get('skip_prompt'):
            cast_weights(1)
        CK('consts')
        rope_tables(posP[:], NKB, cosP[:], sinP[:], 'P')
        rope_tables(posS[:], 1, cosS[:], sinS[:], 'S')
        CK('prologue')

        def fm(v):
            return v.rearrange("(j p) -> p j", p=128)

        def layer_setup(l):
            small_dma(gT[:], norm_g[l].rearrange("(j p) -> p j", p=128), (), ['gT'])
            small_dma(cb[:], fm(conv_b[l]), (), ['cb'])
            for tap in range(4):
                small_dma(cw[:, :, tap], fm(conv_w[l, tap]), (), ['cw'])
            small_dma(hba[:], fm(lru_ba[l]), (), ['hba'])
            small_dma(hbx[:], fm(lru_bx[l]), (), ['hbx'])
            small_dma(lam[:], fm(lru_lam[l]), (), ['lam'])
            small_dma(psc[:], fm(pool_scale[l]), (), ['psc'])
            V('tensor_scalar', ['hba'], ['hba'], out=hba[:], in0=hba[:], scalar1=0.5, scalar2=None, op0=ALU.mult)
            V('tensor_scalar', ['hbx'], ['hbx'], out=hbx[:], in0=hbx[:], scalar1=0.5, scalar2=None, op0=ALU.mult)
            A('activation', ['lam'], ['lam'], out=lam[:], in_=lam[:], func=AF.Exp, scale=-1.0)
            A('activation', ['lam'], ['lam'], out=lam[:], in_=lam[:], func=AF.Ln, bias=1.0, scale=1.0)
            V('tensor_scalar', ['lam'], ['clf'], out=clf[:], in0=lam[:], scalar1=-8.0, scalar2=None, op0=ALU.mult)
            G('memset', (), ['waBD'], ap=waBD[:], constant=0.0)
            G('memset', (), ['wxBD'], ap=wxBD[:], constant=0.0)
            for wsrc, wdst, wname, stg, stgn in ((lru_wa, waBD, 'waBD', rl[0], 'rl0'), (lru_wx, wxBD, 'wxBD', rl[1], 'rl1')):
                v = stg[:, 0:256].rearrange("p (j d) -> p j d", j=4)
                DMA('sp', stgn, (), [stgn], out=v, in_=wsrc[l].rearrange("(j hh) c d -> (hh c) j d", hh=2))
                V('tensor_copy', [stgn, wname], [wname], out=wdst[0:64, :, 0:64], in_=v[0:64])
                V('tensor_copy', [stgn, wname], [wname], out=wdst[64:128, :, 64:128], in_=v[64:128])
            v = rl[2][:, :].rearrange("p (g d) -> p g d", g=4)
            DMA('sp', 'rl2', (), ['rl2'], out=v, in_=pool_w[l].rearrange("g c d -> c g d"))
            V('tensor_copy', ['rl2'], ['pwB'], out=pwB[:], in_=v)

        def load_w_in(l, c0, ncol, pin=False):
            wb, wn = next_w(pin)
            v = wb[:, 0:8 * ncol].rearrange("p (a b) -> p a b", a=8)
            gi = [i for i, (a0, a1) in enumerate(WGROUPS) if a0 == c0 and c0 + ncol == a1]
            assert len(gi) == 1, (c0, ncol)
            src = win_bf[l, gi[0], :, :, 0:ncol]
            DMA('sp', wn, ['win_bf%d_%d' % (l, gi[0])], [wn], out=v, in_=src)
            return v, wn

        def load_w_bo(l, n):
            wb, wn = next_w()
            v = wb[:].rearrange("p (a b) -> p a b", a=4)
            src = wbo_bf[l, n]
            DMA('sp', wn, ['wbo_bf%d' % l], [wn], out=v, in_=src)
            return v, wn

        def load_w_out(l, half):
            wb, wn = next_w()
            v = wb[:].rearrange("p (a b) -> p a b", a=8)
            src = wout_bf[l, half]
            DMA('sp', wn, ['wout_bf%d' % l], [wn], out=v, in_=src)
            return v, wn

        def rope(src, sname, dst, dname, H, tsz, ct, st, tname):
            s3 = src.rearrange("p (h d) -> p h d", h=H)
            d3 = dst.rearrange("p (h d) -> p h d", h=H)
            cb_ = ct.unsqueeze(1).to_broadcast([tsz, H, 8])
            sb_ = st.unsqueeze(1).to_broadcast([tsz, H, 8])
            t = [rt[i][0:tsz, 0:H, :] for i in range(4)]
            CK('r0')
            A('copy', [sname], [dname + 'n'], out=d3[:, :, 16:64], in_=s3[:, :, 16:64])
            CK('r1')
            V('tensor_tensor', [sname, tname], ['rt0'], out=t[0], in0=s3[:, :, 0:8], in1=cb_, op=ALU.mult)
            CK('r2')
            V('tensor_tensor', [sname, tname], ['rt1'], out=t[1], in0=s3[:, :, 8:16], in1=sb_, op=ALU.mult)
            V('tensor_tensor', [sname, tname], ['rt2'], out=t[2], in0=s3[:, :, 8:16], in1=cb_, op=ALU.mult)
            V('tensor_tensor', [sname, tname], ['rt3'], out=t[3], in0=s3[:, :, 0:8], in1=sb_, op=ALU.mult)
            V('tensor_tensor', ['rt0', 'rt1'], [dname + 'a'], out=d3[:, :, 0:8], in0=t[0], in1=t[1], op=ALU.subtract)
            V('tensor_tensor', ['rt2', 'rt3'], [dname + 'b'], out=d3[:, :, 8:16], in0=t[2], in1=t[3], op=ALU.add)
            return [dname + 'n', dname + 'a', dname + 'b']

        def transpose_blocks(src, sres, nblk, tsz):
            pt, ptn = next_pt()
            for c in range(nblk):
                T('transpose', list(sres) + ['ident'], [ptn], out=pt[:, c * 128:c * 128 + tsz], in_=src[0:tsz, c * 128:(c + 1) * 128],
                  identity=ident[0:tsz, 0:tsz])
            return pt, ptn

        def blkview(pt, nblk, tsz):
            return pt[:, 0:nblk * 128].rearrange("p (c t) -> p c t", c=nblk)[:, :, 0:tsz]

        def process_block(S, l, blk, last_layer):
            Tn = blk['T']
            tiles = blk['tiles']
            xsrc = blk['xsrc'][l]
            xdst = blk['xdst'][l]
            cosT, sinT = S['tabs']
            tabres = S['tabres']
            past = S['past']
            stores = []

            def run_zip(ga, na, gb, nb):
                ia = ib = 0
                da = db = False
                while not (da and db):
                    if not da and (db or ia * nb <= ib * na):
                        try:
                            next(ga)
                            ia += 1
                        except StopIteration:
                            da = True
                    elif not db:
                        try:
                            next(gb)
                            ib += 1
                        except StopIteration:
                            db = True

            def drain(g):
                for _ in g:
                    pass


            def rms_rstd(i, tsz, xr):
                A('activation', [xr], ['hn', 'ss%d' % i], out=hn[0:tsz, :], in_=xt[i][0:tsz, :], func=AF.Square,
                  accum_out=st_ss[0:tsz, i:i + 1])
                V('tensor_scalar', ['ss%d' % i], ['rs%d' % i], out=st_rs[0:tsz, i:i + 1], in0=st_ss[0:tsz, i:i + 1],
                  scalar1=1.0 / D, scalar2=EPS, op0=ALU.mult, op1=ALU.add)
                A('activation', ['rs%d' % i], ['rs%d' % i], out=st_rs[0:tsz, i:i + 1], in_=st_rs[0:tsz, i:i + 1], func=AF.Sqrt)
                V('reciprocal', ['rs%d' % i], ['rs%d' % i], out=st_rs[0:tsz, i:i + 1], in_=st_rs[0:tsz, i:i + 1])

            for i, (q0, tsz, a, kb, hm, tix) in enumerate(tiles):
                xr = 'xt%d' % i
                DMA('sp', xr, [blk['xres'][l]], [xr], out=xt[i][0:tsz, :], in_=xsrc[q0:q0 + tsz, :])
                rms_rstd(i, tsz, xr)
                V('tensor_scalar', [xr, 'rs%d' % i], ['hn'], out=hn[0:tsz, :], in0=xt[i][0:tsz, :], scalar1=st_rs[0:tsz, i:i + 1],
                  scalar2=None, op0=ALU.mult)
                pt, ptn = transpose_blocks(hn, ['hn'], 8, tsz)
                for kt in range(8):
                    A('activation', [ptn, 'gT'], ['hnT'], out=hnT[:, kt, q0:q0 + tsz], in_=pt[:, kt * 128:kt * 128 + tsz],
                      func=AF.Identity, scale=gT[:, kt:kt + 1])

            def tm_group(c0, ncol, evac):
                wv, wn = load_w_in(l, c0, ncol)
                for i, (q0, tsz, a, kb, hm, tix) in enumerate(tiles):
                    pa, pan = next_pa()
                    for kt in range(8):
                        T('matmul', ['hnT', wn], [pan], out=pa[0:tsz, 0:ncol], lhsT=hnT[:, kt, q0:q0 + tsz], rhs=wv[:, kt, :],
                          start=(kt == 0), stop=(kt == 7))
                    ct_ = cosT[0:tsz, tix, :]
                    st_ = sinT[0:tsz, tix, :]
                    evac(i, q0, tsz, a, kb, pa, pan, ct_, st_)
                    yield

            def ev_q(i, q0, tsz, a, kb, pa, pan, ct_, st_):
                res = rope(pa[0:tsz, :], pan, qbf[0:tsz, :], 'qbf', 8, tsz, ct_, st_, tabres)
                CK('r3')
                pt, ptn = transpose_blocks(qbf, res, 4, tsz)
                CK('r4')
                A('copy', [ptn], ['qT'], out=qT[0:64, :, 0, q0:q0 + tsz], in_=blkview(pt, 4, tsz)[0:64])
                A('copy', [ptn], ['qT'], out=qT[64:128, :, 1, q0:q0 + tsz], in_=blkview(pt, 4, tsz)[64:128])

            def ev_k(i, q0, tsz, a, kb, pa, pan, ct_, st_):
                res = rope(pa[0:tsz, :], pan, kst[0:tsz, i, :], 'kst%d' % i, 8, tsz, ct_, st_, tabres)
                G('tensor_copy', res, ['kbf'], out=kbf[0:tsz, :], in_=kst[0:tsz, i, :])
                pt, ptn = transpose_blocks(kbf, ['kbf'], 4, tsz)
                A('copy', [ptn], ['kT'], out=kT[:, :, a:a + tsz], in_=blkview(pt, 4, tsz))
                stores.append((res, dict(out=S['k_out'][l][a - past:a - past + tsz, :], in_=kst[0:tsz, i, :])))

            def ev_v(i, q0, tsz, a, kb, pa, pan, ct_, st_):
                A('copy', [pan], ['vst%d' % i], out=vst[0:tsz, i, :], in_=pa[0:tsz, :])
                V('tensor_copy', [pan], ['Vt'], out=Vt[0:tsz, kb, :, 0:64], in_=pa[0:tsz, :].rearrange("p (h d) -> p h d", h=8))
                stores.append((['vst%d' % i], dict(out=S['v_out'][l][a - past:a - past + tsz, :], in_=vst[0:tsz, i, :])))

            def ev_ga(i, q0, tsz, a, kb, pa, pan, ct_, st_):
                A('activation', [pan], ['sga%d' % i], out=sga[0:tsz, i, :], in_=pa[0:tsz, :], func=AF.Tanh, scale=0.5)
                V('scalar_tensor_tensor', [pan, 'sga%d' % i], ['sga%d' % i], out=sga[0:tsz, i, :], in0=sga[0:tsz, i, :], scalar=1.0,
                  in1=pa[0:tsz, :], op0=ALU.add, op1=ALU.mult)

            def ev_idx(i, q0, tsz, a, kb, pa, pan, ct_, st_):
                res = rope(pa[0:tsz, 0:256], pan, qibf[0:tsz, :], 'qibf', 4, tsz, ct_, st_, tabres)
                pt, ptn = transpose_blocks(qibf, res, 2, tsz)
                A('copy', [ptn], ['qiT'], out=qiT[:, :, q0:q0 + tsz], in_=blkview(pt, 2, tsz))
                res2 = rope(pa[0:tsz, 256:320], pan, kist[0:tsz, i, :], 'kist%d' % i, 1, tsz, ct_, st_, tabres)
                G('tensor_copy', res2, ['kibf0'], out=kibf[0:tsz, 0, :], in_=kist[0:tsz, i, :])
                G('tensor_copy', res2, ['kibf1'], out=kibf[0:tsz, 1, :], in_=kist[0:tsz, i, :])
                pt2, ptn2 = next_pt()
                T('transpose', ['kibf0', 'kibf1', 'ident'], [ptn2], out=pt2[:, 0:tsz], in_=kibf[0:tsz, :, :].rearrange("p a b -> p (a b)"),
                  identity=ident[0:tsz, 0:tsz])
                A('copy', [ptn2], ['kiT'], out=kiT[:, a:a + tsz], in_=pt2[:, 0:tsz])
                A('activation', [pan], ['wabs%d' % i], out=wabs[0:tsz, i, :], in_=pa[0:tsz, 320:324], func=AF.Abs)
                A('activation', [pan], ['wsgn%d' % i], out=wsgn[0:tsz, i, :], in_=pa[0:tsz, 320:324], func=AF.Sign)
                stores.append((res2, dict(out=S['ki_out'][l][a - past:a - past + tsz, :], in_=kist[0:tsz, i, :])))

            CK('step1')

            def gen_step2():
                yield from tm_group(C_Q, 512, ev_q)
                yield from tm_group(C_K, 512, ev_k)
                yield from tm_group(C_V, 512, ev_v)
                yield from tm_group(C_GA, 512, ev_ga)
                yield from tm_group(C_QI, 324, ev_idx)

            def fm_proj(wv, wn, j):
                pa, pan = next_pa()
                for kt in range(8):
                    T('matmul', ['hnT', wn], [pan], out=pa[:, 0:Tn], lhsT=wv[:, kt, j * 128:(j + 1) * 128], rhs=hnT[:, kt, 0:Tn],
                      start=(kt == 0), stop=(kt == 7))
                return pa, pan

            def gen_B1():
              yield
              wxb, wxbn = load_w_in(l, C_XB, 512, pin=True)
              for j in range(4):
                  pa, pan = fm_proj(wxb, wxbn, j)
                  xp_, xpn = Bxp[j % 2], 'Bxp%d' % (j % 2)
                  xc_, xcn = Bxc[j % 2], 'Bxc%d' % (j % 2)
                  A('copy', [pan], [xpn + 'b'], out=xp_[:, 3:3 + Tn], in_=pa[:, 0:Tn])
                  G('tensor_copy', ['convst%d' % j], [xpn + 'a'], out=xp_[:, 0:3], in_=convst[:, j, :])
                  G('tensor_copy', [xpn + 'a', xpn + 'b'], ['convst%d' % j], out=convst[:, j, :], in_=xp_[:, Tn:Tn + 3])
                  V('tensor_scalar', [xpn + 'a', xpn + 'b', 'cw', 'cb'], [xcn], out=xc_[:, 0:Tn], in0=xp_[:, 0:Tn], scalar1=cw[:, j, 0:1],
                    scalar2=cb[:, j:j + 1], op0=ALU.mult, op1=ALU.add)
                  for tap in range(1, 4):
                      V('scalar_tensor_tensor', [xpn + 'a', xpn + 'b', 'cw', xcn], [xcn], out=xc_[:, 0:Tn], in0=xp_[:, tap:tap + Tn],
                        scalar=cw[:, j, tap:tap + 1], in1=xc_[:, 0:Tn], op0=ALU.mult, op1=ALU.add)
                  G('tensor_copy', [xcn], ['xcb'], out=xcb[:, 0:Tn], in_=xc_[:, 0:Tn])
                  yield
                  pa1, pan1 = next_pa()
                  T('matmul', ['waBD', 'xcb'], [pan1], out=pa1[:, 0:Tn], lhsT=waBD[:, j, :], rhs=xcb[:, 0:Tn], start=True, stop=True)
                  pa2, pan2 = next_pa()
                  T('matmul', ['wxBD', 'xcb'], [pan2], out=pa2[:, 0:Tn], lhsT=wxBD[:, j, :], rhs=xcb[:, 0:Tn], start=True, stop=True)
                  A('activation', [pan1, 'hba'], ['Bta'], out=Bta[:, 0:Tn], in_=pa1[:, 0:Tn], func=AF.Tanh, bias=hba[:, j:j + 1], scale=0.5)
                  A('activation', [pan2, 'hbx'], ['Btx'], out=Btx[:, 0:Tn], in_=pa2[:, 0:Tn], func=AF.Tanh, bias=hbx[:, j:j + 1], scale=0.5)
                  A('activation', ['Bta', 'clf'], ['Ba2'], out=Ba2[:, 0:Tn], in_=Bta[:, 0:Tn], func=AF.Exp, bias=clf[:, j:j + 1], scale=clf[:, j:j + 1])
                  V('tensor_scalar', ['Ba2'], ['om'], out=om[:, j * TB:j * TB + Tn], in0=Ba2[:, 0:Tn], scalar1=-1.0, scalar2=1.0,
                    op0=ALU.mult, op1=ALU.add)
                  V('scalar_tensor_tensor', ['Btx', xcn], ['ixc'], out=ixc[:, j * TB:j * TB + Tn], in0=Btx[:, 0:Tn], scalar=1.0,
                    in1=xc_[:, 0:Tn], op0=ALU.add, op1=ALU.mult)
                  yield

            run_zip(gen_step2(), 2 * len(tiles) * 5 // 2, gen_B1(), 9)
            unpin_w()
            CK('step2')

            wgb, wgbn = load_w_in(l, C_GB, 512)
            for j in range(4):
                A('activation', ['om'], ['Baj%d' % j], out=Baj[j][:, 0:Tn], in_=om[:, j * TB:j * TB + Tn], func=AF.Sqrt, bias=1.0, scale=-1.0)
            for j in range(4):
                A('activation', ['om'], ['om'], out=om[:, j * TB:j * TB + Tn], in_=om[:, j * TB:j * TB + Tn], func=AF.Sqrt)
            for j in range(4):
                V('scalar_tensor_tensor', ['ixc', 'om'], ['Bbb'], out=Bbb[:, 0:Tn], in0=ixc[:, j * TB:j * TB + Tn], scalar=0.5,
                  in1=om[:, j * TB:j * TB + Tn], op0=ALU.mult, op1=ALU.mult)
                V('tensor_tensor_scan', ['Baj%d' % j, 'Bbb', 'hst%d' % j], ['Bh'], out=Bh[:, 0:Tn], data0=Baj[j][:, 0:Tn], data1=Bbb[:, 0:Tn],
                  initial=hst[:, j:j + 1], op0=ALU.mult, op1=ALU.add)
                G('tensor_copy', ['Bh'], ['hst%d' % j], out=hst[:, j:j + 1], in_=Bh[:, Tn - 1:Tn])
                pa, pan = fm_proj(wgb, wgbn, j)
                A('activation', [pan], ['Btg'], out=Btg[:, 0:Tn], in_=pa[:, 0:Tn], func=AF.Tanh, scale=0.5)
                V('scalar_tensor_tensor', [pan, 'Btg'], ['Btg'], out=Btg[:, 0:Tn], in0=Btg[:, 0:Tn], scalar=1.0, in1=pa[:, 0:Tn],
                  op0=ALU.add, op1=ALU.mult)
                V('scalar_tensor_tensor', ['Btg', 'Bh'], ['ybT'], out=yT[1][:, j, 0:Tn], in0=Btg[:, 0:Tn], scalar=0.5, in1=Bh[:, 0:Tn],
                  op0=ALU.mult, op1=ALU.mult)

            for r, kw in stores:
                DMA('sp', 's_' + r[0], r, (), **kw)

            def gen_branchC():
                wxc, wxcn = load_w_in(l, C_XC, 512)
                wgc, wgcn = load_w_in(l, C_GC, 512)
                L = 15 + Tn
                for g in range(4):
                    wdw = 2 ** (g + 1)
                    pa, pan = fm_proj(wxc, wxcn, g)
                    G('tensor_copy', ['poolst%d' % g], ['ct0a'], out=ctmp[:, 0, 0:15], in_=poolst[:, g, :])
                    A('copy', [pan], ['ct0b'], out=ctmp[:, 0, 15:L], in_=pa[:, 0:Tn])
                    G('tensor_copy', ['ct0a', 'ct0b'], ['poolst%d' % g], out=poolst[:, g, :], in_=ctmp[:, 0, Tn:L])
                    prev, prevn = 0, ['ct0a', 'ct0b']
                    m = 1
                    slot = 1
                    while m < wdw:
                        G('tensor_tensor', prevn, ['ct%d' % slot], out=ctmp[:, slot, 2 * m - 1:L], in0=ctmp[:, prev, 2 * m - 1:L],
                          in1=ctmp[:, prev, m - 1:L - m], op=ALU.add)
                        prev, prevn = slot, ['ct%d' % slot]
                        slot = 1 + (slot % 3)
                        m *= 2
                    yield
                    if blk['first']:
                        V('tensor_tensor', prevn + ['rcnt'], ['ctg'], out=ctg[:, 0:Tn], in0=ctmp[:, prev, 15:L], in1=rcnt[:, g, 0:Tn], op=ALU.mult)
                        V('tensor_tensor', ['ctg', 'ct0b'], ['cpl'], out=cpl[:, 0:Tn], in0=ctg[:, 0:Tn], in1=ctmp[:, 0, 15:L], op=ALU.subtract)
                    else:
                        V('scalar_tensor_tensor', prevn + ['ct0b'], ['cpl'], out=cpl[:, 0:Tn], in0=ctmp[:, prev, 15:L], scalar=1.0 / wdw,
                          in1=ctmp[:, 0, 15:L], op0=ALU.mult, op1=ALU.subtract)
                    pa1, pan1 = next_pa()
                    T('matmul', ['pwB', 'cpl'], [pan1], out=pa1[:, 0:Tn], lhsT=pwB[:, g, :], rhs=cpl[:, 0:Tn], start=True, stop=True)
                    pa2, pan2 = fm_proj(wgc, wgcn, g)
                    A('activation', [pan2], ['ctg'], out=ctg[:, 0:Tn], in_=pa2[:, 0:Tn], func=AF.Tanh, scale=0.5)
                    V('scalar_tensor_tensor', [pan2, 'ctg'], ['ctg'], out=ctg[:, 0:Tn], in0=ctg[:, 0:Tn], scalar=1.0, in1=pa2[:, 0:Tn],
                      op0=ALU.add, op1=ALU.mult)
                    V('tensor_scalar', ['ctg', 'psc'], ['ctg'], out=ctg[:, 0:Tn], in0=ctg[:, 0:Tn], scalar1=psc[:, g:g + 1], scalar2=0.5,
                      op0=ALU.mult, op1=ALU.mult)
                    V('tensor_tensor', ['ctg', pan1], ['ycT'], out=yT[2][:, g, 0:Tn], in0=ctg[:, 0:Tn], in1=pa1[:, 0:Tn], op=ALU.mult)
                    yield

            def rec_scores(i):
                q0, tsz, a, kb, hm, tix = tiles[i]
                Sv = a + tsz
                for c0 in range(0, Sv, 512):
                    n = min(512, Sv - c0)
                    for h in range(4):
                        hs = slice((h % 2) * 64, (h % 2) * 64 + 64)
                        pa, pan = next_pa()
                        T('matmul', ['qiT', 'kiT'], [pan], out=pa[0:tsz, 0:n], lhsT=qiT[hs, h // 2, q0:q0 + tsz], rhs=kiT[hs, c0:c0 + n],
                          start=True, stop=True)
                        r_, rn = rl[h % NRL], 'rl%d' % (h % NRL)
                        A('activation', [pan, 'wabs%d' % i], [rn], out=r_[0:tsz, 0:n], in_=pa[0:tsz, 0:n], func=AF.Relu, scale=wabs[0:tsz, i, h:h + 1])
                        if h == 0:
                            V('tensor_scalar', [rn, 'wsgn%d' % i], ['sc'], out=sc[0:tsz, c0:c0 + n], in0=r_[0:tsz, 0:n], scalar1=wsgn[0:tsz, i, 0:1],
                              scalar2=None, op0=ALU.mult)
                        else:
                            V('scalar_tensor_tensor', [rn, 'wsgn%d' % i, 'sc'], ['sc'], out=sc[0:tsz, c0:c0 + n], in0=r_[0:tsz, 0:n],
                              scalar=wsgn[0:tsz, i, h:h + 1], in1=sc[0:tsz, c0:c0 + n], op0=ALU.mult, op1=ALU.add)

            def gen_bisect(i):
                q0, tsz, a, kb, hm, tix = tiles[i]
                Sv = a + tsz
                V('tensor_reduce', ['sc'], ['bs0'], out=bs[0:tsz, 0:1], in_=sc[0:tsz, 0:Sv], axis=AX.X, op=ALU.min)
                V('tensor_reduce', ['sc'], ['bs1'], out=bs[0:tsz, 1:2], in_=sc[0:tsz, 0:Sv], axis=AX.X, op=ALU.max)
                if hm:
                    G('memset', ['bs0', 'bs1'], ['sc'], ap=sc[0:64, Sv - 64:Sv], constant=NEG)
                V('tensor_scalar', ['bs0', 'bs1'], ['bs2'], out=bs[0:tsz, 2:3], in0=bs[0:tsz, 1:2], scalar1=bs[0:tsz, 0:1], scalar2=1.003,
                  op0=ALU.subtract, op1=ALU.mult)
                V('scalar_tensor_tensor', ['bs0', 'bs2'], ['bs6'], out=bs[0:tsz, 6:7], in0=bs[0:tsz, 2:3], scalar=-0.001, in1=bs[0:tsz, 0:1],
                  op0=ALU.mult, op1=ALU.add)
                V('tensor_scalar', ['bs2', 'hpow'], ['steps'], out=steps[0:tsz, :], in0=hpow[0:tsz, :], scalar1=bs[0:tsz, 2:3], scalar2=None, op0=ALU.mult)
                V('tensor_tensor', ['bs6', 'steps'], ['mid'], out=bs[0:tsz, 3:4], in0=bs[0:tsz, 6:7], in1=steps[0:tsz, 1:2], op=ALU.add)
                yield
                cA = int(Sv * (0.46 if i == 0 else 0.52)) // 2 * 2 if Sv >= 512 else Sv
                nA = Sv - cA
                thr = TOPK - nA / 2.0
                cc = thr - 0.25 - cA
                for k in range(1, NIT + 1):
                    kk = k + 1 if k < NIT else k
                    V('tensor_scalar', ['sc', 'mid'], ['mkV', 'cnt'], out=mk[0:tsz, 0:cA], in0=sc[0:tsz, 0:cA], scalar1=bs[0:tsz, 3:4], scalar2=cc,
                      op0=ALU.is_lt, op1=ALU.add, accum_out=bs[0:tsz, 4:5])
                    if nA:
                        A('activation', ['sc', 'mid'], ['mkA', 'cntA'], out=mk[0:tsz, cA:Sv], in_=sc[0:tsz, cA:Sv], func=AF.Sign, bias=bs[0:tsz, 3:4],
                          scale=-1.0, accum_out=bs[0:tsz, 7:8])
                    V('tensor_scalar', ['mid', 'steps'], ['bu'], out=bs[0:tsz, 9:10], in0=bs[0:tsz, 3:4], scalar1=steps[0:tsz, kk:kk + 1], scalar2=None,
                      op0=ALU.subtract)
                    if nA:
                        V('scalar_tensor_tensor', ['cnt', 'cntA'], ['bd'], out=bs[0:tsz, 5:6], in0=bs[0:tsz, 7:8], scalar=-0.5, in1=bs[0:tsz, 4:5],
                          op0=ALU.mult, op1=ALU.is_ge)
                    else:
                        V('tensor_scalar', ['cnt'], ['bd'], out=bs[0:tsz, 5:6], in0=bs[0:tsz, 4:5], scalar1=0.0, scalar2=None, op0=ALU.is_le)
                    V('scalar_tensor_tensor', ['bd', 'steps', 'bu'], ['mid'], out=bs[0:tsz, 3:4], in0=bs[0:tsz, 5:6], scalar=steps[0:tsz, k:k + 1],
                      in1=bs[0:tsz, 9:10], op0=ALU.mult, op1=ALU.add)
                    yield
                V('tensor_scalar', ['sc', 'mid'], ['mk'], out=mk[0:tsz, 0:Sv], in0=sc[0:tsz, 0:Sv], scalar1=bs[0:tsz, 3:4], scalar2=None, op0=ALU.is_ge)
                yield

            def mask_bias(i):
                return len(tiles) == 2 and i == 0

            def rec_masktrans(i):
                q0, tsz, a, kb, hm, tix = tiles[i]
                Sv = a + tsz
                kbs = [b for b in S['kblocks'] if b[0] + b[1] <= Sv]
                for g0 in range(0, len(kbs), 8):
                    grp = kbs[g0:g0 + 8]
                    pt, ptn = next_pt()
                    for s_, (c0, kr, kbi) in enumerate(grp):
                        T('transpose', ['mk', 'ident'], [ptn], out=pt[0:kr, s_ * 128:s_ * 128 + tsz], in_=mk[0:tsz, c0:c0 + kr], identity=ident[0:tsz, 0:tsz])
                    if mask_bias(i):
                        kwm = dict(func=AF.Identity, scale=30000.0, bias=-30000.0)
                    else:
                        kwm = dict(func=AF.Identity)
                    if all(x[1] == 128 for x in grp):
                        A('activation', [ptn], ['mkT'], out=mkT[:, g0:g0 + len(grp), 0:tsz], in_=blkview(pt, len(grp), tsz), **kwm)
                    else:
                        for s_, (c0, kr, kbi) in enumerate(grp):
                            A('activation', [ptn], ['mkT'], out=mkT[0:kr, g0 + s_, 0:tsz], in_=pt[0:kr, s_ * 128:s_ * 128 + tsz], **kwm)

            def gen_attention(i, npool=4):
                q0, tsz, a, kb, hm, tix = tiles[i]
                Sv = a + tsz
                kbs = [b for b in S['kblocks'] if b[0] + b[1] <= Sv]
                nkb = len(kbs)
                MB = mask_bias(i)

                lbufs = [([psA[0], psA[1]], ['psA0', 'psA1']), ([psA[2], psA[3]], ['psA2', 'psA3'])]
                NLB = len(lbufs)

                def logits(bi):
                    c0, kr, kbi = kbs[bi]
                    pl, pln = lbufs[bi % NLB]
                    for hp in range(4):
                        hb = hp // 2
                        o = (hp % 2) * 256
                        if tsz == 128:
                            T('matmul', ['kT', 'qT'], [pln[hb]], out=pl[hb][0:kr, o:o + 256],
                              lhsT=kT[:, hp, c0:c0 + kr], rhs=qT[:, hp, :, q0:q0 + tsz], start=(hp % 2 == 0), stop=(not MB and hp % 2 == 1))
                        else:
                            for w in range(2):
                                T('matmul', ['kT', 'qT'], [pln[hb]], out=pl[hb][0:kr, o + w * 128:o + w * 128 + tsz],
                                  lhsT=kT[:, hp, c0:c0 + kr], rhs=qT[:, hp, w, q0:q0 + tsz], start=(hp % 2 == 0 and w == 0),
                                  stop=(not MB and hp % 2 == 1 and w == 1))
                    for hb in (range(2) if MB else ()):
                        if tsz == 128:
                            T('matmul', ['mkT', 'ident'], [pln[hb]], out=pl[hb][0:kr, :], lhsT=ident[0:kr, 0:kr],
                              rhs=mkT[0:kr, bi, 0:tsz].unsqueeze(1).to_broadcast([kr, 4, tsz]), start=False, stop=True)
                        else:
                            for s4 in range(4):
                                T('matmul', ['mkT', 'ident'], [pln[hb]], out=pl[hb][0:kr, s4 * 128:s4 * 128 + tsz], lhsT=ident[0:kr, 0:kr],
                                  rhs=mkT[0:kr, bi, 0:tsz], start=False, stop=(s4 == 3))

                def pv(bi):
                    c0, kr, kbi = kbs[bi]
                    px, pxn = pex[bi % 2], 'pex%d' % (bi % 2)
                    pdeps = [pxn + '0', pxn + '1']
                    if not MB:
                        px = pm[bi % 2]
                        pdeps = ['pm%da' % (bi % 2), 'pm%db' % (bi % 2)]
                    for h in range(8):
                        T('matmul', pdeps + ['Vt'], ['psB%d' % (h // 4)], out=psB[h // 4][0:tsz, (h % 4) * 65:(h % 4) * 65 + 65], lhsT=px[0:kr, h, 0:tsz],
                          rhs=Vt[0:kr, kbi, h, :], start=(bi == 0 and h % 4 == 0), stop=(bi == nkb - 1 and h % 4 == 3))
                for b0 in range(min(NLB, nkb)):
                    logits(b0)
                for bi, (c0, kr, kbi) in enumerate(kbs):
                    par = bi % 2
                    pl, pln = lbufs[bi % NLB]
                    px, pxn = pex[par], 'pex%d' % par
                    for hh in range(2):
                        A('activation', [pln[hh]], [pxn + str(hh)], out=px[0:kr, hh * 4:hh * 4 + 4, 0:tsz],
                          in_=pl[hh][0:kr, :].rearrange("p (h t) -> p h t", h=4)[:, :, 0:tsz], func=AF.Exp, scale=0.125)
                    if not MB:
                        pm_ = pm[par]
                        mbp = mkT[0:kr, bi, 0:tsz].unsqueeze(1).to_broadcast([kr, npool, tsz])
                        mbv = mkT[0:kr, bi, 0:tsz].unsqueeze(1).to_broadcast([kr, 8 - npool, tsz])
                        G('tensor_tensor', [pxn + '0', pxn + '1', 'mkT'], ['pm%da' % par], out=pm_[0:kr, 0:npool, 0:tsz], in0=px[0:kr, 0:npool, 0:tsz], in1=mbp, op=ALU.mult)
                        V('tensor_tensor', [pxn + '0', pxn + '1', 'mkT'], ['pm%db' % par], out=pm_[0:kr, npool:8, 0:tsz], in0=px[0:kr, npool:8, 0:tsz], in1=mbv, op=ALU.mult)
                    if bi >= 1:
                        pv(bi - 1)
                    if bi + NLB < nkb:
                        logits(bi + NLB)
                    yield
                pv(nkb - 1)

            def rec_attn_final(i):
                q0, tsz, a, kb, hm, tix = tiles[i]
                for hh in range(2):
                    pv = psB[hh][0:tsz, 0:260].rearrange("p (h e) -> p h e", h=4)
                    V('tensor_scalar', ['psB%d' % hh], ['rec%d' % hh], out=rec[0:tsz, hh * 4:hh * 4 + 4], in0=pv[:, :, 64], scalar1=2.0, scalar2=None,
                      op0=ALU.mult)
                    V('reciprocal', ['rec%d' % hh], ['rec%d' % hh], out=rec[0:tsz, hh * 4:hh * 4 + 4], in_=rec[0:tsz, hh * 4:hh * 4 + 4])
                    V('tensor_tensor', ['psB%d' % hh, 'rec%d' % hh], ['att%d' % hh], out=att[0:tsz, hh * 4:hh * 4 + 4, :], in0=pv[:, :, 0:64],
                      in1=rec[0:tsz, hh * 4:hh * 4 + 4].unsqueeze(2).to_broadcast([tsz, 4, 64]), op=ALU.mult)
                G('tensor_tensor', ['att0', 'att1', 'sga%d' % i], ['yab'], out=yab[0:tsz, :], in0=att[0:tsz, :, :].rearrange("p h d -> p (h d)"),
                  in1=sga[0:tsz, i, :], op=ALU.mult)
                pt, ptn = transpose_blocks(yab, ['yab'], 4, tsz)
                A('copy', [ptn], ['yaT'], out=yT[0][:, :, q0:q0 + tsz], in_=blkview(pt, 4, tsz))

            ynames = ['yaT', 'ybT', 'ycT']

            def gen_merge(ns, first_n, last_n, banks=None):
                bc = [0]

                def bank():
                    if banks is None:
                        return next_pa()
                    b = banks[bc[0] % len(banks)]
                    bc[0] += 1
                    return b
                for n in ns:
                    wb_, wbn = load_w_bo(l, n)
                    for e4 in range(2):
                        wg_, wgn = load_w_in(l, C_GM + n * 1024 + e4 * 512, 512)
                        for ee in range(4):
                            e = e4 * 4 + ee
                            pa, pan = bank()
                            for kt in range(8):
                                T('matmul', ['hnT', wgn], [pan], out=pa[:, 0:Tn], lhsT=wg_[:, kt, ee * 128:(ee + 1) * 128], rhs=hnT[:, kt, 0:Tn],
                                  start=(kt == 0), stop=(kt == 7))
                            pb, pbn = bank()
                            for ct in range(4):
                                T('matmul', [wbn, ynames[n]], [pbn], out=pb[:, 0:Tn], lhsT=wb_[:, ct, e * 128:(e + 1) * 128], rhs=yT[n][:, ct, 0:Tn],
                                  start=(ct == 0), stop=(ct == 3))
                            tg_, tgn = mtg[e % 2], 'mtg%d' % (e % 2)
                            A('activation', [pan], [tgn], out=tg_[:, 0:Tn], in_=pa[:, 0:Tn], func=AF.Tanh, scale=0.5)
                            mge = mg[:, e * TB:e * TB + Tn]
                            if n == first_n:
                                V('scalar_tensor_tensor', [tgn, pbn], ['mg'], out=mge, in0=tg_[:, 0:Tn], scalar=1.0, in1=pb[:, 0:Tn], op0=ALU.add, op1=ALU.mult)
                            else:
                                V('scalar_tensor_tensor', [tgn, pbn], [tgn], out=tg_[:, 0:Tn], in0=tg_[:, 0:Tn], scalar=1.0, in1=pb[:, 0:Tn],
                                  op0=ALU.add, op1=ALU.mult)
                                if n != last_n:
                                    G('tensor_tensor', [tgn, 'mg'], ['mg'], out=mge, in0=mge, in1=tg_[:, 0:Tn], op=ALU.add)
                                else:
                                    G('tensor_tensor', [tgn, 'mg'], ['mgb'], out=mgb[:, e * TB:e * TB + Tn], in0=mge, in1=tg_[:, 0:Tn], op=ALU.add)
                            yield

            psM = [(psT[0][:, :].bitcast(F32), 'psT0'), (psT[1][:, :].bitcast(F32), 'psT1')]

            def nkb_of(i):
                q0, tsz, a, kb, hm, tix = tiles[i]
                return len([b for b in S['kblocks'] if b[0] + b[1] <= a + tsz])

            rec_scores(0)
            run_zip(gen_branchC(), 8, gen_bisect(0), NIT + 2)
            rec_masktrans(0)
            CK('step3')
            if len(tiles) == 2:
                rec_scores(1)
                run_zip(gen_attention(0, 5), nkb_of(0), gen_bisect(1), NIT + 2)
                rec_attn_final(0)
                rec_masktrans(1)
                run_zip(gen_attention(1, 3), nkb_of(1), gen_merge([1, 2], 1, 0, psM), 16)
                rec_attn_final(1)
            else:
                run_zip(gen_attention(0, 3), nkb_of(0), gen_merge([1, 2], 1, 0, psM), 16)
                rec_attn_final(0)
            CK('step4')
            drain(gen_merge([0], 1, 0))
            for half in range(2):
                wo_, won = load_w_out(l, half)
                for i, (q0, tsz, a, kb, hm, tix) in enumerate(tiles):
                    pa, pan = next_pa()
                    for et in range(8):
                        T('matmul', ['mgb', won], [pan], out=pa[0:tsz, :], lhsT=mgb[:, et * TB + q0:et * TB + q0 + tsz], rhs=wo_[:, et, :],
                          start=(et == 0), stop=(et == 7))
                    xs_ = xt[i][0:tsz, half * 512:(half + 1) * 512]
                    V('scalar_tensor_tensor', [pan, 'xt%d' % i], ['xt%d' % i], out=xs_, in0=pa[0:tsz, :], scalar=0.5, in1=xs_, op0=ALU.mult, op1=ALU.add)
            for i, (q0, tsz, a, kb, hm, tix) in enumerate(tiles):
                xr = 'xt%d' % i
                if not last_layer:
                    DMA('act', 's_' + xr, [xr], [blk['xres'][l + 1]], out=xdst[q0:q0 + tsz, :], in_=xt[i][0:tsz, :])
                elif xdst is not None:
                    rms_rstd(i, tsz, xr)
                    V('scalar_tensor_tensor', [xr, 'rs%d' % i, 'gfbc'], [xr], out=xt[i][0:tsz, :], in0=xt[i][0:tsz, :], scalar=st_rs[0:tsz, i:i + 1],
                      in1=gfbc[0:tsz, :], op0=ALU.mult, op1=ALU.mult)
                    DMA('act', 's_' + xr, [xr], (), out=xdst[q0:q0 + tsz, :], in_=xt[i][0:tsz, :])

        def make_prompt():
            S = {'past': 0, 'tabs': (cosP, sinP), 'tabres': 'ropeP',
                 'k_out': [k_p[0], k_p[1]], 'v_out': [v_p[0], v_p[1]], 'ki_out': [ki_p[0], ki_p[1]]}
            S['kblocks'] = [(0, 16, 0)] + [(16 + 128 * j, 128, 1 + j) for j in range(32)]
            blocks = [{'T': 16, 'tiles': [(0, 16, 0, 0, False, 0)], 'first': True,
                       'xsrc': [xp_in[0:16, :], xscr_p[0:16, :]], 'xdst': [xscr_p[0:16, :], None], 'xres': ['xin', 'xscrp_m', 'none']}]
            nb = debug.get('nblk', SEQ // TB)
            for b in range(nb):
                a0 = 16 + b * TB
                tiles = []
                for i in range(TB // 128):
                    a = a0 + i * 128
                    kb = 1 + (a - 16) // 128
                    tiles.append((i * 128, 128, a, kb, True, kb))
                blocks.append({'T': TB, 'tiles': tiles, 'first': False,
                               'xsrc': [xp_in[a0:a0 + TB, :], xscr_p[a0:a0 + TB, :]],
                               'xdst': [xscr_p[a0:a0 + TB, :], y_p[a0 - 16:a0 - 16 + TB, :]], 'xres': ['xin', 'xscrp_%d' % b, 'none']})
            S['blocks'] = blocks
            return S

        def make_sample(si):
            S = {'past': PAST, 'tabs': (cosS, sinS), 'tabres': 'ropeS',
                 'k_out': [k_s[0, si], k_s[1, si]], 'v_out': [v_s[0, si], v_s[1, si]], 'ki_out': [ki_s[0, si], ki_s[1, si]]}
            S['kblocks'] = [(128 * j, 128, j) for j in range(16)] + [(PAST, 64, 16)]
            S['blocks'] = [{'T': TS, 'tiles': [(0, TS, PAST, 16, False, 0)], 'first': False,
                            'xsrc': [xs_in[si], xscr_s[si]], 'xdst': [xscr_s[si], y_s[si]], 'xres': ['xin', 'xscrs_%d' % si, 'none']}]
            return S

        st_names = ['convst%d' % j for j in range(4)] + ['hst%d' % j for j in range(4)] + ['poolst%d' % j for j in range(4)]

        def zero_states():
            G('memset', (), ['convst%d' % j for j in range(4)], ap=convst[:], constant=0.0)
            G('memset', (), ['hst%d' % j for j in range(4)], ap=hst[:], constant=0.0)
            G('memset', (), ['poolst%d' % j for j in range(4)], ap=poolst[:], constant=0.0)

        def load_states(l, si):
            for j in range(4):
                small_dma(convst[:, j, :], sconv_in[l, si][:, j * 128:(j + 1) * 128].rearrange("t p -> p t"), (), ['convst%d' % j])
                small_dma(poolst[:, j, :], spool_in[l, si][:, j * 128:(j + 1) * 128].rearrange("t p -> p t"), (), ['poolst%d' % j])
            small_dma(hst[:], fm(slru_in[l, si]), (), ['hst%d' % j for j in range(4)])

        def store_states(conv_o, lru_o, pool_o):
            for j in range(4):
                small_dma(conv_o[:, j * 128:(j + 1) * 128].rearrange("t p -> p t"), convst[:, j, :], ['convst%d' % j], ())
                small_dma(pool_o[:, j * 128:(j + 1) * 128].rearrange("t p -> p t"), poolst[:, j, :], ['poolst%d' % j], ())
            small_dma(fm(lru_o), hst[:], ['hst%d' % j for j in range(4)], ())

        def load_cache(l, si):
            for j in range(16):
                sl = slice(j * 128, (j + 1) * 128)
                rk, rkn = rl[j % 2], 'rl%d' % (j % 2)
                rv, rvn = rl[2 + j % 2], 'rl%d' % (2 + j % 2)
                kb_, kbn = (kbf, 'kbf') if j % 2 == 0 else (qbf, 'qbf')
                DMA('sp', rkn, (), [rkn], out=rk[:], in_=ck_in[l, si, sl, :])
                DMA('sp', rvn, (), [rvn], out=rv[:], in_=cv_in[l, si, sl, :])
                DMA('sp', 'kif', (), ['kif'], out=kif[:], in_=cki_in[l, si, sl, :])
                V('tensor_copy', [rkn], [kbn], out=kb_[:], in_=rk[:])
                pt, ptn = transpose_blocks(kb_, [kbn], 4, 128)
                A('copy', [ptn], ['kT'], out=kT[:, :, sl], in_=blkview(pt, 4, 128))
                V('tensor_copy', [rvn], ['Vt'], out=Vt[:, j, :, 0:64], in_=rv[:].rearrange("p (h d) -> p h d", h=8))
                A('copy', ['kif'], ['kibf0'], out=kibf[:, 0, :], in_=kif[:])
                V('tensor_copy', ['kif'], ['kibf1'], out=kibf[:, 1, :], in_=kif[:])
                pt2, ptn2 = next_pt()
                T('transpose', ['kibf0', 'kibf1', 'ident'], [ptn2], out=pt2[:, 0:128], in_=kibf[:, :, :].rearrange("p a b -> p (a b)"), identity=ident[:, :])
                A('copy', [ptn2], ['kiT'], out=kiT[:, sl], in_=pt2[:, 0:128])

        nlayers = debug.get('nlayers', 2)
        if not debug.get('skip_prompt'):
            Sp = make_prompt()
            for l in range(nlayers):
                layer_setup(l)
                zero_states()
                for bix, blk in enumerate(Sp['blocks']):
                    process_block(Sp, l, blk, l == 1)
                    if l == 0 and bix == min(2, len(Sp['blocks']) - 1):
                        cast_weights(1)
                store_states(conv_p[l], lru_p[l], pool_p[l])
        if not debug.get('skip_sample'):
            for si in range(debug.get('nsample', 2)):
                Ss = make_sample(si)
                for l in range(nlayers):
                    layer_setup(l)
                    CK('setup')
                    load_states(l, si)
                    CK('states')
                    load_cache(l, si)
                    CK('cache')
                    for blk in Ss['blocks']:
                        process_block(Ss, l, blk, l == 1)
                    store_states(conv_s[l, si], lru_s[l, si], pool_s[l, si])
        P.emit()
        build_program.stats = P.stats
    return nc


_CACHE = {}


def kernel(x_prompt, x_sample, cache_k, cache_v, cache_kidx, state_conv, state_lru, state_pool,
           meta_tokens, norm_g, w_in, conv_w, conv_b, lru_wa, lru_ba, lru_wx, lru_bx, lru_lambda,
           pool_w, pool_scale, w_branch_out, w_out, final_norm_g):
    if 'nc' not in _CACHE:
        _CACHE['nc'] = build_program()
    nc = _CACHE['nc']
    in_maps = _make_in_maps(x_prompt, x_sample, cache_k, cache_v, cache_kidx, state_conv, state_lru, state_pool,
                            meta_tokens, norm_g, w_in, conv_w, conv_b, lru_wa, lru_ba, lru_wx, lru_bx, lru_lambda,
                            pool_w, pool_scale, w_branch_out, w_out, final_norm_g)
    res = run_bass_kernel_spmd(nc, in_maps, core_ids=list(range(8)))
    return _assemble(res.results)


def _make_in_maps(x_prompt, x_sample, cache_k, cache_v, cache_kidx, state_conv, state_lru, state_pool,
                  meta_tokens, norm_g, w_in, conv_w, conv_b, lru_wa, lru_ba, lru_wx, lru_bx, lru_lambda,
                  pool_w, pool_scale, w_branch_out, w_out, final_norm_g):
    f = lambda a: np.ascontiguousarray(np.asarray(a, dtype=np.float32))
    x_prompt = f(x_prompt); x_sample = f(x_sample); meta = f(meta_tokens)
    ck = f(cache_k).reshape(2, 16, PAST, 512)
    cv = f(cache_v).reshape(2, 16, PAST, 512)
    cki = f(cache_kidx)
    sconv = f(state_conv); slru = f(state_lru); spool = f(state_pool)
    posp = np.zeros((128, NKB), np.float32)
    posp[:, 0] = np.arange(128)
    for j in range(1, NKB):
        posp[:, j] = 16 + 128 * (j - 1) + np.arange(128)
    poss = (PAST + np.arange(128, dtype=np.float32)).reshape(128, 1).astype(np.float32)
    shared = {
        "posp": posp, "poss": poss, "norm_g": f(norm_g), "w_in": f(w_in), "conv_w": f(conv_w), "conv_b": f(conv_b),
        "lru_wa": f(lru_wa), "lru_ba": f(lru_ba), "lru_wx": f(lru_wx), "lru_bx": f(lru_bx), "lru_lambda": f(lru_lambda),
        "pool_w": f(pool_w), "pool_scale": f(pool_scale), "w_branch_out": f(w_branch_out), "w_out": f(w_out),
        "final_norm_g": f(final_norm_g),
    }
    in_maps = []
    for c in range(8):
        b = c % 4
        ss = [2 * c, 2 * c + 1]
        m = dict(shared)
        m["xp"] = np.ascontiguousarray(np.concatenate([meta, x_prompt[b]], axis=0))
        m["xs"] = np.ascontiguousarray(x_sample[ss])
        m["ck"] = np.ascontiguousarray(ck[:, ss])
        m["cv"] = np.ascontiguousarray(cv[:, ss])
        m["cki"] = np.ascontiguousarray(cki[:, ss])
        m["sconv"] = np.ascontiguousarray(sconv[:, ss])
        m["slru"] = np.ascontiguousarray(slru[:, ss])
        m["spool"] = np.ascontiguousarray(spool[:, ss])
        in_maps.append(m)
    return in_maps


def _assemble(R):
    y_prompt = np.stack([R[b]["y_p"] for b in range(4)], axis=0)
    y_sample = np.concatenate([R[c]["y_s"] for c in range(8)], axis=0)
    k_prompt = np.stack([R[b]["k_p"] for b in range(4)], axis=1).reshape(2, 4, TP, 8, 64)
    v_prompt = np.stack([R[b]["v_p"] for b in range(4)], axis=1).reshape(2, 4, TP, 8, 64)
    ki_prompt = np.stack([R[b]["ki_p"] for b in range(4)], axis=1)
    conv_prompt = np.stack([R[b]["conv_p"] for b in range(4)], axis=1)
    lru_prompt = np.stack([R[b]["lru_p"] for b in range(4)], axis=1)
    pool_prompt = np.stack([R[b]["pool_p"] for b in range(4)], axis=1)
    k_sample = np.concatenate([R[c]["k_s"] for c in range(8)], axis=1).reshape(2, 16, TS, 8, 64)
    v_sample = np.concatenate([R[c]["v_s"] for c in range(8)], axis=1).reshape(2, 16, TS, 8, 64)
    ki_sample = np.concatenate([R[c]["ki_s"] for c in range(8)], axis=1)
    conv_sample = np.concatenate([R[c]["conv_s"] for c in range(8)], axis=1)
    lru_sample = np.concatenate([R[c]["lru_s"] for c in range(8)], axis=1)
    pool_sample = np.concatenate([R[c]["pool_s"] for c in range(8)], axis=1)
    outs = (y_prompt, y_sample, k_prompt, v_prompt, ki_prompt, conv_prompt, lru_prompt, pool_prompt,
            k_sample, v_sample, ki_sample, conv_sample, lru_sample, pool_sample)
    return tuple(np.ascontiguousarray(o, dtype=np.float32) for o in outs)
```

```python
import math
import contextlib
import numpy as np
import concourse.bass as bass
import concourse.mybir as mybir
from concourse.bass_utils import run_bass_kernel_spmd

F32 = mybir.dt.float32
BF16 = mybir.dt.bfloat16
I32 = mybir.dt.int32
ALU = mybir.AluOpType
AF = mybir.ActivationFunctionType
AX = mybir.AxisListType

D = 1024
SEQ = 4096
NMETA = 16
TP = NMETA + SEQ
NIN = 7492
PAST = 2048
TS = 64
NKEY = TP
NKB = 33
TB = 256
NIT = 14
TOPK = 256.0
C_Q, C_K, C_V, C_GA, C_QI, C_KI, C_WI = 0, 512, 1024, 1536, 2048, 2304, 2368
C_XB, C_GB, C_XC, C_GC, C_GM = 2372, 2884, 3396, 3908, 4420
EPS = 1e-6
NEG = -1.0e30
MASK_BIAS = False


class Prog:
    EPOCH = 24000

    def __init__(self, nc, es):
        self.nc = nc
        self.es = es
        self.ops = []
        self.count = {}
        self.lastw = {}
        self.readers = {}
        self.conf = {}
        self.eng = {'pe': nc.tensor, 'act': nc.scalar, 'dve': nc.vector, 'pool': nc.gpsimd, 'sp': nc.sync}

    def alias(self, names_a, names_b):
        for a in names_a:
            for b in names_b:
                self.conf.setdefault(a, set()).add(b)
                self.conf.setdefault(b, set()).add(a)

    def _names(self, r):
        c = self.conf.get(r)
        if c:
            return [r] + list(c)
        return [r]

    stopped = False

    def _rec(self, agent, queue, fn, reads, writes, is_dma):
        if self.stopped:
            return
        seq = self.count.get(agent, 0) + 1
        self.count[agent] = seq
        deps = {}

        def need(a, s):
            if a.startswith('dma:'):
                s = self.count[a] - (1 if a == agent else 0)
                if s <= 0:
                    return
            if deps.get(a, 0) < s:
                deps[a] = s
        for r0 in list(reads) + list(writes):
            for r in self._names(r0):
                lw = self.lastw.get(r)
                if lw is not None:
                    a, s = lw
                    if a == agent and agent == 'pe':
                        continue
                    need(a, s)
        for w0 in writes:
            for w in self._names(w0):
                for (a, s) in self.readers.get(w, ()):
                    if a == agent and agent == 'pe':
                        continue
                    need(a, s)
        for r in reads:
            if r.startswith('ps'):
                for (a, s) in self.readers.get(r, ()):
                    if a != agent:
                        need(a, s)
        for r in reads:
            self.readers.setdefault(r, []).append((agent, seq))
        for w in writes:
            self.lastw[w] = (agent, seq)
            self.readers[w] = []
        self.ops.append((agent, queue, fn, deps, seq, is_dma))

    def op(self, eng, fn, reads=(), writes=()):
        self._rec(eng, eng, fn, reads, writes, False)

    def dma(self, queue, key, fn, reads=(), writes=()):
        self._rec('dma:' + key, queue, fn, reads, writes, True)

    def opk(self, eng, name, reads, writes, kw):
        m = getattr(self.eng[eng], name)
        self._rec(eng, eng, (lambda: m(**kw)), reads, writes, False)

    def dmak(self, queue, key, reads, writes, kw):
        m = self.eng[queue].dma_start
        self._rec('dma:' + key, queue, (lambda: m(**kw)), reads, writes, True)

    def emit(self):
        nc = self.nc
        waited = {}
        plan = []
        sig = set()
        for (agent, queue, fn, deps, seq, is_dma) in self.ops:
            w = waited.setdefault(queue, {})
            waits = []
            for a, s in deps.items():
                if w.get(a, 0) >= s:
                    continue
                w[a] = s
                waits.append((a, s))
                sig.add((a, s))
            if not is_dma:
                pass
            plan.append(waits)
        semmap = {}
        sems = {}
        sigcount = {}

        def get_sem(agent, ep):
            k = (agent, ep)
            if k not in sems:
                sems[k] = self.es.enter_context(nc.semaphore("s_%s_%d" % (agent.replace(':', '_'), ep)))
            return sems[k]
        incinfo = []
        per_dma = self.EPOCH // 16
        for (agent, queue, fn, deps, seq, is_dma) in self.ops:
            if is_dma:
                n = sigcount.get(agent, 0) + 1
                sigcount[agent] = n
                ep = (n - 1) // per_dma
                semmap[(agent, seq)] = (agent, ep, (n - ep * per_dma) * 16)
                incinfo.append((agent, ep, 16))
            elif (agent, seq) in sig:
                n = sigcount.get(agent, 0) + 1
                sigcount[agent] = n
                ep = (n - 1) // self.EPOCH
                semmap[(agent, seq)] = (agent, ep, n - ep * self.EPOCH)
                incinfo.append((agent, ep, 1))
            else:
                incinfo.append(None)
        nw = 0
        for i, (agent, queue, fn, deps, seq, is_dma) in enumerate(self.ops):
            e = self.eng[queue]
            for (a, s) in plan[i]:
                ag, ep, val = semmap[(a, s)]
                e.wait_ge(get_sem(ag, ep), val)
                nw += 1
            ins = fn()
            if incinfo[i] is not None:
                ag, ep, inc = incinfo[i]
                ins.then_inc(get_sem(ag, ep), inc)
        for agent, n in sigcount.items():
            if agent.startswith('dma:'):
                ep = (n - 1) // per_dma
                nc.sync.wait_ge(get_sem(agent, ep), (n - ep * per_dma) * 16)
        self.stats = (len(self.ops), nw, dict(sigcount))


def build_program(debug=None):
    debug = debug or {}
    nc = bass.Bass("TRN2", target_bir_lowering=False)

    def din(name, shape, dt=F32):
        return nc.dram_tensor(name, list(shape), dt, kind="ExternalInput").ap()

    def dout(name, shape, dt=F32):
        return nc.dram_tensor(name, list(shape), dt, kind="ExternalOutput").ap()

    def dint(name, shape, dt=F32):
        return nc.dram_tensor(name, list(shape), dt, kind="Internal").ap()

    xp_in = din("xp", [TP, D])
    xs_in = din("xs", [2, TS, D])
    ck_in = din("ck", [2, 2, PAST, 512])
    cv_in = din("cv", [2, 2, PAST, 512])
    cki_in = din("cki", [2, 2, PAST, 64])
    sconv_in = din("sconv", [2, 2, 3, 512])
    slru_in = din("slru", [2, 2, 512])
    spool_in = din("spool", [2, 2, 15, 512])
    posp_in = din("posp", [128, NKB])
    poss_in = din("poss", [128, 1])
    norm_g = din("norm_g", [2, D])
    w_in = din("w_in", [2, D, NIN])
    conv_w = din("conv_w", [2, 4, 512])
    conv_b = din("conv_b", [2, 512])
    lru_wa = din("lru_wa", [2, 8, 64, 64])
    lru_ba = din("lru_ba", [2, 512])
    lru_wx = din("lru_wx", [2, 8, 64, 64])
    lru_bx = din("lru_bx", [2, 512])
    lru_lam = din("lru_lambda", [2, 512])
    pool_w = din("pool_w", [2, 4, 128, 128])
    pool_scale = din("pool_scale", [2, 512])
    w_bo = din("w_branch_out", [2, 3, 512, D])
    w_out = din("w_out", [2, D, D])
    fin_g = din("final_norm_g", [D])
    y_p = dout("y_p", [SEQ, D])
    k_p = dout("k_p", [2, TP, 512])
    v_p = dout("v_p", [2, TP, 512])
    ki_p = dout("ki_p", [2, TP, 64])
    conv_p = dout("conv_p", [2, 3, 512])
    lru_p = dout("lru_p", [2, 512])
    pool_p = dout("pool_p", [2, 15, 512])
    y_s = dout("y_s", [2, TS, D])
    k_s = dout("k_s", [2, 2, TS, 512])
    v_s = dout("v_s", [2, 2, TS, 512])
    ki_s = dout("ki_s", [2, 2, TS, 64])
    conv_s = dout("conv_s", [2, 2, 3, 512])
    lru_s = dout("lru_s", [2, 2, 512])
    pool_s = dout("pool_s", [2, 2, 15, 512])
    win_bf = dint("win_bf", [2, 15, 128, 8, 512], BF16)
    wbo_bf = dint("wbo_bf", [2, 3, 128, 4, D], BF16)
    wout_bf = dint("wout_bf", [2, 2, 128, 8, 512], BF16)
    xscr_p = dint("xscr_p", [TP, D])
    xscr_s = dint("xscr_s", [2, TS, D])

    WGROUPS = [(C_Q, C_K), (C_K, C_V), (C_V, C_GA), (C_GA, C_QI), (C_QI, C_XB), (C_XB, C_GB), (C_GB, C_XC), (C_XC, C_GC), (C_GC, C_GM)]
    for n_ in (1, 2, 0):
        for e4_ in range(2):
            WGROUPS.append((C_GM + n_ * 1024 + e4_ * 512, C_GM + n_ * 1024 + e4_ * 512 + 512))
    es = contextlib.ExitStack()
    with es:
        P = Prog(nc, es)

        def sb(name, shape, dt=F32):
            return es.enter_context(nc.sbuf_tensor(name, list(shape), dt))

        def ps(name, shape, dt=F32):
            return es.enter_context(nc.psum_tensor(name, list(shape), dt))

        def V(name, r, w, **kw):
            P.opk('dve', name, r, w, kw)

        def A(name, r, w, **kw):
            P.opk('act', name, r, w, kw)

        def G(name, r, w, **kw):
            P.opk('pool', name, r, w, kw)

        def T(name, r, w, **kw):
            P.opk('pe', name, r, w, kw)

        def DMA(q, key, r, w, **kw):
            P.dmak(q, key, r, w, kw)

        def CK(label):
            if debug.get('stop') == label:
                P.stopped = True

        kT = sb("kT", [128, 4, NKEY], BF16)
        Vt = sb("Vt", [128, NKB, 8, 65], BF16)
        kiT = sb("kiT", [128, NKEY], BF16)
        arA = sb("arA", [128, 4128], F32)
        mk = sb("mk", [128, NKEY], BF16)
        mkT = sb("mkT", [128, NKB, 128], BF16)
        NWB = 3
        wbuf = [sb("wbuf%d" % i, [128, 4096], BF16) for i in range(NWB)]
        xt = [sb("xt%d" % i, [128, D], F32) for i in range(2)]
        hn = sb("hn", [128, D], BF16)
        hnT = sb("hnT", [128, 8, TB], BF16)
        gfbc = sb("gfbc", [128, D], F32)
        qT = sb("qT", [128, 4, 2, TB], BF16)
        qiT = sb("qiT", [128, 2, TB], BF16)
        sga = sb("sga", [128, 2, 512], F32)
        yT = [sb("y%sT" % n, [128, 4, TB], BF16) for n in "abc"]
        kst = sb("kst", [128, 2, 512], F32)
        vst = sb("vst", [128, 2, 512], F32)
        kist = sb("kist", [128, 2, 64], F32)
        rt = [sb("rt%d" % i, [128, 8, 8], F32) for i in range(4)]
        qbf = sb("qbf", [128, 512], BF16)
        kbf = sb("kbf", [128, 512], BF16)
        qibf = sb("qibf", [128, 256], BF16)
        kibf = sb("kibf", [128, 2, 64], BF16)
        kif = sb("kif", [128, 64], F32)
        NRL = 4
        rl = [sb("rl%d" % i, [128, 512], F32) for i in range(NRL)]
        pex = [sb("pex%d" % i, [128, 8, 128], BF16) for i in range(2)]
        pm = [sb("pm%d" % i, [128, 8, 128], BF16) for i in range(2)]
        att = sb("att", [128, 8, 64], F32)
        yab = sb("yab", [128, 512], BF16)
        ctmp = sb("ctmp", [128, 4, 16 + TB], F32)
        cpl = sb("cpl", [128, TB], BF16)
        ctg = sb("ctg", [128, TB], F32)
        xcb = sb("xcb", [128, TB], BF16)
        ident = sb("ident", [128, 128], BF16)
        cosP = sb("cosP", [128, NKB, 8], F32)
        sinP = sb("sinP", [128, NKB, 8], F32)
        cosS = sb("cosS", [128, 1, 8], F32)
        sinS = sb("sinS", [128, 1, 8], F32)
        invf = sb("invf", [128, 8], F32)
        posP = sb("posP", [128, NKB], F32)
        posS = sb("posS", [128, 1], F32)
        hpow = sb("hpow", [128, NIT + 1], F32)
        rcnt = sb("rcnt", [128, 4, 16], F32)
        gT = sb("gT", [128, 8], F32)
        cw = sb("cw", [128, 4, 4], F32)
        cb = sb("cb", [128, 4], F32)
        hba = sb("hba", [128, 4], F32)
        hbx = sb("hbx", [128, 4], F32)
        lam = sb("lam", [128, 4], F32)
        clf = sb("clf", [128, 4], F32)
        psc = sb("psc", [128, 4], F32)
        waBD = sb("waBD", [128, 4, 128], BF16)
        wxBD = sb("wxBD", [128, 4, 128], BF16)
        pwB = sb("pwB", [128, 4, 128], BF16)
        convst = sb("convst", [128, 4, 3], F32)
        hst = sb("hst", [128, 4], F32)
        poolst = sb("poolst", [128, 4, 15], F32)
        st_ss = sb("st_ss", [128, 2], F32)
        st_rs = sb("st_rs", [128, 2], F32)
        wabs = sb("wabs", [128, 2, 4], F32)
        wsgn = sb("wsgn", [128, 2, 4], F32)
        bs = sb("bs", [128, 16], F32)
        steps = sb("steps", [128, NIT + 1], F32)
        rec = sb("rec", [128, 8], F32)
        psA = [ps("psA%d" % i, [128, 512], F32) for i in range(4)]
        psT = [ps("psT%d" % i, [128, 1024], BF16) for i in range(2)]
        psB = [ps("psB%d" % i, [128, 512], F32) for i in range(2)]

        identf = arA[:, 0:128]
        onesf = arA[:, 128:256]
        rtmp = arA[:, 256:256 + NKB * 8].rearrange("p (a b) -> p a b", b=8)
        rtmp2 = arA[:, 768:768 + NKB * 8].rearrange("p (a b) -> p a b", b=8)
        rtmpi = arA[:, 1280:1280 + NKB * 8].bitcast(I32).rearrange("p (a b) -> p a b", b=8)
        sc = arA[:, 0:NKEY]
        om = arA[:, 0:4 * TB]
        ixc = arA[:, 4 * TB:8 * TB]
        o = 8 * TB
        Bxp = []
        for i in range(2):
            Bxp.append(arA[:, o:o + 3 + TB]); o += 4 + TB
        Bxc = []
        for i in range(2):
            Bxc.append(arA[:, o:o + TB]); o += TB
        Bta = arA[:, o:o + TB]; o += TB
        Btx = arA[:, o:o + TB]; o += TB
        Ba2 = arA[:, o:o + TB]; o += TB
        assert o <= 4128, o
        o = 8 * TB
        Baj = []
        for i in range(4):
            Baj.append(arA[:, o:o + TB]); o += TB
        Bbb = arA[:, o:o + TB]; o += TB
        Bh = arA[:, o:o + TB]; o += TB
        Btg = arA[:, o:o + TB]; o += TB
        assert o <= 4128, o
        mg = arA[:, 0:8 * TB]
        mtg = [arA[:, 8 * TB + i * TB: 8 * TB + (i + 1) * TB] for i in range(2)]
        mgb = mk[:, 0:8 * TB]
        p1 = ['Bxp0a', 'Bxp0b', 'Bxp1a', 'Bxp1b', 'Bxc0', 'Bxc1', 'Bta', 'Btx', 'Ba2']
        p2 = ['Baj0', 'Baj1', 'Baj2', 'Baj3', 'Bbb', 'Bh', 'Btg']
        P.alias(['sc'], ['om', 'ixc', 'mg', 'mtg0', 'mtg1'] + p1 + p2)
        P.alias(['mg'], ['om', 'ixc'])
        P.alias(p1, p2 + ['mtg0', 'mtg1'])
        P.alias(['mtg0', 'mtg1'], p2)
        P.alias(['onesf', 'identf', 'rtmp', 'rtmp2', 'rtmpi'], ['sc', 'om', 'ixc', 'mg', 'mtg0', 'mtg1'] + p1 + p2)
        P.alias(['mk'], ['mgb', 'mkV', 'mkA'])
        P.alias(['mgb'], ['mkV', 'mkA'])

        build_program.sbuf_left = nc.sbuf_bytes_remaining
        cnt = {'wb': 0, 'pa': 0, 'pt': 0}

        def next_w(pin=False):
            while True:
                i = cnt['wb'] % NWB
                cnt['wb'] += 1
                if i != cnt.get('pin'):
                    break
            if pin:
                cnt['pin'] = i
            return wbuf[i], 'wbuf%d' % i

        def unpin_w():
            cnt['pin'] = None

        pa_ring = [(psA[0], 'psA0'), (psA[1], 'psA1'), (psA[2], 'psA2'), (psA[3], 'psA3'), (psB[0], 'psB0'), (psB[1], 'psB1')]

        def next_pa():
            i = cnt['pa'] % 6
            cnt['pa'] += 1
            return pa_ring[i]

        def next_pt():
            i = cnt['pt'] % 2
            cnt['pt'] += 1
            return psT[i], 'psT%d' % i

        def small_dma(out, in_, r=(), w=(), q='sp', key=None):
            key = key or ('m_' + (list(w) + list(r))[0])
            DMA(q, key, r, w, out=out, in_=in_, allow_slow_non_contiguous=True)

        CK('casts')
        G('memset', (), ['onesf'], ap=onesf, constant=1.0)
        G('affine_select', ['onesf'], ['identf'], out=identf, in_=onesf, pattern=[[-1, 128]],
          compare_op=ALU.is_equal, fill=0.0, base=0, channel_multiplier=1)
        V('tensor_copy', ['identf'], ['ident'], out=ident[:], in_=identf)
        G('memset', (), ['Vt'], ap=Vt[:, :, :, 64:65], constant=1.0)
        G('memset', (), ['qT'], ap=qT[:], constant=0.0)
        for j in range(8):
            G('memset', (), ['invf'], ap=invf[:, j:j + 1], constant=float(500000.0 ** (-(2.0 * j) / 16.0)))
        for k in range(NIT + 1):
            G('memset', (), ['hpow'], ap=hpow[:, k:k + 1], constant=float(0.5 ** k))
        for g in range(4):
            wdw = 2 ** (g + 1)
            for t in range(16):
                G('memset', (), ['rcnt'], ap=rcnt[:, g, t:t + 1], constant=1.0 / float(min(wdw, t + 1)))
        small_dma(posP[:], posp_in, (), ['posP'])
        small_dma(posS[:], poss_in, (), ['posS'])
        small_dma(gfbc[:], fin_g.partition_broadcast(128), (), ['gfbc'])

        def rope_tables(pos, nt, cosT, sinT, tag):
            a3 = rtmp[:, 0:nt, :]
            b3 = rtmp2[:, 0:nt, :]
            i3 = rtmpi[:, 0:nt, :]
            for shift, dst in ((0.0, sinT), (math.pi / 2.0, cosT)):
                V('tensor_tensor', ['pos' + tag, 'invf'], ['rtmp'], out=a3, in0=pos.unsqueeze(2).to_broadcast([128, nt, 8]),
                  in1=invf[:].unsqueeze(1).to_broadcast([128, nt, 8]), op=ALU.mult)
                if shift:
                    V('tensor_scalar', ['rtmp'], ['rtmp'], out=a3, in0=a3, scalar1=shift, scalar2=None, op0=ALU.add)
                V('tensor_scalar', ['rtmp'], ['rtmp2'], out=b3, in0=a3, scalar1=1.0 / (2.0 * math.pi), scalar2=None, op0=ALU.mult)
                V('tensor_copy', ['rtmp2'], ['rtmpi'], out=i3, in_=b3)
                V('tensor_copy', ['rtmpi'], ['rtmp2'], out=b3, in_=i3)
                V('scalar_tensor_tensor', ['rtmp2', 'rtmp'], ['rtmp'], out=a3, in0=b3, scalar=-2.0 * math.pi, in1=a3, op0=ALU.mult, op1=ALU.add)
                V('tensor_scalar', ['rtmp'], ['rtmp'], out=a3, in0=a3, scalar1=3.14159, scalar2=-3.14159, op0=ALU.min, op1=ALU.max)
                A('activation', ['rtmp'], ['rope' + tag], out=dst, in_=a3, func=AF.Sin)

        def cast_weights(l):
            for gi, (c0, c1) in enumerate(WGROUPS):
                DMA('pool', 'c_win%d_%d' % (l, gi), (), ['win_bf%d_%d' % (l, gi)], out=win_bf[l, gi].rearrange("k kt c -> kt k c")[:, :, 0:c1 - c0],
                    in_=w_in[l, :, c0:c1].rearrange("(kt k) c -> kt k c", k=128))
            for n in range(3):
                DMA('pool', 'c_wbo%d' % l, (), ['wbo_bf%d' % l], out=wbo_bf[l, n].rearrange("c ct e -> ct c e"),
                    in_=w_bo[l, n].rearrange("(ct c) e -> ct c e", c=128))
            for r in range(2):
                DMA('pool', 'c_wout%d' % l, (), ['wout_bf%d' % l], out=wout_bf[l, r].rearrange("e et c -> et e c"),
                    in_=w_out[l, :, r * 512:(r + 1) * 512].rearrange("(et e) c -> et e c", e=128))

        cast_weights(0)
        if debug.get('skip_prompt'):
            cast_weights(1)
        CK('consts')
        rope_tables(posP[:], NKB, cosP[:], sinP[:], 'P')
        rope_tables(posS[:], 1, cosS[:], sinS[:], 'S')
        CK('prologue')

        def fm(v):
            return v.rearrange("(j p) -> p j", p=128)

        def layer_setup(l):
            small_dma(gT[:], norm_g[l].rearrange("(j p) -> p j", p=128), (), ['gT'])
            small_dma(cb[:], fm(conv_b[l]), (), ['cb'])
            for tap in range(4):
                small_dma(cw[:, :, tap], fm(conv_w[l, tap]), (), ['cw'])
            small_dma(hba[:], fm(lru_ba[l]), (), ['hba'])
            small_dma(hbx[:], fm(lru_bx[l]), (), ['hbx'])
            small_dma(lam[:], fm(lru_lam[l]), (), ['lam'])
            small_dma(psc[:], fm(pool_scale[l]), (), ['psc'])
            V('tensor_scalar', ['hba'], ['hba'], out=hba[:], in0=hba[:], scalar1=0.5, scalar2=None, op0=ALU.mult)
            V('tensor_scalar', ['hbx'], ['hbx'], out=hbx[:], in0=hbx[:], scalar1=0.5, scalar2=None, op0=ALU.mult)
            A('activation', ['lam'], ['lam'], out=lam[:], in_=lam[:], func=AF.Exp, scale=-1.0)
            A('activation', ['lam'], ['lam'], out=lam[:], in_=lam[:], func=AF.Ln, bias=1.0, scale=1.0)
            V('tensor_scalar', ['lam'], ['clf'], out=clf[:], in0=lam[:], scalar1=-8.0, scalar2=None, op0=ALU.mult)
            G('memset', (), ['waBD'], ap=waBD[:], constant=0.0)
            G('memset', (), ['wxBD'], ap=wxBD[:], constant=0.0)
            for wsrc, wdst, wname, stg, stgn in ((lru_wa, waBD, 'waBD', rl[0], 'rl0'), (lru_wx, wxBD, 'wxBD', rl[1], 'rl1')):
                v = stg[:, 0:256].rearrange("p (j d) -> p j d", j=4)
                DMA('sp', stgn, (), [stgn], out=v, in_=wsrc[l].rearrange("(j hh) c d -> (hh c) j d", hh=2))
                V('tensor_copy', [stgn, wname], [wname], out=wdst[0:64, :, 0:64], in_=v[0:64])
                V('tensor_copy', [stgn, wname], [wname], out=wdst[64:128, :, 64:128], in_=v[64:128])
            v = rl[2][:, :].rearrange("p (g d) -> p g d", g=4)
            DMA('sp', 'rl2', (), ['rl2'], out=v, in_=pool_w[l].rearrange("g c d -> c g d"))
            V('tensor_copy', ['rl2'], ['pwB'], out=pwB[:], in_=v)

        def load_w_in(l, c0, ncol, pin=False):
            wb, wn = next_w(pin)
            v = wb[:, 0:8 * ncol].rearrange("p (a b) -> p a b", a=8)
            gi = [i for i, (a0, a1) in enumerate(WGROUPS) if a0 == c0 and c0 + ncol == a1]
            assert len(gi) == 1, (c0, ncol)
            src = win_bf[l, gi[0], :, :, 0:ncol]
            DMA('sp', wn, ['win_bf%d_%d' % (l, gi[0])], [wn], out=v, in_=src)
            return v, wn

        def load_w_bo(l, n):
            wb, wn = next_w()
            v = wb[:].rearrange("p (a b) -> p a b", a=4)
            src = wbo_bf[l, n]
            DMA('sp', wn, ['wbo_bf%d' % l], [wn], out=v, in_=src)
            return v, wn

        def load_w_out(l, half):
            wb, wn = next_w()
            v = wb[:].rearrange("p (a b) -> p a b", a=8)
            src = wout_bf[l, half]
            DMA('sp', wn, ['wout_bf%d' % l], [wn], out=v, in_=src)
            return v, wn

        def rope(src, sname, dst, dname, H, tsz, ct, st, tname):
            s3 = src.rearrange("p (h d) -> p h d", h=H)
            d3 = dst.rearrange("p (h d) -> p h d", h=H)
            cb_ = ct.unsqueeze(1).to_broadcast([tsz, H, 8])
            sb_ = st.unsqueeze(1).to_broadcast([tsz, H, 8])
            t = [rt[i][0:tsz, 0:H, :] for i in range(4)]
            CK('r0')
            A('copy', [sname], [dname + 'n'], out=d3[:, :, 16:64], in_=s3[:, :, 16:64])
            CK('r1')
            V('tensor_tensor', [sname, tname], ['rt0'], out=t[0], in0=s3[:, :, 0:8], in1=cb_, op=ALU.mult)
            CK('r2')
            V('tensor_tensor', [sname, tname], ['rt1'], out=t[1], in0=s3[:, :, 8:16], in1=sb_, op=ALU.mult)
            V('tensor_tensor', [sname, tname], ['rt2'], out=t[2], in0=s3[:, :, 8:16], in1=cb_, op=ALU.mult)
            V('tensor_tensor', [sname, tname], ['rt3'], out=t[3], in0=s3[:, :, 0:8], in1=sb_, op=ALU.mult)
            V('tensor_tensor', ['rt0', 'rt1'], [dname + 'a'], out=d3[:, :, 0:8], in0=t[0], in1=t[1], op=ALU.subtract)
            V('tensor_tensor', ['rt2', 'rt3'], [dname + 'b'], out=d3[:, :, 8:16], in0=t[2], in1=t[3], op=ALU.add)
            return [dname + 'n', dname + 'a', dname + 'b']

        def transpose_blocks(src, sres, nblk, tsz):
            pt, ptn = next_pt()
            for c in range(nblk):
                T('transpose', list(sres) + ['ident'], [ptn], out=pt[:, c * 128:c * 128 + tsz], in_=src[0:tsz, c * 128:(c + 1) * 128],
                  identity=ident[0:tsz, 0:tsz])
            return pt, ptn

        def blkview(pt, nblk, tsz):
            return pt[:, 0:nblk * 128].rearrange("p (c t) -> p c t", c=nblk)[:, :, 0:tsz]

        def process_block(S, l, blk, last_layer):
            Tn = blk['T']
            tiles = blk['tiles']
            xsrc = blk['xsrc'][l]
            xdst = blk['xdst'][l]
            cosT, sinT = S['tabs']
            tabres = S['tabres']
            past = S['past']
            stores = []

            def run_zip(ga, na, gb, nb):
                ia = ib = 0
                da = db = False
                while not (da and db):
                    if not da and (db or ia * nb <= ib * na):
                        try:
                            next(ga)
                            ia += 1
                        except StopIteration:
                            da = True
                    elif not db:
                        try:
                            next(gb)
                            ib += 1
                        except StopIteration:
                            db = True

            def drain(g):
                for _ in g:
                    pass


            def rms_rstd(i, tsz, xr):
                A('activation', [xr], ['hn', 'ss%d' % i], out=hn[0:tsz, :], in_=xt[i][0:tsz, :], func=AF.Square,
                  accum_out=st_ss[0:tsz, i:i + 1])
                V('tensor_scalar', ['ss%d' % i], ['rs%d' % i], out=st_rs[0:tsz, i:i + 1], in0=st_ss[0:tsz, i:i + 1],
                  scalar1=1.0 / D, scalar2=EPS, op0=ALU.mult, op1=ALU.add)
                A('activation', ['rs%d' % i], ['rs%d' % i], out=st_rs[0:tsz, i:i + 1], in_=st_rs[0:tsz, i:i + 1], func=AF.Sqrt)
                V('reciprocal', ['rs%d' % i], ['rs%d' % i], out=st_rs[0:tsz, i:i + 1], in_=st_rs[0:tsz, i:i + 1])

            for i, (q0, tsz, a, kb, hm, tix) in enumerate(tiles):
                xr = 'xt%d' % i
                DMA('sp', xr, [blk['xres'][l]], [xr], out=xt[i][0:tsz, :], in_=xsrc[q0:q0 + tsz, :])
                rms_rstd(i, tsz, xr)
                V('tensor_scalar', [xr, 'rs%d' % i], ['hn'], out=hn[0:tsz, :], in0=xt[i][0:tsz, :], scalar1=st_rs[0:tsz, i:i + 1],
                  scalar2=None, op0=ALU.mult)
                pt, ptn = transpose_blocks(hn, ['hn'], 8, tsz)
                for kt in range(8):
                    A('activation', [ptn, 'gT'], ['hnT'], out=hnT[:, kt, q0:q0 + tsz], in_=pt[:, kt * 128:kt * 128 + tsz],
                      func=AF.Identity, scale=gT[:, kt:kt + 1])

            def tm_group(c0, ncol, evac):
                wv, wn = load_w_in(l, c0, ncol)
                for i, (q0, tsz, a, kb, hm, tix) in enumerate(tiles):
                    pa, pan = next_pa()
                    for kt in range(8):
                        T('matmul', ['hnT', wn], [pan], out=pa[0:tsz, 0:ncol], lhsT=hnT[:, kt, q0:q0 + tsz], rhs=wv[:, kt, :],
                          start=(kt == 0), stop=(kt == 7))
                    ct_ = cosT[0:tsz, tix, :]
                    st_ = sinT[0:tsz, tix, :]
                    evac(i, q0, tsz, a, kb, pa, pan, ct_, st_)
                    yield

            def ev_q(i, q0, tsz, a, kb, pa, pan, ct_, st_):
                res = rope(pa[0:tsz, :], pan, qbf[0:tsz, :], 'qbf', 8, tsz, ct_, st_, tabres)
                CK('r3')
                pt, ptn = transpose_blocks(qbf, res, 4, tsz)
                CK('r4')
                A('copy', [ptn], ['qT'], out=qT[0:64, :, 0, q0:q0 + tsz], in_=blkview(pt, 4, tsz)[0:64])
                A('copy', [ptn], ['qT'], out=qT[64:128, :, 1, q0:q0 + tsz], in_=blkview(pt, 4, tsz)[64:128])

            def ev_k(i, q0, tsz, a, kb, pa, pan, ct_, st_):
                res = rope(pa[0:tsz, :], pan, kst[0:tsz, i, :], 'kst%d' % i, 8, tsz, ct_, st_, tabres)
                G('tensor_copy', res, ['kbf'], out=kbf[0:tsz, :], in_=kst[0:tsz, i, :])
                pt, ptn = transpose_blocks(kbf, ['kbf'], 4, tsz)
                A('copy', [ptn], ['kT'], out=kT[:, :, a:a + tsz], in_=blkview(pt, 4, tsz))
                stores.append((res, dict(out=S['k_out'][l][a - past:a - past + tsz, :], in_=kst[0:tsz, i, :])))

            def ev_v(i, q0, tsz, a, kb, pa, pan, ct_, st_):
                A('copy', [pan], ['vst%d' % i], out=vst[0:tsz, i, :], in_=pa[0:tsz, :])
                V('tensor_copy', [pan], ['Vt'], out=Vt[0:tsz, kb, :, 0:64], in_=pa[0:tsz, :].rearrange("p (h d) -> p h d", h=8))
                stores.append((['vst%d' % i], dict(out=S['v_out'][l][a - past:a - past + tsz, :], in_=vst[0:tsz, i, :])))

            def ev_ga(i, q0, tsz, a, kb, pa, pan, ct_, st_):
                A('activation', [pan], ['sga%d' % i], out=sga[0:tsz, i, :], in_=pa[0:tsz, :], func=AF.Tanh, scale=0.5)
                V('scalar_tensor_tensor', [pan, 'sga%d' % i], ['sga%d' % i], out=sga[0:tsz, i, :], in0=sga[0:tsz, i, :], scalar=1.0,
                  in1=pa[0:tsz, :], op0=ALU.add, op1=ALU.mult)

            def ev_idx(i, q0, tsz, a, kb, pa, pan, ct_, st_):
                res = rope(pa[0:tsz, 0:256], pan, qibf[0:tsz, :], 'qibf', 4, tsz, ct_, st_, tabres)
                pt, ptn = transpose_blocks(qibf, res, 2, tsz)
                A('copy', [ptn], ['qiT'], out=qiT[:, :, q0:q0 + tsz], in_=blkview(pt, 2, tsz))
                res2 = rope(pa[0:tsz, 256:320], pan, kist[0:tsz, i, :], 'kist%d' % i, 1, tsz, ct_, st_, tabres)
                G('tensor_copy', res2, ['kibf0'], out=kibf[0:tsz, 0, :], in_=kist[0:tsz, i, :])
                G('tensor_copy', res2, ['kibf1'], out=kibf[0:tsz, 1, :], in_=kist[0:tsz, i, :])
                pt2, ptn2 = next_pt()
                T('transpose', ['kibf0', 'kibf1', 'ident'], [ptn2], out=pt2[:, 0:tsz], in_=kibf[0:tsz, :, :].rearrange("p a b -> p (a b)"),
                  identity=ident[0:tsz, 0:tsz])
                A('copy', [ptn2], ['kiT'], out=kiT[:, a:a + tsz], in_=pt2[:, 0:tsz])
                A('activation', [pan], ['wabs%d' % i], out=wabs[0:tsz, i, :], in_=pa[0:tsz, 320:324], func=AF.Abs)
                A('activation', [pan], ['wsgn%d' % i], out=wsgn[0:tsz, i, :], in_=pa[0:tsz, 320:324], func=AF.Sign)
                stores.append((res2, dict(out=S['ki_out'][l][a - past:a - past + tsz, :], in_=kist[0:tsz, i, :])))

            CK('step1')

            def gen_step2():
                yield from tm_group(C_Q, 512, ev_q)
                yield from tm_group(C_K, 512, ev_k)
                yield from tm_group(C_V, 512, ev_v)
                yield from tm_group(C_GA, 512, ev_ga)
                yield from tm_group(C_QI, 324, ev_idx)

            def fm_proj(wv, wn, j):
                pa, pan = next_pa()
                for kt in range(8):
                    T('matmul', ['hnT', wn], [pan], out=pa[:, 0:Tn], lhsT=wv[:, kt, j * 128:(j + 1) * 128], rhs=hnT[:, kt, 0:Tn],
                      start=(kt == 0), stop=(kt == 7))
                return pa, pan

            def gen_B1():
              yield
              wxb, wxbn = load_w_in(l, C_XB, 512, pin=True)
              for j in range(4):
                  pa, pan = fm_proj(wxb, wxbn, j)
                  xp_, xpn = Bxp[j % 2], 'Bxp%d' % (j % 2)
                  xc_, xcn = Bxc[j % 2], 'Bxc%d' % (j % 2)
                  A('copy', [pan], [xpn + 'b'], out=xp_[:, 3:3 + Tn], in_=pa[:, 0:Tn])
                  G('tensor_copy', ['convst%d' % j], [xpn + 'a'], out=xp_[:, 0:3], in_=convst[:, j, :])
                  G('tensor_copy', [xpn + 'a', xpn + 'b'], ['convst%d' % j], out=convst[:, j, :], in_=xp_[:, Tn:Tn + 3])
                  V('tensor_scalar', [xpn + 'a', xpn + 'b', 'cw', 'cb'], [xcn], out=xc_[:, 0:Tn], in0=xp_[:, 0:Tn], scalar1=cw[:, j, 0:1],
                    scalar2=cb[:, j:j + 1], op0=ALU.mult, op1=ALU.add)
                  for tap in range(1, 4):
                      V('scalar_tensor_tensor', [xpn + 'a', xpn + 'b', 'cw', xcn], [xcn], out=xc_[:, 0:Tn], in0=xp_[:, tap:tap + Tn],
                        scalar=cw[:, j, tap:tap + 1], in1=xc_[:, 0:Tn], op0=ALU.mult, op1=ALU.add)
                  G('tensor_copy', [xcn], ['xcb'], out=xcb[:, 0:Tn], in_=xc_[:, 0:Tn])
                  yield
                  pa1, pan1 = next_pa()
                  T('matmul', ['waBD', 'xcb'], [pan1], out=pa1[:, 0:Tn], lhsT=waBD[:, j, :], rhs=xcb[:, 0:Tn], start=True, stop=True)
                  pa2, pan2 = next_pa()
                  T('matmul', ['wxBD', 'xcb'], [pan2], out=pa2[:, 0:Tn], lhsT=wxBD[:, j, :], rhs=xcb[:, 0:Tn], start=True, stop=True)
                  A('activation', [pan1, 'hba'], ['Bta'], out=Bta[:, 0:Tn], in_=pa1[:, 0:Tn], func=AF.Tanh, bias=hba[:, j:j + 1], scale=0.5)
                  A('activation', [pan2, 'hbx'], ['Btx'], out=Btx[:, 0:Tn], in_=pa2[:, 0:Tn], func=AF.Tanh, bias=hbx[:, j:j + 1], scale=0.5)
                  A('activation', ['Bta', 'clf'], ['Ba2'], out=Ba2[:, 0:Tn], in_=Bta[:, 0:Tn], func=AF.Exp, bias=clf[:, j:j + 1], scale=clf[:, j:j + 1])
                  V('tensor_scalar', ['Ba2'], ['om'], out=om[:, j * TB:j * TB + Tn], in0=Ba2[:, 0:Tn], scalar1=-1.0, scalar2=1.0,
                    op0=ALU.mult, op1=ALU.add)
                  V('scalar_tensor_tensor', ['Btx', xcn], ['ixc'], out=ixc[:, j * TB:j * TB + Tn], in0=Btx[:, 0:Tn], scalar=1.0,
                    in1=xc_[:, 0:Tn], op0=ALU.add, op1=ALU.mult)
                  yield

            run_zip(gen_step2(), 2 * len(tiles) * 5 // 2, gen_B1(), 9)
            unpin_w()
            CK('step2')

            wgb, wgbn = load_w_in(l, C_GB, 512)
            for j in range(4):
                A('activation', ['om'], ['Baj%d' % j], out=Baj[j][:, 0:Tn], in_=om[:, j * TB:j * TB + Tn], func=AF.Sqrt, bias=1.0, scale=-1.0)
            for j in range(4):
                A('activation', ['om'], ['om'], out=om[:, j * TB:j * TB + Tn], in_=om[:, j * TB:j * TB + Tn], func=AF.Sqrt)
            for j in range(4):
                V('scalar_tensor_tensor', ['ixc', 'om'], ['Bbb'], out=Bbb[:, 0:Tn], in0=ixc[:, j * TB:j * TB + Tn], scalar=0.5,
                  in1=om[:, j * TB:j * TB + Tn], op0=ALU.mult, op1=ALU.mult)
                V('tensor_tensor_scan', ['Baj%d' % j, 'Bbb', 'hst%d' % j], ['Bh'], out=Bh[:, 0:Tn], data0=Baj[j][:, 0:Tn], data1=Bbb[:, 0:Tn],
                  initial=hst[:, j:j + 1], op0=ALU.mult, op1=ALU.add)
                G('tensor_copy', ['Bh'], ['hst%d' % j], out=hst[:, j:j + 1], in_=Bh[:, Tn - 1:Tn])
                pa, pan = fm_proj(wgb, wgbn, j)
                A('activation', [pan], ['Btg'], out=Btg[:, 0:Tn], in_=pa[:, 0:Tn], func=AF.Tanh, scale=0.5)
                V('scalar_tensor_tensor', [pan, 'Btg'], ['Btg'], out=Btg[:, 0:Tn], in0=Btg[:, 0:Tn], scalar=1.0, in1=pa[:, 0:Tn],
                  op0=ALU.add, op1=ALU.mult)
                V('scalar_tensor_tensor', ['Btg', 'Bh'], ['ybT'], out=yT[1][:, j, 0:Tn], in0=Btg[:, 0:Tn], scalar=0.5, in1=Bh[:, 0:Tn],
                  op0=ALU.mult, op1=ALU.mult)

            for r, kw in stores:
                DMA('sp', 's_' + r[0], r, (), **kw)

            def gen_branchC():
                wxc, wxcn = load_w_in(l, C_XC, 512)
                wgc, wgcn = load_w_in(l, C_GC, 512)
                L = 15 + Tn
                for g in range(4):
                    wdw = 2 ** (g + 1)
                    pa, pan = fm_proj(wxc, wxcn, g)
                    G('tensor_copy', ['poolst%d' % g], ['ct0a'], out=ctmp[:, 0, 0:15], in_=poolst[:, g, :])
                    A('copy', [pan], ['ct0b'], out=ctmp[:, 0, 15:L], in_=pa[:, 0:Tn])
                    G('tensor_copy', ['ct0a', 'ct0b'], ['poolst%d' % g], out=poolst[:, g, :], in_=ctmp[:, 0, Tn:L])
                    prev, prevn = 0, ['ct0a', 'ct0b']
                    m = 1
                    slot = 1
                    while m < wdw:
                        G('tensor_tensor', prevn, ['ct%d' % slot], out=ctmp[:, slot, 2 * m - 1:L], in0=ctmp[:, prev, 2 * m - 1:L],
                          in1=ctmp[:, prev, m - 1:L - m], op=ALU.add)
                        prev, prevn = slot, ['ct%d' % slot]
                        slot = 1 + (slot % 3)
                        m *= 2
                    yield
                    if blk['first']:
                        V('tensor_tensor', prevn + ['rcnt'], ['ctg'], out=ctg[:, 0:Tn], in0=ctmp[:, prev, 15:L], in1=rcnt[:, g, 0:Tn], op=ALU.mult)
                        V('tensor_tensor', ['ctg', 'ct0b'], ['cpl'], out=cpl[:, 0:Tn], in0=ctg[:, 0:Tn], in1=ctmp[:, 0, 15:L], op=ALU.subtract)
                    else:
                        V('scalar_tensor_tensor', prevn + ['ct0b'], ['cpl'], out=cpl[:, 0:Tn], in0=ctmp[:, prev, 15:L], scalar=1.0 / wdw,
                          in1=ctmp[:, 0, 15:L], op0=ALU.mult, op1=ALU.subtract)
                    pa1, pan1 = next_pa()
                    T('matmul', ['pwB', 'cpl'], [pan1], out=pa1[:, 0:Tn], lhsT=pwB[:, g, :], rhs=cpl[:, 0:Tn], start=True, stop=True)
                    pa2, pan2 = fm_proj(wgc, wgcn, g)
                    A('activation', [pan2], ['ctg'], out=ctg[:, 0:Tn], in_=pa2[:, 0:Tn], func=AF.Tanh, scale=0.5)
                    V('scalar_tensor_tensor', [pan2, 'ctg'], ['ctg'], out=ctg[:, 0:Tn], in0=ctg[:, 0:Tn], scalar=1.0, in1=pa2[:, 0:Tn],
                      op0=ALU.add, op1=ALU.mult)
                    V('tensor_scalar', ['ctg', 'psc'], ['ctg'], out=ctg[:, 0:Tn], in0=ctg[:, 0:Tn], scalar1=psc[:, g:g + 1], scalar2=0.5,
                      op0=ALU.mult, op1=ALU.mult)
                    V('tensor_tensor', ['ctg', pan1], ['ycT'], out=yT[2][:, g, 0:Tn], in0=ctg[:, 0:Tn], in1=pa1[:, 0:Tn], op=ALU.mult)
                    yield

            def rec_scores(i):
                q0, tsz, a, kb, hm, tix = tiles[i]
                Sv = a + tsz
                for c0 in range(0, Sv, 512):
                    n = min(512, Sv - c0)
                    for h in range(4):
                        hs = slice((h % 2) * 64, (h % 2) * 64 + 64)
                        pa, pan = next_pa()
                        T('matmul', ['qiT', 'kiT'], [pan], out=pa[0:tsz, 0:n], lhsT=qiT[hs, h // 2, q0:q0 + tsz], rhs=kiT[hs, c0:c0 + n],
                          start=True, stop=True)
                        r_, rn = rl[h % NRL], 'rl%d' % (h % NRL)
                        A('activation', [pan, 'wabs%d' % i], [rn], out=r_[0:tsz, 0:n], in_=pa[0:tsz, 0:n], func=AF.Relu, scale=wabs[0:tsz, i, h:h + 1])
                        if h == 0:
                            V('tensor_scalar', [rn, 'wsgn%d' % i], ['sc'], out=sc[0:tsz, c0:c0 + n], in0=r_[0:tsz, 0:n], scalar1=wsgn[0:tsz, i, 0:1],
                              scalar2=None, op0=ALU.mult)
                        else:
                            V('scalar_tensor_tensor', [rn, 'wsgn%d' % i, 'sc'], ['sc'], out=sc[0:tsz, c0:c0 + n], in0=r_[0:tsz, 0:n],
                              scalar=wsgn[0:tsz, i, h:h + 1], in1=sc[0:tsz, c0:c0 + n], op0=ALU.mult, op1=ALU.add)

            def gen_bisect(i):
                q0, tsz, a, kb, hm, tix = tiles[i]
                Sv = a + tsz
                V('tensor_reduce', ['sc'], ['bs0'], out=bs[0:tsz, 0:1], in_=sc[0:tsz, 0:Sv], axis=AX.X, op=ALU.min)
                V('tensor_reduce', ['sc'], ['bs1'], out=bs[0:tsz, 1:2], in_=sc[0:tsz, 0:Sv], axis=AX.X, op=ALU.max)
                if hm:
                    G('memset', ['bs0', 'bs1'], ['sc'], ap=sc[0:64, Sv - 64:Sv], constant=NEG)
                V('tensor_scalar', ['bs0', 'bs1'], ['bs2'], out=bs[0:tsz, 2:3], in0=bs[0:tsz, 1:2], scalar1=bs[0:tsz, 0:1], scalar2=1.003,
                  op0=ALU.subtract, op1=ALU.mult)
                V('scalar_tensor_tensor', ['bs0', 'bs2'], ['bs6'], out=bs[0:tsz, 6:7], in0=bs[0:tsz, 2:3], scalar=-0.001, in1=bs[0:tsz, 0:1],
                  op0=ALU.mult, op1=ALU.add)
                V('tensor_scalar', ['bs2', 'hpow'], ['steps'], out=steps[0:tsz, :], in0=hpow[0:tsz, :], scalar1=bs[0:tsz, 2:3], scalar2=None, op0=ALU.mult)
                V('tensor_tensor', ['bs6', 'steps'], ['mid'], out=bs[0:tsz, 3:4], in0=bs[0:tsz, 6:7], in1=steps[0:tsz, 1:2], op=ALU.add)
                yield
                cA = int(Sv * (0.46 if i == 0 else 0.52)) // 2 * 2 if Sv >= 512 else Sv
                nA = Sv - cA
                thr = TOPK - nA / 2.0
                cc = thr - 0.25 - cA
                for k in range(1, NIT + 1):
                    kk = k + 1 if k < NIT else k
                    V('tensor_scalar', ['sc', 'mid'], ['mkV', 'cnt'], out=mk[0:tsz, 0:cA], in0=sc[0:tsz, 0:cA], scalar1=bs[0:tsz, 3:4], scalar2=cc,
                      op0=ALU.is_lt, op1=ALU.add, accum_out=bs[0:tsz, 4:5])
                    if nA:
                        A('activation', ['sc', 'mid'], ['mkA', 'cntA'], out=mk[0:tsz, cA:Sv], in_=sc[0:tsz, cA:Sv], func=AF.Sign, bias=bs[0:tsz, 3:4],
                          scale=-1.0, accum_out=bs[0:tsz, 7:8])
                    V('tensor_scalar', ['mid', 'steps'], ['bu'], out=bs[0:tsz, 9:10], in0=bs[0:tsz, 3:4], scalar1=steps[0:tsz, kk:kk + 1], scalar2=None,
                      op0=ALU.subtract)
                    if nA:
                        V('scalar_tensor_tensor', ['cnt', 'cntA'], ['bd'], out=bs[0:tsz, 5:6], in0=bs[0:tsz, 7:8], scalar=-0.5, in1=bs[0:tsz, 4:5],
                          op0=ALU.mult, op1=ALU.is_ge)
                    else:
                        V('tensor_scalar', ['cnt'], ['bd'], out=bs[0:tsz, 5:6], in0=bs[0:tsz, 4:5], scalar1=0.0, scalar2=None, op0=ALU.is_le)
                    V('scalar_tensor_tensor', ['bd', 'steps', 'bu'], ['mid'], out=bs[0:tsz, 3:4], in0=bs[0:tsz, 5:6], scalar=steps[0:tsz, k:k + 1],
                      in1=bs[0:tsz, 9:10], op0=ALU.mult, op1=ALU.add)
                    yield
                V('tensor_scalar', ['sc', 'mid'], ['mk'], out=mk[0:tsz, 0:Sv], in0=sc[0:tsz, 0:Sv], scalar1=bs[0:tsz, 3:4], scalar2=None, op0=ALU.is_ge)
                yield

            def mask_bias(i):
                return len(tiles) == 2 and i == 0

            def rec_masktrans(i):
                q0, tsz, a, kb, hm, tix = tiles[i]
                Sv = a + tsz
                kbs = [b for b in S['kblocks'] if b[0] + b[1] <= Sv]
                for g0 in range(0, len(kbs), 8):
                    grp = kbs[g0:g0 + 8]
                    pt, ptn = next_pt()
                    for s_, (c0, kr, kbi) in enumerate(grp):
                        T('transpose', ['mk', 'ident'], [ptn], out=pt[0:kr, s_ * 128:s_ * 128 + tsz], in_=mk[0:tsz, c0:c0 + kr], identity=ident[0:tsz, 0:tsz])
                    if mask_bias(i):
                        kwm = dict(func=AF.Identity, scale=30000.0, bias=-30000.0)
                    else:
                        kwm = dict(func=AF.Identity)
                    if all(x[1] == 128 for x in grp):
                        A('activation', [ptn], ['mkT'], out=mkT[:, g0:g0 + len(grp), 0:tsz], in_=blkview(pt, len(grp), tsz), **kwm)
                    else:
                        for s_, (c0, kr, kbi) in enumerate(grp):
                            A('activation', [ptn], ['mkT'], out=mkT[0:kr, g0 + s_, 0:tsz], in_=pt[0:kr, s_ * 128:s_ * 128 + tsz], **kwm)

            def gen_attention(i, npool=4):
                q0, tsz, a, kb, hm, tix = tiles[i]
                Sv = a + tsz
                kbs = [b for b in S['kblocks'] if b[0] + b[1] <= Sv]
                nkb = len(kbs)
                MB = mask_bias(i)

                lbufs = [([psA[0], psA[1]], ['psA0', 'psA1']), ([psA[2], psA[3]], ['psA2', 'psA3'])]
                NLB = len(lbufs)

                def logits(bi):
                    c0, kr, kbi = kbs[bi]
                    pl, pln = lbufs[bi % NLB]
                    for hp in range(4):
                        hb = hp // 2
                        o = (hp % 2) * 256
                        if tsz == 128:
                            T('matmul', ['kT', 'qT'], [pln[hb]], out=pl[hb][0:kr, o:o + 256],
                              lhsT=kT[:, hp, c0:c0 + kr], rhs=qT[:, hp, :, q0:q0 + tsz], start=(hp % 2 == 0), stop=(not MB and hp % 2 == 1))
                        else:
                            for w in range(2):
                                T('matmul', ['kT', 'qT'], [pln[hb]], out=pl[hb][0:kr, o + w * 128:o + w * 128 + tsz],
                                  lhsT=kT[:, hp, c0:c0 + kr], rhs=qT[:, hp, w, q0:q0 + tsz], start=(hp % 2 == 0 and w == 0),
                                  stop=(not MB and hp % 2 == 1 and w == 1))
                    for hb in (range(2) if MB else ()):
                        if tsz == 128:
                            T('matmul', ['mkT', 'ident'], [pln[hb]], out=pl[hb][0:kr, :], lhsT=ident[0:kr, 0:kr],
                              rhs=mkT[0:kr, bi, 0:tsz].unsqueeze(1).to_broadcast([kr, 4, tsz]), start=False, stop=True)
                        else:
                            for s4 in range(4):
                                T('matmul', ['mkT', 'ident'], [pln[hb]], out=pl[hb][0:kr, s4 * 128:s4 * 128 + tsz], lhsT=ident[0:kr, 0:kr],
                                  rhs=mkT[0:kr, bi, 0:tsz], start=False, stop=(s4 == 3))

                def pv(bi):
                    c0, kr, kbi = kbs[bi]
                    px, pxn = pex[bi % 2], 'pex%d' % (bi % 2)
                    pdeps = [pxn + '0', pxn + '1']
                    if not MB:
                        px = pm[bi % 2]
                        pdeps = ['pm%da' % (bi % 2), 'pm%db' % (bi % 2)]
                    for h in range(8):
                        T('matmul', pdeps + ['Vt'], ['psB%d' % (h // 4)], out=psB[h // 4][0:tsz, (h % 4) * 65:(h % 4) * 65 + 65], lhsT=px[0:kr, h, 0:tsz],
                          rhs=Vt[0:kr, kbi, h, :], start=(bi == 0 and h % 4 == 0), stop=(bi == nkb - 1 and h % 4 == 3))
                for b0 in range(min(NLB, nkb)):
                    logits(b0)
                for bi, (c0, kr, kbi) in enumerate(kbs):
                    par = bi % 2
                    pl, pln = lbufs[bi % NLB]
                    px, pxn = pex[par], 'pex%d' % par
                    for hh in range(2):
                        A('activation', [pln[hh]], [pxn + str(hh)], out=px[0:kr, hh * 4:hh * 4 + 4, 0:tsz],
                          in_=pl[hh][0:kr, :].rearrange("p (h t) -> p h t", h=4)[:, :, 0:tsz], func=AF.Exp, scale=0.125)
                    if not MB:
                        pm_ = pm[par]
                        mbp = mkT[0:kr, bi, 0:tsz].unsqueeze(1).to_broadcast([kr, npool, tsz])
                        mbv = mkT[0:kr, bi, 0:tsz].unsqueeze(1).to_broadcast([kr, 8 - npool, tsz])
                        G('tensor_tensor', [pxn + '0', pxn + '1', 'mkT'], ['pm%da' % par], out=pm_[0:kr, 0:npool, 0:tsz], in0=px[0:kr, 0:npool, 0:tsz], in1=mbp, op=ALU.mult)
                        V('tensor_tensor', [pxn + '0', pxn + '1', 'mkT'], ['pm%db' % par], out=pm_[0:kr, npool:8, 0:tsz], in0=px[0:kr, npool:8, 0:tsz], in1=mbv, op=ALU.mult)
                    if bi >= 1:
                        pv(bi - 1)
                    yield
                    if bi + NLB < nkb:
                        logits(bi + NLB)
                pv(nkb - 1)

            def rec_attn_final(i):
                q0, tsz, a, kb, hm, tix = tiles[i]
                for hh in range(2):
                    pv = psB[hh][0:tsz, 0:260].rearrange("p (h e) -> p h e", h=4)
                    V('tensor_scalar', ['psB%d' % hh], ['rec%d' % hh], out=rec[0:tsz, hh * 4:hh * 4 + 4], in0=pv[:, :, 64], scalar1=2.0, scalar2=None,
                      op0=ALU.mult)
                    V('reciprocal', ['rec%d' % hh], ['rec%d' % hh], out=rec[0:tsz, hh * 4:hh * 4 + 4], in_=rec[0:tsz, hh * 4:hh * 4 + 4])
                    V('tensor_tensor', ['psB%d' % hh, 'rec%d' % hh], ['att%d' % hh], out=att[0:tsz, hh * 4:hh * 4 + 4, :], in0=pv[:, :, 0:64],
                      in1=rec[0:tsz, hh * 4:hh * 4 + 4].unsqueeze(2).to_broadcast([tsz, 4, 64]), op=ALU.mult)
                G('tensor_tensor', ['att0', 'att1', 'sga%d' % i], ['yab'], out=yab[0:tsz, :], in0=att[0:tsz, :, :].rearrange("p h d -> p (h d)"),
                  in1=sga[0:tsz, i, :], op=ALU.mult)
                pt, ptn = transpose_blocks(yab, ['yab'], 4, tsz)
                A('copy', [ptn], ['yaT'], out=yT[0][:, :, q0:q0 + tsz], in_=blkview(pt, 4, tsz))

            ynames = ['yaT', 'ybT', 'ycT']

            def gen_merge(ns, first_n, last_n, banks=None):
                bc = [0]

                def bank():
                    if banks is None:
                        return next_pa()
                    b = banks[bc[0] % len(banks)]
                    bc[0] += 1
                    return b
                for n in ns:
                    wb_, wbn = load_w_bo(l, n)
                    for e4 in range(2):
                        wg_, wgn = load_w_in(l, C_GM + n * 1024 + e4 * 512, 512)
                        for ee in range(4):
                            e = e4 * 4 + ee
                            pa, pan = bank()
                            for kt in range(8):
                                T('matmul', ['hnT', wgn], [pan], out=pa[:, 0:Tn], lhsT=wg_[:, kt, ee * 128:(ee + 1) * 128], rhs=hnT[:, kt, 0:Tn],
                                  start=(kt == 0), stop=(kt == 7))
                            pb, pbn = bank()
                            for ct in range(4):
                                T('matmul', [wbn, ynames[n]], [pbn], out=pb[:, 0:Tn], lhsT=wb_[:, ct, e * 128:(e + 1) * 128], rhs=yT[n][:, ct, 0:Tn],
                                  start=(ct == 0), stop=(ct == 3))
                            tg_, tgn = mtg[e % 2], 'mtg%d' % (e % 2)
                            A('activation', [pan], [tgn], out=tg_[:, 0:Tn], in_=pa[:, 0:Tn], func=AF.Tanh, scale=0.5)
                            mge = mg[:, e * TB:e * TB + Tn]
                            if n == first_n:
                                V('scalar_tensor_tensor', [tgn, pbn], ['mg'], out=mge, in0=tg_[:, 0:Tn], scalar=1.0, in1=pb[:, 0:Tn], op0=ALU.add, op1=ALU.mult)
                            else:
                                V('scalar_tensor_tensor', [tgn, pbn], [tgn], out=tg_[:, 0:Tn], in0=tg_[:, 0:Tn], scalar=1.0, in1=pb[:, 0:Tn],
                                  op0=ALU.add, op1=ALU.mult)
                                if n != last_n:
                                    G('tensor_tensor', [tgn, 'mg'], ['mg'], out=mge, in0=mge, in1=tg_[:, 0:Tn], op=ALU.add)
                                else:
                                    G('tensor_tensor', [tgn, 'mg'], ['mgb'], out=mgb[:, e * TB:e * TB + Tn], in0=mge, in1=tg_[:, 0:Tn], op=ALU.add)
                            yield

            psM = [(psT[0][:, :].bitcast(F32), 'psT0'), (psT[1][:, :].bitcast(F32), 'psT1')]

            def nkb_of(i):
                q0, tsz, a, kb, hm, tix = tiles[i]
                return len([b for b in S['kblocks'] if b[0] + b[1] <= a + tsz])

            rec_scores(0)
            run_zip(gen_branchC(), 8, gen_bisect(0), NIT + 2)
            rec_masktrans(0)
            CK('step3')
            if len(tiles) == 2:
                rec_scores(1)
                run_zip(gen_attention(0, 5), nkb_of(0), gen_bisect(1), NIT + 2)
                rec_attn_final(0)
                rec_masktrans(1)
                run_zip(gen_attention(1, 3), nkb_of(1), gen_merge([1, 2], 1, 0, psM), 16)
                rec_attn_final(1)
            else:
                run_zip(gen_attention(0, 3), nkb_of(0), gen_merge([1, 2], 1, 0, psM), 16)
                rec_attn_final(0)
            CK('step4')
            drain(gen_merge([0], 1, 0))
            for half in range(2):
                wo_, won = load_w_out(l, half)
                for i, (q0, tsz, a, kb, hm, tix) in enumerate(tiles):
                    pa, pan = next_pa()
                    for et in range(8):
                        T('matmul', ['mgb', won], [pan], out=pa[0:tsz, :], lhsT=mgb[:, et * TB + q0:et * TB + q0 + tsz], rhs=wo_[:, et, :],
                          start=(et == 0), stop=(et == 7))
                    xs_ = xt[i][0:tsz, half * 512:(half + 1) * 512]
                    V('scalar_tensor_tensor', [pan, 'xt%d' % i], ['xt%d' % i], out=xs_, in0=pa[0:tsz, :], scalar=0.5, in1=xs_, op0=ALU.mult, op1=ALU.add)
            for i, (q0, tsz, a, kb, hm, tix) in enumerate(tiles):
                xr = 'xt%d' % i
                if not last_layer:
                    DMA('act', 's_' + xr, [xr], [blk['xres'][l + 1]], out=xdst[q0:q0 + tsz, :], in_=xt[i][0:tsz, :])
                elif xdst is not None:
                    rms_rstd(i, tsz, xr)
                    V('scalar_tensor_tensor', [xr, 'rs%d' % i, 'gfbc'], [xr], out=xt[i][0:tsz, :], in0=xt[i][0:tsz, :], scalar=st_rs[0:tsz, i:i + 1],
                      in1=gfbc[0:tsz, :], op0=ALU.mult, op1=ALU.mult)
                    DMA('act', 's_' + xr, [xr], (), out=xdst[q0:q0 + tsz, :], in_=xt[i][0:tsz, :])

        def make_prompt():
            S = {'past': 0, 'tabs': (cosP, sinP), 'tabres': 'ropeP',
                 'k_out': [k_p[0], k_p[1]], 'v_out': [v_p[0], v_p[1]], 'ki_out': [ki_p[0], ki_p[1]]}
            S['kblocks'] = [(0, 16, 0)] + [(16 + 128 * j, 128, 1 + j) for j in range(32)]
            blocks = [{'T': 16, 'tiles': [(0, 16, 0, 0, False, 0)], 'first': True,
                       'xsrc': [xp_in[0:16, :], xscr_p[0:16, :]], 'xdst': [xscr_p[0:16, :], None], 'xres': ['xin', 'xscrp_m', 'none']}]
            nb = debug.get('nblk', SEQ // TB)
            for b in range(nb):
                a0 = 16 + b * TB
                tiles = []
                for i in range(TB // 128):
                    a = a0 + i * 128
                    kb = 1 + (a - 16) // 128
                    tiles.append((i * 128, 128, a, kb, True, kb))
                blocks.append({'T': TB, 'tiles': tiles, 'first': False,
                               'xsrc': [xp_in[a0:a0 + TB, :], xscr_p[a0:a0 + TB, :]],
                               'xdst': [xscr_p[a0:a0 + TB, :], y_p[a0 - 16:a0 - 16 + TB, :]], 'xres': ['xin', 'xscrp_%d' % b, 'none']})
            S['blocks'] = blocks
            return S

        def make_sample(si):
            S = {'past': PAST, 'tabs': (cosS, sinS), 'tabres': 'ropeS',
                 'k_out': [k_s[0, si], k_s[1, si]], 'v_out': [v_s[0, si], v_s[1, si]], 'ki_out': [ki_s[0, si], ki_s[1, si]]}
            S['kblocks'] = [(128 * j, 128, j) for j in range(16)] + [(PAST, 64, 16)]
            S['blocks'] = [{'T': TS, 'tiles': [(0, TS, PAST, 16, False, 0)], 'first': False,
                            'xsrc': [xs_in[si], xscr_s[si]], 'xdst': [xscr_s[si], y_s[si]], 'xres': ['xin', 'xscrs_%d' % si, 'none']}]
            return S

        st_names = ['convst%d' % j for j in range(4)] + ['hst%d' % j for j in range(4)] + ['poolst%d' % j for j in range(4)]

        def zero_states():
            G('memset', (), ['convst%d' % j for j in range(4)], ap=convst[:], constant=0.0)
            G('memset', (), ['hst%d' % j for j in range(4)], ap=hst[:], constant=0.0)
            G('memset', (), ['poolst%d' % j for j in range(4)], ap=poolst[:], constant=0.0)

        def load_states(l, si):
            for j in range(4):
                small_dma(convst[:, j, :], sconv_in[l, si][:, j * 128:(j + 1) * 128].rearrange("t p -> p t"), (), ['convst%d' % j])
                small_dma(poolst[:, j, :], spool_in[l, si][:, j * 128:(j + 1) * 128].rearrange("t p -> p t"), (), ['poolst%d' % j])
            small_dma(hst[:], fm(slru_in[l, si]), (), ['hst%d' % j for j in range(4)])

        def store_states(conv_o, lru_o, pool_o):
            for j in range(4):
                small_dma(conv_o[:, j * 128:(j + 1) * 128].rearrange("t p -> p t"), convst[:, j, :], ['convst%d' % j], ())
                small_dma(pool_o[:, j * 128:(j + 1) * 128].rearrange("t p -> p t"), poolst[:, j, :], ['poolst%d' % j], ())
            small_dma(fm(lru_o), hst[:], ['hst%d' % j for j in range(4)], ())

        def load_cache(l, si):
            for j in range(16):
                sl = slice(j * 128, (j + 1) * 128)
                rk, rkn = rl[j % 2], 'rl%d' % (j % 2)
                rv, rvn = rl[2 + j % 2], 'rl%d' % (2 + j % 2)
                kb_, kbn = (kbf, 'kbf') if j % 2 == 0 else (qbf, 'qbf')
                DMA('sp', rkn, (), [rkn], out=rk[:], in_=ck_in[l, si, sl, :])
                DMA('sp', rvn, (), [rvn], out=rv[:], in_=cv_in[l, si, sl, :])
                DMA('sp', 'kif', (), ['kif'], out=kif[:], in_=cki_in[l, si, sl, :])
                V('tensor_copy', [rkn], [kbn], out=kb_[:], in_=rk[:])
                pt, ptn = transpose_blocks(kb_, [kbn], 4, 128)
                A('copy', [ptn], ['kT'], out=kT[:, :, sl], in_=blkview(pt, 4, 128))
                V('tensor_copy', [rvn], ['Vt'], out=Vt[:, j, :, 0:64], in_=rv[:].rearrange("p (h d) -> p h d", h=8))
                A('copy', ['kif'], ['kibf0'], out=kibf[:, 0, :], in_=kif[:])
                V('tensor_copy', ['kif'], ['kibf1'], out=kibf[:, 1, :], in_=kif[:])
                pt2, ptn2 = next_pt()
                T('transpose', ['kibf0', 'kibf1', 'ident'], [ptn2], out=pt2[:, 0:128], in_=kibf[:, :, :].rearrange("p a b -> p (a b)"), identity=ident[:, :])
                A('copy', [ptn2], ['kiT'], out=kiT[:, sl], in_=pt2[:, 0:128])

        nlayers = debug.get('nlayers', 2)
        if not debug.get('skip_prompt'):
            Sp = make_prompt()
            for l in range(nlayers):
                layer_setup(l)
                zero_states()
                for bix, blk in enumerate(Sp['blocks']):
                    process_block(Sp, l, blk, l == 1)
                    if l == 0 and bix == min(2, len(Sp['blocks']) - 1):
                        cast_weights(1)
                store_states(conv_p[l], lru_p[l], pool_p[l])
        if not debug.get('skip_sample'):
            for si in range(debug.get('nsample', 2)):
                Ss = make_sample(si)
                for l in range(nlayers):
                    layer_setup(l)
                    CK('setup')
                    load_states(l, si)
                    CK('states')
                    load_cache(l, si)
                    CK('cache')
                    for blk in Ss['blocks']:
                        process_block(Ss, l, blk, l == 1)
                    store_states(conv_s[l, si], lru_s[l, si], pool_s[l, si])
        P.emit()
        build_program.stats = P.stats
    return nc


_CACHE = {}


def kernel(x_prompt, x_sample, cache_k, cache_v, cache_kidx, state_conv, state_lru, state_pool,
           meta_tokens, norm_g, w_in, conv_w, conv_b, lru_wa, lru_ba, lru_wx, lru_bx, lru_lambda,
           pool_w, pool_scale, w_branch_out, w_out, final_norm_g):
    if 'nc' not in _CACHE:
        _CACHE['nc'] = build_program()
    nc = _CACHE['nc']
    in_maps = _make_in_maps(x_prompt, x_sample, cache_k, cache_v, cache_kidx, state_conv, state_lru, state_pool,
                            meta_tokens, norm_g, w_in, conv_w, conv_b, lru_wa, lru_ba, lru_wx, lru_bx, lru_lambda,
                            pool_w, pool_scale, w_branch_out, w_out, final_norm_g)
    res = run_bass_kernel_spmd(nc, in_maps, core_ids=list(range(8)))
    return _assemble(res.results)


def _make_in_maps(x_prompt, x_sample, cache_k, cache_v, cache_kidx, state_conv, state_lru, state_pool,
                  meta_tokens, norm_g, w_in, conv_w, conv_b, lru_wa, lru_ba, lru_wx, lru_bx, lru_lambda,
                  pool_w, pool_scale, w_branch_out, w_out, final_norm_g):
    f = lambda a: np.ascontiguousarray(np.asarray(a, dtype=np.float32))
    x_prompt = f(x_prompt); x_sample = f(x_sample); meta = f(meta_tokens)
    ck = f(cache_k).reshape(2, 16, PAST, 512)
    cv = f(cache_v).reshape(2, 16, PAST, 512)
    cki = f(cache_kidx)
    sconv = f(state_conv); slru = f(state_lru); spool = f(state_pool)
    posp = np.zeros((128, NKB), np.float32)
    posp[:, 0] = np.arange(128)
    for j in range(1, NKB):
        posp[:, j] = 16 + 128 * (j - 1) + np.arange(128)
    poss = (PAST + np.arange(128, dtype=np.float32)).reshape(128, 1).astype(np.float32)
    shared = {
        "posp": posp, "poss": poss, "norm_g": f(norm_g), "w_in": f(w_in), "conv_w": f(conv_w), "conv_b": f(conv_b),
        "lru_wa": f(lru_wa), "lru_ba": f(lru_ba), "lru_wx": f(lru_wx), "lru_bx": f(lru_bx), "lru_lambda": f(lru_lambda),
        "pool_w": f(pool_w), "pool_scale": f(pool_scale), "w_branch_out": f(w_branch_out), "w_out": f(w_out),
        "final_norm_g": f(final_norm_g),
    }
    in_maps = []
    for c in range(8):
        b = c % 4
        ss = [2 * c, 2 * c + 1]
        m = dict(shared)
        m["xp"] = np.ascontiguousarray(np.concatenate([meta, x_prompt[b]], axis=0))
        m["xs"] = np.ascontiguousarray(x_sample[ss])
        m["ck"] = np.ascontiguousarray(ck[:, ss])
        m["cv"] = np.ascontiguousarray(cv[:, ss])
        m["cki"] = np.ascontiguousarray(cki[:, ss])
        m["sconv"] = np.ascontiguousarray(sconv[:, ss])
        m["slru"] = np.ascontiguousarray(slru[:, ss])
        m["spool"] = np.ascontiguousarray(spool[:, ss])
        in_maps.append(m)
    return in_maps


def _assemble(R):
    y_prompt = np.stack([R[b]["y_p"] for b in range(4)], axis=0)
    y_sample = np.concatenate([R[c]["y_s"] for c in range(8)], axis=0)
    k_prompt = np.stack([R[b]["k_p"] for b in range(4)], axis=1).reshape(2, 4, TP, 8, 64)
    v_prompt = np.stack([R[b]["v_p"] for b in range(4)], axis=1).reshape(2, 4, TP, 8, 64)
    ki_prompt = np.stack([R[b]["ki_p"] for b in range(4)], axis=1)
    conv_prompt = np.stack([R[b]["conv_p"] for b in range(4)], axis=1)
    lru_prompt = np.stack([R[b]["lru_p"] for b in range(4)], axis=1)
    pool_prompt = np.stack([R[b]["pool_p"] for b in range(4)], axis=1)
    k_sample = np.concatenate([R[c]["k_s"] for c in range(8)], axis=1).reshape(2, 16, TS, 8, 64)
    v_sample = np.concatenate([R[c]["v_s"] for c in range(8)], axis=1).reshape(2, 16, TS, 8, 64)
    ki_sample = np.concatenate([R[c]["ki_s"] for c in range(8)], axis=1)
    conv_sample = np.concatenate([R[c]["conv_s"] for c in range(8)], axis=1)
    lru_sample = np.concatenate([R[c]["lru_s"] for c in range(8)], axis=1)
    pool_sample = np.concatenate([R[c]["pool_s"] for c in range(8)], axis=1)
    outs = (y_prompt, y_sample, k_prompt, v_prompt, ki_prompt, conv_prompt, lru_prompt, pool_prompt,
            k_sample, v_sample, ki_sample, conv_sample, lru_sample, pool_sample)
    return tuple(np.ascontiguousarray(o, dtype=np.float32) for o in outs)
```
